# Optimizing a Trainium2 kernel written in Bass

```python
import math
import jax, jax.numpy as jnp
from jax import lax
import numpy as np

D_MODEL = 4096
BATCH = 4
SEQ = 2048
DEPTH = 1
DEC_BATCH = 128
DEC_SEQ = 8
PAST_LEN = 16384
PAGE_SIZE = 128

N_META = 16
RET_WIDTH = D_MODEL // 2
S5_WIDTH = D_MODEL - RET_WIDTH
RET_HEADS = 8
RET_HEAD_DIM = RET_WIDTH // RET_HEADS
RET_CHUNK = 128
ROPE_BASE = 10000.0
S5_GROUP = 16
S5_GROUPS = S5_WIDTH // S5_GROUP
S5_STATE = 64
D_FF = -(-8 * D_MODEL // (3 * 256)) * 256
IN_COLS = 4 * RET_WIDTH + S5_WIDTH
MIX_WIDTH = RET_WIDTH + S5_WIDTH
EPS = 1e-6
GN_EPS = 1e-5

kernel_name = "hymba_retention_s5_step"


def rms_norm(x, g):
    x32 = x.astype(jnp.float32)
    y = x32 * lax.rsqrt(jnp.mean(x32 * x32, axis=-1, keepdims=True) + EPS)
    return (y * g.astype(jnp.float32)).astype(x.dtype)


def retention_log_gamma():
    return jnp.log(1.0 - 2.0 ** (-5.0 - jnp.arange(RET_HEADS, dtype=jnp.float32)))


def rotary(x, pos):
    half = RET_HEAD_DIM // 2
    inv = ROPE_BASE ** (-jnp.arange(half, dtype=jnp.float32) / half)
    ang = pos.astype(jnp.float32)[:, None] * inv[None, :]
    cos = jnp.cos(ang)[None, :, None, :]
    sin = jnp.sin(ang)[None, :, None, :]
    x32 = x.astype(jnp.float32)
    x1, x2 = x32[..., :half], x32[..., half:]
    return jnp.concatenate([x1 * cos - x2 * sin, x1 * sin + x2 * cos], axis=-1)


def retention_chunk(S, q, k, v, log_gamma):
    L = q.shape[2]
    idx = jnp.arange(L, dtype=jnp.float32)
    lg = log_gamma[:, None]
    diff = idx[:, None] - idx[None, :]
    mask = jnp.where(diff[None] >= 0,
                     jnp.exp(jnp.maximum(diff, 0.0)[None] * lg[:, :, None]), 0.0)
    scores = jnp.einsum('bhid,bhjd->bhij', q, k) * mask[None]
    intra = jnp.einsum('bhij,bhjv->bhiv', scores, v)
    q_dec = q * jnp.exp(lg * (idx + 1.0)[None, :])[None, :, :, None]
    inter = jnp.einsum('bhid,bhdv->bhiv', q_dec, S)
    k_dec = k * jnp.exp(lg * (L - 1.0 - idx)[None, :])[None, :, :, None]
    S_new = jnp.exp(log_gamma * L)[None, :, None, None] * S + jnp.einsum('bhjd,bhjv->bhdv', k_dec, v)
    return intra + inter, S_new


def retention_seq(S, q, k, v, lead, log_gamma):
    o0, S = retention_chunk(S, q[:, :, :lead], k[:, :, :lead], v[:, :, :lead], log_gamma)
    rest = q.shape[2] - lead
    if rest == 0:
        return o0, S
    n = rest // RET_CHUNK
    b, h = q.shape[0], q.shape[1]

    def to_chunks(t):
        return jnp.moveaxis(t[:, :, lead:].reshape(b, h, n, RET_CHUNK, t.shape[-1]), 2, 0)

    def step(S_c, xs):
        o, S_n = retention_chunk(S_c, xs[0], xs[1], xs[2], log_gamma)
        return S_n, o

    S, o_rest = lax.scan(step, S, (to_chunks(q), to_chunks(k), to_chunks(v)))
    o_rest = jnp.moveaxis(o_rest, 0, 2).reshape(b, h, rest, RET_HEAD_DIM)
    return jnp.concatenate([o0, o_rest], axis=2), S


def head_group_norm(o, g):
    mu = jnp.mean(o, axis=-1, keepdims=True)
    var = jnp.mean(jnp.square(o - mu), axis=-1, keepdims=True)
    y = (o - mu) * lax.rsqrt(var + GN_EPS)
    b, h, l, d = o.shape
    y = jnp.transpose(y, (0, 2, 1, 3)).reshape(b, l, h * d)
    return y * g.astype(jnp.float32)


def s5_discretize(lam_re, lam_im, log_dt, b_re, b_im):
    lam_re = lam_re.astype(jnp.float32)
    lam_im = lam_im.astype(jnp.float32)
    dt = jnp.exp(log_dt.astype(jnp.float32))[:, None]
    mag = jnp.exp(lam_re * dt)
    ar = mag * jnp.cos(lam_im * dt)
    ai = mag * jnp.sin(lam_im * dt)
    nr, ni = ar - 1.0, ai
    den = lam_re * lam_re + lam_im * lam_im
    fr = (nr * lam_re + ni * lam_im) / den
    fi = (ni * lam_re - nr * lam_im) / den
    b_re = b_re.astype(jnp.float32)
    b_im = b_im.astype(jnp.float32)
    bbr = fr[..., None] * b_re - fi[..., None] * b_im
    bbi = fr[..., None] * b_im + fi[..., None] * b_re
    return ar, ai, bbr, bbi


def s5_scan(u, x0_re, x0_im, ar, ai, bbr, bbi):
    bu_re = jnp.einsum('blgc,gpc->blgp', u, bbr)
    bu_im = jnp.einsum('blgc,gpc->blgp', u, bbi)
    bu_re = bu_re.at[:, 0].add(ar * x0_re - ai * x0_im)
    bu_im = bu_im.at[:, 0].add(ar * x0_im + ai * x0_re)
    a_re = jnp.broadcast_to(ar, bu_re.shape)
    a_im = jnp.broadcast_to(ai, bu_im.shape)

    def combine(e1, e2):
        a1r, a1i, b1r, b1i = e1
        a2r, a2i, b2r, b2i = e2
        return (a2r * a1r - a2i * a1i,
                a2r * a1i + a2i * a1r,
                a2r * b1r - a2i * b1i + b2r,
                a2r * b1i + a2i * b1r + b2i)

    _, _, xr, xi = lax.associative_scan(combine, (a_re, a_im, bu_re, bu_im), axis=1)
    return xr, xi


def block(x, pos, S0, x0r, x0i, lead, norm1_g, w_in, ret_gn_g, s5_lam_re, s5_lam_im,
          s5_log_dt, s5_b_re, s5_b_im, s5_c_re, s5_c_im, s5_d, w_glu, b_glu, s5_norm_g,
          w_out, norm2_g, w_gate, w_up, w_down):
    B, L, _ = x.shape
    h = rms_norm(x, norm1_g)
    proj = h @ w_in
    q = proj[..., :RET_WIDTH].reshape(B, L, RET_HEADS, RET_HEAD_DIM)
    k = proj[..., RET_WIDTH:2 * RET_WIDTH].reshape(B, L, RET_HEADS, RET_HEAD_DIM)
    v = proj[..., 2 * RET_WIDTH:3 * RET_WIDTH].reshape(B, L, RET_HEADS, RET_HEAD_DIM)
    g = proj[..., 3 * RET_WIDTH:4 * RET_WIDTH]
    u = proj[..., 4 * RET_WIDTH:]

    q = jnp.transpose(rotary(q, pos), (0, 2, 1, 3))
    k = jnp.transpose(rotary(k, pos) * (RET_HEAD_DIM ** -0.5), (0, 2, 1, 3))
    v = jnp.transpose(v.astype(jnp.float32), (0, 2, 1, 3))
    o, S_new = retention_seq(S0.astype(jnp.float32), q, k, v, lead, retention_log_gamma())
    ret_out = head_group_norm(o, ret_gn_g) * jax.nn.silu(g.astype(jnp.float32))

    u32 = u.astype(jnp.float32).reshape(B, L, S5_GROUPS, S5_GROUP)
    ar, ai, bbr, bbi = s5_discretize(s5_lam_re, s5_lam_im, s5_log_dt, s5_b_re, s5_b_im)
    xr, xi = s5_scan(u32, x0r.astype(jnp.float32), x0i.astype(jnp.float32), ar, ai, bbr, bbi)
    y = (jnp.einsum('blgp,gcp->blgc', xr, s5_c_re.astype(jnp.float32))
         - jnp.einsum('blgp,gcp->blgc', xi, s5_c_im.astype(jnp.float32))
         + s5_d.astype(jnp.float32).reshape(S5_GROUPS, S5_GROUP) * u32).reshape(B, L, S5_WIDTH)
    z = jax.nn.gelu(y)
    s5_out = z * jax.nn.sigmoid(z @ w_glu.astype(jnp.float32) + b_glu.astype(jnp.float32))
    s5_out = rms_norm(s5_out, s5_norm_g)

    mix = jnp.concatenate([ret_out, s5_out], axis=-1).astype(x.dtype) @ w_out
    x = x + mix
    h2 = rms_norm(x, norm2_g)
    x = x + (jax.nn.silu(h2 @ w_gate) * (h2 @ w_up)) @ w_down
    return x, S_new, xr[:, -1], xi[:, -1]


def setup_inputs(seed: int = 0) -> dict:
    key = jax.random.key(seed)
    ks = jax.random.split(key, 32)
    f32 = jnp.float32
    nrm = lambda k, shape, s: jax.random.normal(k, shape, f32) * s
    lam_re = -0.5 + nrm(ks[0], (DEPTH, S5_GROUPS, S5_STATE), 0.01)
    lam_im = (math.pi * jnp.arange(S5_STATE, dtype=f32))[None, None, :] + nrm(ks[1], (DEPTH, S5_GROUPS, S5_STATE), 0.01)
    log_dt = jax.random.uniform(ks[2], (DEPTH, S5_GROUPS), f32, math.log(0.001), math.log(0.1))
    return {
        "x_prompt": nrm(ks[3], (BATCH, SEQ, D_MODEL), 1.0),
        "x_sample": nrm(ks[4], (DEC_BATCH, DEC_SEQ, D_MODEL), 1.0),
        "state_ret": nrm(ks[5], (DEPTH, DEC_BATCH, RET_HEADS, RET_HEAD_DIM, RET_HEAD_DIM), 0.5),
        "state_s5_re": nrm(ks[6], (DEPTH, DEC_BATCH, S5_GROUPS, S5_STATE), 1.0),
        "state_s5_im": nrm(ks[7], (DEPTH, DEC_BATCH, S5_GROUPS, S5_STATE), 1.0),
        "meta_tokens": nrm(ks[8], (N_META, D_MODEL), 1.0),
        "norm1_g": 1.0 + nrm(ks[9], (DEPTH, D_MODEL), 0.02),
        "w_in": nrm(ks[10], (DEPTH, D_MODEL, IN_COLS), D_MODEL ** -0.5),
        "ret_gn_g": 1.0 + nrm(ks[11], (DEPTH, RET_WIDTH), 0.02),
        "s5_lam_re": lam_re,
        "s5_lam_im": lam_im,
        "s5_log_dt": log_dt,
        "s5_b_re": nrm(ks[12], (DEPTH, S5_GROUPS, S5_STATE, S5_GROUP), (2 * S5_GROUP) ** -0.5),
        "s5_b_im": nrm(ks[13], (DEPTH, S5_GROUPS, S5_STATE, S5_GROUP), (2 * S5_GROUP) ** -0.5),
        "s5_c_re": nrm(ks[14], (DEPTH, S5_GROUPS, S5_GROUP, S5_STATE), (2 * S5_STATE) ** -0.5),
        "s5_c_im": nrm(ks[15], (DEPTH, S5_GROUPS, S5_GROUP, S5_STATE), (2 * S5_STATE) ** -0.5),
        "s5_d": nrm(ks[16], (DEPTH, S5_WIDTH), 1.0),
        "w_glu": nrm(ks[17], (DEPTH, S5_WIDTH, S5_WIDTH), S5_WIDTH ** -0.5),
        "b_glu": nrm(ks[18], (DEPTH, S5_WIDTH), 0.01),
        "s5_norm_g": 1.0 + nrm(ks[19], (DEPTH, S5_WIDTH), 0.02),
        "w_out": nrm(ks[20], (DEPTH, MIX_WIDTH, D_MODEL), MIX_WIDTH ** -0.5),
        "norm2_g": 1.0 + nrm(ks[21], (DEPTH, D_MODEL), 0.02),
        "w_gate": nrm(ks[22], (DEPTH, D_MODEL, D_FF), D_MODEL ** -0.5),
        "w_up": nrm(ks[23], (DEPTH, D_MODEL, D_FF), D_MODEL ** -0.5),
        "w_down": nrm(ks[24], (DEPTH, D_FF, D_MODEL), D_FF ** -0.5),
        "final_norm_g": 1.0 + nrm(ks[25], (D_MODEL,), 0.02),
    }


def reference(x_prompt, x_sample, state_ret, state_s5_re, state_s5_im, meta_tokens,
              norm1_g, w_in, ret_gn_g, s5_lam_re, s5_lam_im, s5_log_dt, s5_b_re, s5_b_im,
              s5_c_re, s5_c_im, s5_d, w_glu, b_glu, s5_norm_g, w_out, norm2_g,
              w_gate, w_up, w_down, final_norm_g):
    T = N_META + x_prompt.shape[1]
    xp = jnp.concatenate(
        [jnp.broadcast_to(meta_tokens.astype(x_prompt.dtype)[None], (x_prompt.shape[0], N_META, D_MODEL)),
         x_prompt], axis=1)
    xs = x_sample
    pos_p = jnp.arange(T, dtype=jnp.int32)
    pos_s = PAST_LEN + jnp.arange(xs.shape[1], dtype=jnp.int32)
    bp = x_prompt.shape[0]
    zero_ret = jnp.zeros((bp, RET_HEADS, RET_HEAD_DIM, RET_HEAD_DIM), jnp.float32)
    zero_s5 = jnp.zeros((bp, S5_GROUPS, S5_STATE), jnp.float32)

    ret_p, s5r_p, s5i_p, ret_s, s5r_s, s5i_s = [], [], [], [], [], []
    for l in range(DEPTH):
        w = (norm1_g[l], w_in[l], ret_gn_g[l], s5_lam_re[l], s5_lam_im[l], s5_log_dt[l],
             s5_b_re[l], s5_b_im[l], s5_c_re[l], s5_c_im[l], s5_d[l], w_glu[l], b_glu[l],
             s5_norm_g[l], w_out[l], norm2_g[l], w_gate[l], w_up[l], w_down[l])
        xp, Sp, rp, ip = block(xp, pos_p, zero_ret, zero_s5, zero_s5, N_META, *w)
        xs, Ss, rs, is_ = block(xs, pos_s, state_ret[l], state_s5_re[l], state_s5_im[l], xs.shape[1], *w)
        ret_p.append(Sp); s5r_p.append(rp); s5i_p.append(ip)
        ret_s.append(Ss); s5r_s.append(rs); s5i_s.append(is_)

    y_prompt = rms_norm(xp, final_norm_g)[:, N_META:]
    y_sample = rms_norm(xs, final_norm_g)
    return (y_prompt, y_sample,
            jnp.stack(ret_p), jnp.stack(s5r_p), jnp.stack(s5i_p),
            jnp.stack(ret_s), jnp.stack(s5r_s), jnp.stack(s5i_s))
```

```python
import numpy as np
from contextlib import ExitStack
import concourse.bass as bass
import concourse.mybir as mybir
from concourse.bass_utils import run_bass_kernel_spmd

F32 = mybir.dt.float32
BF16 = mybir.dt.bfloat16
AF = mybir.ActivationFunctionType
ALU = mybir.AluOpType

D = 4096
DFF = 11008
NCORES = 8
HALF = 1024
NPRE = 1040
NT_OWN = 9
NT_PRE = 9
ROWS_OWN = NT_OWN * 128
ROWS_PRE = NT_PRE * 128
EPS = 1e-6
CG = 256
DEBUG = False
LAST = {}


class Buf:
    def __init__(self, name):
        self.name = name
        self.writers = []
        self.readers = []
        self.dsem = None
        self.dcount = 0


class Sched:
    def __init__(self, nc, es):
        self.nc = nc
        self.es = es
        self.eng = {}
        for n in ("pe", "act", "dve", "pool", "sp"):
            self.eng[n] = dict(sem=es.enter_context(nc.semaphore("s_" + n)), count=0, ops=[], seen={})
        self.dsems = []
        self.nsem = 0

    def _dsem(self, buf):
        if buf.dsem is None:
            buf.dsem = self.es.enter_context(self.nc.semaphore("d%d" % self.nsem))
            self.nsem += 1
            self.dsems.append(buf)
        return buf.dsem

    def _waits(self, e, reads, writes):
        need = {}

        def add(st):
            s, v = st
            k = id(s)
            if k not in need or need[k][1] < v:
                need[k] = (s, v)
        for b in reads:
            for st in b.writers:
                add(st)
        for b in writes:
            for st in b.writers:
                add(st)
            for st in b.readers:
                add(st)
        out = []
        seen = self.eng[e]["seen"]
        for k, (s, v) in need.items():
            if seen.get(k, 0) < v:
                seen[k] = v
                out.append((s, v))
        return out

    def _commit(self, st, reads, writes, multi=False):
        for b in writes:
            if multi:
                b.writers.append(st)
            else:
                b.writers = [st]
                b.readers = []
        for b in reads:
            b.readers.append(st)
            if len(b.readers) > 64:
                best = {}
                for s, v in b.readers:
                    if id(s) not in best or best[id(s)][1] < v:
                        best[id(s)] = (s, v)
                b.readers = list(best.values())

    def op(self, e, fn, reads=(), writes=()):
        E = self.eng[e]
        waits = self._waits(e, reads, writes)
        E["count"] += 1
        st = (E["sem"], E["count"])
        E["ops"].append((waits, fn, E["sem"], 1))
        self._commit(st, reads, writes)
        return st

    def dma(self, q, out, in_, reads=(), writes=(), owner=None, multi=False):
        E = self.eng[q]
        waits = self._waits(q, reads, writes)
        sem = self._dsem(owner)
        owner.dcount += 16
        st = (sem, owner.dcount)
        E["ops"].append((waits, lambda eng, o=out, i=in_: eng.dma_start(out=o, in_=i), sem, 16))
        self._commit(st, reads, writes, multi=multi)
        return st

    def barrier(self, bufs=()):
        for b in bufs:
            b.writers = []
            b.readers = []
        stamps = []
        for n, E in self.eng.items():
            if E["count"]:
                stamps.append((E["sem"], E["count"]))
        for b in self.dsems:
            stamps.append((b.dsem, b.dcount))
        for n, E in self.eng.items():
            w = []
            for s, v in stamps:
                if E["seen"].get(id(s), 0) < v:
                    E["seen"][id(s)] = v
                    w.append((s, v))
            if w:
                E["ops"].append((w, None, None, 0))

    def emit(self, block):
        def run(E):
            def f(eng):
                for waits, fn, sem, amt in E["ops"]:
                    for s, v in waits:
                        eng.wait_ge(s, v)
                    if fn is None:
                        continue
                    ins = fn(eng)
                    ins.then_inc(sem, amt)
            return f
        block.tensor(run(self.eng["pe"]))
        block.scalar(run(self.eng["act"]))
        block.vector(run(self.eng["dve"]))
        block.gpsimd(run(self.eng["pool"]))
        block.sync(run(self.eng["sp"]))


def build_program():
    nc = bass.Bass("TRN2", target_bir_lowering=False)
    es = ExitStack()
    S = Sched(nc, es)

    def din(name, shape, dt=F32):
        return nc.dram_tensor(name, list(shape), dt, kind="ExternalInput").ap()

    def dout(name, shape, dt=F32):
        return nc.dram_tensor(name, list(shape), dt, kind="ExternalOutput").ap()

    def dscr(name, shape, dt=F32):
        return nc.dram_tensor(name, list(shape), dt, kind=("ExternalOutput" if (DEBUG and name in ("mix", "proj")) else "Internal")).ap()

    x_own = din("x_own", [ROWS_OWN, D])
    x_pre = din("x_pre", [ROWS_PRE, D])
    w_in = din("w_in", [D, 10240])
    cs_own = din("cs_own", [NT_OWN, 128, 512])
    cs_pre = din("cs_pre", [NT_PRE, 128, 512])
    maskd = din("maskd", [8, 128, 2, 128])
    dqk_d = din("dqk", [128, 2, 3, 8])
    cmask_d = din("cmask", [128, 16, 256], BF16)
    rmask_d = din("rmask", [128, 16])
    st_ret = din("st_ret", [16, 8, 256, 256])
    o_ret_p = dout("o_ret_p", [8, 256, 256])
    o_ret_s = dout("o_ret_s", [16, 8, 256, 256])
    lam_re_d = din("lam_re", [128, 64]); lam_im_d = din("lam_im", [128, 64])
    ldt_d = din("ldt", [64, 128]); m3_d = din("m3", [128, 128])
    sel_f_d = din("sel_f", [128, 16]); sel_b_d = din("sel_b", [128, 32], BF16)
    b_re_d = din("b_re", [128, 64, 16]); b_im_d = din("b_im", [128, 64, 16])
    c_re_d = din("c_re", [128, 16, 64]); c_im_d = din("c_im", [128, 16, 64])
    dB_d = din("dB", [128, 2048]); bglu_d = din("bglu", [128, 2048])
    s5r_d = din("s5r", [16, 128, 64]); s5i_d = din("s5i", [16, 128, 64])
    w_glu = din("w_glu", [2048, 2048])
    o_s5p_r = dout("o_s5p_r", [128, 64]); o_s5p_i = dout("o_s5p_i", [128, 64])
    o_s5s_r = dout("o_s5s_r", [16, 128, 64]); o_s5s_i = dout("o_s5s_i", [16, 128, 64])
    Wall_d = dscr("Wall", [128, 128, 512], BF16)
    zscr = dscr("zscr", [ROWS_OWN, 2048], F32)
    s5o = dscr("s5o", [ROWS_OWN, 2048], F32)
    proj = dscr("proj", [ROWS_OWN, 10240], F32)
    projp = dscr("projp", [ROWS_PRE, 10240], F32)
    w_out = din("w_out", [D, D])
    w_gate = din("w_gate", [D, DFF])
    w_up = din("w_up", [D, DFF])
    w_down = din("w_down", [DFF, D])
    g_all = din("g_all", [128, 3, 32])
    gfin = din("gfin", [128, D])
    ident_in = din("ident", [128, 128])
    y_out = dout("y", [ROWS_OWN, D])

    mix = dscr("mix", [ROWS_OWN, D], F32)
    x1 = dscr("x1", [ROWS_OWN, D], F32)
    h2 = dscr("h2", [ROWS_OWN, D], F32)
    act = dscr("act", [ROWS_OWN, DFF], F32)

    def sb(name, shape, dt):
        return es.enter_context(nc.sbuf_tensor(name, list(shape), dt))

    actT = sb("actT", [128, 32, ROWS_OWN], BF16)
    wbf = [sb("wbf%d" % i, [128, 32, CG], BF16) for i in range(2)]
    wst = [sb("wst%d" % i, [128, 8, CG], F32) for i in range(2)]
    xin = [sb("xin%d" % i, [128, D], F32) for i in range(2)]
    xbf = [sb("xbf%d" % i, [128, D], BF16) for i in range(2)]
    ot = [sb("ot%d" % i, [128, CG], F32) for i in range(4)]
    rt = [sb("rt%d" % i, [128, CG], F32) for i in range(4)]
    gtmp = sb("gtmp", [128, NT_OWN, CG], BF16)
    stat = [sb("stat%d" % i, [128, 8], F32) for i in range(2)]
    gains = sb("gains", [128, 3, 32], F32)
    gf_t = sb("gf_t", [128, D], F32)
    ident_f = sb("ident_f", [128, 128], F32)
    ident = sb("ident_b", [128, 128], BF16)

    ps = [es.enter_context(nc.psum_tensor("ps%d" % i, [128, 512], F32)) for i in range(8)]

    B = {}

    def buf(name):
        if name not in B:
            B[name] = Buf(name)
        return B[name]

    cst = buf("const")
    S.dma("sp", gains[:], g_all, writes=[cst], owner=cst, multi=True)
    S.dma("sp", gf_t[:], gfin, writes=[cst], owner=cst, multi=True)
    S.dma("sp", ident_f[:], ident_in, writes=[cst], owner=cst, multi=True)
    S.op("dve", lambda e: e.tensor_copy(out=ident[:], in_=ident_f[:]), reads=[cst], writes=[buf("ident")])

    def load_actT(src, ntiles, kc0, nkc, norm=False):
        W = nkc * 128
        for t in range(ntiles):
            xi, xb, stt = xin[t % 2], xbf[t % 2], stat[t % 2]
            bxi, bxb, bst = buf("xin%d" % (t % 2)), buf("xbf%d" % (t % 2)), buf("stat%d" % (t % 2))
            S.dma("pool", xi[:, 0:W], src[t * 128:(t + 1) * 128, kc0 * 128:kc0 * 128 + W], writes=[bxi], owner=bxi)
            if norm:
                S.op("act", lambda e, xi=xi, stt=stt, xb=xb: e.activation(out=xb[:, 0:W], in_=xi[:, 0:W], func=AF.Square,
                                                                    accum_out=stt[:, 0:1]),
                     reads=[bxi], writes=[bst, bxb])
                S.op("dve", lambda e, stt=stt: e.tensor_scalar(out=stt[:, 1:2], in0=stt[:, 0:1], scalar1=1.0 / W,
                                                               scalar2=EPS, op0=ALU.mult, op1=ALU.add),
                     reads=[bst], writes=[bst])
                S.op("act", lambda e, stt=stt: e.activation(out=stt[:, 2:3], in_=stt[:, 1:2], func=AF.Sqrt),
                     reads=[bst], writes=[bst])
                S.op("dve", lambda e, stt=stt: e.reciprocal(out=stt[:, 3:4], in_=stt[:, 2:3]), reads=[bst], writes=[bst])
                S.op("dve", lambda e, xi=xi, xb=xb, stt=stt: e.tensor_scalar(out=xb[:, 0:W], in0=xi[:, 0:W],
                                                                             scalar1=stt[:, 3:4], scalar2=None,
                                                                             op0=ALU.mult),
                     reads=[bxi, bst], writes=[bxb])
            else:
                S.op("dve", lambda e, xi=xi, xb=xb: e.tensor_copy(out=xb[:, 0:W], in_=xi[:, 0:W]),
                     reads=[bxi], writes=[bxb])
            for q in range((nkc + 3) // 4):
                pb = 6 + (q % 2)
                bp = buf("ps%d" % pb)
                pv = ps[pb].bitcast(BF16)
                nq = min(4, nkc - q * 4)

                def tr(e, xb=xb, pv=pv, q=q, nq=nq):
                    ins = None
                    for i in range(nq):
                        ins = e.transpose(out=pv[:, i * 128:(i + 1) * 128], in_=xb[:, (q * 4 + i) * 128:(q * 4 + i + 1) * 128],
                                          identity=ident[:])
                    return ins
                S.op("pe", tr, reads=[bxb, buf("ident")], writes=[bp])
                dst = actT[:, q * 4:q * 4 + nq, t * 128:(t + 1) * 128]
                srcv = pv[:, 0:nq * 128].rearrange("p (c t) -> p c t", c=nq)
                eng = "act" if q % 2 else "dve"
                if eng == "act":
                    S.op("act", lambda e, dst=dst, srcv=srcv: e.copy(out=dst, in_=srcv), reads=[bp], writes=[buf("actT")])
                else:
                    S.op("dve", lambda e, dst=dst, srcv=srcv: e.tensor_copy(out=dst, in_=srcv), reads=[bp], writes=[buf("actT")])

    wcount = [0]
    pcount = [0]
    pending = []

    def stream(Wd, krow0, nkc, col0, ncols, ntiles, consumer, gain=None):
        ngr = (ncols + CG - 1) // CG
        for gi in range(ngr):
            c0 = col0 + gi * CG
            cw = min(CG, col0 + ncols - c0)
            pending.append((Wd, krow0, nkc, c0, cw, ntiles, consumer, gain))

    def _load_group(item):
        Wd, krow0, nkc, c0, cw, ntiles, consumer, gain = item
        wi = wcount[0] % 2
        wcount[0] += 1
        wb, bwb = wbf[wi], buf("wbf%d" % wi)
        nh = (nkc + 7) // 8
        for hh in range(nh):
            k0 = hh * 8
            kn = min(8, nkc - k0)
            si = pcount[0] % 2
            pcount[0] += 1
            ws, bws = wst[si], buf("wst%d" % si)
            S.dma("sp", ws[:, 0:kn, 0:cw],
                  Wd[krow0 + k0 * 128: krow0 + (k0 + kn) * 128, c0:c0 + cw].rearrange("(k p) n -> p k n", p=128),
                  writes=[bws], owner=bws)
            if gain is None:
                if hh % 2:
                    S.op("act", lambda e, wb=wb, ws=ws, k0=k0, kn=kn, cw=cw: e.copy(out=wb[:, k0:k0 + kn, 0:cw], in_=ws[:, 0:kn, 0:cw]),
                         reads=[bws], writes=[bwb])
                else:
                    S.op("dve", lambda e, wb=wb, ws=ws, k0=k0, kn=kn, cw=cw: e.tensor_copy(out=wb[:, k0:k0 + kn, 0:cw], in_=ws[:, 0:kn, 0:cw]),
                         reads=[bws], writes=[bwb])
            else:
                for kk in range(kn):
                    kc = k0 + kk
                    gcol = gains[:, gain, (krow0 // 128 + kc):(krow0 // 128 + kc) + 1]
                    if kk % 2:
                        S.op("act", lambda e, wb=wb, ws=ws, kc=kc, kk=kk, cw=cw, gcol=gcol: e.activation(
                            out=wb[:, kc, 0:cw], in_=ws[:, kk, 0:cw], func=AF.Copy, scale=gcol),
                            reads=[bws, cst], writes=[bwb])
                    else:
                        S.op("dve", lambda e, wb=wb, ws=ws, kc=kc, kk=kk, cw=cw, gcol=gcol: e.tensor_scalar(
                            out=wb[:, kc, 0:cw], in0=ws[:, kk, 0:cw], scalar1=gcol, scalar2=None, op0=ALU.mult),
                            reads=[bws, cst], writes=[bwb])
        return wb, bwb

    def _compute_group(item, wb, bwb):
        Wd, krow0, nkc, c0, cw, ntiles, consumer, gain = item
        for t in range(ntiles):
            pb = pcount[1] % 6 if len(pcount) > 1 else 0
            pcount[1] += 1
            bp = buf("ps%d" % pb)
            pt = ps[pb][:, 0:cw]

            def mm(e, pt=pt, wb=wb, t=t, cw=cw):
                ins = None
                for kc in range(nkc):
                    ins = e.matmul(pt, actT[:, kc, t * 128:(t + 1) * 128], wb[:, kc, 0:cw], start=(kc == 0), stop=(kc == nkc - 1))
                return ins
            S.op("pe", mm, reads=[buf("actT"), bwb], writes=[bp])
            consumer(t, c0, cw, pt, bp)

    pcount.append(0)

    def flush_stream():
        items = pending[:]
        del pending[:]
        if not items:
            return
        cur = _load_group(items[0])
        for n_, item in enumerate(items):
            nxt = _load_group(items[n_ + 1]) if n_ + 1 < len(items) else None
            _compute_group(item, *cur)
            cur = nxt

    ocnt = [0]

    def next_ot():
        i = ocnt[0] % 4
        ocnt[0] += 1
        return i


    def mk_store(dst):
        def cons(t, c0, cw, pt, bp):
            i = next_ot()
            o, bo = ot[i], buf("ot%d" % i)
            if i % 2:
                S.op("act", lambda e: e.copy(out=o[:, 0:cw], in_=pt), reads=[bp], writes=[bo])
            else:
                S.op("dve", lambda e: e.tensor_copy(out=o[:, 0:cw], in_=pt), reads=[bp], writes=[bo])
            S.dma("pool", dst[t * 128:(t + 1) * 128, c0:c0 + cw], o[:, 0:cw], reads=[bo], writes=[buf("projd")], owner=bo, multi=True)
        return cons

    load_actT(x_pre, NT_PRE, 0, 32, norm=True)
    stream(w_in, 0, 32, 2048, 4096, NT_PRE, mk_store(projp), gain=0)
    stream(w_in, 0, 32, 8192, 2048, NT_PRE, mk_store(projp), gain=0)
    flush_stream()
    S.barrier(B.values())
    load_actT(x_own, NT_OWN, 0, 32, norm=True)
    stream(w_in, 0, 32, 0, 10240, NT_OWN, mk_store(proj), gain=0)
    flush_stream()
    S.barrier(B.values())

    flat = actT[:].rearrange("p a b -> p (a b)")
    coff = [0]

    def carve(shape, dt):
        n = 1
        for d_ in shape[1:]:
            n *= d_
        nb = n * (4 if dt == F32 else 2)
        a = flat[:, coff[0] // 2:(coff[0] + nb) // 2]
        coff[0] += nb
        if dt == F32:
            a = a.bitcast(F32)
        if len(shape) == 3:
            a = a.rearrange("p (a b) -> p a b", a=shape[1])
        return a

    qkvg = [carve([128, 4, 256], F32) for _ in range(2)]
    cst_ = [carve([128, 2, 256], F32) for _ in range(2)]
    tqs = [[carve([128, 256], F32) for _ in range(4)] for _ in range(2)]
    qb, qdb, kb, kdb, vb = [[carve([128, 256], BF16) for _ in range(2)] for _ in range(5)]
    trs = [carve([128, 6, 128], BF16) for _ in range(2)]
    smb = [carve([128, 128], BF16) for _ in range(2)]
    Sf2 = [carve([128, 2, 256], F32) for _ in range(2)]
    Sb2 = [carve([128, 2, 256], BF16) for _ in range(2)]
    msk2 = [carve([128, 2, 128], F32) for _ in range(2)]
    bnst = [carve([128, 8], F32) for _ in range(2)]
    yt = [carve([128, 256], F32) for _ in range(2)]
    sgt = [carve([128, 256], F32) for _ in range(2)]
    S0 = [carve([128, 2, 256], F32) for _ in range(2)]
    S0b = [carve([128, 2, 256], BF16) for _ in range(2)]
    So = [carve([128, 2, 256], F32) for _ in range(2)]
    vm = [carve([128, 256], BF16) for _ in range(2)]
    qm = [carve([128, 2, 128], BF16) for _ in range(2)]
    cmask = carve([128, 16, 256], BF16)
    rmask = carve([128, 16], F32)
    dqk = carve([128, 2, 24], F32)
    odbg = [carve([128, 256], F32) for _ in range(2)]

    cst2 = buf("const2")
    S.dma("sp", cmask, cmask_d, writes=[cst2], owner=cst2, multi=True)
    S.dma("sp", rmask, rmask_d, writes=[cst2], owner=cst2, multi=True)
    S.dma("sp", dqk, dqk_d.rearrange("p a k h -> p a (k h)"), writes=[cst2], owner=cst2, multi=True)
    GAM = [1.0 - 2.0 ** (-5.0 - h_) for h_ in range(8)]
    rcount = [0]

    def ret_tile(hd, par, src, csd, t, kind, full, sample):
        i = par
        rcount[0] += 1
        Sf, Sb, msk = Sf2[par], Sb2[par], msk2[par]
        tq1, tq2, tk1, tk2 = tqs[par]
        TRB, SCB, STB = (6, 2)[par], (7, 3)[par], (5, 4)[par]
        qi, bqi = qkvg[i], buf("qkvg%d" % i)
        ci, bci = cst_[i], buf("cs%d" % i)
        srcv = src[t * 128:(t + 1) * 128, :].rearrange("p (s h d) -> p s h d", s=5, h=8)[:, 0:4, hd, :]
        S.dma("pool", qi, srcv, writes=[bqi], owner=bqi)
        S.dma("pool", ci, csd[t].rearrange("p (a d) -> p a d", a=2), writes=[bci], owner=bci)
        dq_col = dqk[:, 0, kind * 8 + hd:kind * 8 + hd + 1]
        dk_col = dqk[:, 1, kind * 8 + hd:kind * 8 + hd + 1]

        def rot(x, t1, t2, bt1, bt2):
            S.op("dve", lambda e: e.tensor_tensor(out=t1, in0=x, in1=ci[:, 0, :], op=ALU.mult), reads=[bqi, bci], writes=[bt1])
            S.op("dve", lambda e: e.tensor_tensor(out=t2[:, 0:128], in0=x[:, 128:256], in1=ci[:, 1, 0:128], op=ALU.mult), reads=[bqi, bci], writes=[bt2])
            S.op("dve", lambda e: e.tensor_tensor(out=t2[:, 128:256], in0=x[:, 0:128], in1=ci[:, 1, 128:256], op=ALU.mult), reads=[bqi, bci], writes=[bt2])
            S.op("dve", lambda e: e.tensor_tensor(out=t1, in0=t1, in1=t2, op=ALU.add), reads=[bt1, bt2], writes=[bt1])

        bk1, bk2 = buf("tk1_%d" % par), buf("tk2_%d" % par)
        rot(qi[:, 1, :], tk1, tk2, bk1, bk2)
        kd_, bkd = kdb[i], buf("kdb%d" % i)
        v_, bv = vb[i], buf("vb%d" % i)
        S.op("dve", lambda e: e.tensor_scalar(out=kd_, in0=tk1, scalar1=dk_col, scalar2=None, op0=ALU.mult), reads=[bk1, cst2], writes=[bkd])
        S.op("act", lambda e: e.copy(out=v_, in_=qi[:, 2, :]), reads=[bqi], writes=[bv])
        bSf, bSb = buf("Sf%d" % par), buf("Sb%d" % par)
        gl = GAM[hd] ** {0: 128, 1: 8, 2: 16}[kind]
        if full:
            bq1, bq2 = buf("tq1_%d" % par), buf("tq2_%d" % par)
            rot(qi[:, 0, :], tq1, tq2, bq1, bq2)
            q_, bq = qb[i], buf("qb%d" % i)
            qd_, bqd = qdb[i], buf("qdb%d" % i)
            k_, bk = kb[i], buf("kb%d" % i)
            S.op("act", lambda e: e.copy(out=q_, in_=tq1), reads=[bq1], writes=[bq])
            S.op("dve", lambda e: e.tensor_scalar(out=qd_, in0=tq1, scalar1=dq_col, scalar2=None, op0=ALU.mult), reads=[bq1, cst2], writes=[bqd])
            S.op("act", lambda e: e.mul(out=k_, in_=tk1, mul=1.0 / 16.0), reads=[bk1], writes=[bk])
            bp6 = buf("ps%d" % TRB)
            pv = ps[TRB].bitcast(BF16)

            def tr(e):
                ins = None
                for n_, srcb in enumerate((q_, qd_, k_)):
                    for c_ in range(2):
                        ins = e.transpose(out=pv[:, (2 * n_ + c_) * 128:(2 * n_ + c_ + 1) * 128], in_=srcb[:, c_ * 128:(c_ + 1) * 128], identity=ident[:])
                return ins
            S.op("pe", tr, reads=[bq, bqd, bk, buf("ident")], writes=[bp6])
            tr_, btr = trs[i], buf("trs%d" % i)
            S.op("dve", lambda e: e.tensor_copy(out=tr_, in_=pv[:, 0:768].rearrange("p (a b) -> p a b", a=6)), reads=[bp6], writes=[btr])
            bp7 = buf("ps%d" % SCB)

            def sc(e):
                e.matmul(ps[SCB][:, 0:128], tr_[:, 4, :], tr_[:, 0, :], start=True, stop=False)
                return e.matmul(ps[SCB][:, 0:128], tr_[:, 5, :], tr_[:, 1, :], start=False, stop=True)
            S.op("pe", sc, reads=[btr], writes=[bp7])
            sm_, bsm = smb[i], buf("smb%d" % i)
            S.op("dve", lambda e: e.tensor_tensor(out=sm_, in0=ps[SCB][:, 0:128], in1=msk[:, kind, :], op=ALU.mult), reads=[bp7, buf("msk%d" % par)], writes=[bsm])
            pbo = par
            bpo = buf("ps%d" % pbo)
            po = ps[pbo][:, 0:256]
            if not sample:
                def om(e):
                    e.matmul(po, sm_, v_, start=True, stop=False)
                    e.matmul(po, tr_[:, 2, :], Sb[:, 0, :], start=False, stop=False)
                    return e.matmul(po, tr_[:, 3, :], Sb[:, 1, :], start=False, stop=True)
                S.op("pe", om, reads=[bsm, bv, btr, bSb], writes=[bpo])
            else:
                S.op("pe", lambda e: e.matmul(po, sm_, v_, start=True, stop=False, skip_group_check=True), reads=[bsm, bv], writes=[bpo])
        if not sample:
            bp5 = buf("ps%d" % STB)

            def su(e):
                e.matmul(ps[STB][:, 0:256], kd_[:, 0:128], v_, start=True, stop=True)
                return e.matmul(ps[STB][:, 256:512], kd_[:, 128:256], v_, start=True, stop=True)
            S.op("pe", su, reads=[bkd, bv], writes=[bp5])
            S.op("dve", lambda e: e.scalar_tensor_tensor(out=Sf, in0=Sf, scalar=float(gl), in1=ps[STB][:, 0:512].rearrange("p (a b) -> p a b", a=2),
                                                         op0=ALU.mult, op1=ALU.add), reads=[bSf, bp5], writes=[bSf])
            S.op("act", lambda e: e.copy(out=Sb, in_=Sf), reads=[bSf], writes=[bSb])
        else:
            for sq in range(16):
                j = sq % 2
                s0, bs0 = S0[j], buf("S0_%d" % j)
                s0b, bs0b = S0b[j], buf("S0b_%d" % j)
                so, bso = So[j], buf("So_%d" % j)
                vm_, bvm = vm[j], buf("vm%d" % j)
                qm_, bqm = qm[j], buf("qm%d" % j)
                S.dma("sp", s0, st_ret[sq, hd].rearrange("(c p) v -> p c v", p=128), writes=[bs0], owner=bs0)
                S.op("act", lambda e, s0=s0, s0b=s0b: e.copy(out=s0b, in_=s0), reads=[bs0], writes=[bs0b])
                S.op("dve", lambda e, qm_=qm_, sq=sq: e.tensor_tensor(out=qm_, in0=tr_[:, 2:4, :], in1=cmask[:, sq, :].rearrange("p (a b) -> p a b", a=2), op=ALU.mult),
                     reads=[btr, cst2], writes=[bqm])

                def im(e, qm_=qm_, s0b=s0b, sq=sq):
                    e.matmul(po, qm_[:, 0, :], s0b[:, 0, :], start=False, stop=False, skip_group_check=True)
                    return e.matmul(po, qm_[:, 1, :], s0b[:, 1, :], start=False, stop=(sq == 15), skip_group_check=True)
                S.op("pe", im, reads=[bqm, bs0b], writes=[bpo])
                S.op("dve", lambda e, vm_=vm_, sq=sq: e.tensor_scalar(out=vm_, in0=v_, scalar1=rmask[:, sq:sq + 1], scalar2=None, op0=ALU.mult),
                     reads=[bv, cst2], writes=[bvm])
                pbs = 4 + (sq % 2)
                bps = buf("ps%d" % pbs)

                def su2(e, vm_=vm_, pbs=pbs):
                    e.matmul(ps[pbs][:, 0:256], kd_[:, 0:128], vm_, start=True, stop=True)
                    return e.matmul(ps[pbs][:, 256:512], kd_[:, 128:256], vm_, start=True, stop=True)
                S.op("pe", su2, reads=[bkd, bvm], writes=[bps])
                S.op("dve", lambda e, so=so, s0=s0, pbs=pbs: e.scalar_tensor_tensor(out=so, in0=s0, scalar=float(gl), in1=ps[pbs][:, 0:512].rearrange("p (a b) -> p a b", a=2),
                                                                                   op0=ALU.mult, op1=ALU.add), reads=[bs0, bps], writes=[bso])
                S.dma("sp", o_ret_s[sq, hd].rearrange("(c p) v -> p c v", p=128), so, reads=[bso], writes=[buf("orets")], owner=bso, multi=True)
        if full:
            bb, bbn = bnst[i], buf("bnst%d" % i)
            y_, by = yt[i], buf("yt%d" % i)
            sg_, bsg = sgt[i], buf("sgt%d" % i)
            S.op("act", lambda e: e.activation(out=sg_, in_=qi[:, 3, :], func=AF.Silu), reads=[bqi], writes=[bsg])
            od_, bod = odbg[i], buf("odbg%d" % i)
            S.op("act", lambda e: e.copy(out=od_, in_=po), reads=[bpo], writes=[bod])
            if DEBUG:
                S.dma("pool", mix[t * 128:(t + 1) * 128, 2048 + hd * 256:2048 + (hd + 1) * 256], od_, reads=[bod], writes=[buf("mixd")], owner=bod, multi=True)
            S.op("dve", lambda e: e.bn_stats(out=bb[:, 0:6], in_=od_), reads=[bod], writes=[bbn])
            S.op("dve", lambda e: e.bn_aggr(out=bb[:, 6:8], in_=bb[:, 0:6]), reads=[bbn], writes=[bbn])
            S.op("dve", lambda e: e.tensor_scalar(out=bb[:, 0:1], in0=bb[:, 7:8], scalar1=1e-5, scalar2=None, op0=ALU.add), reads=[bbn], writes=[bbn])
            S.op("act", lambda e: e.activation(out=bb[:, 1:2], in_=bb[:, 0:1], func=AF.Sqrt), reads=[bbn], writes=[bbn])
            S.op("dve", lambda e: e.reciprocal(out=bb[:, 2:3], in_=bb[:, 1:2]), reads=[bbn], writes=[bbn])
            S.op("dve", lambda e: e.tensor_scalar(out=y_, in0=od_, scalar1=bb[:, 6:7], scalar2=bb[:, 2:3], op0=ALU.subtract, op1=ALU.mult),
                 reads=[bod, bbn], writes=[by])
            S.op("dve", lambda e: e.tensor_tensor(out=y_, in0=y_, in1=sg_, op=ALU.mult), reads=[by, bsg], writes=[by])
            S.dma("pool", mix[t * 128:(t + 1) * 128, hd * 256:(hd + 1) * 256], y_, reads=[by], writes=[buf("mixd")], owner=by, multi=True)

    for hp in range(4):
        hds = (2 * hp, 2 * hp + 1)
        for par, hd in enumerate(hds):
            S.dma("sp", msk2[par], maskd[hd], reads=[], writes=[buf("msk%d" % par)], owner=buf("msk%d" % par))
            S.op("dve", lambda e, par=par: e.memset(Sf2[par], 0.0), writes=[buf("Sf%d" % par)])
            S.op("dve", lambda e, par=par: e.memset(Sb2[par], 0.0), writes=[buf("Sb%d" % par)])
        for t in range(NT_PRE):
            for par, hd in enumerate(hds):
                ret_tile(hd, par, projp, cs_pre, t, 0 if t < 8 else 2, False, False)
        for t in range(8):
            for par, hd in enumerate(hds):
                ret_tile(hd, par, proj, cs_own, t, 0, True, False)
        for par, hd in enumerate(hds):
            S.dma("sp", o_ret_p[hd].rearrange("(c p) v -> p c v", p=128), Sf2[par], reads=[buf("Sf%d" % par)], writes=[buf("oretp")], owner=buf("Sf%d" % par), multi=True)
        for par, hd in enumerate(hds):
            ret_tile(hd, par, proj, cs_own, 8, 1, True, True)
    flush_stream()
    S.barrier(B.values())

    coff[0] = 0
    PI = 3.141592653589793
    MAGIC = 12582912.0
    bS = buf("s5setup")

    def T64(n=1):
        a = carve([128, n, 128], F32) if n > 1 else carve([128, 128], F32)
        return a

    lam_sb = carve([128, 2, 64], F32)
    S.dma("sp", lam_sb[:, 0, :], lam_re_d, writes=[bS], owner=bS, multi=True)
    S.dma("sp", lam_sb[:, 1, :], lam_im_d, writes=[bS], owner=bS, multi=True)
    lamT = T64(2)
    ldt = T64()
    S.dma("sp", ldt[0:64, :], ldt_d, writes=[bS], owner=bS, multi=True)
    m3 = carve([128, 128], F32)
    S.dma("sp", m3, m3_d, writes=[bS], owner=bS, multi=True)
    selc = carve([128, 64], F32)
    S.dma("sp", selc[:, 0:16], sel_f_d, writes=[bS], owner=bS, multi=True)
    selb = carve([128, 32], BF16)
    S.dma("sp", selb, sel_b_d, writes=[bS], owner=bS, multi=True)
    mask8, tmask = selc[:, 0:8], selc[:, 8:16]
    Jsel, Csel = selb[:, 0:16], selb[:, 16:32]

    A1, A2 = carve([128, 2, 128], F32), carve([128, 2, 128], F32)
    main_off = coff[0]

    def sop(eng, fn):
        S.op(eng, fn, reads=[bS], writes=[bS])

    def trans_f32(dst, src, rows_in, cols_in):
        bp = buf("ps7")
        S.op("pe", lambda e: e.transpose(out=ps[7][0:cols_in, 0:rows_in], in_=src, identity=ident_f[0:rows_in, 0:rows_in]), reads=[bS, cst], writes=[bp])
        S.op("dve", lambda e: e.tensor_copy(out=dst, in_=ps[7][0:cols_in, 0:rows_in]), reads=[bp, bS], writes=[bS])

    trans_f32(lamT[0:64, 0, :], lam_sb[:, 0, :], 128, 64)
    trans_f32(lamT[0:64, 1, :], lam_sb[:, 1, :], 128, 64)
    dtT = T64()
    lrT, liT = T64(), T64()
    sop("act", lambda e: e.activation(out=dtT[0:64, :], in_=ldt[0:64, :], func=AF.Exp))
    sop("dve", lambda e: e.tensor_tensor(out=lrT[0:64, :], in0=lamT[0:64, 0, :], in1=dtT[0:64, :], op=ALU.mult))
    sop("dve", lambda e: e.tensor_tensor(out=liT[0:64, :], in0=lamT[0:64, 1, :], in1=dtT[0:64, :], op=ALU.mult))
    AR, AI, NR, NI = T64(9), T64(9), T64(9), T64(9)
    tA, tB, tC, tD = T64(), T64(), T64(), T64()

    def trig(dst, k, off):
        sop("dve", lambda e: e.tensor_scalar(out=tA[0:64, :], in0=liT[0:64, :], scalar1=float(k), scalar2=float(off), op0=ALU.mult, op1=ALU.add))
        sop("dve", lambda e: e.tensor_scalar(out=tB[0:64, :], in0=tA[0:64, :], scalar1=1.0 / (2 * PI), scalar2=MAGIC, op0=ALU.mult, op1=ALU.add))
        sop("dve", lambda e: e.tensor_scalar(out=tB[0:64, :], in0=tB[0:64, :], scalar1=MAGIC, scalar2=2 * PI, op0=ALU.subtract, op1=ALU.mult))
        sop("dve", lambda e: e.tensor_tensor(out=tA[0:64, :], in0=tA[0:64, :], in1=tB[0:64, :], op=ALU.subtract))
        sop("dve", lambda e: e.tensor_scalar(out=tA[0:64, :], in0=tA[0:64, :], scalar1=-3.1415925, scalar2=3.1415925, op0=ALU.max, op1=ALU.min))
        sop("act", lambda e: e.activation(out=dst, in_=tA[0:64, :], func=AF.Sin))

    for k in range(9):
        trig(tC[0:64, :], k, PI / 2)
        trig(tD[0:64, :], k, 0.0)
        sop("act", lambda e, k=k: e.activation(out=AR[0:64, k, :], in_=lrT[0:64, :], func=AF.Exp, scale=float(k)))
        sop("act", lambda e, k=k: e.activation(out=NR[0:64, k, :], in_=lrT[0:64, :], func=AF.Exp, scale=float(-k)))
        sop("dve", lambda e, k=k: e.tensor_tensor(out=AI[0:64, k, :], in0=AR[0:64, k, :], in1=tD[0:64, :], op=ALU.mult))
        sop("dve", lambda e, k=k: e.tensor_tensor(out=AR[0:64, k, :], in0=AR[0:64, k, :], in1=tC[0:64, :], op=ALU.mult))
        sop("dve", lambda e, k=k: e.scalar_tensor_tensor(out=NI[0:64, k, :], in0=NR[0:64, k, :], scalar=-1.0, in1=tD[0:64, :], op0=ALU.mult, op1=ALU.mult))
        sop("dve", lambda e, k=k: e.tensor_tensor(out=NR[0:64, k, :], in0=NR[0:64, k, :], in1=tC[0:64, :], op=ALU.mult))
    fr, fi = T64(), T64()
    sop("dve", lambda e: e.tensor_scalar(out=tA[0:64, :], in0=AR[0:64, 1, :], scalar1=-1.0, scalar2=None, op0=ALU.add))
    sop("dve", lambda e: e.tensor_tensor(out=tB[0:64, :], in0=lamT[0:64, 0, :], in1=lamT[0:64, 0, :], op=ALU.mult))
    sop("dve", lambda e: e.tensor_tensor(out=tC[0:64, :], in0=lamT[0:64, 1, :], in1=lamT[0:64, 1, :], op=ALU.mult))
    sop("dve", lambda e: e.tensor_tensor(out=tB[0:64, :], in0=tB[0:64, :], in1=tC[0:64, :], op=ALU.add))
    sop("dve", lambda e: e.reciprocal(out=tB[0:64, :], in_=tB[0:64, :]))
    sop("dve", lambda e: e.tensor_tensor(out=tC[0:64, :], in0=tA[0:64, :], in1=lamT[0:64, 0, :], op=ALU.mult))
    sop("dve", lambda e: e.tensor_tensor(out=tD[0:64, :], in0=AI[0:64, 1, :], in1=lamT[0:64, 1, :], op=ALU.mult))
    sop("dve", lambda e: e.tensor_tensor(out=tC[0:64, :], in0=tC[0:64, :], in1=tD[0:64, :], op=ALU.add))
    sop("dve", lambda e: e.tensor_tensor(out=fr[0:64, :], in0=tC[0:64, :], in1=tB[0:64, :], op=ALU.mult))
    sop("dve", lambda e: e.tensor_tensor(out=tC[0:64, :], in0=AI[0:64, 1, :], in1=lamT[0:64, 0, :], op=ALU.mult))
    sop("dve", lambda e: e.tensor_tensor(out=tD[0:64, :], in0=tA[0:64, :], in1=lamT[0:64, 1, :], op=ALU.mult))
    sop("dve", lambda e: e.tensor_tensor(out=tC[0:64, :], in0=tC[0:64, :], in1=tD[0:64, :], op=ALU.subtract))
    sop("dve", lambda e: e.tensor_tensor(out=fi[0:64, :], in0=tC[0:64, :], in1=tB[0:64, :], op=ALU.mult))
    ER, EI, ENR, ENI = T64(8), T64(8), T64(8), T64(8)

    def cmul_f(dr, di, xr, xi):
        sop("dve", lambda e: e.tensor_tensor(out=tA[0:64, :], in0=xr, in1=fr[0:64, :], op=ALU.mult))
        sop("dve", lambda e: e.tensor_tensor(out=tB[0:64, :], in0=xi, in1=fi[0:64, :], op=ALU.mult))
        sop("dve", lambda e: e.tensor_tensor(out=dr, in0=tA[0:64, :], in1=tB[0:64, :], op=ALU.subtract))
        sop("dve", lambda e: e.tensor_tensor(out=tA[0:64, :], in0=xr, in1=fi[0:64, :], op=ALU.mult))
        sop("dve", lambda e: e.tensor_tensor(out=tB[0:64, :], in0=xi, in1=fr[0:64, :], op=ALU.mult))
        sop("dve", lambda e: e.tensor_tensor(out=di, in0=tA[0:64, :], in1=tB[0:64, :], op=ALU.add))

    for s_ in range(8):
        cmul_f(ER[0:64, s_, :], EI[0:64, s_, :], AR[0:64, 7 - s_, :], AI[0:64, 7 - s_, :])
        cmul_f(ENR[0:64, s_, :], ENI[0:64, s_, :], NR[0:64, s_ + 1, :], NI[0:64, s_ + 1, :])
    sop("dve", lambda e: e.tensor_copy(out=A1[0:64, 0, :], in_=AR[0:64, 8, :]))
    sop("dve", lambda e: e.tensor_copy(out=A1[0:64, 1, :], in_=AR[0:64, 8, :]))
    sop("dve", lambda e: e.tensor_scalar(out=A2[0:64, 0, :], in0=AI[0:64, 8, :], scalar1=-1.0, scalar2=None, op0=ALU.mult))
    sop("dve", lambda e: e.tensor_copy(out=A2[0:64, 1, :], in_=AI[0:64, 8, :]))

    Bp = carve([128, 2, 128], F32)
    Cblk = carve([128, 2, 64], F32)
    CT = carve([128, 2, 128], F32)
    W1p = xin[0][:, 0:2048].rearrange("p (a b) -> p a b", a=2)
    W1n = xin[1][:, 0:2048].rearrange("p (a b) -> p a b", a=2)
    W2p = carve([128, 2, 1024], F32)
    tW = carve([128, 1024], F32)
    Wst = carve([128, 8, 512], BF16)
    S.op("dve", lambda e: e.memset(Wst, 0.0), reads=[bS], writes=[bS])

    def v4(ap):
        return ap.rearrange("p (g s c) -> p g s c", g=8, s=8)

    def bc_tab(tab, g0):
        return tab[0:64, :, g0:g0 + 8].rearrange("p s g -> p g s").unsqueeze(3).broadcast_to([64, 8, 8, 16])

    def bc_gc(x):
        return x.rearrange("p (g c) -> p g c", g=8).unsqueeze(2).broadcast_to([64, 8, 8, 16])

    def cprod(dst_r, dst_i, tr_, ti_, xr, xi, g0, neg_i=False):
        sop("dve", lambda e: e.tensor_tensor(out=v4(dst_r), in0=bc_tab(tr_, g0), in1=bc_gc(xr), op=ALU.mult))
        sop("dve", lambda e: e.tensor_tensor(out=v4(tW[0:64, :]), in0=bc_tab(ti_, g0), in1=bc_gc(xi), op=ALU.mult))
        sop("dve", lambda e: e.tensor_tensor(out=dst_r, in0=dst_r, in1=tW[0:64, :], op=ALU.subtract))
        sop("dve", lambda e: e.tensor_tensor(out=v4(dst_i), in0=bc_tab(tr_, g0), in1=bc_gc(xi), op=ALU.mult))
        sop("dve", lambda e: e.tensor_tensor(out=v4(tW[0:64, :]), in0=bc_tab(ti_, g0), in1=bc_gc(xr), op=ALU.mult))
        if neg_i:
            sop("dve", lambda e: e.scalar_tensor_tensor(out=dst_i, in0=dst_i, scalar=-1.0, in1=tW[0:64, :], op0=ALU.mult, op1=ALU.subtract))
        else:
            sop("dve", lambda e: e.tensor_tensor(out=dst_i, in0=dst_i, in1=tW[0:64, :], op=ALU.add))

    for fc in range(16):
        g0 = fc * 8
        S.dma("sp", Bp[0:64, 0, :].rearrange("p (g c) -> p g c", g=8), b_re_d[g0:g0 + 8].rearrange("g p c -> p g c"), reads=[bS], writes=[bS], owner=bS, multi=True)
        S.dma("sp", Bp[0:64, 1, :].rearrange("p (g c) -> p g c", g=8), b_im_d[g0:g0 + 8].rearrange("g p c -> p g c"), reads=[bS], writes=[bS], owner=bS, multi=True)
        S.dma("sp", Cblk[:, 0, :], c_re_d[g0:g0 + 8].rearrange("g c p -> (g c) p"), reads=[bS], writes=[bS], owner=bS, multi=True)
        S.dma("sp", Cblk[:, 1, :], c_im_d[g0:g0 + 8].rearrange("g c p -> (g c) p"), reads=[bS], writes=[bS], owner=bS, multi=True)
        trans_f32(CT[0:64, 0, :], Cblk[:, 0, :], 128, 64)
        trans_f32(CT[0:64, 1, :], Cblk[:, 1, :], 128, 64)
        cprod(W1p[0:64, 0, :], W1p[0:64, 1, :], ER, EI, Bp[0:64, 0, :], Bp[0:64, 1, :], g0)
        cprod(W1n[0:64, 0, :], W1n[0:64, 1, :], ENR, ENI, Bp[0:64, 0, :], Bp[0:64, 1, :], g0)
        cprod(W2p[0:64, 0, :], W2p[0:64, 1, :], AR[:, 1:9, :], AI[:, 1:9, :], CT[0:64, 0, :], CT[0:64, 1, :], g0, neg_i=True)
        for half in range(2):
            pb = 2 + half
            bp = buf("ps%d" % pb)

            def w3mm(e, half=half, pb=pb):
                ins = None
                for gg in range(4):
                    g = half * 4 + gg
                    e.matmul(ps[pb][:, gg * 128:(gg + 1) * 128], W1n[0:64, 0, g * 128:(g + 1) * 128], W2p[0:64, 0, g * 128:(g + 1) * 128], start=True, stop=False)
                    ins = e.matmul(ps[pb][:, gg * 128:(gg + 1) * 128], W1n[0:64, 1, g * 128:(g + 1) * 128], W2p[0:64, 1, g * 128:(g + 1) * 128], start=False, stop=True)
                return ins
            S.op("pe", w3mm, reads=[bS], writes=[bp])
            S.op("dve", lambda e, half=half, pb=pb: e.tensor_tensor(out=Wst[:, half * 4:(half + 1) * 4, 128:256],
                                                                   in0=ps[pb][:, 0:512].rearrange("p (g n) -> p g n", g=4),
                                                                   in1=m3.unsqueeze(1).broadcast_to([128, 4, 128]), op=ALU.mult),
                 reads=[bp, bS], writes=[bS])
        for half in range(2):
            pb = 4 + half
            bp = buf("ps%d" % pb)

            def w1tr(e, half=half, pb=pb):
                ins = None
                for gg in range(4):
                    g = half * 4 + gg
                    for ri in range(2):
                        ins = e.transpose(out=ps[pb][:, (gg * 2 + ri) * 64:(gg * 2 + ri + 1) * 64], in_=W1p[0:64, ri, g * 128:(g + 1) * 128],
                                          identity=ident_f[0:64, 0:64])
                return ins
            S.op("pe", w1tr, reads=[bS, cst], writes=[bp])
            S.op("act", lambda e, half=half, pb=pb: e.copy(out=Wst[:, half * 4:(half + 1) * 4, 0:128],
                                                           in_=ps[pb][:, 0:512].rearrange("p (g n) -> p g n", g=4)),
                 reads=[bp, bS], writes=[bS])
        S.op("act", lambda e: e.copy(out=Wst[0:64, :, 256:384], in_=W2p[0:64, 0, :].rearrange("p (g n) -> p g n", g=8)), reads=[bS], writes=[bS])
        S.op("act", lambda e: e.copy(out=Wst[0:64, :, 384:512], in_=W2p[0:64, 1, :].rearrange("p (g n) -> p g n", g=8)), reads=[bS], writes=[bS])
        S.dma("sp", Wall_d[g0:g0 + 8].rearrange("g p w -> p g w"), Wst, reads=[bS], writes=[bS], owner=bS, multi=True)
    flush_stream()
    S.barrier(B.values())

    coff[0] = main_off
    GB = 32
    NW = GB * 16
    def wview(tn, is_f32):
        f = tn[:].bitcast(BF16) if is_f32 else tn[:].rearrange("p a b -> p (a b)")
        return f.rearrange("p (g w) -> p g w", g=16)
    Wsets = [(wview(wbf[0], False), wview(wbf[1], False)), (wview(xin[0], True), wview(xin[1], True))]
    Wbufs = [("wbf0", "wbf1"), ("xin0", "xin1")]
    u32 = [carve([128, NW], F32) for _ in range(2)]
    Uexp = [carve([128, GB, 128], BF16) for _ in range(2)]
    Usb = [carve([128, GB, 16], BF16) for _ in range(2)]
    Vsb = [carve([128, 2, NW], F32) for _ in range(2)]
    Xh = carve([128, 2, GB * 17], F32)
    Xall = carve([128, 2, NW], BF16)
    sT1, sT2 = carve([128, 2, GB], F32), carve([128, 2, GB], F32)
    Ysb = carve([128, GB, 16], BF16)
    Yexp = carve([128, GB, 128], BF16)
    ytmp = [carve([128, NW], F32) for _ in range(2)]
    zt_ = [carve([128, NW], F32) for _ in range(2)]
    dB = xbf[0][:].bitcast(F32)
    X0 = carve([128, 2, NW], F32)
    Xn = carve([128, 2, NW], F32)
    bT1, bT2 = sb("bT1", [128, 2, NW], F32), sb("bT2", [128, 2, NW], F32)
    st_in = Yexp[:].rearrange("p a b -> p (a b)").bitcast(F32).rearrange("p (r q) -> p r q", r=2)
    st_o = [carve([128, 2, 64], F32) for _ in range(2)]
    cst3 = buf("const3")
    S.dma("sp", dB, dB_d, writes=[cst3], owner=cst3, multi=True)
    ucount = [0]
    xhv = Xh[0:64, :, :].rearrange("p r (g j) -> p r g j", g=GB)

    def Wg(gb, g):
        return Wsets[gb % 2][g // 16][:, g % 16, :]

    def s5_front(gb, src, t):
        g0 = gb * GB
        i = ucount[0] % 2
        ucount[0] += 1
        bWs = [buf(n) for n in Wbufs[gb % 2]]
        u_, bu = u32[i], buf("u32_%d" % i)
        S.dma("sp", u_, src[t * 128:(t + 1) * 128, 8192 + g0 * 16:8192 + (g0 + GB) * 16], writes=[bu], owner=bu)
        ue, bUe = Uexp[i], buf("Uexp%d" % i)
        for s_ in range(8):
            if s_ % 2:
                S.op("act", lambda e, s_=s_: e.activation(out=ue[:, :, s_ * 16:(s_ + 1) * 16], in_=u_.rearrange("p (g c) -> p g c", g=GB),
                                                           func=AF.Copy, scale=mask8[:, s_:s_ + 1]), reads=[bu, bS], writes=[bUe])
            else:
                S.op("dve", lambda e, s_=s_: e.tensor_scalar(out=ue[:, :, s_ * 16:(s_ + 1) * 16], in0=u_.rearrange("p (g c) -> p g c", g=GB),
                                                              scalar1=mask8[:, s_:s_ + 1], scalar2=None, op0=ALU.mult), reads=[bu, bS], writes=[bUe])
        bp0 = buf("ps0")

        def umm(e):
            ins = None
            for g in range(GB):
                ins = e.matmul(ps[0][:, g * 16:(g + 1) * 16], ue[:, g, :], Jsel, start=True, stop=True, skip_group_check=True)
            return ins
        S.op("pe", umm, reads=[bUe, bS], writes=[bp0])
        us, bUs = Usb[i], buf("Usb%d" % i)
        S.op("act", lambda e: e.copy(out=us, in_=ps[0][:, 0:NW].rearrange("p (g j) -> p g j", g=GB)), reads=[bp0], writes=[bUs])
        bp1, bp2 = buf("ps1"), buf("ps2")

        def vmm(e):
            ins = None
            for g in range(GB):
                e.matmul(ps[1][0:64, g * 16:(g + 1) * 16], Wg(gb, g)[:, 0:64], us[:, g, :], start=True, stop=True, skip_group_check=True)
                ins = e.matmul(ps[2][0:64, g * 16:(g + 1) * 16], Wg(gb, g)[:, 64:128], us[:, g, :], start=True, stop=True, skip_group_check=True)
            return ins
        S.op("pe", vmm, reads=bWs + [bUs], writes=[bp1, bp2])
        vs, bVs = Vsb[i], buf("Vsb%d" % i)
        S.op("act", lambda e: e.copy(out=vs[0:64, 0, :], in_=ps[1][0:64, 0:NW]), reads=[bp1], writes=[bVs])
        S.op("act", lambda e: e.copy(out=vs[0:64, 1, :], in_=ps[2][0:64, 0:NW]), reads=[bp2], writes=[bVs])
        return i

    def s5_rest(gb, t, i, nvalid, full, sample):
        g0 = gb * GB
        bWs = [buf(n) for n in Wbufs[gb % 2]]
        u_, bu = u32[i], buf("u32_%d" % i)
        us, bUs = Usb[i], buf("Usb%d" % i)
        vs, bVs = Vsb[i], buf("Vsb%d" % i)
        vsv = vs[0:64, :, :].rearrange("p r (g j) -> p r g j", g=GB)
        bXh, bXa = buf("Xh"), buf("Xall")
        a1 = A1[0:64, :, g0:g0 + GB]
        a2 = A2[0:64, :, g0:g0 + GB]
        if not sample:
            for j in range(nvalid):
                S.op("dve", lambda e, j=j: e.tensor_tensor(out=sT1[0:64, :, :], in0=xhv[:, :, :, j], in1=a1, op=ALU.mult), reads=[bXh, bS], writes=[buf("sT1")])
                S.op("dve", lambda e, j=j: e.tensor_tensor(out=sT2[0:64, 0, :], in0=xhv[:, 1, :, j], in1=a2[:, 0, :], op=ALU.mult), reads=[bXh, bS], writes=[buf("sT2")])
                S.op("dve", lambda e, j=j: e.tensor_tensor(out=sT2[0:64, 1, :], in0=xhv[:, 0, :, j], in1=a2[:, 1, :], op=ALU.mult), reads=[bXh, bS], writes=[buf("sT2")])
                S.op("dve", lambda e: e.tensor_tensor(out=sT1[0:64, :, :], in0=sT1[0:64, :, :], in1=sT2[0:64, :, :], op=ALU.add), reads=[buf("sT1"), buf("sT2")], writes=[buf("sT1")])
                S.op("dve", lambda e, j=j: e.tensor_tensor(out=xhv[:, :, :, j + 1], in0=sT1[0:64, :, :], in1=vsv[:, :, :, j], op=ALU.add), reads=[buf("sT1"), bVs], writes=[bXh])
            if full:
                S.op("act", lambda e: e.copy(out=Xall[0:64, :, :].rearrange("p r (g j) -> p r g j", g=GB), in_=xhv[:, :, :, 0:16]), reads=[bXh], writes=[bXa])
            S.op("dve", lambda e: e.tensor_copy(out=xhv[:, :, :, 0], in_=xhv[:, :, :, nvalid]), reads=[bXh, bXa], writes=[bXh])
        else:
            x0v = X0[0:64, :, :].rearrange("p r (g j) -> p r g j", g=GB)
            t1v = bT1[0:64, :, :].rearrange("p r (g j) -> p r g j", g=GB)
            t2v = bT2[0:64, :, :].rearrange("p r (g j) -> p r g j", g=GB)
            S.op("act", lambda e: e.copy(out=Xall[0:64, :, :], in_=X0[0:64, :, :]), reads=[buf("X0")], writes=[bXa])
            S.op("dve", lambda e: e.tensor_tensor(out=t1v, in0=x0v, in1=a1.unsqueeze(3).broadcast_to([64, 2, GB, 16]), op=ALU.mult), reads=[buf("X0"), bS], writes=[buf("bT1")])
            S.op("dve", lambda e: e.tensor_tensor(out=t2v[:, 0, :, :], in0=x0v[:, 1, :, :], in1=a2[:, 0, :].unsqueeze(2).broadcast_to([64, GB, 16]), op=ALU.mult), reads=[buf("X0"), bS], writes=[buf("bT2")])
            S.op("dve", lambda e: e.tensor_tensor(out=t2v[:, 1, :, :], in0=x0v[:, 0, :, :], in1=a2[:, 1, :].unsqueeze(2).broadcast_to([64, GB, 16]), op=ALU.mult), reads=[buf("X0"), bS], writes=[buf("bT2")])
            S.op("dve", lambda e: e.tensor_tensor(out=bT1[0:64, :, :], in0=bT1[0:64, :, :], in1=bT2[0:64, :, :], op=ALU.add), reads=[buf("bT1"), buf("bT2")], writes=[buf("bT1")])
            S.op("dve", lambda e: e.tensor_tensor(out=Xn[0:64, :, :], in0=bT1[0:64, :, :], in1=vs[0:64, :, :], op=ALU.add), reads=[buf("bT1"), bVs], writes=[buf("Xn")])
        if not full:
            return
        bp3 = buf("ps3")
        xr_ = Xall[0:64, 0, :].rearrange("p (g j) -> p g j", g=GB)
        xi_ = Xall[0:64, 1, :].rearrange("p (g j) -> p g j", g=GB)

        def ymm(e):
            ins = None
            for g in range(GB):
                o_ = ps[3][:, g * 16:(g + 1) * 16]
                W = Wg(gb, g)
                e.matmul(o_, W[0:64, 256:384], xr_[:, g, :], start=True, stop=False, skip_group_check=True)
                e.matmul(o_, W[0:64, 384:512], xi_[:, g, :], start=False, stop=False, skip_group_check=True)
                ins = e.matmul(o_, W[:, 128:256], us[:, g, :], start=False, stop=True, skip_group_check=True)
            return ins
        S.op("pe", ymm, reads=bWs + [bXa, bUs], writes=[bp3])
        bYs, bYe = buf("Ysb"), buf("Yexp")
        S.op("act", lambda e: e.copy(out=Ysb, in_=ps[3][:, 0:NW].rearrange("p (g j) -> p g j", g=GB)), reads=[bp3], writes=[bYs])
        yev = Yexp[:, :, :].rearrange("p g (j t) -> p g j t", j=16)
        for t_ in range(8):
            if t_ % 2:
                S.op("act", lambda e, t_=t_: e.activation(out=yev[:, :, :, t_], in_=Ysb, func=AF.Copy, scale=tmask[:, t_:t_ + 1]),
                     reads=[bYs, bS], writes=[bYe])
            else:
                S.op("dve", lambda e, t_=t_: e.tensor_scalar(out=yev[:, :, :, t_], in0=Ysb, scalar1=tmask[:, t_:t_ + 1], scalar2=None, op0=ALU.mult),
                     reads=[bYs, bS], writes=[bYe])
        bp4 = buf("ps4")

        def pmm(e):
            ins = None
            for g in range(GB):
                ins = e.matmul(ps[4][:, g * 16:(g + 1) * 16], Yexp[:, g, :], Csel, start=True, stop=True, skip_group_check=True)
            return ins
        S.op("pe", pmm, reads=[bYe, bS], writes=[bp4])
        yt_, byt = ytmp[i], buf("ytmp%d" % i)
        z_, bz = zt_[i], buf("zt%d" % i)
        S.op("dve", lambda e: e.tensor_tensor(out=yt_, in0=u_, in1=dB[:, g0 * 16:(g0 + GB) * 16], op=ALU.mult), reads=[bu, cst3], writes=[byt])
        S.op("dve", lambda e: e.tensor_tensor(out=yt_, in0=yt_, in1=ps[4][:, 0:NW], op=ALU.add), reads=[byt, bp4], writes=[byt])
        S.op("act", lambda e: e.activation(out=z_, in_=yt_, func=AF.Square), reads=[byt], writes=[bz])
        S.op("dve", lambda e: e.tensor_scalar(out=z_, in0=z_, scalar1=0.044715, scalar2=1.0, op0=ALU.mult, op1=ALU.add), reads=[bz], writes=[bz])
        S.op("dve", lambda e: e.tensor_tensor(out=z_, in0=z_, in1=yt_, op=ALU.mult), reads=[bz, byt], writes=[bz])
        S.op("act", lambda e: e.activation(out=z_, in_=z_, func=AF.Sigmoid, scale=1.5957691216057308), reads=[bz], writes=[bz])
        S.op("dve", lambda e: e.tensor_tensor(out=z_, in0=z_, in1=yt_, op=ALU.mult), reads=[bz, byt], writes=[bz])
        S.dma("sp", zscr[t * 128:(t + 1) * 128, g0 * 16:(g0 + GB) * 16], z_, reads=[bz], writes=[buf("zscrd")], owner=bz, multi=True)

    def emit_state(gb, srcX, dst_r, dst_i, bsrc):
        g0 = gb * GB
        k = ucount[0] % 2
        ucount[0] += 1
        bp = buf("ps5")

        def tr(e):
            e.transpose(out=ps[5][0:GB, 0:64], in_=srcX[:, 0, :], identity=ident_f[0:64, 0:64])
            return e.transpose(out=ps[5][0:GB, 64:128], in_=srcX[:, 1, :], identity=ident_f[0:64, 0:64])
        S.op("pe", tr, reads=[bsrc, cst], writes=[bp])
        so_, bso = st_o[k], buf("st_o%d" % k)
        S.op("dve", lambda e: e.tensor_copy(out=so_[0:GB, :, :], in_=ps[5][0:GB, 0:128].rearrange("p (r q) -> p r q", r=2)), reads=[bp], writes=[bso])
        S.dma("sp", dst_r[g0:g0 + GB, :], so_[0:GB, 0, :], reads=[bso], writes=[buf("os5")], owner=bso, multi=True)
        S.dma("sp", dst_i[g0:g0 + GB, :], so_[0:GB, 1, :], reads=[bso], writes=[buf("os5")], owner=bso, multi=True)

    for gb in range(128 // GB):
        g0 = gb * GB
        for hh in range(2):
            bW = buf(Wbufs[gb % 2][hh])
            S.dma("sp", Wsets[gb % 2][hh], Wall_d[g0 + hh * 16:g0 + (hh + 1) * 16].rearrange("g p w -> p g w"), writes=[bW], owner=bW)
        bsi = buf("Yexp")
        S.dma("sp", st_in[0:GB, 0, :].rearrange("g (s p) -> g s p", s=16), s5r_d[:, g0:g0 + GB, :].rearrange("s g p -> g s p"), writes=[bsi], owner=bsi)
        S.dma("sp", st_in[0:GB, 1, :].rearrange("g (s p) -> g s p", s=16), s5i_d[:, g0:g0 + GB, :].rearrange("s g p -> g s p"), reads=[bsi], writes=[bsi], owner=bsi)
        x0v = X0[0:64, :, :].rearrange("p r (g j) -> p r g j", g=GB)
        for ri in range(2):
            for q4 in range(4):
                bp = buf("ps6")

                def trs_(e, ri=ri, q4=q4):
                    ins = None
                    for jj in range(4):
                        j = q4 * 4 + jj
                        ins = e.transpose(out=ps[6][0:64, jj * GB:(jj + 1) * GB], in_=st_in[0:GB, ri, j * 64:(j + 1) * 64], identity=ident_f[0:GB, 0:GB])
                    return ins
                S.op("pe", trs_, reads=[bsi, cst], writes=[bp])
                S.op("dve", lambda e, ri=ri, q4=q4: e.tensor_copy(out=x0v[:, ri, :, q4 * 4:(q4 + 1) * 4],
                                                                  in_=ps[6][0:64, 0:4 * GB].rearrange("p (j g) -> p g j", j=4)),
                     reads=[bp], writes=[buf("X0")])
        S.op("dve", lambda e: e.memset(Xh, 0.0), writes=[buf("Xh")])
        work = [(projp, t, 16 if t < 8 else 2, False, False) for t in range(NT_PRE)]
        work += [(proj, t, 16, True, False) for t in range(8)]
        work += [(proj, 8, 16, True, True)]
        nxt = s5_front(gb, work[0][0], work[0][1])
        for wi, (src, t, nv, full, sample) in enumerate(work):
            cur = nxt
            if wi + 1 < len(work):
                nxt = s5_front(gb, work[wi + 1][0], work[wi + 1][1])
            if sample:
                emit_state(gb, xhv[:, :, :, 0], o_s5p_r, o_s5p_i, buf("Xh"))
            s5_rest(gb, t, cur, nv, full, sample)
        xnv = Xn[0:64, :, :].rearrange("p r (g j) -> p r g j", g=GB)
        for j in range(16):
            emit_state(gb, xnv[:, :, :, j], o_s5s_r[j], o_s5s_i[j], buf("Xn"))
    flush_stream()
    S.barrier(B.values())

    coff[0] = 16 * ROWS_OWN * 2
    bgl = carve([128, 2048], F32)
    S.dma("sp", bgl, bglu_d, writes=[cst3], owner=cst3, multi=True)

    def cons_glu(t, c0, cw, pt, bp):
        i = next_ot()
        o, bo, r, br = ot[i], buf("ot%d" % i), rt[i], buf("rt%d" % i)
        S.dma("pool", r[:, 0:cw], zscr[t * 128:(t + 1) * 128, c0:c0 + cw], writes=[br], owner=br)
        S.op("dve", lambda e: e.tensor_tensor(out=o[:, 0:cw], in0=pt, in1=bgl[:, c0:c0 + cw], op=ALU.add), reads=[bp, cst3], writes=[bo])
        S.op("act", lambda e: e.activation(out=o[:, 0:cw], in_=o[:, 0:cw], func=AF.Sigmoid), reads=[bo], writes=[bo])
        S.op("dve", lambda e: e.tensor_tensor(out=o[:, 0:cw], in0=o[:, 0:cw], in1=r[:, 0:cw], op=ALU.mult), reads=[bo, br], writes=[bo])
        S.dma("pool", s5o[t * 128:(t + 1) * 128, c0:c0 + cw], o[:, 0:cw], reads=[bo], writes=[buf("s5od")], owner=bo, multi=True)

    load_actT(zscr, NT_OWN, 0, 16)
    stream(w_glu, 0, 16, 0, 2048, NT_OWN, cons_glu)
    flush_stream()
    S.barrier(B.values())
    for t in range(NT_OWN):
        xi, stt = xin[t % 2], stat[t % 2]
        bxi, bst = buf("xin%d" % (t % 2)), buf("stat%d" % (t % 2))
        S.dma("pool", xi[:, 0:2048], s5o[t * 128:(t + 1) * 128, :], writes=[bxi], owner=bxi)
        S.op("act", lambda e, xi=xi, stt=stt, t=t: e.activation(out=xbf[t % 2][:, 0:2048], in_=xi[:, 0:2048], func=AF.Square, accum_out=stt[:, 0:1]),
             reads=[bxi], writes=[bst, buf("xbf%d" % (t % 2))])
        S.op("dve", lambda e, stt=stt: e.tensor_scalar(out=stt[:, 1:2], in0=stt[:, 0:1], scalar1=1.0 / 2048, scalar2=EPS,
                                                       op0=ALU.mult, op1=ALU.add), reads=[bst], writes=[bst])
        S.op("act", lambda e, stt=stt: e.activation(out=stt[:, 2:3], in_=stt[:, 1:2], func=AF.Sqrt), reads=[bst], writes=[bst])
        S.op("dve", lambda e, stt=stt: e.reciprocal(out=stt[:, 3:4], in_=stt[:, 2:3]), reads=[bst], writes=[bst])
        S.op("dve", lambda e, xi=xi, stt=stt: e.tensor_scalar(out=xi[:, 0:2048], in0=xi[:, 0:2048], scalar1=stt[:, 3:4], scalar2=None, op0=ALU.mult),
             reads=[bxi, bst], writes=[bxi])
        S.dma("pool", mix[t * 128:(t + 1) * 128, 2048:4096], xi[:, 0:2048], reads=[bxi], writes=[buf("mixd")], owner=bxi, multi=True)
    flush_stream()
    S.barrier(B.values())

    def cons_wout(t, c0, cw, pt, bp):
        i = next_ot()
        o, bo, r, br = ot[i], buf("ot%d" % i), rt[i], buf("rt%d" % i)
        S.dma("pool", r[:, 0:cw], x_own[t * 128:(t + 1) * 128, c0:c0 + cw], writes=[br], owner=br)
        S.op("dve", lambda e: e.tensor_tensor(out=o[:, 0:cw], in0=pt, in1=r[:, 0:cw], op=ALU.add), reads=[bp, br], writes=[bo])
        S.dma("pool", x1[t * 128:(t + 1) * 128, c0:c0 + cw], o[:, 0:cw], reads=[bo], writes=[buf("x1d")], owner=bo, multi=True)

    load_actT(mix, NT_OWN, 0, 32, norm=False)
    stream(w_out, 0, 32, 0, D, NT_OWN, cons_wout, gain=1)
    flush_stream()
    S.barrier(B.values())

    load_actT(x1, NT_OWN, 0, 32, norm=True)

    def cons_gate(t, c0, cw, pt, bp):
        S.op("act", lambda e: e.activation(out=gtmp[:, t, 0:cw], in_=pt, func=AF.Silu), reads=[bp], writes=[buf("gtmp%d" % t)])

    def cons_up(t, c0, cw, pt, bp):
        i = next_ot()
        o, bo = ot[i], buf("ot%d" % i)
        S.op("dve", lambda e: e.tensor_tensor(out=o[:, 0:cw], in0=pt, in1=gtmp[:, t, 0:cw], op=ALU.mult),
             reads=[bp, buf("gtmp%d" % t)], writes=[bo])
        S.dma("pool", act[t * 128:(t + 1) * 128, c0:c0 + cw], o[:, 0:cw], reads=[bo], writes=[buf("actd")], owner=bo, multi=True)

    for gi in range(DFF // CG):
        stream(w_gate, 0, 32, gi * CG, CG, NT_OWN, cons_gate, gain=2)
        stream(w_up, 0, 32, gi * CG, CG, NT_OWN, cons_up, gain=2)
    flush_stream()
    S.barrier(B.values())

    def cons_down(t, c0, cw, pt, bp):
        i = next_ot()
        o, bo, r, br = ot[i], buf("ot%d" % i), rt[i], buf("rt%d" % i)
        S.dma("pool", r[:, 0:cw], x1[t * 128:(t + 1) * 128, c0:c0 + cw], writes=[br], owner=br)
        S.op("dve", lambda e: e.tensor_tensor(out=o[:, 0:cw], in0=pt, in1=r[:, 0:cw], op=ALU.add), reads=[bp, br], writes=[bo])
        S.dma("pool", x1[t * 128:(t + 1) * 128, c0:c0 + cw], o[:, 0:cw], reads=[bo], writes=[buf("x1d")], owner=bo, multi=True)

    for kb0, kbn in ((0, 32), (32, 32), (64, 22)):
        load_actT(act, NT_OWN, kb0, kbn)
        stream(w_down, kb0 * 128, kbn, 0, D, NT_OWN, cons_down)
        flush_stream()
    S.barrier(B.values())

    for t in range(NT_OWN):
        xi, stt = xin[t % 2], stat[t % 2]
        bxi, bst = buf("xin%d" % (t % 2)), buf("stat%d" % (t % 2))
        S.dma("pool", xi[:], x1[t * 128:(t + 1) * 128, :], writes=[bxi], owner=bxi)
        S.op("act", lambda e, xi=xi, stt=stt, t=t: e.activation(out=xbf[t % 2][:], in_=xi[:], func=AF.Square, accum_out=stt[:, 0:1]),
             reads=[bxi], writes=[bst, buf("xbf%d" % (t % 2))])
        S.op("dve", lambda e, stt=stt: e.tensor_scalar(out=stt[:, 1:2], in0=stt[:, 0:1], scalar1=1.0 / D, scalar2=EPS,
                                                       op0=ALU.mult, op1=ALU.add), reads=[bst], writes=[bst])
        S.op("act", lambda e, stt=stt: e.activation(out=stt[:, 2:3], in_=stt[:, 1:2], func=AF.Sqrt), reads=[bst], writes=[bst])
        S.op("dve", lambda e, stt=stt: e.reciprocal(out=stt[:, 3:4], in_=stt[:, 2:3]), reads=[bst], writes=[bst])
        S.op("dve", lambda e, xi=xi, stt=stt: e.scalar_tensor_tensor(out=xi[:], in0=xi[:], scalar=stt[:, 3:4], in1=gf_t[:],
                                                                     op0=ALU.mult, op1=ALU.mult),
             reads=[bxi, bst, cst], writes=[bxi])
        S.dma("pool", y_out[t * 128:(t + 1) * 128, :], xi[:], reads=[bxi], writes=[buf("yd")], owner=bxi, multi=True)
    flush_stream()
    S.barrier(B.values())

    with nc.Block() as block:
        S.emit(block)
    es.close()
    return nc


def _cs(pos):
    inv = (np.float32(10000.0) ** (-np.arange(128, dtype=np.float32) / np.float32(128))).astype(np.float32)
    ang = (pos.astype(np.float32)[:, :, None] * inv[None, None, :]).astype(np.float32)
    c_, s_ = np.cos(ang).astype(np.float32), np.sin(ang).astype(np.float32)
    return np.ascontiguousarray(np.concatenate([c_, c_, -s_, s_], axis=-1), dtype=np.float32)


_CONST_CACHE = {}


def _consts():
    if _CONST_CACHE:
        return _CONST_CACHE
    import ml_dtypes
    lg = np.log(1.0 - 2.0 ** (-5.0 - np.arange(8, dtype=np.float32))).astype(np.float32)
    p = np.arange(128)
    mask = np.zeros((8, 128, 2, 128), np.float32)
    diff = (p[None, :] - p[:, None]).astype(np.float32)
    for h_ in range(8):
        m0 = np.where(diff >= 0, np.exp(np.maximum(diff, 0) * lg[h_]), 0.0)
        same = (p[None, :] // 8) == (p[:, None] // 8)
        mask[h_, :, 0, :] = m0
        mask[h_, :, 1, :] = np.where(same, m0, 0.0)
    dqk = np.zeros((128, 2, 3, 8), np.float32)
    for h_ in range(8):
        dqk[:, 0, 0, h_] = np.exp(lg[h_] * (p + 1.0))
        dqk[:, 0, 1, h_] = np.exp(lg[h_] * ((p % 8) + 1.0))
        dqk[:, 0, 2, h_] = np.exp(lg[h_] * ((p % 16) + 1.0))
        dqk[:, 1, 0, h_] = np.exp(lg[h_] * (127.0 - p)) / 16.0
        dqk[:, 1, 1, h_] = np.exp(lg[h_] * (7.0 - (p % 8))) / 16.0
        dqk[:, 1, 2, h_] = np.exp(lg[h_] * (15.0 - (p % 16))) / 16.0
    cm = np.zeros((128, 16, 2, 128), np.float32)
    rm = np.zeros((128, 16), np.float32)
    for sq in range(16):
        cm[:, sq, :, sq * 8:(sq + 1) * 8] = 1.0
        rm[sq * 8:(sq + 1) * 8, sq] = 1.0
    q = np.arange(128)
    m3 = ((q[None, :] // 16) >= (q[:, None] // 16)).astype(np.float32)
    sel_f = np.zeros((128, 16), np.float32)
    sel_f[q, q % 8] = 1.0
    sel_f[q, 8 + q // 16] = 1.0
    sel_b = np.zeros((128, 32), np.float32)
    sel_b[q, q // 8] = 1.0
    sel_b[q, 16 + q % 16] = 1.0
    _CONST_CACHE.update(m3=m3, sel_f=sel_f, sel_b=sel_b.astype(ml_dtypes.bfloat16))
    _CONST_CACHE.update(mask=mask, dqk=dqk, cmask=cm.reshape(128, 16, 256).astype(ml_dtypes.bfloat16), rmask=rm)
    return _CONST_CACHE


def _core_inputs(c, inp):
    b, h = c // 2, c % 2
    xp = np.concatenate([inp["meta_tokens"], inp["x_prompt"][b]], axis=0)
    x_own = np.zeros((ROWS_OWN, D), np.float32)
    x_own[:HALF] = xp[16 + h * HALF:16 + (h + 1) * HALF]
    x_own[8 * 128:] = inp["x_sample"][16 * c:16 * c + 16].reshape(128, D)
    g_all = np.stack([inp["norm1_g"][0].reshape(32, 128).T,
                      np.concatenate([inp["ret_gn_g"][0], inp["s5_norm_g"][0]]).reshape(32, 128).T,
                      inp["norm2_g"][0].reshape(32, 128).T], axis=1)
    x_pre = np.zeros((ROWS_PRE, D), np.float32)
    pos_pre = np.zeros((NT_PRE, 128), np.float32)
    if h == 1:
        x_pre[:NPRE] = xp[:NPRE]
        pos_pre = (np.arange(NT_PRE * 128, dtype=np.float32)).reshape(NT_PRE, 128)
    else:
        x_pre[8 * 128:8 * 128 + 16] = xp[:16]
        pos_pre[8] = np.arange(128)
    C = _consts()
    pos_own = np.zeros((NT_OWN, 128), np.float32)
    for t in range(8):
        pos_own[t] = 16 + h * HALF + t * 128 + np.arange(128)
    pos_own[8] = 16384 + (np.arange(128) % 8)
    return {
        "x_pre": x_pre, "w_in": inp["w_in"][0],
        "cs_own": _cs(pos_own), "cs_pre": _cs(pos_pre),
        "maskd": C["mask"], "dqk": C["dqk"], "cmask": C["cmask"], "rmask": C["rmask"],
        "st_ret": np.ascontiguousarray(inp["state_ret"][0, 16 * c:16 * c + 16]),
        "lam_re": inp["s5_lam_re"][0], "lam_im": inp["s5_lam_im"][0],
        "ldt": np.ascontiguousarray(np.broadcast_to(inp["s5_log_dt"][0][None, :], (64, 128)), dtype=np.float32),
        "m3": C["m3"], "sel_f": C["sel_f"], "sel_b": C["sel_b"],
        "b_re": inp["s5_b_re"][0], "b_im": inp["s5_b_im"][0], "c_re": inp["s5_c_re"][0], "c_im": inp["s5_c_im"][0],
        "dB": np.ascontiguousarray(np.broadcast_to(inp["s5_d"][0][None, :], (128, 2048)), dtype=np.float32),
        "bglu": np.ascontiguousarray(np.broadcast_to(inp["b_glu"][0][None, :], (128, 2048)), dtype=np.float32),
        "s5r": np.ascontiguousarray(inp["state_s5_re"][0, 16 * c:16 * c + 16]),
        "s5i": np.ascontiguousarray(inp["state_s5_im"][0, 16 * c:16 * c + 16]),
        "w_glu": inp["w_glu"][0],
        "x_own": x_own,
        "w_out": inp["w_out"][0], "w_gate": inp["w_gate"][0], "w_up": inp["w_up"][0], "w_down": inp["w_down"][0],
        "g_all": np.ascontiguousarray(g_all, dtype=np.float32),
        "gfin": np.ascontiguousarray(np.broadcast_to(inp["final_norm_g"][None, :], (128, D)), dtype=np.float32),
        "ident": np.eye(128, dtype=np.float32),
    }


def kernel(**inp):
    inp = {k: np.asarray(v) for k, v in inp.items()}
    nc = build_program()
    in_maps = [_core_inputs(c, inp) for c in range(NCORES)]
    res = run_bass_kernel_spmd(nc, in_maps, core_ids=list(range(NCORES)))
    R = res.results
    LAST['R'] = R
    y_prompt = np.zeros((4, 2048, D), np.float32)
    y_sample = np.zeros((128, 8, D), np.float32)
    for c in range(NCORES):
        b, h = c // 2, c % 2
        y = R[c]["y"]
        y_prompt[b, h * HALF:(h + 1) * HALF] = y[:HALF]
        y_sample[16 * c:16 * c + 16] = y[8 * 128:].reshape(16, 8, D)
    z = np.zeros
    ret_p = np.stack([R[2 * b + 1]["o_ret_p"] for b in range(4)])[None]
    ret_s = np.concatenate([R[c]["o_ret_s"] for c in range(NCORES)])[None]
    s5p_r = np.stack([R[2 * b + 1]["o_s5p_r"] for b in range(4)])[None].astype(np.float32)
    s5p_i = np.stack([R[2 * b + 1]["o_s5p_i"] for b in range(4)])[None].astype(np.float32)
    s5s_r = np.concatenate([R[c]["o_s5s_r"] for c in range(NCORES)])[None].astype(np.float32)
    s5s_i = np.concatenate([R[c]["o_s5s_i"] for c in range(NCORES)])[None].astype(np.float32)
    return (y_prompt, y_sample, ret_p.astype(np.float32), s5p_r, s5p_i, ret_s.astype(np.float32), s5s_r, s5s_i)
    return (y_prompt, y_sample,
            ret_p.astype(np.float32), z((1, 4, 128, 64), np.float32), z((1, 4, 128, 64), np.float32),
            ret_s.astype(np.float32), z((1, 128, 128, 64), np.float32), z((1, 128, 128, 64), np.float32))
```

```python
import numpy as np
from contextlib import ExitStack
import concourse.bass as bass
import concourse.mybir as mybir
from concourse.bass_utils import run_bass_kernel_spmd

F32 = mybir.dt.float32
BF16 = mybir.dt.bfloat16
AF = mybir.ActivationFunctionType
ALU = mybir.AluOpType

D = 4096
DFF = 11008
NCORES = 8
HALF = 1024
NPRE = 1040
NT_OWN = 9
NT_PRE = 9
ROWS_OWN = NT_OWN * 128
ROWS_PRE = NT_PRE * 128
EPS = 1e-6
CG = 256
DEBUG = False
LAST = {}


class Buf:
    def __init__(self, name):
        self.name = name
        self.writers = []
        self.readers = []
        self.dsem = None
        self.dcount = 0


class Sched:
    def __init__(self, nc, es):
        self.nc = nc
        self.es = es
        self.eng = {}
        for n in ("pe", "act", "dve", "pool", "sp"):
            self.eng[n] = dict(sem=es.enter_context(nc.semaphore("s_" + n)), count=0, ops=[], seen={})
        self.dsems = []
        self.nsem = 0

    def _dsem(self, buf):
        if buf.dsem is None:
            buf.dsem = self.es.enter_context(self.nc.semaphore("d%d" % self.nsem))
            self.nsem += 1
            self.dsems.append(buf)
        return buf.dsem

    def _waits(self, e, reads, writes):
        need = {}

        def add(st):
            s, v = st
            k = id(s)
            if k not in need or need[k][1] < v:
                need[k] = (s, v)
        for b in reads:
            for st in b.writers:
                add(st)
        for b in writes:
            for st in b.writers:
                add(st)
            for st in b.readers:
                add(st)
        out = []
        seen = self.eng[e]["seen"]
        for k, (s, v) in need.items():
            if seen.get(k, 0) < v:
                seen[k] = v
                out.append((s, v))
        return out

    def _commit(self, st, reads, writes, multi=False):
        for b in writes:
            if multi:
                b.writers.append(st)
            else:
                b.writers = [st]
                b.readers = []
        for b in reads:
            b.readers.append(st)
            if len(b.readers) > 64:
                best = {}
                for s, v in b.readers:
                    if id(s) not in best or best[id(s)][1] < v:
                        best[id(s)] = (s, v)
                b.readers = list(best.values())

    def op(self, e, fn, reads=(), writes=()):
        E = self.eng[e]
        waits = self._waits(e, reads, writes)
        E["count"] += 1
        st = (E["sem"], E["count"])
        E["ops"].append((waits, fn, E["sem"], 1))
        self._commit(st, reads, writes)
        return st

    def dma(self, q, out, in_, reads=(), writes=(), owner=None, multi=False):
        E = self.eng[q]
        waits = self._waits(q, reads, writes)
        sem = self._dsem(owner)
        owner.dcount += 16
        st = (sem, owner.dcount)
        E["ops"].append((waits, lambda eng, o=out, i=in_: eng.dma_start(out=o, in_=i), sem, 16))
        self._commit(st, reads, writes, multi=multi)
        return st

    def barrier(self, bufs=()):
        for b in bufs:
            b.writers = []
            b.readers = []
        stamps = []
        for n, E in self.eng.items():
            if E["count"]:
                stamps.append((E["sem"], E["count"]))
        for b in self.dsems:
            stamps.append((b.dsem, b.dcount))
        for n, E in self.eng.items():
            w = []
            for s, v in stamps:
                if E["seen"].get(id(s), 0) < v:
                    E["seen"][id(s)] = v
                    w.append((s, v))
            if w:
                E["ops"].append((w, None, None, 0))

    def emit(self, block):
        def run(E):
            def f(eng):
                for waits, fn, sem, amt in E["ops"]:
                    for s, v in waits:
                        eng.wait_ge(s, v)
                    if fn is None:
                        continue
                    ins = fn(eng)
                    ins.then_inc(sem, amt)
            return f
        block.tensor(run(self.eng["pe"]))
        block.scalar(run(self.eng["act"]))
        block.vector(run(self.eng["dve"]))
        block.gpsimd(run(self.eng["pool"]))
        block.sync(run(self.eng["sp"]))


def build_program():
    nc = bass.Bass("TRN2", target_bir_lowering=False)
    es = ExitStack()
    S = Sched(nc, es)

    def din(name, shape, dt=F32):
        return nc.dram_tensor(name, list(shape), dt, kind="ExternalInput").ap()

    def dout(name, shape, dt=F32):
        return nc.dram_tensor(name, list(shape), dt, kind="ExternalOutput").ap()

    def dscr(name, shape, dt=F32):
        return nc.dram_tensor(name, list(shape), dt, kind=("ExternalOutput" if (DEBUG and name in ("mix", "proj")) else "Internal")).ap()

    x_own = din("x_own", [ROWS_OWN, D])
    x_pre = din("x_pre", [ROWS_PRE, D])
    w_in = din("w_in", [D, 10240])
    cs_own = din("cs_own", [NT_OWN, 128, 512])
    cs_pre = din("cs_pre", [NT_PRE, 128, 512])
    maskd = din("maskd", [8, 128, 2, 128])
    dqk_d = din("dqk", [128, 2, 3, 8])
    cmask_d = din("cmask", [128, 16, 256], BF16)
    rmask_d = din("rmask", [128, 16])
    st_ret = din("st_ret", [16, 8, 256, 256])
    o_ret_p = dout("o_ret_p", [8, 256, 256])
    o_ret_s = dout("o_ret_s", [16, 8, 256, 256])
    lam_re_d = din("lam_re", [128, 64]); lam_im_d = din("lam_im", [128, 64])
    ldt_d = din("ldt", [64, 128]); m3_d = din("m3", [128, 128])
    sel_f_d = din("sel_f", [128, 16]); sel_b_d = din("sel_b", [128, 32], BF16)
    b_re_d = din("b_re", [128, 64, 16]); b_im_d = din("b_im", [128, 64, 16])
    c_re_d = din("c_re", [128, 16, 64]); c_im_d = din("c_im", [128, 16, 64])
    dB_d = din("dB", [128, 2048]); bglu_d = din("bglu", [128, 2048])
    s5r_d = din("s5r", [16, 128, 64]); s5i_d = din("s5i", [16, 128, 64])
    w_glu = din("w_glu", [2048, 2048])
    o_s5p_r = dout("o_s5p_r", [128, 64]); o_s5p_i = dout("o_s5p_i", [128, 64])
    o_s5s_r = dout("o_s5s_r", [16, 128, 64]); o_s5s_i = dout("o_s5s_i", [16, 128, 64])
    Wall_d = dscr("Wall", [128, 128, 512], BF16)
    zscr = dscr("zscr", [ROWS_OWN, 2048], F32)
    s5o = dscr("s5o", [ROWS_OWN, 2048], F32)
    proj = dscr("proj", [ROWS_OWN, 10240], F32)
    projp = dscr("projp", [ROWS_PRE, 10240], F32)
    w_out = din("w_out", [D, D])
    w_gate = din("w_gate", [D, DFF])
    w_up = din("w_up", [D, DFF])
    w_down = din("w_down", [DFF, D])
    g_all = din("g_all", [128, 3, 32])
    gfin = din("gfin", [128, D])
    ident_in = din("ident", [128, 128])
    y_out = dout("y", [ROWS_OWN, D])

    mix = dscr("mix", [ROWS_OWN, D], F32)
    x1 = dscr("x1", [ROWS_OWN, D], F32)
    h2 = dscr("h2", [ROWS_OWN, D], F32)
    act = dscr("act", [ROWS_OWN, DFF], F32)

    def sb(name, shape, dt):
        return es.enter_context(nc.sbuf_tensor(name, list(shape), dt))

    actT = sb("actT", [128, 32, ROWS_OWN], BF16)
    wbf = [sb("wbf%d" % i, [128, 32, CG], BF16) for i in range(2)]
    wst = [sb("wst%d" % i, [128, 8, CG], F32) for i in range(2)]
    xin = [sb("xin%d" % i, [128, D], F32) for i in range(2)]
    xbf = [sb("xbf%d" % i, [128, D], BF16) for i in range(2)]
    ot = [sb("ot%d" % i, [128, CG], F32) for i in range(4)]
    rt = [sb("rt%d" % i, [128, CG], F32) for i in range(4)]
    gtmp = sb("gtmp", [128, NT_OWN, CG], BF16)
    stat = [sb("stat%d" % i, [128, 8], F32) for i in range(2)]
    gains = sb("gains", [128, 3, 32], F32)
    gf_t = sb("gf_t", [128, D], F32)
    ident_f = sb("ident_f", [128, 128], F32)
    ident = sb("ident_b", [128, 128], BF16)

    ps = [es.enter_context(nc.psum_tensor("ps%d" % i, [128, 512], F32)) for i in range(8)]

    B = {}

    def buf(name):
        if name not in B:
            B[name] = Buf(name)
        return B[name]

    cst = buf("const")
    S.dma("sp", gains[:], g_all, writes=[cst], owner=cst, multi=True)
    S.dma("sp", gf_t[:], gfin, writes=[cst], owner=cst, multi=True)
    S.dma("sp", ident_f[:], ident_in, writes=[cst], owner=cst, multi=True)
    S.op("dve", lambda e: e.tensor_copy(out=ident[:], in_=ident_f[:]), reads=[cst], writes=[buf("ident")])

    def load_actT(src, ntiles, kc0, nkc, norm=False):
        W = nkc * 128
        for t in range(ntiles):
            xi, xb, stt = xin[t % 2], xbf[t % 2], stat[t % 2]
            bxi, bxb, bst = buf("xin%d" % (t % 2)), buf("xbf%d" % (t % 2)), buf("stat%d" % (t % 2))
            S.dma("pool", xi[:, 0:W], src[t * 128:(t + 1) * 128, kc0 * 128:kc0 * 128 + W], writes=[bxi], owner=bxi)
            if norm:
                S.op("act", lambda e, xi=xi, stt=stt, xb=xb: e.activation(out=xb[:, 0:W], in_=xi[:, 0:W], func=AF.Square,
                                                                    accum_out=stt[:, 0:1]),
                     reads=[bxi], writes=[bst, bxb])
                S.op("dve", lambda e, stt=stt: e.tensor_scalar(out=stt[:, 1:2], in0=stt[:, 0:1], scalar1=1.0 / W,
                                                               scalar2=EPS, op0=ALU.mult, op1=ALU.add),
                     reads=[bst], writes=[bst])
                S.op("act", lambda e, stt=stt: e.activation(out=stt[:, 2:3], in_=stt[:, 1:2], func=AF.Sqrt),
                     reads=[bst], writes=[bst])
                S.op("dve", lambda e, stt=stt: e.reciprocal(out=stt[:, 3:4], in_=stt[:, 2:3]), reads=[bst], writes=[bst])
                S.op("dve", lambda e, xi=xi, xb=xb, stt=stt: e.tensor_scalar(out=xb[:, 0:W], in0=xi[:, 0:W],
                                                                             scalar1=stt[:, 3:4], scalar2=None,
                                                                             op0=ALU.mult),
                     reads=[bxi, bst], writes=[bxb])
            else:
                S.op("dve", lambda e, xi=xi, xb=xb: e.tensor_copy(out=xb[:, 0:W], in_=xi[:, 0:W]),
                     reads=[bxi], writes=[bxb])
            for q in range((nkc + 3) // 4):
                pb = 6 + (q % 2)
                bp = buf("ps%d" % pb)
                pv = ps[pb].bitcast(BF16)
                nq = min(4, nkc - q * 4)

                def tr(e, xb=xb, pv=pv, q=q, nq=nq):
                    ins = None
                    for i in range(nq):
                        ins = e.transpose(out=pv[:, i * 128:(i + 1) * 128], in_=xb[:, (q * 4 + i) * 128:(q * 4 + i + 1) * 128],
                                          identity=ident[:])
                    return ins
                S.op("pe", tr, reads=[bxb, buf("ident")], writes=[bp])
                dst = actT[:, q * 4:q * 4 + nq, t * 128:(t + 1) * 128]
                srcv = pv[:, 0:nq * 128].rearrange("p (c t) -> p c t", c=nq)
                eng = "act" if q % 2 else "dve"
                if eng == "act":
                    S.op("act", lambda e, dst=dst, srcv=srcv: e.copy(out=dst, in_=srcv), reads=[bp], writes=[buf("actT")])
                else:
                    S.op("dve", lambda e, dst=dst, srcv=srcv: e.tensor_copy(out=dst, in_=srcv), reads=[bp], writes=[buf("actT")])

    wcount = [0]
    pcount = [0]
    pending = []

    def stream(Wd, krow0, nkc, col0, ncols, ntiles, consumer, gain=None):
        ngr = (ncols + CG - 1) // CG
        for gi in range(ngr):
            c0 = col0 + gi * CG
            cw = min(CG, col0 + ncols - c0)
            pending.append((Wd, krow0, nkc, c0, cw, ntiles, consumer, gain))

    def _load_group(item):
        Wd, krow0, nkc, c0, cw, ntiles, consumer, gain = item
        wi = wcount[0] % 2
        wcount[0] += 1
        wb, bwb = wbf[wi], buf("wbf%d" % wi)
        nh = (nkc + 7) // 8
        for hh in range(nh):
            k0 = hh * 8
            kn = min(8, nkc - k0)
            si = pcount[0] % 2
            pcount[0] += 1
            ws, bws = wst[si], buf("wst%d" % si)
            S.dma("sp", ws[:, 0:kn, 0:cw],
                  Wd[krow0 + k0 * 128: krow0 + (k0 + kn) * 128, c0:c0 + cw].rearrange("(k p) n -> p k n", p=128),
                  writes=[bws], owner=bws)
            if gain is None:
                if hh % 2:
                    S.op("act", lambda e, wb=wb, ws=ws, k0=k0, kn=kn, cw=cw: e.copy(out=wb[:, k0:k0 + kn, 0:cw], in_=ws[:, 0:kn, 0:cw]),
                         reads=[bws], writes=[bwb])
                else:
                    S.op("dve", lambda e, wb=wb, ws=ws, k0=k0, kn=kn, cw=cw: e.tensor_copy(out=wb[:, k0:k0 + kn, 0:cw], in_=ws[:, 0:kn, 0:cw]),
                         reads=[bws], writes=[bwb])
            else:
                for kk in range(kn):
                    kc = k0 + kk
                    gcol = gains[:, gain, (krow0 // 128 + kc):(krow0 // 128 + kc) + 1]
                    if kk % 2:
                        S.op("act", lambda e, wb=wb, ws=ws, kc=kc, kk=kk, cw=cw, gcol=gcol: e.activation(
                            out=wb[:, kc, 0:cw], in_=ws[:, kk, 0:cw], func=AF.Copy, scale=gcol),
                            reads=[bws, cst], writes=[bwb])
                    else:
                        S.op("dve", lambda e, wb=wb, ws=ws, kc=kc, kk=kk, cw=cw, gcol=gcol: e.tensor_scalar(
                            out=wb[:, kc, 0:cw], in0=ws[:, kk, 0:cw], scalar1=gcol, scalar2=None, op0=ALU.mult),
                            reads=[bws, cst], writes=[bwb])
        return wb, bwb

    def _compute_group(item, wb, bwb):
        Wd, krow0, nkc, c0, cw, ntiles, consumer, gain = item
        for t in range(ntiles):
            pb = pcount[1] % 6 if len(pcount) > 1 else 0
            pcount[1] += 1
            bp = buf("ps%d" % pb)
            pt = ps[pb][:, 0:cw]

            def mm(e, pt=pt, wb=wb, t=t, cw=cw):
                ins = None
                for kc in range(nkc):
                    ins = e.matmul(pt, actT[:, kc, t * 128:(t + 1) * 128], wb[:, kc, 0:cw], start=(kc == 0), stop=(kc == nkc - 1))
                return ins
            S.op("pe", mm, reads=[buf("actT"), bwb], writes=[bp])
            consumer(t, c0, cw, pt, bp)

    pcount.append(0)

    def flush_stream():
        items = pending[:]
        del pending[:]
        if not items:
            return
        cur = _load_group(items[0])
        for n_, item in enumerate(items):
            nxt = _load_group(items[n_ + 1]) if n_ + 1 < len(items) else None
            _compute_group(item, *cur)
            cur = nxt

    ocnt = [0]

    def next_ot():
        i = ocnt[0] % 4
        ocnt[0] += 1
        return i


    def mk_store(dst):
        def cons(t, c0, cw, pt, bp):
            i = next_ot()
            o, bo = ot[i], buf("ot%d" % i)
            if i % 2:
                S.op("act", lambda e: e.copy(out=o[:, 0:cw], in_=pt), reads=[bp], writes=[bo])
            else:
                S.op("dve", lambda e: e.tensor_copy(out=o[:, 0:cw], in_=pt), reads=[bp], writes=[bo])
            S.dma("pool", dst[t * 128:(t + 1) * 128, c0:c0 + cw], o[:, 0:cw], reads=[bo], writes=[buf("projd")], owner=bo, multi=True)
        return cons

    load_actT(x_pre, NT_PRE, 0, 32, norm=True)
    stream(w_in, 0, 32, 2048, 4096, NT_PRE, mk_store(projp), gain=0)
    stream(w_in, 0, 32, 8192, 2048, NT_PRE, mk_store(projp), gain=0)
    flush_stream()
    S.barrier(B.values())
    load_actT(x_own, NT_OWN, 0, 32, norm=True)
    stream(w_in, 0, 32, 0, 10240, NT_OWN, mk_store(proj), gain=0)
    flush_stream()
    S.barrier(B.values())

    flat = actT[:].rearrange("p a b -> p (a b)")
    coff = [0]

    def carve(shape, dt):
        n = 1
        for d_ in shape[1:]:
            n *= d_
        nb = n * (4 if dt == F32 else 2)
        a = flat[:, coff[0] // 2:(coff[0] + nb) // 2]
        coff[0] += nb
        if dt == F32:
            a = a.bitcast(F32)
        if len(shape) == 3:
            a = a.rearrange("p (a b) -> p a b", a=shape[1])
        return a

    qkvg = [carve([128, 4, 256], F32) for _ in range(2)]
    cst_ = [carve([128, 2, 256], F32) for _ in range(2)]
    tqs = [[carve([128, 256], F32) for _ in range(4)] for _ in range(2)]
    qb, qdb, kb, kdb, vb = [[carve([128, 256], BF16) for _ in range(2)] for _ in range(5)]
    trs = [carve([128, 6, 128], BF16) for _ in range(2)]
    smb = [carve([128, 128], BF16) for _ in range(2)]
    Sf2 = [carve([128, 2, 256], F32) for _ in range(2)]
    Sb2 = [carve([128, 2, 256], BF16) for _ in range(2)]
    msk2 = [carve([128, 2, 128], F32) for _ in range(2)]
    bnst = [carve([128, 8], F32) for _ in range(2)]
    yt = [carve([128, 256], F32) for _ in range(2)]
    sgt = [carve([128, 256], F32) for _ in range(2)]
    S0 = [carve([128, 2, 256], F32) for _ in range(2)]
    S0b = [carve([128, 2, 256], BF16) for _ in range(2)]
    So = [carve([128, 2, 256], F32) for _ in range(2)]
    vm = [carve([128, 256], BF16) for _ in range(2)]
    qm = [carve([128, 2, 128], BF16) for _ in range(2)]
    cmask = carve([128, 16, 256], BF16)
    rmask = carve([128, 16], F32)
    dqk = carve([128, 2, 24], F32)
    odbg = [carve([128, 256], F32) for _ in range(2)]

    cst2 = buf("const2")
    S.dma("sp", cmask, cmask_d, writes=[cst2], owner=cst2, multi=True)
    S.dma("sp", rmask, rmask_d, writes=[cst2], owner=cst2, multi=True)
    S.dma("sp", dqk, dqk_d.rearrange("p a k h -> p a (k h)"), writes=[cst2], owner=cst2, multi=True)
    GAM = [1.0 - 2.0 ** (-5.0 - h_) for h_ in range(8)]
    rcount = [0]

    def ret_tile(hd, par, src, csd, t, kind, full, sample):
        i = par
        rcount[0] += 1
        Sf, Sb, msk = Sf2[par], Sb2[par], msk2[par]
        tq1, tq2, tk1, tk2 = tqs[par]
        TRB, SCB, STB = (6, 2)[par], (7, 3)[par], (5, 4)[par]
        qi, bqi = qkvg[i], buf("qkvg%d" % i)
        ci, bci = cst_[i], buf("cs%d" % i)
        srcv = src[t * 128:(t + 1) * 128, :].rearrange("p (s h d) -> p s h d", s=5, h=8)[:, 0:4, hd, :]
        S.dma("sp", qi, srcv, writes=[bqi], owner=bqi)
        S.dma("sp", ci, csd[t].rearrange("p (a d) -> p a d", a=2), writes=[bci], owner=bci)
        dq_col = dqk[:, 0, kind * 8 + hd:kind * 8 + hd + 1]
        dk_col = dqk[:, 1, kind * 8 + hd:kind * 8 + hd + 1]

        def rot(x, t1, t2, bt1, bt2):
            S.op("dve", lambda e: e.tensor_tensor(out=t1, in0=x, in1=ci[:, 0, :], op=ALU.mult), reads=[bqi, bci], writes=[bt1])
            S.op("dve", lambda e: e.tensor_tensor(out=t2[:, 0:128], in0=x[:, 128:256], in1=ci[:, 1, 0:128], op=ALU.mult), reads=[bqi, bci], writes=[bt2])
            S.op("dve", lambda e: e.tensor_tensor(out=t2[:, 128:256], in0=x[:, 0:128], in1=ci[:, 1, 128:256], op=ALU.mult), reads=[bqi, bci], writes=[bt2])
            S.op("dve", lambda e: e.tensor_tensor(out=t1, in0=t1, in1=t2, op=ALU.add), reads=[bt1, bt2], writes=[bt1])

        bk1, bk2 = buf("tk1_%d" % par), buf("tk2_%d" % par)
        rot(qi[:, 1, :], tk1, tk2, bk1, bk2)
        kd_, bkd = kdb[i], buf("kdb%d" % i)
        v_, bv = vb[i], buf("vb%d" % i)
        S.op("dve", lambda e: e.tensor_scalar(out=kd_, in0=tk1, scalar1=dk_col, scalar2=None, op0=ALU.mult), reads=[bk1, cst2], writes=[bkd])
        S.op("act", lambda e: e.copy(out=v_, in_=qi[:, 2, :]), reads=[bqi], writes=[bv])
        bSf, bSb = buf("Sf%d" % par), buf("Sb%d" % par)
        gl = GAM[hd] ** {0: 128, 1: 8, 2: 16}[kind]
        if full:
            bq1, bq2 = buf("tq1_%d" % par), buf("tq2_%d" % par)
            rot(qi[:, 0, :], tq1, tq2, bq1, bq2)
            q_, bq = qb[i], buf("qb%d" % i)
            qd_, bqd = qdb[i], buf("qdb%d" % i)
            k_, bk = kb[i], buf("kb%d" % i)
            S.op("act", lambda e: e.copy(out=q_, in_=tq1), reads=[bq1], writes=[bq])
            S.op("dve", lambda e: e.tensor_scalar(out=qd_, in0=tq1, scalar1=dq_col, scalar2=None, op0=ALU.mult), reads=[bq1, cst2], writes=[bqd])
            S.op("act", lambda e: e.mul(out=k_, in_=tk1, mul=1.0 / 16.0), reads=[bk1], writes=[bk])
            bp6 = buf("ps%d" % TRB)
            pv = ps[TRB].bitcast(BF16)

            def tr(e):
                ins = None
                for n_, srcb in enumerate((q_, qd_, k_)):
                    for c_ in range(2):
                        ins = e.transpose(out=pv[:, (2 * n_ + c_) * 128:(2 * n_ + c_ + 1) * 128], in_=srcb[:, c_ * 128:(c_ + 1) * 128], identity=ident[:])
                return ins
            S.op("pe", tr, reads=[bq, bqd, bk, buf("ident")], writes=[bp6])
            tr_, btr = trs[i], buf("trs%d" % i)
            S.op("dve", lambda e: e.tensor_copy(out=tr_, in_=pv[:, 0:768].rearrange("p (a b) -> p a b", a=6)), reads=[bp6], writes=[btr])
            bp7 = buf("ps%d" % SCB)

            def sc(e):
                e.matmul(ps[SCB][:, 0:128], tr_[:, 4, :], tr_[:, 0, :], start=True, stop=False)
                return e.matmul(ps[SCB][:, 0:128], tr_[:, 5, :], tr_[:, 1, :], start=False, stop=True)
            S.op("pe", sc, reads=[btr], writes=[bp7])
            sm_, bsm = smb[i], buf("smb%d" % i)
            S.op("dve", lambda e: e.tensor_tensor(out=sm_, in0=ps[SCB][:, 0:128], in1=msk[:, kind, :], op=ALU.mult), reads=[bp7, buf("msk%d" % par)], writes=[bsm])
            pbo = par
            bpo = buf("ps%d" % pbo)
            po = ps[pbo][:, 0:256]
            if not sample:
                def om(e):
                    e.matmul(po, sm_, v_, start=True, stop=False)
                    e.matmul(po, tr_[:, 2, :], Sb[:, 0, :], start=False, stop=False)
                    return e.matmul(po, tr_[:, 3, :], Sb[:, 1, :], start=False, stop=True)
                S.op("pe", om, reads=[bsm, bv, btr, bSb], writes=[bpo])
            else:
                S.op("pe", lambda e: e.matmul(po, sm_, v_, start=True, stop=False, skip_group_check=True), reads=[bsm, bv], writes=[bpo])
        if not sample:
            bp5 = buf("ps%d" % STB)

            def su(e):
                e.matmul(ps[STB][:, 0:256], kd_[:, 0:128], v_, start=True, stop=True)
                return e.matmul(ps[STB][:, 256:512], kd_[:, 128:256], v_, start=True, stop=True)
            S.op("pe", su, reads=[bkd, bv], writes=[bp5])
            S.op("dve", lambda e: e.scalar_tensor_tensor(out=Sf, in0=Sf, scalar=float(gl), in1=ps[STB][:, 0:512].rearrange("p (a b) -> p a b", a=2),
                                                         op0=ALU.mult, op1=ALU.add), reads=[bSf, bp5], writes=[bSf])
            S.op("act", lambda e: e.copy(out=Sb, in_=Sf), reads=[bSf], writes=[bSb])
        else:
            for sq in range(16):
                j = sq % 2
                s0, bs0 = S0[j], buf("S0_%d" % j)
                s0b, bs0b = S0b[j], buf("S0b_%d" % j)
                so, bso = So[j], buf("So_%d" % j)
                vm_, bvm = vm[j], buf("vm%d" % j)
                qm_, bqm = qm[j], buf("qm%d" % j)
                S.dma("sp", s0, st_ret[sq, hd].rearrange("(c p) v -> p c v", p=128), writes=[bs0], owner=bs0)
                S.op("act", lambda e, s0=s0, s0b=s0b: e.copy(out=s0b, in_=s0), reads=[bs0], writes=[bs0b])
                S.op("dve", lambda e, qm_=qm_, sq=sq: e.tensor_tensor(out=qm_, in0=tr_[:, 2:4, :], in1=cmask[:, sq, :].rearrange("p (a b) -> p a b", a=2), op=ALU.mult),
                     reads=[btr, cst2], writes=[bqm])

                def im(e, qm_=qm_, s0b=s0b, sq=sq):
                    e.matmul(po, qm_[:, 0, :], s0b[:, 0, :], start=False, stop=False, skip_group_check=True)
                    return e.matmul(po, qm_[:, 1, :], s0b[:, 1, :], start=False, stop=(sq == 15), skip_group_check=True)
                S.op("pe", im, reads=[bqm, bs0b], writes=[bpo])
                S.op("dve", lambda e, vm_=vm_, sq=sq: e.tensor_scalar(out=vm_, in0=v_, scalar1=rmask[:, sq:sq + 1], scalar2=None, op0=ALU.mult),
                     reads=[bv, cst2], writes=[bvm])
                pbs = 4 + (sq % 2)
                bps = buf("ps%d" % pbs)

                def su2(e, vm_=vm_, pbs=pbs):
                    e.matmul(ps[pbs][:, 0:256], kd_[:, 0:128], vm_, start=True, stop=True)
                    return e.matmul(ps[pbs][:, 256:512], kd_[:, 128:256], vm_, start=True, stop=True)
                S.op("pe", su2, reads=[bkd, bvm], writes=[bps])
                S.op("dve", lambda e, so=so, s0=s0, pbs=pbs: e.scalar_tensor_tensor(out=so, in0=s0, scalar=float(gl), in1=ps[pbs][:, 0:512].rearrange("p (a b) -> p a b", a=2),
                                                                                   op0=ALU.mult, op1=ALU.add), reads=[bs0, bps], writes=[bso])
                S.dma("pool", o_ret_s[sq, hd].rearrange("(c p) v -> p c v", p=128), so, reads=[bso], writes=[buf("orets")], owner=bso, multi=True)
        if full:
            bb, bbn = bnst[i], buf("bnst%d" % i)
            y_, by = yt[i], buf("yt%d" % i)
            sg_, bsg = sgt[i], buf("sgt%d" % i)
            S.op("act", lambda e: e.activation(out=sg_, in_=qi[:, 3, :], func=AF.Silu), reads=[bqi], writes=[bsg])
            od_, bod = odbg[i], buf("odbg%d" % i)
            S.op("act", lambda e: e.copy(out=od_, in_=po), reads=[bpo], writes=[bod])
            if DEBUG:
                S.dma("pool", mix[t * 128:(t + 1) * 128, 2048 + hd * 256:2048 + (hd + 1) * 256], od_, reads=[bod], writes=[buf("mixd")], owner=bod, multi=True)
            S.op("dve", lambda e: e.bn_stats(out=bb[:, 0:6], in_=od_), reads=[bod], writes=[bbn])
            S.op("dve", lambda e: e.bn_aggr(out=bb[:, 6:8], in_=bb[:, 0:6]), reads=[bbn], writes=[bbn])
            S.op("dve", lambda e: e.tensor_scalar(out=bb[:, 0:1], in0=bb[:, 7:8], scalar1=1e-5, scalar2=None, op0=ALU.add), reads=[bbn], writes=[bbn])
            S.op("act", lambda e: e.activation(out=bb[:, 1:2], in_=bb[:, 0:1], func=AF.Sqrt), reads=[bbn], writes=[bbn])
            S.op("dve", lambda e: e.reciprocal(out=bb[:, 2:3], in_=bb[:, 1:2]), reads=[bbn], writes=[bbn])
            S.op("dve", lambda e: e.tensor_scalar(out=y_, in0=od_, scalar1=bb[:, 6:7], scalar2=bb[:, 2:3], op0=ALU.subtract, op1=ALU.mult),
                 reads=[bod, bbn], writes=[by])
            S.op("dve", lambda e: e.tensor_tensor(out=y_, in0=y_, in1=sg_, op=ALU.mult), reads=[by, bsg], writes=[by])
            S.dma("pool", mix[t * 128:(t + 1) * 128, hd * 256:(hd + 1) * 256], y_, reads=[by], writes=[buf("mixd")], owner=by, multi=True)

    for hp in range(4):
        hds = (2 * hp, 2 * hp + 1)
        for par, hd in enumerate(hds):
            S.dma("sp", msk2[par], maskd[hd], reads=[], writes=[buf("msk%d" % par)], owner=buf("msk%d" % par))
            S.op("dve", lambda e, par=par: e.memset(Sf2[par], 0.0), writes=[buf("Sf%d" % par)])
            S.op("dve", lambda e, par=par: e.memset(Sb2[par], 0.0), writes=[buf("Sb%d" % par)])
        for t in range(NT_PRE):
            for par, hd in enumerate(hds):
                ret_tile(hd, par, projp, cs_pre, t, 0 if t < 8 else 2, False, False)
        for t in range(8):
            for par, hd in enumerate(hds):
                ret_tile(hd, par, proj, cs_own, t, 0, True, False)
        for par, hd in enumerate(hds):
            S.dma("sp", o_ret_p[hd].rearrange("(c p) v -> p c v", p=128), Sf2[par], reads=[buf("Sf%d" % par)], writes=[buf("oretp")], owner=buf("Sf%d" % par), multi=True)
        for par, hd in enumerate(hds):
            ret_tile(hd, par, proj, cs_own, 8, 1, True, True)
    flush_stream()
    S.barrier(B.values())

    coff[0] = 0
    PI = 3.141592653589793
    MAGIC = 12582912.0
    bS = buf("s5setup")

    def T64(n=1):
        a = carve([128, n, 128], F32) if n > 1 else carve([128, 128], F32)
        return a

    lam_sb = carve([128, 2, 64], F32)
    S.dma("sp", lam_sb[:, 0, :], lam_re_d, writes=[bS], owner=bS, multi=True)
    S.dma("sp", lam_sb[:, 1, :], lam_im_d, writes=[bS], owner=bS, multi=True)
    lamT = T64(2)
    ldt = T64()
    S.dma("sp", ldt[0:64, :], ldt_d, writes=[bS], owner=bS, multi=True)
    m3 = carve([128, 128], F32)
    S.dma("sp", m3, m3_d, writes=[bS], owner=bS, multi=True)
    selc = carve([128, 64], F32)
    S.dma("sp", selc[:, 0:16], sel_f_d, writes=[bS], owner=bS, multi=True)
    selb = carve([128, 32], BF16)
    S.dma("sp", selb, sel_b_d, writes=[bS], owner=bS, multi=True)
    mask8, tmask = selc[:, 0:8], selc[:, 8:16]
    Jsel, Csel = selb[:, 0:16], selb[:, 16:32]

    A1, A2 = carve([128, 2, 128], F32), carve([128, 2, 128], F32)
    B1, B2 = carve([128, 2, 128], F32), carve([128, 2, 128], F32)
    main_off = coff[0]

    def sop(eng, fn):
        S.op(eng, fn, reads=[bS], writes=[bS])

    def trans_f32(dst, src, rows_in, cols_in):
        bp = buf("ps7")
        S.op("pe", lambda e: e.transpose(out=ps[7][0:cols_in, 0:rows_in], in_=src, identity=ident_f[0:rows_in, 0:rows_in]), reads=[bS, cst], writes=[bp])
        S.op("dve", lambda e: e.tensor_copy(out=dst, in_=ps[7][0:cols_in, 0:rows_in]), reads=[bp, bS], writes=[bS])

    trans_f32(lamT[0:64, 0, :], lam_sb[:, 0, :], 128, 64)
    trans_f32(lamT[0:64, 1, :], lam_sb[:, 1, :], 128, 64)
    dtT = T64()
    lrT, liT = T64(), T64()
    sop("act", lambda e: e.activation(out=dtT[0:64, :], in_=ldt[0:64, :], func=AF.Exp))
    sop("dve", lambda e: e.tensor_tensor(out=lrT[0:64, :], in0=lamT[0:64, 0, :], in1=dtT[0:64, :], op=ALU.mult))
    sop("dve", lambda e: e.tensor_tensor(out=liT[0:64, :], in0=lamT[0:64, 1, :], in1=dtT[0:64, :], op=ALU.mult))
    AR, AI, NR, NI = T64(9), T64(9), T64(9), T64(9)
    tA, tB, tC, tD = T64(), T64(), T64(), T64()

    def trig(dst, k, off):
        sop("dve", lambda e: e.tensor_scalar(out=tA[0:64, :], in0=liT[0:64, :], scalar1=float(k), scalar2=float(off), op0=ALU.mult, op1=ALU.add))
        sop("dve", lambda e: e.tensor_scalar(out=tB[0:64, :], in0=tA[0:64, :], scalar1=1.0 / (2 * PI), scalar2=MAGIC, op0=ALU.mult, op1=ALU.add))
        sop("dve", lambda e: e.tensor_scalar(out=tB[0:64, :], in0=tB[0:64, :], scalar1=MAGIC, scalar2=2 * PI, op0=ALU.subtract, op1=ALU.mult))
        sop("dve", lambda e: e.tensor_tensor(out=tA[0:64, :], in0=tA[0:64, :], in1=tB[0:64, :], op=ALU.subtract))
        sop("dve", lambda e: e.tensor_scalar(out=tA[0:64, :], in0=tA[0:64, :], scalar1=-3.1415925, scalar2=3.1415925, op0=ALU.max, op1=ALU.min))
        sop("act", lambda e: e.activation(out=dst, in_=tA[0:64, :], func=AF.Sin))

    for k in range(9):
        trig(tC[0:64, :], k, PI / 2)
        trig(tD[0:64, :], k, 0.0)
        sop("act", lambda e, k=k: e.activation(out=AR[0:64, k, :], in_=lrT[0:64, :], func=AF.Exp, scale=float(k)))
        sop("act", lambda e, k=k: e.activation(out=NR[0:64, k, :], in_=lrT[0:64, :], func=AF.Exp, scale=float(-k)))
        sop("dve", lambda e, k=k: e.tensor_tensor(out=AI[0:64, k, :], in0=AR[0:64, k, :], in1=tD[0:64, :], op=ALU.mult))
        sop("dve", lambda e, k=k: e.tensor_tensor(out=AR[0:64, k, :], in0=AR[0:64, k, :], in1=tC[0:64, :], op=ALU.mult))
        sop("dve", lambda e, k=k: e.scalar_tensor_tensor(out=NI[0:64, k, :], in0=NR[0:64, k, :], scalar=-1.0, in1=tD[0:64, :], op0=ALU.mult, op1=ALU.mult))
        sop("dve", lambda e, k=k: e.tensor_tensor(out=NR[0:64, k, :], in0=NR[0:64, k, :], in1=tC[0:64, :], op=ALU.mult))
    fr, fi = T64(), T64()
    sop("dve", lambda e: e.tensor_scalar(out=tA[0:64, :], in0=AR[0:64, 1, :], scalar1=-1.0, scalar2=None, op0=ALU.add))
    sop("dve", lambda e: e.tensor_tensor(out=tB[0:64, :], in0=lamT[0:64, 0, :], in1=lamT[0:64, 0, :], op=ALU.mult))
    sop("dve", lambda e: e.tensor_tensor(out=tC[0:64, :], in0=lamT[0:64, 1, :], in1=lamT[0:64, 1, :], op=ALU.mult))
    sop("dve", lambda e: e.tensor_tensor(out=tB[0:64, :], in0=tB[0:64, :], in1=tC[0:64, :], op=ALU.add))
    sop("dve", lambda e: e.reciprocal(out=tB[0:64, :], in_=tB[0:64, :]))
    sop("dve", lambda e: e.tensor_tensor(out=tC[0:64, :], in0=tA[0:64, :], in1=lamT[0:64, 0, :], op=ALU.mult))
    sop("dve", lambda e: e.tensor_tensor(out=tD[0:64, :], in0=AI[0:64, 1, :], in1=lamT[0:64, 1, :], op=ALU.mult))
    sop("dve", lambda e: e.tensor_tensor(out=tC[0:64, :], in0=tC[0:64, :], in1=tD[0:64, :], op=ALU.add))
    sop("dve", lambda e: e.tensor_tensor(out=fr[0:64, :], in0=tC[0:64, :], in1=tB[0:64, :], op=ALU.mult))
    sop("dve", lambda e: e.tensor_tensor(out=tC[0:64, :], in0=AI[0:64, 1, :], in1=lamT[0:64, 0, :], op=ALU.mult))
    sop("dve", lambda e: e.tensor_tensor(out=tD[0:64, :], in0=tA[0:64, :], in1=lamT[0:64, 1, :], op=ALU.mult))
    sop("dve", lambda e: e.tensor_tensor(out=tC[0:64, :], in0=tC[0:64, :], in1=tD[0:64, :], op=ALU.subtract))
    sop("dve", lambda e: e.tensor_tensor(out=fi[0:64, :], in0=tC[0:64, :], in1=tB[0:64, :], op=ALU.mult))
    ER, EI, ENR, ENI = T64(8), T64(8), T64(8), T64(8)

    def cmul_f(dr, di, xr, xi):
        sop("dve", lambda e: e.tensor_tensor(out=tA[0:64, :], in0=xr, in1=fr[0:64, :], op=ALU.mult))
        sop("dve", lambda e: e.tensor_tensor(out=tB[0:64, :], in0=xi, in1=fi[0:64, :], op=ALU.mult))
        sop("dve", lambda e: e.tensor_tensor(out=dr, in0=tA[0:64, :], in1=tB[0:64, :], op=ALU.subtract))
        sop("dve", lambda e: e.tensor_tensor(out=tA[0:64, :], in0=xr, in1=fi[0:64, :], op=ALU.mult))
        sop("dve", lambda e: e.tensor_tensor(out=tB[0:64, :], in0=xi, in1=fr[0:64, :], op=ALU.mult))
        sop("dve", lambda e: e.tensor_tensor(out=di, in0=tA[0:64, :], in1=tB[0:64, :], op=ALU.add))

    for s_ in range(8):
        cmul_f(ER[0:64, s_, :], EI[0:64, s_, :], AR[0:64, 7 - s_, :], AI[0:64, 7 - s_, :])
        cmul_f(ENR[0:64, s_, :], ENI[0:64, s_, :], NR[0:64, s_ + 1, :], NI[0:64, s_ + 1, :])
    sop("dve", lambda e: e.tensor_copy(out=A1[0:64, 0, :], in_=AR[0:64, 8, :]))
    sop("dve", lambda e: e.tensor_copy(out=A1[0:64, 1, :], in_=AR[0:64, 8, :]))
    sop("dve", lambda e: e.tensor_scalar(out=A2[0:64, 0, :], in0=AI[0:64, 8, :], scalar1=-1.0, scalar2=None, op0=ALU.mult))
    sop("dve", lambda e: e.tensor_copy(out=A2[0:64, 1, :], in_=AI[0:64, 8, :]))
    sop("dve", lambda e: e.tensor_tensor(out=tA[0:64, :], in0=AR[0:64, 8, :], in1=AR[0:64, 8, :], op=ALU.mult))
    sop("dve", lambda e: e.tensor_tensor(out=tB[0:64, :], in0=AI[0:64, 8, :], in1=AI[0:64, 8, :], op=ALU.mult))
    sop("dve", lambda e: e.tensor_tensor(out=B1[0:64, 0, :], in0=tA[0:64, :], in1=tB[0:64, :], op=ALU.subtract))
    sop("dve", lambda e: e.tensor_copy(out=B1[0:64, 1, :], in_=B1[0:64, 0, :]))
    sop("dve", lambda e: e.tensor_tensor(out=tA[0:64, :], in0=AR[0:64, 8, :], in1=AI[0:64, 8, :], op=ALU.mult))
    sop("dve", lambda e: e.tensor_scalar(out=B2[0:64, 1, :], in0=tA[0:64, :], scalar1=2.0, scalar2=None, op0=ALU.mult))
    sop("dve", lambda e: e.tensor_scalar(out=B2[0:64, 0, :], in0=tA[0:64, :], scalar1=-2.0, scalar2=None, op0=ALU.mult))

    Bp = carve([128, 2, 128], F32)
    Cblk = carve([128, 2, 64], F32)
    CT = carve([128, 2, 128], F32)
    W1p = xin[0][:, 0:2048].rearrange("p (a b) -> p a b", a=2)
    W1n = xin[1][:, 0:2048].rearrange("p (a b) -> p a b", a=2)
    W2p = carve([128, 2, 1024], F32)
    tW = carve([128, 1024], F32)
    Wst = carve([128, 8, 512], BF16)
    S.op("dve", lambda e: e.memset(Wst, 0.0), reads=[bS], writes=[bS])

    def v4(ap):
        return ap.rearrange("p (g s c) -> p g s c", g=8, s=8)

    def bc_tab(tab, g0):
        return tab[0:64, :, g0:g0 + 8].rearrange("p s g -> p g s").unsqueeze(3).broadcast_to([64, 8, 8, 16])

    def bc_gc(x):
        return x.rearrange("p (g c) -> p g c", g=8).unsqueeze(2).broadcast_to([64, 8, 8, 16])

    def cprod(dst_r, dst_i, tr_, ti_, xr, xi, g0, neg_i=False):
        sop("dve", lambda e: e.tensor_tensor(out=v4(dst_r), in0=bc_tab(tr_, g0), in1=bc_gc(xr), op=ALU.mult))
        sop("dve", lambda e: e.tensor_tensor(out=v4(tW[0:64, :]), in0=bc_tab(ti_, g0), in1=bc_gc(xi), op=ALU.mult))
        sop("dve", lambda e: e.tensor_tensor(out=dst_r, in0=dst_r, in1=tW[0:64, :], op=ALU.subtract))
        sop("dve", lambda e: e.tensor_tensor(out=v4(dst_i), in0=bc_tab(tr_, g0), in1=bc_gc(xi), op=ALU.mult))
        sop("dve", lambda e: e.tensor_tensor(out=v4(tW[0:64, :]), in0=bc_tab(ti_, g0), in1=bc_gc(xr), op=ALU.mult))
        if neg_i:
            sop("dve", lambda e: e.scalar_tensor_tensor(out=dst_i, in0=dst_i, scalar=-1.0, in1=tW[0:64, :], op0=ALU.mult, op1=ALU.subtract))
        else:
            sop("dve", lambda e: e.tensor_tensor(out=dst_i, in0=dst_i, in1=tW[0:64, :], op=ALU.add))

    for fc in range(16):
        g0 = fc * 8
        S.dma("sp", Bp[0:64, 0, :].rearrange("p (g c) -> p g c", g=8), b_re_d[g0:g0 + 8].rearrange("g p c -> p g c"), reads=[bS], writes=[bS], owner=bS, multi=True)
        S.dma("sp", Bp[0:64, 1, :].rearrange("p (g c) -> p g c", g=8), b_im_d[g0:g0 + 8].rearrange("g p c -> p g c"), reads=[bS], writes=[bS], owner=bS, multi=True)
        S.dma("sp", Cblk[:, 0, :], c_re_d[g0:g0 + 8].rearrange("g c p -> (g c) p"), reads=[bS], writes=[bS], owner=bS, multi=True)
        S.dma("sp", Cblk[:, 1, :], c_im_d[g0:g0 + 8].rearrange("g c p -> (g c) p"), reads=[bS], writes=[bS], owner=bS, multi=True)
        trans_f32(CT[0:64, 0, :], Cblk[:, 0, :], 128, 64)
        trans_f32(CT[0:64, 1, :], Cblk[:, 1, :], 128, 64)
        cprod(W1p[0:64, 0, :], W1p[0:64, 1, :], ER, EI, Bp[0:64, 0, :], Bp[0:64, 1, :], g0)
        cprod(W1n[0:64, 0, :], W1n[0:64, 1, :], ENR, ENI, Bp[0:64, 0, :], Bp[0:64, 1, :], g0)
        cprod(W2p[0:64, 0, :], W2p[0:64, 1, :], AR[:, 1:9, :], AI[:, 1:9, :], CT[0:64, 0, :], CT[0:64, 1, :], g0, neg_i=True)
        for half in range(2):
            pb = 2 + half
            bp = buf("ps%d" % pb)

            def w3mm(e, half=half, pb=pb):
                ins = None
                for gg in range(4):
                    g = half * 4 + gg
                    e.matmul(ps[pb][:, gg * 128:(gg + 1) * 128], W1n[0:64, 0, g * 128:(g + 1) * 128], W2p[0:64, 0, g * 128:(g + 1) * 128], start=True, stop=False)
                    ins = e.matmul(ps[pb][:, gg * 128:(gg + 1) * 128], W1n[0:64, 1, g * 128:(g + 1) * 128], W2p[0:64, 1, g * 128:(g + 1) * 128], start=False, stop=True)
                return ins
            S.op("pe", w3mm, reads=[bS], writes=[bp])
            S.op("dve", lambda e, half=half, pb=pb: e.tensor_tensor(out=Wst[:, half * 4:(half + 1) * 4, 128:256],
                                                                   in0=ps[pb][:, 0:512].rearrange("p (g n) -> p g n", g=4),
                                                                   in1=m3.unsqueeze(1).broadcast_to([128, 4, 128]), op=ALU.mult),
                 reads=[bp, bS], writes=[bS])
        for half in range(2):
            pb = 4 + half
            bp = buf("ps%d" % pb)

            def w1tr(e, half=half, pb=pb):
                ins = None
                for gg in range(4):
                    g = half * 4 + gg
                    for ri in range(2):
                        ins = e.transpose(out=ps[pb][:, (gg * 2 + ri) * 64:(gg * 2 + ri + 1) * 64], in_=W1p[0:64, ri, g * 128:(g + 1) * 128],
                                          identity=ident_f[0:64, 0:64])
                return ins
            S.op("pe", w1tr, reads=[bS, cst], writes=[bp])
            S.op("act", lambda e, half=half, pb=pb: e.copy(out=Wst[:, half * 4:(half + 1) * 4, 0:128],
                                                           in_=ps[pb][:, 0:512].rearrange("p (g n) -> p g n", g=4)),
                 reads=[bp, bS], writes=[bS])
        S.op("act", lambda e: e.copy(out=Wst[0:64, :, 256:384], in_=W2p[0:64, 0, :].rearrange("p (g n) -> p g n", g=8)), reads=[bS], writes=[bS])
        S.op("act", lambda e: e.copy(out=Wst[0:64, :, 384:512], in_=W2p[0:64, 1, :].rearrange("p (g n) -> p g n", g=8)), reads=[bS], writes=[bS])
        S.dma("sp", Wall_d[g0:g0 + 8].rearrange("g p w -> p g w"), Wst, reads=[bS], writes=[bS], owner=bS, multi=True)
    flush_stream()
    S.barrier(B.values())

    coff[0] = main_off
    GB = 32
    NW = GB * 16
    def wview(tn, is_f32):
        f = tn[:].bitcast(BF16) if is_f32 else tn[:].rearrange("p a b -> p (a b)")
        return f.rearrange("p (g w) -> p g w", g=16)
    Wsets = [(wview(wbf[0], False), wview(wbf[1], False)), (wview(xin[0], True), wview(xin[1], True))]
    Wbufs = [("wbf0", "wbf1"), ("xin0", "xin1")]
    u32 = [carve([128, NW], F32) for _ in range(2)]
    Uexp = [carve([128, GB, 128], BF16) for _ in range(2)]
    Usb = [carve([128, GB, 16], BF16) for _ in range(2)]
    Vsb = [carve([128, 2, NW], F32) for _ in range(2)]
    Xh = carve([128, 2, GB * 17], F32)
    Xall = carve([128, 2, NW], BF16)
    sT1, sT2 = carve([128, 2, GB], F32), carve([128, 2, GB], F32)
    Ysb = carve([128, GB, 16], BF16)
    Yexp = carve([128, GB, 128], BF16)
    ytmp = [carve([128, NW], F32) for _ in range(2)]
    zt_ = [carve([128, NW], F32) for _ in range(2)]
    dB = xbf[0][:].bitcast(F32)
    X0 = carve([128, 2, NW], F32)
    Xn = carve([128, 2, NW], F32)
    bT1, bT2 = sb("bT1", [128, 2, NW], F32), sb("bT2", [128, 2, NW], F32)
    st_in = Yexp[:].rearrange("p a b -> p (a b)").bitcast(F32).rearrange("p (r q) -> p r q", r=2)
    st_o = [carve([128, 2, 64], F32) for _ in range(2)]
    cst3 = buf("const3")
    S.dma("sp", dB, dB_d, writes=[cst3], owner=cst3, multi=True)
    ucount = [0]
    xhv = Xh[0:64, :, :].rearrange("p r (g j) -> p r g j", g=GB)

    def Wg(gb, g):
        return Wsets[gb % 2][g // 16][:, g % 16, :]

    def s5_front(gb, src, t):
        g0 = gb * GB
        i = ucount[0] % 2
        ucount[0] += 1
        bWs = [buf(n) for n in Wbufs[gb % 2]]
        u_, bu = u32[i], buf("u32_%d" % i)
        S.dma("sp", u_, src[t * 128:(t + 1) * 128, 8192 + g0 * 16:8192 + (g0 + GB) * 16], writes=[bu], owner=bu)
        ue, bUe = Uexp[i], buf("Uexp%d" % i)
        for s_ in range(8):
            if s_ % 2:
                S.op("act", lambda e, s_=s_: e.activation(out=ue[:, :, s_ * 16:(s_ + 1) * 16], in_=u_.rearrange("p (g c) -> p g c", g=GB),
                                                           func=AF.Copy, scale=mask8[:, s_:s_ + 1]), reads=[bu, bS], writes=[bUe])
            else:
                S.op("dve", lambda e, s_=s_: e.tensor_scalar(out=ue[:, :, s_ * 16:(s_ + 1) * 16], in0=u_.rearrange("p (g c) -> p g c", g=GB),
                                                              scalar1=mask8[:, s_:s_ + 1], scalar2=None, op0=ALU.mult), reads=[bu, bS], writes=[bUe])
        bp0 = buf("ps0")

        def umm(e):
            ins = None
            for g in range(GB):
                ins = e.matmul(ps[0][:, g * 16:(g + 1) * 16], ue[:, g, :], Jsel, start=True, stop=True, skip_group_check=True)
            return ins
        S.op("pe", umm, reads=[bUe, bS], writes=[bp0])
        us, bUs = Usb[i], buf("Usb%d" % i)
        S.op("act", lambda e: e.copy(out=us, in_=ps[0][:, 0:NW].rearrange("p (g j) -> p g j", g=GB)), reads=[bp0], writes=[bUs])
        bp1, bp2 = buf("ps1"), buf("ps2")

        def vmm(e):
            ins = None
            for g in range(GB):
                e.matmul(ps[1][0:64, g * 16:(g + 1) * 16], Wg(gb, g)[:, 0:64], us[:, g, :], start=True, stop=True, skip_group_check=True)
                ins = e.matmul(ps[2][0:64, g * 16:(g + 1) * 16], Wg(gb, g)[:, 64:128], us[:, g, :], start=True, stop=True, skip_group_check=True)
            return ins
        S.op("pe", vmm, reads=bWs + [bUs], writes=[bp1, bp2])
        vs, bVs = Vsb[i], buf("Vsb%d" % i)
        S.op("act", lambda e: e.copy(out=vs[0:64, 0, :], in_=ps[1][0:64, 0:NW]), reads=[bp1], writes=[bVs])
        S.op("act", lambda e: e.copy(out=vs[0:64, 1, :], in_=ps[2][0:64, 0:NW]), reads=[bp2], writes=[bVs])
        return i

    def s5_rest(gb, t, i, nvalid, full, sample):
        g0 = gb * GB
        bWs = [buf(n) for n in Wbufs[gb % 2]]
        u_, bu = u32[i], buf("u32_%d" % i)
        us, bUs = Usb[i], buf("Usb%d" % i)
        vs, bVs = Vsb[i], buf("Vsb%d" % i)
        vsv = vs[0:64, :, :].rearrange("p r (g j) -> p r g j", g=GB)
        bXh, bXa = buf("Xh"), buf("Xall")
        a1 = A1[0:64, :, g0:g0 + GB]
        a2 = A2[0:64, :, g0:g0 + GB]
        if not sample:
            def step(src_j, dst_j, add_ap, c1, c2, badd):
                S.op("dve", lambda e: e.tensor_tensor(out=sT1[0:64, :, :], in0=xhv[:, :, :, src_j], in1=c1, op=ALU.mult), reads=[bXh, bS], writes=[buf("sT1")])
                S.op("dve", lambda e: e.tensor_tensor(out=sT2[0:64, 0, :], in0=xhv[:, 1, :, src_j], in1=c2[:, 0, :], op=ALU.mult), reads=[bXh, bS], writes=[buf("sT2")])
                S.op("dve", lambda e: e.tensor_tensor(out=sT2[0:64, 1, :], in0=xhv[:, 0, :, src_j], in1=c2[:, 1, :], op=ALU.mult), reads=[bXh, bS], writes=[buf("sT2")])
                S.op("dve", lambda e: e.tensor_tensor(out=sT1[0:64, :, :], in0=sT1[0:64, :, :], in1=sT2[0:64, :, :], op=ALU.add), reads=[buf("sT1"), buf("sT2")], writes=[buf("sT1")])
                S.op("dve", lambda e: e.tensor_tensor(out=xhv[:, :, :, dst_j], in0=sT1[0:64, :, :], in1=add_ap, op=ALU.add), reads=[buf("sT1"), badd], writes=[bXh])
            if nvalid == 16:
                pv_ = bT1[0:64, :, 0:GB * 8].rearrange("p r (g m) -> p r g m", g=GB)
                p2_ = bT2[0:64, :, 0:GB * 8].rearrange("p r (g m) -> p r g m", g=GB)
                ve = vsv[:, :, :, 0:16:2]
                vo = vsv[:, :, :, 1:16:2]
                bP, bP2 = buf("bT1"), buf("bT2")
                S.op("dve", lambda e: e.tensor_tensor(out=pv_, in0=ve, in1=a1.unsqueeze(3).broadcast_to([64, 2, GB, 8]), op=ALU.mult), reads=[bVs, bS], writes=[bP])
                S.op("dve", lambda e: e.tensor_tensor(out=p2_[:, 0, :, :], in0=ve[:, 1, :, :], in1=a2[:, 0, :].unsqueeze(2).broadcast_to([64, GB, 8]), op=ALU.mult), reads=[bVs, bS], writes=[bP2])
                S.op("dve", lambda e: e.tensor_tensor(out=p2_[:, 1, :, :], in0=ve[:, 0, :, :], in1=a2[:, 1, :].unsqueeze(2).broadcast_to([64, GB, 8]), op=ALU.mult), reads=[bVs, bS], writes=[bP2])
                S.op("dve", lambda e: e.tensor_tensor(out=pv_, in0=pv_, in1=p2_, op=ALU.add), reads=[bP, bP2], writes=[bP])
                S.op("dve", lambda e: e.tensor_tensor(out=pv_, in0=pv_, in1=vo, op=ALU.add), reads=[bP, bVs], writes=[bP])
                b1 = B1[0:64, :, g0:g0 + GB]
                b2 = B2[0:64, :, g0:g0 + GB]
                for m in range(8):
                    step(2 * m, 2 * m + 2, pv_[:, :, :, m], b1, b2, bP)
                xe = xhv[:, :, :, 0:16:2]
                xo = xhv[:, :, :, 1:16:2]
                S.op("dve", lambda e: e.tensor_tensor(out=pv_, in0=xe, in1=a1.unsqueeze(3).broadcast_to([64, 2, GB, 8]), op=ALU.mult), reads=[bXh, bS], writes=[bP])
                S.op("dve", lambda e: e.tensor_tensor(out=p2_[:, 0, :, :], in0=xe[:, 1, :, :], in1=a2[:, 0, :].unsqueeze(2).broadcast_to([64, GB, 8]), op=ALU.mult), reads=[bXh, bS], writes=[bP2])
                S.op("dve", lambda e: e.tensor_tensor(out=p2_[:, 1, :, :], in0=xe[:, 0, :, :], in1=a2[:, 1, :].unsqueeze(2).broadcast_to([64, GB, 8]), op=ALU.mult), reads=[bXh, bS], writes=[bP2])
                S.op("dve", lambda e: e.tensor_tensor(out=pv_, in0=pv_, in1=p2_, op=ALU.add), reads=[bP, bP2], writes=[bP])
                S.op("dve", lambda e: e.tensor_tensor(out=xo, in0=pv_, in1=ve, op=ALU.add), reads=[bP, bVs, bXh], writes=[bXh])
            else:
                for j in range(nvalid):
                    step(j, j + 1, vsv[:, :, :, j], a1, a2, bVs)
            if full:
                S.op("act", lambda e: e.copy(out=Xall[0:64, :, :].rearrange("p r (g j) -> p r g j", g=GB), in_=xhv[:, :, :, 0:16]), reads=[bXh], writes=[bXa])
            S.op("dve", lambda e: e.tensor_copy(out=xhv[:, :, :, 0], in_=xhv[:, :, :, nvalid]), reads=[bXh, bXa], writes=[bXh])
        else:
            x0v = X0[0:64, :, :].rearrange("p r (g j) -> p r g j", g=GB)
            t1v = bT1[0:64, :, :].rearrange("p r (g j) -> p r g j", g=GB)
            t2v = bT2[0:64, :, :].rearrange("p r (g j) -> p r g j", g=GB)
            S.op("act", lambda e: e.copy(out=Xall[0:64, :, :], in_=X0[0:64, :, :]), reads=[buf("X0")], writes=[bXa])
            S.op("dve", lambda e: e.tensor_tensor(out=t1v, in0=x0v, in1=a1.unsqueeze(3).broadcast_to([64, 2, GB, 16]), op=ALU.mult), reads=[buf("X0"), bS], writes=[buf("bT1")])
            S.op("dve", lambda e: e.tensor_tensor(out=t2v[:, 0, :, :], in0=x0v[:, 1, :, :], in1=a2[:, 0, :].unsqueeze(2).broadcast_to([64, GB, 16]), op=ALU.mult), reads=[buf("X0"), bS], writes=[buf("bT2")])
            S.op("dve", lambda e: e.tensor_tensor(out=t2v[:, 1, :, :], in0=x0v[:, 0, :, :], in1=a2[:, 1, :].unsqueeze(2).broadcast_to([64, GB, 16]), op=ALU.mult), reads=[buf("X0"), bS], writes=[buf("bT2")])
            S.op("dve", lambda e: e.tensor_tensor(out=bT1[0:64, :, :], in0=bT1[0:64, :, :], in1=bT2[0:64, :, :], op=ALU.add), reads=[buf("bT1"), buf("bT2")], writes=[buf("bT1")])
            S.op("dve", lambda e: e.tensor_tensor(out=Xn[0:64, :, :], in0=bT1[0:64, :, :], in1=vs[0:64, :, :], op=ALU.add), reads=[buf("bT1"), bVs], writes=[buf("Xn")])
        if not full:
            return
        bp3 = buf("ps3")
        xr_ = Xall[0:64, 0, :].rearrange("p (g j) -> p g j", g=GB)
        xi_ = Xall[0:64, 1, :].rearrange("p (g j) -> p g j", g=GB)

        def ymm(e):
            ins = None
            for g in range(GB):
                o_ = ps[3][:, g * 16:(g + 1) * 16]
                W = Wg(gb, g)
                e.matmul(o_, W[0:64, 256:384], xr_[:, g, :], start=True, stop=False, skip_group_check=True)
                e.matmul(o_, W[0:64, 384:512], xi_[:, g, :], start=False, stop=False, skip_group_check=True)
                ins = e.matmul(o_, W[:, 128:256], us[:, g, :], start=False, stop=True, skip_group_check=True)
            return ins
        S.op("pe", ymm, reads=bWs + [bXa, bUs], writes=[bp3])
        bYs, bYe = buf("Ysb"), buf("Yexp")
        S.op("act", lambda e: e.copy(out=Ysb, in_=ps[3][:, 0:NW].rearrange("p (g j) -> p g j", g=GB)), reads=[bp3], writes=[bYs])
        yev = Yexp[:, :, :].rearrange("p g (j t) -> p g j t", j=16)
        for t_ in range(8):
            if t_ % 2:
                S.op("act", lambda e, t_=t_: e.activation(out=yev[:, :, :, t_], in_=Ysb, func=AF.Copy, scale=tmask[:, t_:t_ + 1]),
                     reads=[bYs, bS], writes=[bYe])
            else:
                S.op("dve", lambda e, t_=t_: e.tensor_scalar(out=yev[:, :, :, t_], in0=Ysb, scalar1=tmask[:, t_:t_ + 1], scalar2=None, op0=ALU.mult),
                     reads=[bYs, bS], writes=[bYe])
        bp4 = buf("ps4")

        def pmm(e):
            ins = None
            for g in range(GB):
                ins = e.matmul(ps[4][:, g * 16:(g + 1) * 16], Yexp[:, g, :], Csel, start=True, stop=True, skip_group_check=True)
            return ins
        S.op("pe", pmm, reads=[bYe, bS], writes=[bp4])
        yt_, byt = ytmp[i], buf("ytmp%d" % i)
        z_, bz = zt_[i], buf("zt%d" % i)
        S.op("dve", lambda e: e.tensor_tensor(out=yt_, in0=u_, in1=dB[:, g0 * 16:(g0 + GB) * 16], op=ALU.mult), reads=[bu, cst3], writes=[byt])
        S.op("dve", lambda e: e.tensor_tensor(out=yt_, in0=yt_, in1=ps[4][:, 0:NW], op=ALU.add), reads=[byt, bp4], writes=[byt])
        S.op("act", lambda e: e.activation(out=z_, in_=yt_, func=AF.Square), reads=[byt], writes=[bz])
        S.op("dve", lambda e: e.tensor_scalar(out=z_, in0=z_, scalar1=0.044715, scalar2=1.0, op0=ALU.mult, op1=ALU.add), reads=[bz], writes=[bz])
        S.op("dve", lambda e: e.tensor_tensor(out=z_, in0=z_, in1=yt_, op=ALU.mult), reads=[bz, byt], writes=[bz])
        S.op("act", lambda e: e.activation(out=z_, in_=z_, func=AF.Sigmoid, scale=1.5957691216057308), reads=[bz], writes=[bz])
        S.op("dve", lambda e: e.tensor_tensor(out=z_, in0=z_, in1=yt_, op=ALU.mult), reads=[bz, byt], writes=[bz])
        S.dma("pool", zscr[t * 128:(t + 1) * 128, g0 * 16:(g0 + GB) * 16], z_, reads=[bz], writes=[buf("zscrd")], owner=bz, multi=True)

    def emit_state(gb, srcX, dst_r, dst_i, bsrc):
        g0 = gb * GB
        k = ucount[0] % 2
        ucount[0] += 1
        bp = buf("ps5")

        def tr(e):
            e.transpose(out=ps[5][0:GB, 0:64], in_=srcX[:, 0, :], identity=ident_f[0:64, 0:64])
            return e.transpose(out=ps[5][0:GB, 64:128], in_=srcX[:, 1, :], identity=ident_f[0:64, 0:64])
        S.op("pe", tr, reads=[bsrc, cst], writes=[bp])
        so_, bso = st_o[k], buf("st_o%d" % k)
        S.op("dve", lambda e: e.tensor_copy(out=so_[0:GB, :, :], in_=ps[5][0:GB, 0:128].rearrange("p (r q) -> p r q", r=2)), reads=[bp], writes=[bso])
        S.dma("pool", dst_r[g0:g0 + GB, :], so_[0:GB, 0, :], reads=[bso], writes=[buf("os5")], owner=bso, multi=True)
        S.dma("pool", dst_i[g0:g0 + GB, :], so_[0:GB, 1, :], reads=[bso], writes=[buf("os5")], owner=bso, multi=True)

    for gb in range(128 // GB):
        g0 = gb * GB
        for hh in range(2):
            bW = buf(Wbufs[gb % 2][hh])
            S.dma("sp", Wsets[gb % 2][hh], Wall_d[g0 + hh * 16:g0 + (hh + 1) * 16].rearrange("g p w -> p g w"), writes=[bW], owner=bW)
        bsi = buf("Yexp")
        S.dma("sp", st_in[0:GB, 0, :].rearrange("g (s p) -> g s p", s=16), s5r_d[:, g0:g0 + GB, :].rearrange("s g p -> g s p"), writes=[bsi], owner=bsi)
        S.dma("sp", st_in[0:GB, 1, :].rearrange("g (s p) -> g s p", s=16), s5i_d[:, g0:g0 + GB, :].rearrange("s g p -> g s p"), reads=[bsi], writes=[bsi], owner=bsi)
        x0v = X0[0:64, :, :].rearrange("p r (g j) -> p r g j", g=GB)
        for ri in range(2):
            for q4 in range(4):
                bp = buf("ps6")

                def trs_(e, ri=ri, q4=q4):
                    ins = None
                    for jj in range(4):
                        j = q4 * 4 + jj
                        ins = e.transpose(out=ps[6][0:64, jj * GB:(jj + 1) * GB], in_=st_in[0:GB, ri, j * 64:(j + 1) * 64], identity=ident_f[0:GB, 0:GB])
                    return ins
                S.op("pe", trs_, reads=[bsi, cst], writes=[bp])
                S.op("dve", lambda e, ri=ri, q4=q4: e.tensor_copy(out=x0v[:, ri, :, q4 * 4:(q4 + 1) * 4],
                                                                  in_=ps[6][0:64, 0:4 * GB].rearrange("p (j g) -> p g j", j=4)),
                     reads=[bp], writes=[buf("X0")])
        S.op("dve", lambda e: e.memset(Xh, 0.0), writes=[buf("Xh")])
        work = [(projp, t, 16 if t < 8 else 2, False, False) for t in range(NT_PRE)]
        work += [(proj, t, 16, True, False) for t in range(8)]
        work += [(proj, 8, 16, True, True)]
        nxt = s5_front(gb, work[0][0], work[0][1])
        for wi, (src, t, nv, full, sample) in enumerate(work):
            cur = nxt
            if wi + 1 < len(work):
                nxt = s5_front(gb, work[wi + 1][0], work[wi + 1][1])
            if sample:
                emit_state(gb, xhv[:, :, :, 0], o_s5p_r, o_s5p_i, buf("Xh"))
            s5_rest(gb, t, cur, nv, full, sample)
        xnv = Xn[0:64, :, :].rearrange("p r (g j) -> p r g j", g=GB)
        for j in range(16):
            emit_state(gb, xnv[:, :, :, j], o_s5s_r[j], o_s5s_i[j], buf("Xn"))
    flush_stream()
    S.barrier(B.values())

    coff[0] = 16 * ROWS_OWN * 2
    bgl = carve([128, 2048], F32)
    S.dma("sp", bgl, bglu_d, writes=[cst3], owner=cst3, multi=True)

    def cons_glu(t, c0, cw, pt, bp):
        i = next_ot()
        o, bo, r, br = ot[i], buf("ot%d" % i), rt[i], buf("rt%d" % i)
        S.dma("sp", r[:, 0:cw], zscr[t * 128:(t + 1) * 128, c0:c0 + cw], writes=[br], owner=br)
        S.op("dve", lambda e: e.tensor_tensor(out=o[:, 0:cw], in0=pt, in1=bgl[:, c0:c0 + cw], op=ALU.add), reads=[bp, cst3], writes=[bo])
        S.op("act", lambda e: e.activation(out=o[:, 0:cw], in_=o[:, 0:cw], func=AF.Sigmoid), reads=[bo], writes=[bo])
        S.op("dve", lambda e: e.tensor_tensor(out=o[:, 0:cw], in0=o[:, 0:cw], in1=r[:, 0:cw], op=ALU.mult), reads=[bo, br], writes=[bo])
        S.dma("pool", s5o[t * 128:(t + 1) * 128, c0:c0 + cw], o[:, 0:cw], reads=[bo], writes=[buf("s5od")], owner=bo, multi=True)

    load_actT(zscr, NT_OWN, 0, 16)
    stream(w_glu, 0, 16, 0, 2048, NT_OWN, cons_glu)
    flush_stream()
    S.barrier(B.values())
    for t in range(NT_OWN):
        xi, stt = xin[t % 2], stat[t % 2]
        bxi, bst = buf("xin%d" % (t % 2)), buf("stat%d" % (t % 2))
        S.dma("pool", xi[:, 0:2048], s5o[t * 128:(t + 1) * 128, :], writes=[bxi], owner=bxi)
        S.op("act", lambda e, xi=xi, stt=stt, t=t: e.activation(out=xbf[t % 2][:, 0:2048], in_=xi[:, 0:2048], func=AF.Square, accum_out=stt[:, 0:1]),
             reads=[bxi], writes=[bst, buf("xbf%d" % (t % 2))])
        S.op("dve", lambda e, stt=stt: e.tensor_scalar(out=stt[:, 1:2], in0=stt[:, 0:1], scalar1=1.0 / 2048, scalar2=EPS,
                                                       op0=ALU.mult, op1=ALU.add), reads=[bst], writes=[bst])
        S.op("act", lambda e, stt=stt: e.activation(out=stt[:, 2:3], in_=stt[:, 1:2], func=AF.Sqrt), reads=[bst], writes=[bst])
        S.op("dve", lambda e, stt=stt: e.reciprocal(out=stt[:, 3:4], in_=stt[:, 2:3]), reads=[bst], writes=[bst])
        S.op("dve", lambda e, xi=xi, stt=stt: e.tensor_scalar(out=xi[:, 0:2048], in0=xi[:, 0:2048], scalar1=stt[:, 3:4], scalar2=None, op0=ALU.mult),
             reads=[bxi, bst], writes=[bxi])
        S.dma("pool", mix[t * 128:(t + 1) * 128, 2048:4096], xi[:, 0:2048], reads=[bxi], writes=[buf("mixd")], owner=bxi, multi=True)
    flush_stream()
    S.barrier(B.values())

    def cons_wout(t, c0, cw, pt, bp):
        i = next_ot()
        o, bo, r, br = ot[i], buf("ot%d" % i), rt[i], buf("rt%d" % i)
        S.dma("sp", r[:, 0:cw], x_own[t * 128:(t + 1) * 128, c0:c0 + cw], writes=[br], owner=br)
        S.op("dve", lambda e: e.tensor_tensor(out=o[:, 0:cw], in0=pt, in1=r[:, 0:cw], op=ALU.add), reads=[bp, br], writes=[bo])
        S.dma("pool", x1[t * 128:(t + 1) * 128, c0:c0 + cw], o[:, 0:cw], reads=[bo], writes=[buf("x1d")], owner=bo, multi=True)

    load_actT(mix, NT_OWN, 0, 32, norm=False)
    stream(w_out, 0, 32, 0, D, NT_OWN, cons_wout, gain=1)
    flush_stream()
    S.barrier(B.values())

    load_actT(x1, NT_OWN, 0, 32, norm=True)

    def cons_gate(t, c0, cw, pt, bp):
        S.op("act", lambda e: e.activation(out=gtmp[:, t, 0:cw], in_=pt, func=AF.Silu), reads=[bp], writes=[buf("gtmp%d" % t)])

    def cons_up(t, c0, cw, pt, bp):
        i = next_ot()
        o, bo = ot[i], buf("ot%d" % i)
        S.op("dve", lambda e: e.tensor_tensor(out=o[:, 0:cw], in0=pt, in1=gtmp[:, t, 0:cw], op=ALU.mult),
             reads=[bp, buf("gtmp%d" % t)], writes=[bo])
        S.dma("pool", act[t * 128:(t + 1) * 128, c0:c0 + cw], o[:, 0:cw], reads=[bo], writes=[buf("actd")], owner=bo, multi=True)

    for gi in range(DFF // CG):
        stream(w_gate, 0, 32, gi * CG, CG, NT_OWN, cons_gate, gain=2)
        stream(w_up, 0, 32, gi * CG, CG, NT_OWN, cons_up, gain=2)
    flush_stream()
    S.barrier(B.values())

    def cons_down(t, c0, cw, pt, bp):
        i = next_ot()
        o, bo, r, br = ot[i], buf("ot%d" % i), rt[i], buf("rt%d" % i)
        S.dma("sp", r[:, 0:cw], x1[t * 128:(t + 1) * 128, c0:c0 + cw], writes=[br], owner=br)
        S.op("dve", lambda e: e.tensor_tensor(out=o[:, 0:cw], in0=pt, in1=r[:, 0:cw], op=ALU.add), reads=[bp, br], writes=[bo])
        S.dma("pool", x1[t * 128:(t + 1) * 128, c0:c0 + cw], o[:, 0:cw], reads=[bo], writes=[buf("x1d")], owner=bo, multi=True)

    for kb0, kbn in ((0, 32), (32, 32), (64, 22)):
        load_actT(act, NT_OWN, kb0, kbn)
        stream(w_down, kb0 * 128, kbn, 0, D, NT_OWN, cons_down)
        flush_stream()
    S.barrier(B.values())

    for t in range(NT_OWN):
        xi, stt = xin[t % 2], stat[t % 2]
        bxi, bst = buf("xin%d" % (t % 2)), buf("stat%d" % (t % 2))
        S.dma("pool", xi[:], x1[t * 128:(t + 1) * 128, :], writes=[bxi], owner=bxi)
        S.op("act", lambda e, xi=xi, stt=stt, t=t: e.activation(out=xbf[t % 2][:], in_=xi[:], func=AF.Square, accum_out=stt[:, 0:1]),
             reads=[bxi], writes=[bst, buf("xbf%d" % (t % 2))])
        S.op("dve", lambda e, stt=stt: e.tensor_scalar(out=stt[:, 1:2], in0=stt[:, 0:1], scalar1=1.0 / D, scalar2=EPS,
                                                       op0=ALU.mult, op1=ALU.add), reads=[bst], writes=[bst])
        S.op("act", lambda e, stt=stt: e.activation(out=stt[:, 2:3], in_=stt[:, 1:2], func=AF.Sqrt), reads=[bst], writes=[bst])
        S.op("dve", lambda e, stt=stt: e.reciprocal(out=stt[:, 3:4], in_=stt[:, 2:3]), reads=[bst], writes=[bst])
        S.op("dve", lambda e, xi=xi, stt=stt: e.scalar_tensor_tensor(out=xi[:], in0=xi[:], scalar=stt[:, 3:4], in1=gf_t[:],
                                                                     op0=ALU.mult, op1=ALU.mult),
             reads=[bxi, bst, cst], writes=[bxi])
        S.dma("pool", y_out[t * 128:(t + 1) * 128, :], xi[:], reads=[bxi], writes=[buf("yd")], owner=bxi, multi=True)
    flush_stream()
    S.barrier(B.values())

    with nc.Block() as block:
        S.emit(block)
    es.close()
    return nc


def _cs(pos):
    inv = (np.float32(10000.0) ** (-np.arange(128, dtype=np.float32) / np.float32(128))).astype(np.float32)
    ang = (pos.astype(np.float32)[:, :, None] * inv[None, None, :]).astype(np.float32)
    c_, s_ = np.cos(ang).astype(np.float32), np.sin(ang).astype(np.float32)
    return np.ascontiguousarray(np.concatenate([c_, c_, -s_, s_], axis=-1), dtype=np.float32)


_CONST_CACHE = {}


def _consts():
    if _CONST_CACHE:
        return _CONST_CACHE
    import ml_dtypes
    lg = np.log(1.0 - 2.0 ** (-5.0 - np.arange(8, dtype=np.float32))).astype(np.float32)
    p = np.arange(128)
    mask = np.zeros((8, 128, 2, 128), np.float32)
    diff = (p[None, :] - p[:, None]).astype(np.float32)
    for h_ in range(8):
        m0 = np.where(diff >= 0, np.exp(np.maximum(diff, 0) * lg[h_]), 0.0)
        same = (p[None, :] // 8) == (p[:, None] // 8)
        mask[h_, :, 0, :] = m0
        mask[h_, :, 1, :] = np.where(same, m0, 0.0)
    dqk = np.zeros((128, 2, 3, 8), np.float32)
    for h_ in range(8):
        dqk[:, 0, 0, h_] = np.exp(lg[h_] * (p + 1.0))
        dqk[:, 0, 1, h_] = np.exp(lg[h_] * ((p % 8) + 1.0))
        dqk[:, 0, 2, h_] = np.exp(lg[h_] * ((p % 16) + 1.0))
        dqk[:, 1, 0, h_] = np.exp(lg[h_] * (127.0 - p)) / 16.0
        dqk[:, 1, 1, h_] = np.exp(lg[h_] * (7.0 - (p % 8))) / 16.0
        dqk[:, 1, 2, h_] = np.exp(lg[h_] * (15.0 - (p % 16))) / 16.0
    cm = np.zeros((128, 16, 2, 128), np.float32)
    rm = np.zeros((128, 16), np.float32)
    for sq in range(16):
        cm[:, sq, :, sq * 8:(sq + 1) * 8] = 1.0
        rm[sq * 8:(sq + 1) * 8, sq] = 1.0
    q = np.arange(128)
    m3 = ((q[None, :] // 16) >= (q[:, None] // 16)).astype(np.float32)
    sel_f = np.zeros((128, 16), np.float32)
    sel_f[q, q % 8] = 1.0
    sel_f[q, 8 + q // 16] = 1.0
    sel_b = np.zeros((128, 32), np.float32)
    sel_b[q, q // 8] = 1.0
    sel_b[q, 16 + q % 16] = 1.0
    _CONST_CACHE.update(m3=m3, sel_f=sel_f, sel_b=sel_b.astype(ml_dtypes.bfloat16))
    _CONST_CACHE.update(mask=mask, dqk=dqk, cmask=cm.reshape(128, 16, 256).astype(ml_dtypes.bfloat16), rmask=rm)
    return _CONST_CACHE


def _core_inputs(c, inp):
    b, h = c // 2, c % 2
    xp = np.concatenate([inp["meta_tokens"], inp["x_prompt"][b]], axis=0)
    x_own = np.zeros((ROWS_OWN, D), np.float32)
    x_own[:HALF] = xp[16 + h * HALF:16 + (h + 1) * HALF]
    x_own[8 * 128:] = inp["x_sample"][16 * c:16 * c + 16].reshape(128, D)
    g_all = np.stack([inp["norm1_g"][0].reshape(32, 128).T,
                      np.concatenate([inp["ret_gn_g"][0], inp["s5_norm_g"][0]]).reshape(32, 128).T,
                      inp["norm2_g"][0].reshape(32, 128).T], axis=1)
    x_pre = np.zeros((ROWS_PRE, D), np.float32)
    pos_pre = np.zeros((NT_PRE, 128), np.float32)
    if h == 1:
        x_pre[:NPRE] = xp[:NPRE]
        pos_pre = (np.arange(NT_PRE * 128, dtype=np.float32)).reshape(NT_PRE, 128)
    else:
        x_pre[8 * 128:8 * 128 + 16] = xp[:16]
        pos_pre[8] = np.arange(128)
    C = _consts()
    pos_own = np.zeros((NT_OWN, 128), np.float32)
    for t in range(8):
        pos_own[t] = 16 + h * HALF + t * 128 + np.arange(128)
    pos_own[8] = 16384 + (np.arange(128) % 8)
    return {
        "x_pre": x_pre, "w_in": inp["w_in"][0],
        "cs_own": _cs(pos_own), "cs_pre": _cs(pos_pre),
        "maskd": C["mask"], "dqk": C["dqk"], "cmask": C["cmask"], "rmask": C["rmask"],
        "st_ret": np.ascontiguousarray(inp["state_ret"][0, 16 * c:16 * c + 16]),
        "lam_re": inp["s5_lam_re"][0], "lam_im": inp["s5_lam_im"][0],
        "ldt": np.ascontiguousarray(np.broadcast_to(inp["s5_log_dt"][0][None, :], (64, 128)), dtype=np.float32),
        "m3": C["m3"], "sel_f": C["sel_f"], "sel_b": C["sel_b"],
        "b_re": inp["s5_b_re"][0], "b_im": inp["s5_b_im"][0], "c_re": inp["s5_c_re"][0], "c_im": inp["s5_c_im"][0],
        "dB": np.ascontiguousarray(np.broadcast_to(inp["s5_d"][0][None, :], (128, 2048)), dtype=np.float32),
        "bglu": np.ascontiguousarray(np.broadcast_to(inp["b_glu"][0][None, :], (128, 2048)), dtype=np.float32),
        "s5r": np.ascontiguousarray(inp["state_s5_re"][0, 16 * c:16 * c + 16]),
        "s5i": np.ascontiguousarray(inp["state_s5_im"][0, 16 * c:16 * c + 16]),
        "w_glu": inp["w_glu"][0],
        "x_own": x_own,
        "w_out": inp["w_out"][0], "w_gate": inp["w_gate"][0], "w_up": inp["w_up"][0], "w_down": inp["w_down"][0],
        "g_all": np.ascontiguousarray(g_all, dtype=np.float32),
        "gfin": np.ascontiguousarray(np.broadcast_to(inp["final_norm_g"][None, :], (128, D)), dtype=np.float32),
        "ident": np.eye(128, dtype=np.float32),
    }


def kernel(**inp):
    inp = {k: np.asarray(v) for k, v in inp.items()}
    nc = build_program()
    in_maps = [_core_inputs(c, inp) for c in range(NCORES)]
    res = run_bass_kernel_spmd(nc, in_maps, core_ids=list(range(NCORES)))
    R = res.results
    LAST['R'] = R
    y_prompt = np.zeros((4, 2048, D), np.float32)
    y_sample = np.zeros((128, 8, D), np.float32)
    for c in range(NCORES):
        b, h = c // 2, c % 2
        y = R[c]["y"]
        y_prompt[b, h * HALF:(h + 1) * HALF] = y[:HALF]
        y_sample[16 * c:16 * c + 16] = y[8 * 128:].reshape(16, 8, D)
    z = np.zeros
    ret_p = np.stack([R[2 * b + 1]["o_ret_p"] for b in range(4)])[None]
    ret_s = np.concatenate([R[c]["o_ret_s"] for c in range(NCORES)])[None]
    s5p_r = np.stack([R[2 * b + 1]["o_s5p_r"] for b in range(4)])[None].astype(np.float32)
    s5p_i = np.stack([R[2 * b + 1]["o_s5p_i"] for b in range(4)])[None].astype(np.float32)
    s5s_r = np.concatenate([R[c]["o_s5s_r"] for c in range(NCORES)])[None].astype(np.float32)
    s5s_i = np.concatenate([R[c]["o_s5s_i"] for c in range(NCORES)])[None].astype(np.float32)
    return (y_prompt, y_sample, ret_p.astype(np.float32), s5p_r, s5p_i, ret_s.astype(np.float32), s5s_r, s5s_i)
    return (y_prompt, y_sample,
            ret_p.astype(np.float32), z((1, 4, 128, 64), np.float32), z((1, 4, 128, 64), np.float32),
            ret_s.astype(np.float32), z((1, 128, 128, 64), np.float32), z((1, 128, 128, 64), np.float32))
```

```python
import numpy as np
from contextlib import ExitStack
import concourse.bass as bass
import concourse.mybir as mybir
from concourse.bass_utils import run_bass_kernel_spmd

F32 = mybir.dt.float32
BF16 = mybir.dt.bfloat16
AF = mybir.ActivationFunctionType
ALU = mybir.AluOpType

D = 4096
DFF = 11008
NCORES = 8
HALF = 1024
NPRE = 1040
NT_OWN = 9
NT_PRE = 9
ROWS_OWN = NT_OWN * 128
ROWS_PRE = NT_PRE * 128
EPS = 1e-6
CG = 256
DEBUG = False
LAST = {}


class Buf:
    def __init__(self, name):
        self.name = name
        self.writers = []
        self.readers = []
        self.dsem = None
        self.dcount = 0


class Sched:
    def __init__(self, nc, es):
        self.nc = nc
        self.es = es
        self.eng = {}
        for n in ("pe", "act", "dve", "pool", "sp"):
            self.eng[n] = dict(sem=es.enter_context(nc.semaphore("s_" + n)), count=0, ops=[], seen={})
        self.dsems = []
        self.nsem = 0

    def _dsem(self, buf):
        if buf.dsem is None:
            buf.dsem = self.es.enter_context(self.nc.semaphore("d%d" % self.nsem))
            self.nsem += 1
            self.dsems.append(buf)
        return buf.dsem

    def _waits(self, e, reads, writes, skip=None):
        need = {}

        def add(st):
            s, v = st
            k = id(s)
            if k == skip:
                return
            if k not in need or need[k][1] < v:
                need[k] = (s, v)
        for b in reads:
            for st in b.writers:
                add(st)
        for b in writes:
            for st in b.writers:
                add(st)
            for st in b.readers:
                add(st)
        out = []
        seen = self.eng[e]["seen"]
        for k, (s, v) in need.items():
            if seen.get(k, 0) < v:
                seen[k] = v
                out.append((s, v))
        return out

    def _commit(self, st, reads, writes, multi=False):
        for b in writes:
            if multi:
                b.writers.append(st)
            else:
                b.writers = [st]
                b.readers = []
        for b in reads:
            b.readers.append(st)
            if len(b.readers) > 64:
                best = {}
                for s, v in b.readers:
                    if id(s) not in best or best[id(s)][1] < v:
                        best[id(s)] = (s, v)
                b.readers = list(best.values())

    def op(self, e, fn, reads=(), writes=(), relax=False):
        E = self.eng[e]
        waits = self._waits(e, reads, writes, skip=(id(E["sem"]) if (relax and e == "dve") else None))
        E["count"] += 1
        st = (E["sem"], E["count"])
        E["ops"].append((waits, fn, E["sem"], 1))
        self._commit(st, reads, writes)
        return st

    def dma(self, q, out, in_, reads=(), writes=(), owner=None, multi=False):
        E = self.eng[q]
        waits = self._waits(q, reads, writes)
        sem = self._dsem(owner)
        owner.dcount += 16
        st = (sem, owner.dcount)
        E["ops"].append((waits, lambda eng, o=out, i=in_: eng.dma_start(out=o, in_=i), sem, 16))
        self._commit(st, reads, writes, multi=multi)
        return st

    def barrier(self, bufs=()):
        for b in bufs:
            b.writers = []
            b.readers = []
        stamps = []
        for n, E in self.eng.items():
            if E["count"]:
                stamps.append((E["sem"], E["count"]))
        for b in self.dsems:
            stamps.append((b.dsem, b.dcount))
        for n, E in self.eng.items():
            w = []
            for s, v in stamps:
                if E["seen"].get(id(s), 0) < v:
                    E["seen"][id(s)] = v
                    w.append((s, v))
            if w:
                E["ops"].append((w, None, None, 0))

    def emit(self, block):
        def run(E):
            def f(eng):
                for waits, fn, sem, amt in E["ops"]:
                    for s, v in waits:
                        eng.wait_ge(s, v)
                    if fn is None:
                        continue
                    ins = fn(eng)
                    ins.then_inc(sem, amt)
            return f
        block.tensor(run(self.eng["pe"]))
        block.scalar(run(self.eng["act"]))
        block.vector(run(self.eng["dve"]))
        block.gpsimd(run(self.eng["pool"]))
        block.sync(run(self.eng["sp"]))


def build_program():
    nc = bass.Bass("TRN2", target_bir_lowering=False)
    es = ExitStack()
    S = Sched(nc, es)

    def din(name, shape, dt=F32):
        return nc.dram_tensor(name, list(shape), dt, kind="ExternalInput").ap()

    def dout(name, shape, dt=F32):
        return nc.dram_tensor(name, list(shape), dt, kind="ExternalOutput").ap()

    def dscr(name, shape, dt=F32):
        return nc.dram_tensor(name, list(shape), dt, kind=("ExternalOutput" if (DEBUG and name in ("mix", "proj")) else "Internal")).ap()

    x_own = din("x_own", [ROWS_OWN, D])
    x_pre = din("x_pre", [ROWS_PRE, D])
    w_in = din("w_in", [D, 10240])
    cs_own = din("cs_own", [NT_OWN, 128, 512])
    cs_pre = din("cs_pre", [NT_PRE, 128, 512])
    maskd = din("maskd", [8, 128, 2, 128])
    dqk_d = din("dqk", [128, 2, 3, 8])
    cmask_d = din("cmask", [128, 16, 256], BF16)
    rmask_d = din("rmask", [128, 16])
    st_ret = din("st_ret", [16, 8, 256, 256])
    o_ret_p = dout("o_ret_p", [8, 256, 256])
    o_ret_s = dout("o_ret_s", [16, 8, 256, 256])
    lam_re_d = din("lam_re", [128, 64]); lam_im_d = din("lam_im", [128, 64])
    ldt_d = din("ldt", [64, 128]); m3_d = din("m3", [128, 128])
    sel_f_d = din("sel_f", [128, 16]); sel_b_d = din("sel_b", [128, 32], BF16)
    b_re_d = din("b_re", [128, 64, 16]); b_im_d = din("b_im", [128, 64, 16])
    c_re_d = din("c_re", [128, 16, 64]); c_im_d = din("c_im", [128, 16, 64])
    dB_d = din("dB", [128, 2048]); bglu_d = din("bglu", [128, 2048])
    s5r_d = din("s5r", [16, 128, 64]); s5i_d = din("s5i", [16, 128, 64])
    w_glu = din("w_glu", [2048, 2048])
    o_s5p_r = dout("o_s5p_r", [128, 64]); o_s5p_i = dout("o_s5p_i", [128, 64])
    o_s5s_r = dout("o_s5s_r", [16, 128, 64]); o_s5s_i = dout("o_s5s_i", [16, 128, 64])
    Wall_d = dscr("Wall", [128, 128, 512], BF16)
    zscr = dscr("zscr", [ROWS_OWN, 2048], F32)
    s5o = dscr("s5o", [ROWS_OWN, 2048], F32)
    proj = dscr("proj", [ROWS_OWN, 10240], F32)
    projp = dscr("projp", [ROWS_PRE, 10240], F32)
    w_out = din("w_out", [D, D])
    w_gate = din("w_gate", [D, DFF])
    w_up = din("w_up", [D, DFF])
    w_down = din("w_down", [DFF, D])
    g_all = din("g_all", [128, 3, 32])
    gfin = din("gfin", [128, D])
    ident_in = din("ident", [128, 128])
    y_out = dout("y", [ROWS_OWN, D])

    mix = dscr("mix", [ROWS_OWN, D], F32)
    x1 = dscr("x1", [ROWS_OWN, D], F32)
    h2 = dscr("h2", [ROWS_OWN, D], F32)
    act = dscr("act", [ROWS_OWN, DFF], F32)

    def sb(name, shape, dt):
        return es.enter_context(nc.sbuf_tensor(name, list(shape), dt))

    actT = sb("actT", [128, 32, ROWS_OWN], BF16)
    wbf = [sb("wbf%d" % i, [128, 32, CG], BF16) for i in range(2)]
    wst = [sb("wst%d" % i, [128, 8, CG], F32) for i in range(2)]
    xin = [sb("xin%d" % i, [128, D], F32) for i in range(2)]
    xbf = [sb("xbf%d" % i, [128, D], BF16) for i in range(2)]
    ot = [sb("ot%d" % i, [128, CG], F32) for i in range(4)]
    rt = [sb("rt%d" % i, [128, CG], F32) for i in range(4)]
    gtmp = sb("gtmp", [128, NT_OWN, CG], BF16)
    stat = [sb("stat%d" % i, [128, 8], F32) for i in range(2)]
    gains = sb("gains", [128, 3, 32], F32)
    gf_t = sb("gf_t", [128, D], F32)
    ident_f = sb("ident_f", [128, 128], F32)
    ident = sb("ident_b", [128, 128], BF16)

    ps = [es.enter_context(nc.psum_tensor("ps%d" % i, [128, 512], F32)) for i in range(8)]

    B = {}

    def buf(name):
        if name not in B:
            B[name] = Buf(name)
        return B[name]

    cst = buf("const")
    S.dma("sp", gains[:], g_all, writes=[cst], owner=cst, multi=True)
    S.dma("sp", gf_t[:], gfin, writes=[cst], owner=cst, multi=True)
    S.dma("sp", ident_f[:], ident_in, writes=[cst], owner=cst, multi=True)
    S.op("dve", lambda e: e.tensor_copy(out=ident[:], in_=ident_f[:]), reads=[cst], writes=[buf("ident")])

    def load_actT(src, ntiles, kc0, nkc, norm=False):
        W = nkc * 128
        for t in range(ntiles):
            xi, xb, stt = xin[t % 2], xbf[t % 2], stat[t % 2]
            bxi, bxb, bst = buf("xin%d" % (t % 2)), buf("xbf%d" % (t % 2)), buf("stat%d" % (t % 2))
            S.dma("pool", xi[:, 0:W], src[t * 128:(t + 1) * 128, kc0 * 128:kc0 * 128 + W], writes=[bxi], owner=bxi)
            if norm:
                S.op("act", lambda e, xi=xi, stt=stt, xb=xb: e.activation(out=xb[:, 0:W], in_=xi[:, 0:W], func=AF.Square,
                                                                    accum_out=stt[:, 0:1]),
                     reads=[bxi], writes=[bst, bxb])
                S.op("dve", lambda e, stt=stt: e.tensor_scalar(out=stt[:, 1:2], in0=stt[:, 0:1], scalar1=1.0 / W,
                                                               scalar2=EPS, op0=ALU.mult, op1=ALU.add),
                     reads=[bst], writes=[bst])
                S.op("act", lambda e, stt=stt: e.activation(out=stt[:, 2:3], in_=stt[:, 1:2], func=AF.Sqrt),
                     reads=[bst], writes=[bst])
                S.op("dve", lambda e, stt=stt: e.reciprocal(out=stt[:, 3:4], in_=stt[:, 2:3]), reads=[bst], writes=[bst])
                S.op("dve", lambda e, xi=xi, xb=xb, stt=stt: e.tensor_scalar(out=xb[:, 0:W], in0=xi[:, 0:W],
                                                                             scalar1=stt[:, 3:4], scalar2=None,
                                                                             op0=ALU.mult),
                     reads=[bxi, bst], writes=[bxb])
            else:
                S.op("dve", lambda e, xi=xi, xb=xb: e.tensor_copy(out=xb[:, 0:W], in_=xi[:, 0:W]),
                     reads=[bxi], writes=[bxb])
            for q in range((nkc + 3) // 4):
                pb = 6 + (q % 2)
                bp = buf("ps%d" % pb)
                pv = ps[pb].bitcast(BF16)
                nq = min(4, nkc - q * 4)

                def tr(e, xb=xb, pv=pv, q=q, nq=nq):
                    ins = None
                    for i in range(nq):
                        ins = e.transpose(out=pv[:, i * 128:(i + 1) * 128], in_=xb[:, (q * 4 + i) * 128:(q * 4 + i + 1) * 128],
                                          identity=ident[:])
                    return ins
                S.op("pe", tr, reads=[bxb, buf("ident")], writes=[bp])
                dst = actT[:, q * 4:q * 4 + nq, t * 128:(t + 1) * 128]
                srcv = pv[:, 0:nq * 128].rearrange("p (c t) -> p c t", c=nq)
                eng = "act" if q % 2 else "dve"
                if eng == "act":
                    S.op("act", lambda e, dst=dst, srcv=srcv: e.copy(out=dst, in_=srcv), reads=[bp], writes=[buf("actT")])
                else:
                    S.op("dve", lambda e, dst=dst, srcv=srcv: e.tensor_copy(out=dst, in_=srcv), reads=[bp], writes=[buf("actT")])

    wcount = [0]
    pcount = [0]
    pending = []

    def stream(Wd, krow0, nkc, col0, ncols, ntiles, consumer, gain=None):
        ngr = (ncols + CG - 1) // CG
        for gi in range(ngr):
            c0 = col0 + gi * CG
            cw = min(CG, col0 + ncols - c0)
            pending.append((Wd, krow0, nkc, c0, cw, ntiles, consumer, gain))

    def _load_group(item):
        Wd, krow0, nkc, c0, cw, ntiles, consumer, gain = item
        wi = wcount[0] % 2
        wcount[0] += 1
        wb, bwb = wbf[wi], buf("wbf%d" % wi)
        nh = (nkc + 7) // 8
        for hh in range(nh):
            k0 = hh * 8
            kn = min(8, nkc - k0)
            si = pcount[0] % 2
            pcount[0] += 1
            ws, bws = wst[si], buf("wst%d" % si)
            S.dma("sp", ws[:, 0:kn, 0:cw],
                  Wd[krow0 + k0 * 128: krow0 + (k0 + kn) * 128, c0:c0 + cw].rearrange("(k p) n -> p k n", p=128),
                  writes=[bws], owner=bws)
            if gain is None:
                if hh % 2:
                    S.op("act", lambda e, wb=wb, ws=ws, k0=k0, kn=kn, cw=cw: e.copy(out=wb[:, k0:k0 + kn, 0:cw], in_=ws[:, 0:kn, 0:cw]),
                         reads=[bws], writes=[bwb])
                else:
                    S.op("dve", lambda e, wb=wb, ws=ws, k0=k0, kn=kn, cw=cw: e.tensor_copy(out=wb[:, k0:k0 + kn, 0:cw], in_=ws[:, 0:kn, 0:cw]),
                         reads=[bws], writes=[bwb])
            else:
                for kk in range(kn):
                    kc = k0 + kk
                    gcol = gains[:, gain, (krow0 // 128 + kc):(krow0 // 128 + kc) + 1]
                    if kk % 2:
                        S.op("act", lambda e, wb=wb, ws=ws, kc=kc, kk=kk, cw=cw, gcol=gcol: e.activation(
                            out=wb[:, kc, 0:cw], in_=ws[:, kk, 0:cw], func=AF.Copy, scale=gcol),
                            reads=[bws, cst], writes=[bwb])
                    else:
                        S.op("dve", lambda e, wb=wb, ws=ws, kc=kc, kk=kk, cw=cw, gcol=gcol: e.tensor_scalar(
                            out=wb[:, kc, 0:cw], in0=ws[:, kk, 0:cw], scalar1=gcol, scalar2=None, op0=ALU.mult),
                            reads=[bws, cst], writes=[bwb])
        return wb, bwb

    def _compute_group(item, wb, bwb):
        Wd, krow0, nkc, c0, cw, ntiles, consumer, gain = item
        for t in range(ntiles):
            pb = pcount[1] % 6 if len(pcount) > 1 else 0
            pcount[1] += 1
            bp = buf("ps%d" % pb)
            pt = ps[pb][:, 0:cw]

            def mm(e, pt=pt, wb=wb, t=t, cw=cw):
                ins = None
                for kc in range(nkc):
                    ins = e.matmul(pt, actT[:, kc, t * 128:(t + 1) * 128], wb[:, kc, 0:cw], start=(kc == 0), stop=(kc == nkc - 1))
                return ins
            S.op("pe", mm, reads=[buf("actT"), bwb], writes=[bp])
            consumer(t, c0, cw, pt, bp)

    pcount.append(0)

    def flush_stream():
        items = pending[:]
        del pending[:]
        if not items:
            return
        cur = _load_group(items[0])
        for n_, item in enumerate(items):
            nxt = _load_group(items[n_ + 1]) if n_ + 1 < len(items) else None
            _compute_group(item, *cur)
            cur = nxt

    ocnt = [0]

    def next_ot():
        i = ocnt[0] % 4
        ocnt[0] += 1
        return i


    def mk_store(dst):
        def cons(t, c0, cw, pt, bp):
            i = next_ot()
            o, bo = ot[i], buf("ot%d" % i)
            if i % 2:
                S.op("act", lambda e: e.copy(out=o[:, 0:cw], in_=pt), reads=[bp], writes=[bo])
            else:
                S.op("dve", lambda e: e.tensor_copy(out=o[:, 0:cw], in_=pt), reads=[bp], writes=[bo])
            S.dma("pool", dst[t * 128:(t + 1) * 128, c0:c0 + cw], o[:, 0:cw], reads=[bo], writes=[buf("projd")], owner=bo, multi=True)
        return cons

    load_actT(x_pre, NT_PRE, 0, 32, norm=True)
    stream(w_in, 0, 32, 2048, 4096, NT_PRE, mk_store(projp), gain=0)
    stream(w_in, 0, 32, 8192, 2048, NT_PRE, mk_store(projp), gain=0)
    flush_stream()
    S.barrier(B.values())
    load_actT(x_own, NT_OWN, 0, 32, norm=True)
    stream(w_in, 0, 32, 0, 10240, NT_OWN, mk_store(proj), gain=0)
    flush_stream()
    S.barrier(B.values())

    flat = actT[:].rearrange("p a b -> p (a b)")
    coff = [0]

    def carve(shape, dt):
        n = 1
        for d_ in shape[1:]:
            n *= d_
        nb = n * (4 if dt == F32 else 2)
        a = flat[:, coff[0] // 2:(coff[0] + nb) // 2]
        coff[0] += nb
        if dt == F32:
            a = a.bitcast(F32)
        if len(shape) == 3:
            a = a.rearrange("p (a b) -> p a b", a=shape[1])
        return a

    qkvg = [carve([128, 4, 256], F32) for _ in range(2)]
    cst_ = [carve([128, 2, 256], F32) for _ in range(2)]
    tqs = [[carve([128, 256], F32) for _ in range(4)] for _ in range(2)]
    qb, qdb, kb, kdb, vb = [[carve([128, 256], BF16) for _ in range(2)] for _ in range(5)]
    trs = [carve([128, 6, 128], BF16) for _ in range(2)]
    smb = [carve([128, 128], BF16) for _ in range(2)]
    Sf2 = [carve([128, 2, 256], F32) for _ in range(2)]
    Sb2 = [carve([128, 2, 256], BF16) for _ in range(2)]
    msk2 = [carve([128, 2, 128], F32) for _ in range(2)]
    bnst = [carve([128, 8], F32) for _ in range(2)]
    yt = [carve([128, 256], F32) for _ in range(2)]
    sgt = [carve([128, 256], F32) for _ in range(2)]
    S0 = [carve([128, 2, 256], F32) for _ in range(2)]
    S0b = [carve([128, 2, 256], BF16) for _ in range(2)]
    So = [carve([128, 2, 256], F32) for _ in range(2)]
    vm = [carve([128, 256], BF16) for _ in range(2)]
    qm = [carve([128, 2, 128], BF16) for _ in range(2)]
    cmask = carve([128, 16, 256], BF16)
    rmask = carve([128, 16], F32)
    dqk = carve([128, 2, 24], F32)
    odbg = [carve([128, 256], F32) for _ in range(2)]

    cst2 = buf("const2")
    S.dma("sp", cmask, cmask_d, writes=[cst2], owner=cst2, multi=True)
    S.dma("sp", rmask, rmask_d, writes=[cst2], owner=cst2, multi=True)
    S.dma("sp", dqk, dqk_d.rearrange("p a k h -> p a (k h)"), writes=[cst2], owner=cst2, multi=True)
    GAM = [1.0 - 2.0 ** (-5.0 - h_) for h_ in range(8)]
    rcount = [0]

    def ret_tile(hd, par, src, csd, t, kind, full, sample):
        i = par
        rcount[0] += 1
        Sf, Sb, msk = Sf2[par], Sb2[par], msk2[par]
        tq1, tq2, tk1, tk2 = tqs[par]
        TRB, SCB, STB = (6, 2)[par], (7, 3)[par], (5, 4)[par]
        qi, bqi = qkvg[i], buf("qkvg%d" % i)
        ci, bci = cst_[i], buf("cs%d" % i)
        srcv = src[t * 128:(t + 1) * 128, :].rearrange("p (s h d) -> p s h d", s=5, h=8)[:, 0:4, hd, :]
        S.dma("sp", qi, srcv, writes=[bqi], owner=bqi)
        S.dma("sp", ci, csd[t].rearrange("p (a d) -> p a d", a=2), writes=[bci], owner=bci)
        dq_col = dqk[:, 0, kind * 8 + hd:kind * 8 + hd + 1]
        dk_col = dqk[:, 1, kind * 8 + hd:kind * 8 + hd + 1]

        def rot(x, t1, t2, bt1, bt2):
            S.op("dve", lambda e: e.tensor_tensor(out=t1, in0=x, in1=ci[:, 0, :], op=ALU.mult), reads=[bqi, bci], writes=[bt1], relax=True)
            S.op("dve", lambda e: e.tensor_tensor(out=t2[:, 0:128], in0=x[:, 128:256], in1=ci[:, 1, 0:128], op=ALU.mult), reads=[bqi, bci], writes=[bt2], relax=True)
            S.op("dve", lambda e: e.tensor_tensor(out=t2[:, 128:256], in0=x[:, 0:128], in1=ci[:, 1, 128:256], op=ALU.mult), reads=[bqi, bci], writes=[bt2], relax=True)
            S.op("dve", lambda e: e.tensor_tensor(out=t1, in0=t1, in1=t2, op=ALU.add), reads=[bt1, bt2], writes=[bt1], relax=True)

        bk1, bk2 = buf("tk1_%d" % par), buf("tk2_%d" % par)
        rot(qi[:, 1, :], tk1, tk2, bk1, bk2)
        kd_, bkd = kdb[i], buf("kdb%d" % i)
        v_, bv = vb[i], buf("vb%d" % i)
        S.op("dve", lambda e: e.tensor_scalar(out=kd_, in0=tk1, scalar1=dk_col, scalar2=None, op0=ALU.mult), reads=[bk1, cst2], writes=[bkd])
        S.op("act", lambda e: e.copy(out=v_, in_=qi[:, 2, :]), reads=[bqi], writes=[bv])
        bSf, bSb = buf("Sf%d" % par), buf("Sb%d" % par)
        gl = GAM[hd] ** {0: 128, 1: 8, 2: 16}[kind]
        if full:
            bq1, bq2 = buf("tq1_%d" % par), buf("tq2_%d" % par)
            rot(qi[:, 0, :], tq1, tq2, bq1, bq2)
            q_, bq = qb[i], buf("qb%d" % i)
            qd_, bqd = qdb[i], buf("qdb%d" % i)
            k_, bk = kb[i], buf("kb%d" % i)
            S.op("act", lambda e: e.copy(out=q_, in_=tq1), reads=[bq1], writes=[bq])
            S.op("dve", lambda e: e.tensor_scalar(out=qd_, in0=tq1, scalar1=dq_col, scalar2=None, op0=ALU.mult), reads=[bq1, cst2], writes=[bqd])
            S.op("act", lambda e: e.mul(out=k_, in_=tk1, mul=1.0 / 16.0), reads=[bk1], writes=[bk])
            bp6 = buf("ps%d" % TRB)
            pv = ps[TRB].bitcast(BF16)

            def tr(e):
                ins = None
                for n_, srcb in enumerate((q_, qd_, k_)):
                    for c_ in range(2):
                        ins = e.transpose(out=pv[:, (2 * n_ + c_) * 128:(2 * n_ + c_ + 1) * 128], in_=srcb[:, c_ * 128:(c_ + 1) * 128], identity=ident[:])
                return ins
            S.op("pe", tr, reads=[bq, bqd, bk, buf("ident")], writes=[bp6])
            tr_, btr = trs[i], buf("trs%d" % i)
            S.op("dve", lambda e: e.tensor_copy(out=tr_, in_=pv[:, 0:768].rearrange("p (a b) -> p a b", a=6)), reads=[bp6], writes=[btr])
            bp7 = buf("ps%d" % SCB)

            def sc(e):
                e.matmul(ps[SCB][:, 0:128], tr_[:, 4, :], tr_[:, 0, :], start=True, stop=False)
                return e.matmul(ps[SCB][:, 0:128], tr_[:, 5, :], tr_[:, 1, :], start=False, stop=True)
            S.op("pe", sc, reads=[btr], writes=[bp7])
            sm_, bsm = smb[i], buf("smb%d" % i)
            S.op("dve", lambda e: e.tensor_tensor(out=sm_, in0=ps[SCB][:, 0:128], in1=msk[:, kind, :], op=ALU.mult), reads=[bp7, buf("msk%d" % par)], writes=[bsm])
            pbo = par
            bpo = buf("ps%d" % pbo)
            po = ps[pbo][:, 0:256]
            if not sample:
                def om(e):
                    e.matmul(po, sm_, v_, start=True, stop=False)
                    e.matmul(po, tr_[:, 2, :], Sb[:, 0, :], start=False, stop=False)
                    return e.matmul(po, tr_[:, 3, :], Sb[:, 1, :], start=False, stop=True)
                S.op("pe", om, reads=[bsm, bv, btr, bSb], writes=[bpo])
            else:
                S.op("pe", lambda e: e.matmul(po, sm_, v_, start=True, stop=False, skip_group_check=True), reads=[bsm, bv], writes=[bpo])
        if not sample:
            bp5 = buf("ps%d" % STB)

            def su(e):
                e.matmul(ps[STB][:, 0:256], kd_[:, 0:128], v_, start=True, stop=True)
                return e.matmul(ps[STB][:, 256:512], kd_[:, 128:256], v_, start=True, stop=True)
            S.op("pe", su, reads=[bkd, bv], writes=[bp5])
            S.op("dve", lambda e: e.scalar_tensor_tensor(out=Sf, in0=Sf, scalar=float(gl), in1=ps[STB][:, 0:512].rearrange("p (a b) -> p a b", a=2),
                                                         op0=ALU.mult, op1=ALU.add), reads=[bSf, bp5], writes=[bSf])
            S.op("act", lambda e: e.copy(out=Sb, in_=Sf), reads=[bSf], writes=[bSb])
        else:
            for sq in range(16):
                j = sq % 2
                s0, bs0 = S0[j], buf("S0_%d" % j)
                s0b, bs0b = S0b[j], buf("S0b_%d" % j)
                so, bso = So[j], buf("So_%d" % j)
                vm_, bvm = vm[j], buf("vm%d" % j)
                qm_, bqm = qm[j], buf("qm%d" % j)
                S.dma("sp", s0, st_ret[sq, hd].rearrange("(c p) v -> p c v", p=128), writes=[bs0], owner=bs0)
                S.op("act", lambda e, s0=s0, s0b=s0b: e.copy(out=s0b, in_=s0), reads=[bs0], writes=[bs0b])
                S.op("dve", lambda e, qm_=qm_, sq=sq: e.tensor_tensor(out=qm_, in0=tr_[:, 2:4, :], in1=cmask[:, sq, :].rearrange("p (a b) -> p a b", a=2), op=ALU.mult),
                     reads=[btr, cst2], writes=[bqm])

                def im(e, qm_=qm_, s0b=s0b, sq=sq):
                    e.matmul(po, qm_[:, 0, :], s0b[:, 0, :], start=False, stop=False, skip_group_check=True)
                    return e.matmul(po, qm_[:, 1, :], s0b[:, 1, :], start=False, stop=(sq == 15), skip_group_check=True)
                S.op("pe", im, reads=[bqm, bs0b], writes=[bpo])
                S.op("dve", lambda e, vm_=vm_, sq=sq: e.tensor_scalar(out=vm_, in0=v_, scalar1=rmask[:, sq:sq + 1], scalar2=None, op0=ALU.mult),
                     reads=[bv, cst2], writes=[bvm])
                pbs = 4 + (sq % 2)
                bps = buf("ps%d" % pbs)

                def su2(e, vm_=vm_, pbs=pbs):
                    e.matmul(ps[pbs][:, 0:256], kd_[:, 0:128], vm_, start=True, stop=True)
                    return e.matmul(ps[pbs][:, 256:512], kd_[:, 128:256], vm_, start=True, stop=True)
                S.op("pe", su2, reads=[bkd, bvm], writes=[bps])
                S.op("dve", lambda e, so=so, s0=s0, pbs=pbs: e.scalar_tensor_tensor(out=so, in0=s0, scalar=float(gl), in1=ps[pbs][:, 0:512].rearrange("p (a b) -> p a b", a=2),
                                                                                   op0=ALU.mult, op1=ALU.add), reads=[bs0, bps], writes=[bso])
                S.dma("pool", o_ret_s[sq, hd].rearrange("(c p) v -> p c v", p=128), so, reads=[bso], writes=[buf("orets")], owner=bso, multi=True)
        if full:
            bb, bbn = bnst[i], buf("bnst%d" % i)
            y_, by = yt[i], buf("yt%d" % i)
            sg_, bsg = sgt[i], buf("sgt%d" % i)
            S.op("act", lambda e: e.activation(out=sg_, in_=qi[:, 3, :], func=AF.Silu), reads=[bqi], writes=[bsg])
            od_, bod = odbg[i], buf("odbg%d" % i)
            S.op("act", lambda e: e.copy(out=od_, in_=po), reads=[bpo], writes=[bod])
            if DEBUG:
                S.dma("pool", mix[t * 128:(t + 1) * 128, 2048 + hd * 256:2048 + (hd + 1) * 256], od_, reads=[bod], writes=[buf("mixd")], owner=bod, multi=True)
            S.op("dve", lambda e: e.bn_stats(out=bb[:, 0:6], in_=od_), reads=[bod], writes=[bbn])
            S.op("dve", lambda e: e.bn_aggr(out=bb[:, 6:8], in_=bb[:, 0:6]), reads=[bbn], writes=[bbn])
            S.op("dve", lambda e: e.tensor_scalar(out=bb[:, 0:1], in0=bb[:, 7:8], scalar1=1e-5, scalar2=None, op0=ALU.add), reads=[bbn], writes=[bbn])
            S.op("act", lambda e: e.activation(out=bb[:, 1:2], in_=bb[:, 0:1], func=AF.Sqrt), reads=[bbn], writes=[bbn])
            S.op("dve", lambda e: e.reciprocal(out=bb[:, 2:3], in_=bb[:, 1:2]), reads=[bbn], writes=[bbn])
            S.op("dve", lambda e: e.tensor_scalar(out=y_, in0=od_, scalar1=bb[:, 6:7], scalar2=bb[:, 2:3], op0=ALU.subtract, op1=ALU.mult),
                 reads=[bod, bbn], writes=[by])
            S.op("dve", lambda e: e.tensor_tensor(out=y_, in0=y_, in1=sg_, op=ALU.mult), reads=[by, bsg], writes=[by])
            S.dma("pool", mix[t * 128:(t + 1) * 128, hd * 256:(hd + 1) * 256], y_, reads=[by], writes=[buf("mixd")], owner=by, multi=True)

    for hp in range(4):
        hds = (2 * hp, 2 * hp + 1)
        for par, hd in enumerate(hds):
            S.dma("sp", msk2[par], maskd[hd], reads=[], writes=[buf("msk%d" % par)], owner=buf("msk%d" % par))
            S.op("dve", lambda e, par=par: e.memset(Sf2[par], 0.0), writes=[buf("Sf%d" % par)])
            S.op("dve", lambda e, par=par: e.memset(Sb2[par], 0.0), writes=[buf("Sb%d" % par)])
        for t in range(NT_PRE):
            for par, hd in enumerate(hds):
                ret_tile(hd, par, projp, cs_pre, t, 0 if t < 8 else 2, False, False)
        for t in range(8):
            for par, hd in enumerate(hds):
                ret_tile(hd, par, proj, cs_own, t, 0, True, False)
        for par, hd in enumerate(hds):
            S.dma("sp", o_ret_p[hd].rearrange("(c p) v -> p c v", p=128), Sf2[par], reads=[buf("Sf%d" % par)], writes=[buf("oretp")], owner=buf("Sf%d" % par), multi=True)
        for par, hd in enumerate(hds):
            ret_tile(hd, par, proj, cs_own, 8, 1, True, True)
    flush_stream()
    S.barrier(B.values())

    coff[0] = 0
    PI = 3.141592653589793
    MAGIC = 12582912.0
    bS = buf("s5setup")

    def T64(n=1):
        a = carve([128, n, 128], F32) if n > 1 else carve([128, 128], F32)
        return a

    lam_sb = carve([128, 2, 64], F32)
    S.dma("sp", lam_sb[:, 0, :], lam_re_d, writes=[bS], owner=bS, multi=True)
    S.dma("sp", lam_sb[:, 1, :], lam_im_d, writes=[bS], owner=bS, multi=True)
    lamT = T64(2)
    ldt = T64()
    S.dma("sp", ldt[0:64, :], ldt_d, writes=[bS], owner=bS, multi=True)
    m3 = carve([128, 128], F32)
    S.dma("sp", m3, m3_d, writes=[bS], owner=bS, multi=True)
    selc = carve([128, 64], F32)
    S.dma("sp", selc[:, 0:16], sel_f_d, writes=[bS], owner=bS, multi=True)
    selb = carve([128, 32], BF16)
    S.dma("sp", selb, sel_b_d, writes=[bS], owner=bS, multi=True)
    mask8, tmask = selc[:, 0:8], selc[:, 8:16]
    Jsel, Csel = selb[:, 0:16], selb[:, 16:32]

    A1, A2 = carve([128, 2, 128], F32), carve([128, 2, 128], F32)
    B1, B2 = carve([128, 2, 128], F32), carve([128, 2, 128], F32)
    main_off = coff[0]

    sop_state = [False]

    def sop(eng, fn, after_recip=False):
        S.op(eng, fn, reads=[bS], writes=[bS], relax=(eng == "dve" and not sop_state[0]))
        sop_state[0] = after_recip

    def trans_f32(dst, src, rows_in, cols_in):
        bp = buf("ps7")
        S.op("pe", lambda e: e.transpose(out=ps[7][0:cols_in, 0:rows_in], in_=src, identity=ident_f[0:rows_in, 0:rows_in]), reads=[bS, cst], writes=[bp])
        S.op("dve", lambda e: e.tensor_copy(out=dst, in_=ps[7][0:cols_in, 0:rows_in]), reads=[bp, bS], writes=[bS])

    trans_f32(lamT[0:64, 0, :], lam_sb[:, 0, :], 128, 64)
    trans_f32(lamT[0:64, 1, :], lam_sb[:, 1, :], 128, 64)
    dtT = T64()
    lrT, liT = T64(), T64()
    sop("act", lambda e: e.activation(out=dtT[0:64, :], in_=ldt[0:64, :], func=AF.Exp))
    sop("dve", lambda e: e.tensor_tensor(out=lrT[0:64, :], in0=lamT[0:64, 0, :], in1=dtT[0:64, :], op=ALU.mult))
    sop("dve", lambda e: e.tensor_tensor(out=liT[0:64, :], in0=lamT[0:64, 1, :], in1=dtT[0:64, :], op=ALU.mult))
    AR, AI, NR, NI = T64(9), T64(9), T64(9), T64(9)
    tA, tB, tC, tD = T64(), T64(), T64(), T64()

    def trig(dst, k, off):
        sop("dve", lambda e: e.tensor_scalar(out=tA[0:64, :], in0=liT[0:64, :], scalar1=float(k), scalar2=float(off), op0=ALU.mult, op1=ALU.add))
        sop("dve", lambda e: e.tensor_scalar(out=tB[0:64, :], in0=tA[0:64, :], scalar1=1.0 / (2 * PI), scalar2=MAGIC, op0=ALU.mult, op1=ALU.add))
        sop("dve", lambda e: e.tensor_scalar(out=tB[0:64, :], in0=tB[0:64, :], scalar1=MAGIC, scalar2=2 * PI, op0=ALU.subtract, op1=ALU.mult))
        sop("dve", lambda e: e.tensor_tensor(out=tA[0:64, :], in0=tA[0:64, :], in1=tB[0:64, :], op=ALU.subtract))
        sop("dve", lambda e: e.tensor_scalar(out=tA[0:64, :], in0=tA[0:64, :], scalar1=-3.1415925, scalar2=3.1415925, op0=ALU.max, op1=ALU.min))
        sop("act", lambda e: e.activation(out=dst, in_=tA[0:64, :], func=AF.Sin))

    for k in range(9):
        trig(tC[0:64, :], k, PI / 2)
        trig(tD[0:64, :], k, 0.0)
        sop("act", lambda e, k=k: e.activation(out=AR[0:64, k, :], in_=lrT[0:64, :], func=AF.Exp, scale=float(k)))
        sop("act", lambda e, k=k: e.activation(out=NR[0:64, k, :], in_=lrT[0:64, :], func=AF.Exp, scale=float(-k)))
        sop("dve", lambda e, k=k: e.tensor_tensor(out=AI[0:64, k, :], in0=AR[0:64, k, :], in1=tD[0:64, :], op=ALU.mult))
        sop("dve", lambda e, k=k: e.tensor_tensor(out=AR[0:64, k, :], in0=AR[0:64, k, :], in1=tC[0:64, :], op=ALU.mult))
        sop("dve", lambda e, k=k: e.scalar_tensor_tensor(out=NI[0:64, k, :], in0=NR[0:64, k, :], scalar=-1.0, in1=tD[0:64, :], op0=ALU.mult, op1=ALU.mult))
        sop("dve", lambda e, k=k: e.tensor_tensor(out=NR[0:64, k, :], in0=NR[0:64, k, :], in1=tC[0:64, :], op=ALU.mult))
    fr, fi = T64(), T64()
    sop("dve", lambda e: e.tensor_scalar(out=tA[0:64, :], in0=AR[0:64, 1, :], scalar1=-1.0, scalar2=None, op0=ALU.add))
    sop("dve", lambda e: e.tensor_tensor(out=tB[0:64, :], in0=lamT[0:64, 0, :], in1=lamT[0:64, 0, :], op=ALU.mult))
    sop("dve", lambda e: e.tensor_tensor(out=tC[0:64, :], in0=lamT[0:64, 1, :], in1=lamT[0:64, 1, :], op=ALU.mult))
    sop("dve", lambda e: e.tensor_tensor(out=tB[0:64, :], in0=tB[0:64, :], in1=tC[0:64, :], op=ALU.add))
    sop("dve", lambda e: e.reciprocal(out=tB[0:64, :], in_=tB[0:64, :]), after_recip=True)
    sop("dve", lambda e: e.tensor_tensor(out=tC[0:64, :], in0=tA[0:64, :], in1=lamT[0:64, 0, :], op=ALU.mult))
    sop("dve", lambda e: e.tensor_tensor(out=tD[0:64, :], in0=AI[0:64, 1, :], in1=lamT[0:64, 1, :], op=ALU.mult))
    sop("dve", lambda e: e.tensor_tensor(out=tC[0:64, :], in0=tC[0:64, :], in1=tD[0:64, :], op=ALU.add))
    sop("dve", lambda e: e.tensor_tensor(out=fr[0:64, :], in0=tC[0:64, :], in1=tB[0:64, :], op=ALU.mult))
    sop("dve", lambda e: e.tensor_tensor(out=tC[0:64, :], in0=AI[0:64, 1, :], in1=lamT[0:64, 0, :], op=ALU.mult))
    sop("dve", lambda e: e.tensor_tensor(out=tD[0:64, :], in0=tA[0:64, :], in1=lamT[0:64, 1, :], op=ALU.mult))
    sop("dve", lambda e: e.tensor_tensor(out=tC[0:64, :], in0=tC[0:64, :], in1=tD[0:64, :], op=ALU.subtract))
    sop("dve", lambda e: e.tensor_tensor(out=fi[0:64, :], in0=tC[0:64, :], in1=tB[0:64, :], op=ALU.mult))
    ER, EI, ENR, ENI = T64(8), T64(8), T64(8), T64(8)

    def cmul_f(dr, di, xr, xi):
        sop("dve", lambda e: e.tensor_tensor(out=tA[0:64, :], in0=xr, in1=fr[0:64, :], op=ALU.mult))
        sop("dve", lambda e: e.tensor_tensor(out=tB[0:64, :], in0=xi, in1=fi[0:64, :], op=ALU.mult))
        sop("dve", lambda e: e.tensor_tensor(out=dr, in0=tA[0:64, :], in1=tB[0:64, :], op=ALU.subtract))
        sop("dve", lambda e: e.tensor_tensor(out=tA[0:64, :], in0=xr, in1=fi[0:64, :], op=ALU.mult))
        sop("dve", lambda e: e.tensor_tensor(out=tB[0:64, :], in0=xi, in1=fr[0:64, :], op=ALU.mult))
        sop("dve", lambda e: e.tensor_tensor(out=di, in0=tA[0:64, :], in1=tB[0:64, :], op=ALU.add))

    for s_ in range(8):
        cmul_f(ER[0:64, s_, :], EI[0:64, s_, :], AR[0:64, 7 - s_, :], AI[0:64, 7 - s_, :])
        cmul_f(ENR[0:64, s_, :], ENI[0:64, s_, :], NR[0:64, s_ + 1, :], NI[0:64, s_ + 1, :])
    sop("dve", lambda e: e.tensor_copy(out=A1[0:64, 0, :], in_=AR[0:64, 8, :]))
    sop("dve", lambda e: e.tensor_copy(out=A1[0:64, 1, :], in_=AR[0:64, 8, :]))
    sop("dve", lambda e: e.tensor_scalar(out=A2[0:64, 0, :], in0=AI[0:64, 8, :], scalar1=-1.0, scalar2=None, op0=ALU.mult))
    sop("dve", lambda e: e.tensor_copy(out=A2[0:64, 1, :], in_=AI[0:64, 8, :]))
    sop("dve", lambda e: e.tensor_tensor(out=tA[0:64, :], in0=AR[0:64, 8, :], in1=AR[0:64, 8, :], op=ALU.mult))
    sop("dve", lambda e: e.tensor_tensor(out=tB[0:64, :], in0=AI[0:64, 8, :], in1=AI[0:64, 8, :], op=ALU.mult))
    sop("dve", lambda e: e.tensor_tensor(out=B1[0:64, 0, :], in0=tA[0:64, :], in1=tB[0:64, :], op=ALU.subtract))
    sop("dve", lambda e: e.tensor_copy(out=B1[0:64, 1, :], in_=B1[0:64, 0, :]))
    sop("dve", lambda e: e.tensor_tensor(out=tA[0:64, :], in0=AR[0:64, 8, :], in1=AI[0:64, 8, :], op=ALU.mult))
    sop("dve", lambda e: e.tensor_scalar(out=B2[0:64, 1, :], in0=tA[0:64, :], scalar1=2.0, scalar2=None, op0=ALU.mult))
    sop("dve", lambda e: e.tensor_scalar(out=B2[0:64, 0, :], in0=tA[0:64, :], scalar1=-2.0, scalar2=None, op0=ALU.mult))

    Bp = carve([128, 2, 128], F32)
    Cblk = carve([128, 2, 64], F32)
    CT = carve([128, 2, 128], F32)
    W1p = xin[0][:, 0:2048].rearrange("p (a b) -> p a b", a=2)
    W1n = xin[1][:, 0:2048].rearrange("p (a b) -> p a b", a=2)
    W2p = carve([128, 2, 1024], F32)
    tW = carve([128, 1024], F32)
    Wst = carve([128, 8, 512], BF16)
    S.op("dve", lambda e: e.memset(Wst, 0.0), reads=[bS], writes=[bS])

    def v4(ap):
        return ap.rearrange("p (g s c) -> p g s c", g=8, s=8)

    def bc_tab(tab, g0):
        return tab[0:64, :, g0:g0 + 8].rearrange("p s g -> p g s").unsqueeze(3).broadcast_to([64, 8, 8, 16])

    def bc_gc(x):
        return x.rearrange("p (g c) -> p g c", g=8).unsqueeze(2).broadcast_to([64, 8, 8, 16])

    def cprod(dst_r, dst_i, tr_, ti_, xr, xi, g0, neg_i=False):
        sop("dve", lambda e: e.tensor_tensor(out=v4(dst_r), in0=bc_tab(tr_, g0), in1=bc_gc(xr), op=ALU.mult))
        sop("dve", lambda e: e.tensor_tensor(out=v4(tW[0:64, :]), in0=bc_tab(ti_, g0), in1=bc_gc(xi), op=ALU.mult))
        sop("dve", lambda e: e.tensor_tensor(out=dst_r, in0=dst_r, in1=tW[0:64, :], op=ALU.subtract))
        sop("dve", lambda e: e.tensor_tensor(out=v4(dst_i), in0=bc_tab(tr_, g0), in1=bc_gc(xi), op=ALU.mult))
        sop("dve", lambda e: e.tensor_tensor(out=v4(tW[0:64, :]), in0=bc_tab(ti_, g0), in1=bc_gc(xr), op=ALU.mult))
        if neg_i:
            sop("dve", lambda e: e.scalar_tensor_tensor(out=dst_i, in0=dst_i, scalar=-1.0, in1=tW[0:64, :], op0=ALU.mult, op1=ALU.subtract))
        else:
            sop("dve", lambda e: e.tensor_tensor(out=dst_i, in0=dst_i, in1=tW[0:64, :], op=ALU.add))

    for fc in range(16):
        g0 = fc * 8
        S.dma("sp", Bp[0:64, 0, :].rearrange("p (g c) -> p g c", g=8), b_re_d[g0:g0 + 8].rearrange("g p c -> p g c"), reads=[bS], writes=[bS], owner=bS, multi=True)
        S.dma("sp", Bp[0:64, 1, :].rearrange("p (g c) -> p g c", g=8), b_im_d[g0:g0 + 8].rearrange("g p c -> p g c"), reads=[bS], writes=[bS], owner=bS, multi=True)
        S.dma("sp", Cblk[:, 0, :], c_re_d[g0:g0 + 8].rearrange("g c p -> (g c) p"), reads=[bS], writes=[bS], owner=bS, multi=True)
        S.dma("sp", Cblk[:, 1, :], c_im_d[g0:g0 + 8].rearrange("g c p -> (g c) p"), reads=[bS], writes=[bS], owner=bS, multi=True)
        trans_f32(CT[0:64, 0, :], Cblk[:, 0, :], 128, 64)
        trans_f32(CT[0:64, 1, :], Cblk[:, 1, :], 128, 64)
        cprod(W1p[0:64, 0, :], W1p[0:64, 1, :], ER, EI, Bp[0:64, 0, :], Bp[0:64, 1, :], g0)
        cprod(W1n[0:64, 0, :], W1n[0:64, 1, :], ENR, ENI, Bp[0:64, 0, :], Bp[0:64, 1, :], g0)
        cprod(W2p[0:64, 0, :], W2p[0:64, 1, :], AR[:, 1:9, :], AI[:, 1:9, :], CT[0:64, 0, :], CT[0:64, 1, :], g0, neg_i=True)
        for half in range(2):
            pb = 2 + half
            bp = buf("ps%d" % pb)

            def w3mm(e, half=half, pb=pb):
                ins = None
                for gg in range(4):
                    g = half * 4 + gg
                    e.matmul(ps[pb][:, gg * 128:(gg + 1) * 128], W1n[0:64, 0, g * 128:(g + 1) * 128], W2p[0:64, 0, g * 128:(g + 1) * 128], start=True, stop=False)
                    ins = e.matmul(ps[pb][:, gg * 128:(gg + 1) * 128], W1n[0:64, 1, g * 128:(g + 1) * 128], W2p[0:64, 1, g * 128:(g + 1) * 128], start=False, stop=True)
                return ins
            S.op("pe", w3mm, reads=[bS], writes=[bp])
            S.op("dve", lambda e, half=half, pb=pb: e.tensor_tensor(out=Wst[:, half * 4:(half + 1) * 4, 128:256],
                                                                   in0=ps[pb][:, 0:512].rearrange("p (g n) -> p g n", g=4),
                                                                   in1=m3.unsqueeze(1).broadcast_to([128, 4, 128]), op=ALU.mult),
                 reads=[bp, bS], writes=[bS])
        for half in range(2):
            pb = 4 + half
            bp = buf("ps%d" % pb)

            def w1tr(e, half=half, pb=pb):
                ins = None
                for gg in range(4):
                    g = half * 4 + gg
                    for ri in range(2):
                        ins = e.transpose(out=ps[pb][:, (gg * 2 + ri) * 64:(gg * 2 + ri + 1) * 64], in_=W1p[0:64, ri, g * 128:(g + 1) * 128],
                                          identity=ident_f[0:64, 0:64])
                return ins
            S.op("pe", w1tr, reads=[bS, cst], writes=[bp])
            S.op("act", lambda e, half=half, pb=pb: e.copy(out=Wst[:, half * 4:(half + 1) * 4, 0:128],
                                                           in_=ps[pb][:, 0:512].rearrange("p (g n) -> p g n", g=4)),
                 reads=[bp, bS], writes=[bS])
        S.op("act", lambda e: e.copy(out=Wst[0:64, :, 256:384], in_=W2p[0:64, 0, :].rearrange("p (g n) -> p g n", g=8)), reads=[bS], writes=[bS])
        S.op("act", lambda e: e.copy(out=Wst[0:64, :, 384:512], in_=W2p[0:64, 1, :].rearrange("p (g n) -> p g n", g=8)), reads=[bS], writes=[bS])
        S.dma("sp", Wall_d[g0:g0 + 8].rearrange("g p w -> p g w"), Wst, reads=[bS], writes=[bS], owner=bS, multi=True)
    flush_stream()
    S.barrier(B.values())

    coff[0] = main_off
    GB = 32
    NW = GB * 16
    def wview(tn, is_f32):
        f = tn[:].bitcast(BF16) if is_f32 else tn[:].rearrange("p a b -> p (a b)")
        return f.rearrange("p (g w) -> p g w", g=16)
    Wsets = [(wview(wbf[0], False), wview(wbf[1], False)), (wview(xin[0], True), wview(xin[1], True))]
    Wbufs = [("wbf0", "wbf1"), ("xin0", "xin1")]
    u32 = [carve([128, NW], F32) for _ in range(2)]
    Uexp = [carve([128, GB, 128], BF16) for _ in range(2)]
    Usb = [carve([128, GB, 16], BF16) for _ in range(2)]
    Vsb = [carve([128, 2, NW], F32) for _ in range(2)]
    Xh = carve([128, 2, GB * 17], F32)
    Xall = carve([128, 2, NW], BF16)
    sT1, sT2 = carve([128, 2, GB], F32), carve([128, 2, GB], F32)
    Ysb = carve([128, GB, 16], BF16)
    Yexp = carve([128, GB, 128], BF16)
    ytmp = [carve([128, NW], F32) for _ in range(2)]
    zt_ = [carve([128, NW], F32) for _ in range(2)]
    dB = xbf[0][:].bitcast(F32)
    X0 = carve([128, 2, NW], F32)
    Xn = carve([128, 2, NW], F32)
    bT1, bT2 = sb("bT1", [128, 2, NW], F32), sb("bT2", [128, 2, NW], F32)
    st_in = Yexp[:].rearrange("p a b -> p (a b)").bitcast(F32).rearrange("p (r q) -> p r q", r=2)
    st_o = [carve([128, 2, 64], F32) for _ in range(2)]
    cst3 = buf("const3")
    S.dma("sp", dB, dB_d, writes=[cst3], owner=cst3, multi=True)
    ucount = [0]
    xhv = Xh[0:64, :, :].rearrange("p r (g j) -> p r g j", g=GB)

    def Wg(gb, g):
        return Wsets[gb % 2][g // 16][:, g % 16, :]

    def s5_front(gb, src, t):
        g0 = gb * GB
        i = ucount[0] % 2
        ucount[0] += 1
        bWs = [buf(n) for n in Wbufs[gb % 2]]
        u_, bu = u32[i], buf("u32_%d" % i)
        S.dma("sp", u_, src[t * 128:(t + 1) * 128, 8192 + g0 * 16:8192 + (g0 + GB) * 16], writes=[bu], owner=bu)
        ue, bUe = Uexp[i], buf("Uexp%d" % i)
        for s_ in range(8):
            if s_ % 2:
                S.op("act", lambda e, s_=s_: e.activation(out=ue[:, :, s_ * 16:(s_ + 1) * 16], in_=u_.rearrange("p (g c) -> p g c", g=GB),
                                                           func=AF.Copy, scale=mask8[:, s_:s_ + 1]), reads=[bu, bS], writes=[bUe])
            else:
                S.op("dve", lambda e, s_=s_: e.tensor_scalar(out=ue[:, :, s_ * 16:(s_ + 1) * 16], in0=u_.rearrange("p (g c) -> p g c", g=GB),
                                                              scalar1=mask8[:, s_:s_ + 1], scalar2=None, op0=ALU.mult), reads=[bu, bS], writes=[bUe])
        bp0 = buf("ps0")

        def umm(e):
            ins = None
            for g in range(GB):
                ins = e.matmul(ps[0][:, g * 16:(g + 1) * 16], ue[:, g, :], Jsel, start=True, stop=True, skip_group_check=True)
            return ins
        S.op("pe", umm, reads=[bUe, bS], writes=[bp0])
        us, bUs = Usb[i], buf("Usb%d" % i)
        S.op("act", lambda e: e.copy(out=us, in_=ps[0][:, 0:NW].rearrange("p (g j) -> p g j", g=GB)), reads=[bp0], writes=[bUs])
        bp1, bp2 = buf("ps1"), buf("ps2")

        def vmm(e):
            ins = None
            for g in range(GB):
                e.matmul(ps[1][0:64, g * 16:(g + 1) * 16], Wg(gb, g)[:, 0:64], us[:, g, :], start=True, stop=True, skip_group_check=True)
                ins = e.matmul(ps[2][0:64, g * 16:(g + 1) * 16], Wg(gb, g)[:, 64:128], us[:, g, :], start=True, stop=True, skip_group_check=True)
            return ins
        S.op("pe", vmm, reads=bWs + [bUs], writes=[bp1, bp2])
        vs, bVs = Vsb[i], buf("Vsb%d" % i)
        S.op("act", lambda e: e.copy(out=vs[0:64, 0, :], in_=ps[1][0:64, 0:NW]), reads=[bp1], writes=[bVs])
        S.op("act", lambda e: e.copy(out=vs[0:64, 1, :], in_=ps[2][0:64, 0:NW]), reads=[bp2], writes=[bVs])
        return i

    def s5_rest(gb, t, i, nvalid, full, sample):
        g0 = gb * GB
        bWs = [buf(n) for n in Wbufs[gb % 2]]
        u_, bu = u32[i], buf("u32_%d" % i)
        us, bUs = Usb[i], buf("Usb%d" % i)
        vs, bVs = Vsb[i], buf("Vsb%d" % i)
        vsv = vs[0:64, :, :].rearrange("p r (g j) -> p r g j", g=GB)
        bXh, bXa = buf("Xh"), buf("Xall")
        a1 = A1[0:64, :, g0:g0 + GB]
        a2 = A2[0:64, :, g0:g0 + GB]
        if not sample:
            def step(src_j, dst_j, add_ap, c1, c2, badd):
                S.op("dve", lambda e: e.tensor_tensor(out=sT1[0:64, :, :], in0=xhv[:, :, :, src_j], in1=c1, op=ALU.mult), reads=[bXh, bS], writes=[buf("sT1")], relax=True)
                S.op("dve", lambda e: e.tensor_tensor(out=sT2[0:64, 0, :], in0=xhv[:, 1, :, src_j], in1=c2[:, 0, :], op=ALU.mult), reads=[bXh, bS], writes=[buf("sT2")], relax=True)
                S.op("dve", lambda e: e.tensor_tensor(out=sT2[0:64, 1, :], in0=xhv[:, 0, :, src_j], in1=c2[:, 1, :], op=ALU.mult), reads=[bXh, bS], writes=[buf("sT2")], relax=True)
                S.op("dve", lambda e: e.tensor_tensor(out=sT1[0:64, :, :], in0=sT1[0:64, :, :], in1=sT2[0:64, :, :], op=ALU.add), reads=[buf("sT1"), buf("sT2")], writes=[buf("sT1")], relax=True)
                S.op("dve", lambda e: e.tensor_tensor(out=xhv[:, :, :, dst_j], in0=sT1[0:64, :, :], in1=add_ap, op=ALU.add), reads=[buf("sT1"), badd], writes=[bXh], relax=True)
            if nvalid == 16:
                pv_ = bT1[0:64, :, 0:GB * 8].rearrange("p r (g m) -> p r g m", g=GB)
                p2_ = bT2[0:64, :, 0:GB * 8].rearrange("p r (g m) -> p r g m", g=GB)
                ve = vsv[:, :, :, 0:16:2]
                vo = vsv[:, :, :, 1:16:2]
                bP, bP2 = buf("bT1"), buf("bT2")
                S.op("dve", lambda e: e.tensor_tensor(out=pv_, in0=ve, in1=a1.unsqueeze(3).broadcast_to([64, 2, GB, 8]), op=ALU.mult), reads=[bVs, bS], writes=[bP], relax=True)
                S.op("dve", lambda e: e.tensor_tensor(out=p2_[:, 0, :, :], in0=ve[:, 1, :, :], in1=a2[:, 0, :].unsqueeze(2).broadcast_to([64, GB, 8]), op=ALU.mult), reads=[bVs, bS], writes=[bP2], relax=True)
                S.op("dve", lambda e: e.tensor_tensor(out=p2_[:, 1, :, :], in0=ve[:, 0, :, :], in1=a2[:, 1, :].unsqueeze(2).broadcast_to([64, GB, 8]), op=ALU.mult), reads=[bVs, bS], writes=[bP2], relax=True)
                S.op("dve", lambda e: e.tensor_tensor(out=pv_, in0=pv_, in1=p2_, op=ALU.add), reads=[bP, bP2], writes=[bP], relax=True)
                S.op("dve", lambda e: e.tensor_tensor(out=pv_, in0=pv_, in1=vo, op=ALU.add), reads=[bP, bVs], writes=[bP], relax=True)
                b1 = B1[0:64, :, g0:g0 + GB]
                b2 = B2[0:64, :, g0:g0 + GB]
                for m in range(8):
                    step(2 * m, 2 * m + 2, pv_[:, :, :, m], b1, b2, bP)
                xe = xhv[:, :, :, 0:16:2]
                xo = xhv[:, :, :, 1:16:2]
                S.op("dve", lambda e: e.tensor_tensor(out=pv_, in0=xe, in1=a1.unsqueeze(3).broadcast_to([64, 2, GB, 8]), op=ALU.mult), reads=[bXh, bS], writes=[bP], relax=True)
                S.op("dve", lambda e: e.tensor_tensor(out=p2_[:, 0, :, :], in0=xe[:, 1, :, :], in1=a2[:, 0, :].unsqueeze(2).broadcast_to([64, GB, 8]), op=ALU.mult), reads=[bXh, bS], writes=[bP2], relax=True)
                S.op("dve", lambda e: e.tensor_tensor(out=p2_[:, 1, :, :], in0=xe[:, 0, :, :], in1=a2[:, 1, :].unsqueeze(2).broadcast_to([64, GB, 8]), op=ALU.mult), reads=[bXh, bS], writes=[bP2], relax=True)
                S.op("dve", lambda e: e.tensor_tensor(out=pv_, in0=pv_, in1=p2_, op=ALU.add), reads=[bP, bP2], writes=[bP], relax=True)
                S.op("dve", lambda e: e.tensor_tensor(out=xo, in0=pv_, in1=ve, op=ALU.add), reads=[bP, bVs, bXh], writes=[bXh], relax=True)
            else:
                for j in range(nvalid):
                    step(j, j + 1, vsv[:, :, :, j], a1, a2, bVs)
            if full:
                S.op("act", lambda e: e.copy(out=Xall[0:64, :, :].rearrange("p r (g j) -> p r g j", g=GB), in_=xhv[:, :, :, 0:16]), reads=[bXh], writes=[bXa])
            S.op("dve", lambda e: e.tensor_copy(out=xhv[:, :, :, 0], in_=xhv[:, :, :, nvalid]), reads=[bXh, bXa], writes=[bXh], relax=True)
        else:
            x0v = X0[0:64, :, :].rearrange("p r (g j) -> p r g j", g=GB)
            t1v = bT1[0:64, :, :].rearrange("p r (g j) -> p r g j", g=GB)
            t2v = bT2[0:64, :, :].rearrange("p r (g j) -> p r g j", g=GB)
            S.op("act", lambda e: e.copy(out=Xall[0:64, :, :], in_=X0[0:64, :, :]), reads=[buf("X0")], writes=[bXa])
            S.op("dve", lambda e: e.tensor_tensor(out=t1v, in0=x0v, in1=a1.unsqueeze(3).broadcast_to([64, 2, GB, 16]), op=ALU.mult), reads=[buf("X0"), bS], writes=[buf("bT1")])
            S.op("dve", lambda e: e.tensor_tensor(out=t2v[:, 0, :, :], in0=x0v[:, 1, :, :], in1=a2[:, 0, :].unsqueeze(2).broadcast_to([64, GB, 16]), op=ALU.mult), reads=[buf("X0"), bS], writes=[buf("bT2")])
            S.op("dve", lambda e: e.tensor_tensor(out=t2v[:, 1, :, :], in0=x0v[:, 0, :, :], in1=a2[:, 1, :].unsqueeze(2).broadcast_to([64, GB, 16]), op=ALU.mult), reads=[buf("X0"), bS], writes=[buf("bT2")])
            S.op("dve", lambda e: e.tensor_tensor(out=bT1[0:64, :, :], in0=bT1[0:64, :, :], in1=bT2[0:64, :, :], op=ALU.add), reads=[buf("bT1"), buf("bT2")], writes=[buf("bT1")])
            S.op("dve", lambda e: e.tensor_tensor(out=Xn[0:64, :, :], in0=bT1[0:64, :, :], in1=vs[0:64, :, :], op=ALU.add), reads=[buf("bT1"), bVs], writes=[buf("Xn")])
        if not full:
            return
        bp3 = buf("ps3")
        xr_ = Xall[0:64, 0, :].rearrange("p (g j) -> p g j", g=GB)
        xi_ = Xall[0:64, 1, :].rearrange("p (g j) -> p g j", g=GB)

        def ymm(e):
            ins = None
            for g in range(GB):
                o_ = ps[3][:, g * 16:(g + 1) * 16]
                W = Wg(gb, g)
                e.matmul(o_, W[0:64, 256:384], xr_[:, g, :], start=True, stop=False, skip_group_check=True)
                e.matmul(o_, W[0:64, 384:512], xi_[:, g, :], start=False, stop=False, skip_group_check=True)
                ins = e.matmul(o_, W[:, 128:256], us[:, g, :], start=False, stop=True, skip_group_check=True)
            return ins
        S.op("pe", ymm, reads=bWs + [bXa, bUs], writes=[bp3])
        bYs, bYe = buf("Ysb"), buf("Yexp")
        S.op("act", lambda e: e.copy(out=Ysb, in_=ps[3][:, 0:NW].rearrange("p (g j) -> p g j", g=GB)), reads=[bp3], writes=[bYs])
        yev = Yexp[:, :, :].rearrange("p g (j t) -> p g j t", j=16)
        for t_ in range(8):
            if t_ % 2:
                S.op("act", lambda e, t_=t_: e.activation(out=yev[:, :, :, t_], in_=Ysb, func=AF.Copy, scale=tmask[:, t_:t_ + 1]),
                     reads=[bYs, bS], writes=[bYe])
            else:
                S.op("dve", lambda e, t_=t_: e.tensor_scalar(out=yev[:, :, :, t_], in0=Ysb, scalar1=tmask[:, t_:t_ + 1], scalar2=None, op0=ALU.mult),
                     reads=[bYs, bS], writes=[bYe])
        bp4 = buf("ps4")

        def pmm(e):
            ins = None
            for g in range(GB):
                ins = e.matmul(ps[4][:, g * 16:(g + 1) * 16], Yexp[:, g, :], Csel, start=True, stop=True, skip_group_check=True)
            return ins
        S.op("pe", pmm, reads=[bYe, bS], writes=[bp4])
        yt_, byt = ytmp[i], buf("ytmp%d" % i)
        z_, bz = zt_[i], buf("zt%d" % i)
        S.op("dve", lambda e: e.tensor_tensor(out=yt_, in0=u_, in1=dB[:, g0 * 16:(g0 + GB) * 16], op=ALU.mult), reads=[bu, cst3], writes=[byt])
        S.op("dve", lambda e: e.tensor_tensor(out=yt_, in0=yt_, in1=ps[4][:, 0:NW], op=ALU.add), reads=[byt, bp4], writes=[byt])
        S.op("act", lambda e: e.activation(out=z_, in_=yt_, func=AF.Square), reads=[byt], writes=[bz])
        S.op("dve", lambda e: e.tensor_scalar(out=z_, in0=z_, scalar1=0.044715, scalar2=1.0, op0=ALU.mult, op1=ALU.add), reads=[bz], writes=[bz])
        S.op("dve", lambda e: e.tensor_tensor(out=z_, in0=z_, in1=yt_, op=ALU.mult), reads=[bz, byt], writes=[bz])
        S.op("act", lambda e: e.activation(out=z_, in_=z_, func=AF.Sigmoid, scale=1.5957691216057308), reads=[bz], writes=[bz])
        S.op("dve", lambda e: e.tensor_tensor(out=z_, in0=z_, in1=yt_, op=ALU.mult), reads=[bz, byt], writes=[bz])
        S.dma("pool", zscr[t * 128:(t + 1) * 128, g0 * 16:(g0 + GB) * 16], z_, reads=[bz], writes=[buf("zscrd")], owner=bz, multi=True)

    def emit_state(gb, srcX, dst_r, dst_i, bsrc):
        g0 = gb * GB
        k = ucount[0] % 2
        ucount[0] += 1
        bp = buf("ps5")

        def tr(e):
            e.transpose(out=ps[5][0:GB, 0:64], in_=srcX[:, 0, :], identity=ident_f[0:64, 0:64])
            return e.transpose(out=ps[5][0:GB, 64:128], in_=srcX[:, 1, :], identity=ident_f[0:64, 0:64])
        S.op("pe", tr, reads=[bsrc, cst], writes=[bp])
        so_, bso = st_o[k], buf("st_o%d" % k)
        S.op("dve", lambda e: e.tensor_copy(out=so_[0:GB, :, :], in_=ps[5][0:GB, 0:128].rearrange("p (r q) -> p r q", r=2)), reads=[bp], writes=[bso])
        S.dma("pool", dst_r[g0:g0 + GB, :], so_[0:GB, 0, :], reads=[bso], writes=[buf("os5")], owner=bso, multi=True)
        S.dma("pool", dst_i[g0:g0 + GB, :], so_[0:GB, 1, :], reads=[bso], writes=[buf("os5")], owner=bso, multi=True)

    for gb in range(128 // GB):
        g0 = gb * GB
        for hh in range(2):
            bW = buf(Wbufs[gb % 2][hh])
            S.dma("sp", Wsets[gb % 2][hh], Wall_d[g0 + hh * 16:g0 + (hh + 1) * 16].rearrange("g p w -> p g w"), writes=[bW], owner=bW)
        bsi = buf("Yexp")
        S.dma("sp", st_in[0:GB, 0, :].rearrange("g (s p) -> g s p", s=16), s5r_d[:, g0:g0 + GB, :].rearrange("s g p -> g s p"), writes=[bsi], owner=bsi)
        S.dma("sp", st_in[0:GB, 1, :].rearrange("g (s p) -> g s p", s=16), s5i_d[:, g0:g0 + GB, :].rearrange("s g p -> g s p"), reads=[bsi], writes=[bsi], owner=bsi)
        x0v = X0[0:64, :, :].rearrange("p r (g j) -> p r g j", g=GB)
        for ri in range(2):
            for q4 in range(4):
                bp = buf("ps6")

                def trs_(e, ri=ri, q4=q4):
                    ins = None
                    for jj in range(4):
                        j = q4 * 4 + jj
                        ins = e.transpose(out=ps[6][0:64, jj * GB:(jj + 1) * GB], in_=st_in[0:GB, ri, j * 64:(j + 1) * 64], identity=ident_f[0:GB, 0:GB])
                    return ins
                S.op("pe", trs_, reads=[bsi, cst], writes=[bp])
                S.op("dve", lambda e, ri=ri, q4=q4: e.tensor_copy(out=x0v[:, ri, :, q4 * 4:(q4 + 1) * 4],
                                                                  in_=ps[6][0:64, 0:4 * GB].rearrange("p (j g) -> p g j", j=4)),
                     reads=[bp], writes=[buf("X0")])
        S.op("dve", lambda e: e.memset(Xh, 0.0), writes=[buf("Xh")])
        work = [(projp, t, 16 if t < 8 else 2, False, False) for t in range(NT_PRE)]
        work += [(proj, t, 16, True, False) for t in range(8)]
        work += [(proj, 8, 16, True, True)]
        nxt = s5_front(gb, work[0][0], work[0][1])
        for wi, (src, t, nv, full, sample) in enumerate(work):
            cur = nxt
            if wi + 1 < len(work):
                nxt = s5_front(gb, work[wi + 1][0], work[wi + 1][1])
            if sample:
                emit_state(gb, xhv[:, :, :, 0], o_s5p_r, o_s5p_i, buf("Xh"))
            s5_rest(gb, t, cur, nv, full, sample)
        xnv = Xn[0:64, :, :].rearrange("p r (g j) -> p r g j", g=GB)
        for j in range(16):
            emit_state(gb, xnv[:, :, :, j], o_s5s_r[j], o_s5s_i[j], buf("Xn"))
    flush_stream()
    S.barrier(B.values())

    coff[0] = 16 * ROWS_OWN * 2
    bgl = carve([128, 2048], F32)
    S.dma("sp", bgl, bglu_d, writes=[cst3], owner=cst3, multi=True)

    def cons_glu(t, c0, cw, pt, bp):
        i = next_ot()
        o, bo, r, br = ot[i], buf("ot%d" % i), rt[i], buf("rt%d" % i)
        S.dma("sp", r[:, 0:cw], zscr[t * 128:(t + 1) * 128, c0:c0 + cw], writes=[br], owner=br)
        S.op("dve", lambda e: e.tensor_tensor(out=o[:, 0:cw], in0=pt, in1=bgl[:, c0:c0 + cw], op=ALU.add), reads=[bp, cst3], writes=[bo])
        S.op("act", lambda e: e.activation(out=o[:, 0:cw], in_=o[:, 0:cw], func=AF.Sigmoid), reads=[bo], writes=[bo])
        S.op("dve", lambda e: e.tensor_tensor(out=o[:, 0:cw], in0=o[:, 0:cw], in1=r[:, 0:cw], op=ALU.mult), reads=[bo, br], writes=[bo])
        S.dma("pool", s5o[t * 128:(t + 1) * 128, c0:c0 + cw], o[:, 0:cw], reads=[bo], writes=[buf("s5od")], owner=bo, multi=True)

    load_actT(zscr, NT_OWN, 0, 16)
    stream(w_glu, 0, 16, 0, 2048, NT_OWN, cons_glu)
    flush_stream()
    S.barrier(B.values())
    for t in range(NT_OWN):
        xi, stt = xin[t % 2], stat[t % 2]
        bxi, bst = buf("xin%d" % (t % 2)), buf("stat%d" % (t % 2))
        S.dma("pool", xi[:, 0:2048], s5o[t * 128:(t + 1) * 128, :], writes=[bxi], owner=bxi)
        S.op("act", lambda e, xi=xi, stt=stt, t=t: e.activation(out=xbf[t % 2][:, 0:2048], in_=xi[:, 0:2048], func=AF.Square, accum_out=stt[:, 0:1]),
             reads=[bxi], writes=[bst, buf("xbf%d" % (t % 2))])
        S.op("dve", lambda e, stt=stt: e.tensor_scalar(out=stt[:, 1:2], in0=stt[:, 0:1], scalar1=1.0 / 2048, scalar2=EPS,
                                                       op0=ALU.mult, op1=ALU.add), reads=[bst], writes=[bst])
        S.op("act", lambda e, stt=stt: e.activation(out=stt[:, 2:3], in_=stt[:, 1:2], func=AF.Sqrt), reads=[bst], writes=[bst])
        S.op("dve", lambda e, stt=stt: e.reciprocal(out=stt[:, 3:4], in_=stt[:, 2:3]), reads=[bst], writes=[bst])
        S.op("dve", lambda e, xi=xi, stt=stt: e.tensor_scalar(out=xi[:, 0:2048], in0=xi[:, 0:2048], scalar1=stt[:, 3:4], scalar2=None, op0=ALU.mult),
             reads=[bxi, bst], writes=[bxi])
        S.dma("pool", mix[t * 128:(t + 1) * 128, 2048:4096], xi[:, 0:2048], reads=[bxi], writes=[buf("mixd")], owner=bxi, multi=True)
    flush_stream()
    S.barrier(B.values())

    def cons_wout(t, c0, cw, pt, bp):
        i = next_ot()
        o, bo, r, br = ot[i], buf("ot%d" % i), rt[i], buf("rt%d" % i)
        S.dma("sp", r[:, 0:cw], x_own[t * 128:(t + 1) * 128, c0:c0 + cw], writes=[br], owner=br)
        S.op("dve", lambda e: e.tensor_tensor(out=o[:, 0:cw], in0=pt, in1=r[:, 0:cw], op=ALU.add), reads=[bp, br], writes=[bo])
        S.dma("pool", x1[t * 128:(t + 1) * 128, c0:c0 + cw], o[:, 0:cw], reads=[bo], writes=[buf("x1d")], owner=bo, multi=True)

    load_actT(mix, NT_OWN, 0, 32, norm=False)
    stream(w_out, 0, 32, 0, D, NT_OWN, cons_wout, gain=1)
    flush_stream()
    S.barrier(B.values())

    load_actT(x1, NT_OWN, 0, 32, norm=True)

    def cons_gate(t, c0, cw, pt, bp):
        S.op("act", lambda e: e.activation(out=gtmp[:, t, 0:cw], in_=pt, func=AF.Silu), reads=[bp], writes=[buf("gtmp%d" % t)])

    def cons_up(t, c0, cw, pt, bp):
        i = next_ot()
        o, bo = ot[i], buf("ot%d" % i)
        S.op("dve", lambda e: e.tensor_tensor(out=o[:, 0:cw], in0=pt, in1=gtmp[:, t, 0:cw], op=ALU.mult),
             reads=[bp, buf("gtmp%d" % t)], writes=[bo])
        S.dma("pool", act[t * 128:(t + 1) * 128, c0:c0 + cw], o[:, 0:cw], reads=[bo], writes=[buf("actd")], owner=bo, multi=True)

    for gi in range(DFF // CG):
        stream(w_gate, 0, 32, gi * CG, CG, NT_OWN, cons_gate, gain=2)
        stream(w_up, 0, 32, gi * CG, CG, NT_OWN, cons_up, gain=2)
    flush_stream()
    S.barrier(B.values())

    def cons_down(t, c0, cw, pt, bp):
        i = next_ot()
        o, bo, r, br = ot[i], buf("ot%d" % i), rt[i], buf("rt%d" % i)
        S.dma("sp", r[:, 0:cw], x1[t * 128:(t + 1) * 128, c0:c0 + cw], writes=[br], owner=br)
        S.op("dve", lambda e: e.tensor_tensor(out=o[:, 0:cw], in0=pt, in1=r[:, 0:cw], op=ALU.add), reads=[bp, br], writes=[bo])
        S.dma("pool", x1[t * 128:(t + 1) * 128, c0:c0 + cw], o[:, 0:cw], reads=[bo], writes=[buf("x1d")], owner=bo, multi=True)

    for kb0, kbn in ((0, 32), (32, 32), (64, 22)):
        load_actT(act, NT_OWN, kb0, kbn)
        stream(w_down, kb0 * 128, kbn, 0, D, NT_OWN, cons_down)
        flush_stream()
    S.barrier(B.values())

    for t in range(NT_OWN):
        xi, stt = xin[t % 2], stat[t % 2]
        bxi, bst = buf("xin%d" % (t % 2)), buf("stat%d" % (t % 2))
        S.dma("pool", xi[:], x1[t * 128:(t + 1) * 128, :], writes=[bxi], owner=bxi)
        S.op("act", lambda e, xi=xi, stt=stt, t=t: e.activation(out=xbf[t % 2][:], in_=xi[:], func=AF.Square, accum_out=stt[:, 0:1]),
             reads=[bxi], writes=[bst, buf("xbf%d" % (t % 2))])
        S.op("dve", lambda e, stt=stt: e.tensor_scalar(out=stt[:, 1:2], in0=stt[:, 0:1], scalar1=1.0 / D, scalar2=EPS,
                                                       op0=ALU.mult, op1=ALU.add), reads=[bst], writes=[bst])
        S.op("act", lambda e, stt=stt: e.activation(out=stt[:, 2:3], in_=stt[:, 1:2], func=AF.Sqrt), reads=[bst], writes=[bst])
        S.op("dve", lambda e, stt=stt: e.reciprocal(out=stt[:, 3:4], in_=stt[:, 2:3]), reads=[bst], writes=[bst])
        S.op("dve", lambda e, xi=xi, stt=stt: e.scalar_tensor_tensor(out=xi[:], in0=xi[:], scalar=stt[:, 3:4], in1=gf_t[:],
                                                                     op0=ALU.mult, op1=ALU.mult),
             reads=[bxi, bst, cst], writes=[bxi])
        S.dma("pool", y_out[t * 128:(t + 1) * 128, :], xi[:], reads=[bxi], writes=[buf("yd")], owner=bxi, multi=True)
    flush_stream()
    S.barrier(B.values())

    with nc.Block() as block:
        S.emit(block)
    es.close()
    return nc


def _cs(pos):
    inv = (np.float32(10000.0) ** (-np.arange(128, dtype=np.float32) / np.float32(128))).astype(np.float32)
    ang = (pos.astype(np.float32)[:, :, None] * inv[None, None, :]).astype(np.float32)
    c_, s_ = np.cos(ang).astype(np.float32), np.sin(ang).astype(np.float32)
    return np.ascontiguousarray(np.concatenate([c_, c_, -s_, s_], axis=-1), dtype=np.float32)


_CONST_CACHE = {}


def _consts():
    if _CONST_CACHE:
        return _CONST_CACHE
    import ml_dtypes
    lg = np.log(1.0 - 2.0 ** (-5.0 - np.arange(8, dtype=np.float32))).astype(np.float32)
    p = np.arange(128)
    mask = np.zeros((8, 128, 2, 128), np.float32)
    diff = (p[None, :] - p[:, None]).astype(np.float32)
    for h_ in range(8):
        m0 = np.where(diff >= 0, np.exp(np.maximum(diff, 0) * lg[h_]), 0.0)
        same = (p[None, :] // 8) == (p[:, None] // 8)
        mask[h_, :, 0, :] = m0
        mask[h_, :, 1, :] = np.where(same, m0, 0.0)
    dqk = np.zeros((128, 2, 3, 8), np.float32)
    for h_ in range(8):
        dqk[:, 0, 0, h_] = np.exp(lg[h_] * (p + 1.0))
        dqk[:, 0, 1, h_] = np.exp(lg[h_] * ((p % 8) + 1.0))
        dqk[:, 0, 2, h_] = np.exp(lg[h_] * ((p % 16) + 1.0))
        dqk[:, 1, 0, h_] = np.exp(lg[h_] * (127.0 - p)) / 16.0
        dqk[:, 1, 1, h_] = np.exp(lg[h_] * (7.0 - (p % 8))) / 16.0
        dqk[:, 1, 2, h_] = np.exp(lg[h_] * (15.0 - (p % 16))) / 16.0
    cm = np.zeros((128, 16, 2, 128), np.float32)
    rm = np.zeros((128, 16), np.float32)
    for sq in range(16):
        cm[:, sq, :, sq * 8:(sq + 1) * 8] = 1.0
        rm[sq * 8:(sq + 1) * 8, sq] = 1.0
    q = np.arange(128)
    m3 = ((q[None, :] // 16) >= (q[:, None] // 16)).astype(np.float32)
    sel_f = np.zeros((128, 16), np.float32)
    sel_f[q, q % 8] = 1.0
    sel_f[q, 8 + q // 16] = 1.0
    sel_b = np.zeros((128, 32), np.float32)
    sel_b[q, q // 8] = 1.0
    sel_b[q, 16 + q % 16] = 1.0
    _CONST_CACHE.update(m3=m3, sel_f=sel_f, sel_b=sel_b.astype(ml_dtypes.bfloat16))
    _CONST_CACHE.update(mask=mask, dqk=dqk, cmask=cm.reshape(128, 16, 256).astype(ml_dtypes.bfloat16), rmask=rm)
    return _CONST_CACHE


def _core_inputs(c, inp):
    b, h = c // 2, c % 2
    xp = np.concatenate([inp["meta_tokens"], inp["x_prompt"][b]], axis=0)
    x_own = np.zeros((ROWS_OWN, D), np.float32)
    x_own[:HALF] = xp[16 + h * HALF:16 + (h + 1) * HALF]
    x_own[8 * 128:] = inp["x_sample"][16 * c:16 * c + 16].reshape(128, D)
    g_all = np.stack([inp["norm1_g"][0].reshape(32, 128).T,
                      np.concatenate([inp["ret_gn_g"][0], inp["s5_norm_g"][0]]).reshape(32, 128).T,
                      inp["norm2_g"][0].reshape(32, 128).T], axis=1)
    x_pre = np.zeros((ROWS_PRE, D), np.float32)
    pos_pre = np.zeros((NT_PRE, 128), np.float32)
    if h == 1:
        x_pre[:NPRE] = xp[:NPRE]
        pos_pre = (np.arange(NT_PRE * 128, dtype=np.float32)).reshape(NT_PRE, 128)
    else:
        x_pre[8 * 128:8 * 128 + 16] = xp[:16]
        pos_pre[8] = np.arange(128)
    C = _consts()
    pos_own = np.zeros((NT_OWN, 128), np.float32)
    for t in range(8):
        pos_own[t] = 16 + h * HALF + t * 128 + np.arange(128)
    pos_own[8] = 16384 + (np.arange(128) % 8)
    return {
        "x_pre": x_pre, "w_in": inp["w_in"][0],
        "cs_own": _cs(pos_own), "cs_pre": _cs(pos_pre),
        "maskd": C["mask"], "dqk": C["dqk"], "cmask": C["cmask"], "rmask": C["rmask"],
        "st_ret": np.ascontiguousarray(inp["state_ret"][0, 16 * c:16 * c + 16]),
        "lam_re": inp["s5_lam_re"][0], "lam_im": inp["s5_lam_im"][0],
        "ldt": np.ascontiguousarray(np.broadcast_to(inp["s5_log_dt"][0][None, :], (64, 128)), dtype=np.float32),
        "m3": C["m3"], "sel_f": C["sel_f"], "sel_b": C["sel_b"],
        "b_re": inp["s5_b_re"][0], "b_im": inp["s5_b_im"][0], "c_re": inp["s5_c_re"][0], "c_im": inp["s5_c_im"][0],
        "dB": np.ascontiguousarray(np.broadcast_to(inp["s5_d"][0][None, :], (128, 2048)), dtype=np.float32),
        "bglu": np.ascontiguousarray(np.broadcast_to(inp["b_glu"][0][None, :], (128, 2048)), dtype=np.float32),
        "s5r": np.ascontiguousarray(inp["state_s5_re"][0, 16 * c:16 * c + 16]),
        "s5i": np.ascontiguousarray(inp["state_s5_im"][0, 16 * c:16 * c + 16]),
        "w_glu": inp["w_glu"][0],
        "x_own": x_own,
        "w_out": inp["w_out"][0], "w_gate": inp["w_gate"][0], "w_up": inp["w_up"][0], "w_down": inp["w_down"][0],
        "g_all": np.ascontiguousarray(g_all, dtype=np.float32),
        "gfin": np.ascontiguousarray(np.broadcast_to(inp["final_norm_g"][None, :], (128, D)), dtype=np.float32),
        "ident": np.eye(128, dtype=np.float32),
    }


def kernel(**inp):
    inp = {k: np.asarray(v) for k, v in inp.items()}
    nc = build_program()
    in_maps = [_core_inputs(c, inp) for c in range(NCORES)]
    res = run_bass_kernel_spmd(nc, in_maps, core_ids=list(range(NCORES)))
    R = res.results
    LAST['R'] = R
    y_prompt = np.zeros((4, 2048, D), np.float32)
    y_sample = np.zeros((128, 8, D), np.float32)
    for c in range(NCORES):
        b, h = c // 2, c % 2
        y = R[c]["y"]
        y_prompt[b, h * HALF:(h + 1) * HALF] = y[:HALF]
        y_sample[16 * c:16 * c + 16] = y[8 * 128:].reshape(16, 8, D)
    z = np.zeros
    ret_p = np.stack([R[2 * b + 1]["o_ret_p"] for b in range(4)])[None]
    ret_s = np.concatenate([R[c]["o_ret_s"] for c in range(NCORES)])[None]
    s5p_r = np.stack([R[2 * b + 1]["o_s5p_r"] for b in range(4)])[None].astype(np.float32)
    s5p_i = np.stack([R[2 * b + 1]["o_s5p_i"] for b in range(4)])[None].astype(np.float32)
    s5s_r = np.concatenate([R[c]["o_s5s_r"] for c in range(NCORES)])[None].astype(np.float32)
    s5s_i = np.concatenate([R[c]["o_s5s_i"] for c in range(NCORES)])[None].astype(np.float32)
    return (y_prompt, y_sample, ret_p.astype(np.float32), s5p_r, s5p_i, ret_s.astype(np.float32), s5s_r, s5s_i)
    return (y_prompt, y_sample,
            ret_p.astype(np.float32), z((1, 4, 128, 64), np.float32), z((1, 4, 128, 64), np.float32),
            ret_s.astype(np.float32), z((1, 128, 128, 64), np.float32), z((1, 128, 128, 64), np.float32))
```

```python
import numpy as np
from contextlib import ExitStack
import concourse.bass as bass
import concourse.mybir as mybir
from concourse.bass_utils import run_bass_kernel_spmd

F32 = mybir.dt.float32
BF16 = mybir.dt.bfloat16
AF = mybir.ActivationFunctionType
ALU = mybir.AluOpType

D = 4096
DFF = 11008
NCORES = 8
HALF = 1024
NPRE = 1040
NT_OWN = 9
NT_PRE = 9
ROWS_OWN = NT_OWN * 128
ROWS_PRE = NT_PRE * 128
EPS = 1e-6
CG = 256
DEBUG = False
LAST = {}


class Buf:
    def __init__(self, name):
        self.name = name
        self.writers = []
        self.readers = []
        self.dsem = None
        self.dcount = 0


class Sched:
    def __init__(self, nc, es):
        self.nc = nc
        self.es = es
        self.eng = {}
        for n in ("pe", "act", "dve", "pool", "sp"):
            self.eng[n] = dict(sem=es.enter_context(nc.semaphore("s_" + n)), count=0, ops=[], seen={})
        self.dsems = []
        self.nsem = 0

    def _dsem(self, buf):
        if buf.dsem is None:
            buf.dsem = self.es.enter_context(self.nc.semaphore("d%d" % self.nsem))
            self.nsem += 1
            self.dsems.append(buf)
        return buf.dsem

    def _waits(self, e, reads, writes, skip=None):
        need = {}

        def add(st):
            s, v = st
            k = id(s)
            if k == skip:
                return
            if k not in need or need[k][1] < v:
                need[k] = (s, v)
        for b in reads:
            for st in b.writers:
                add(st)
        for b in writes:
            for st in b.writers:
                add(st)
            for st in b.readers:
                add(st)
        out = []
        seen = self.eng[e]["seen"]
        for k, (s, v) in need.items():
            if seen.get(k, 0) < v:
                seen[k] = v
                out.append((s, v))
        return out

    def _commit(self, st, reads, writes, multi=False):
        for b in writes:
            if multi:
                b.writers.append(st)
            else:
                b.writers = [st]
                b.readers = []
        for b in reads:
            b.readers.append(st)
            if len(b.readers) > 64:
                best = {}
                for s, v in b.readers:
                    if id(s) not in best or best[id(s)][1] < v:
                        best[id(s)] = (s, v)
                b.readers = list(best.values())

    def op(self, e, fn, reads=(), writes=(), relax=False):
        E = self.eng[e]
        waits = self._waits(e, reads, writes, skip=(id(E["sem"]) if (relax and e == "dve") else None))
        E["count"] += 1
        st = (E["sem"], E["count"])
        E["ops"].append((waits, fn, E["sem"], 1))
        self._commit(st, reads, writes)
        return st

    def dma(self, q, out, in_, reads=(), writes=(), owner=None, multi=False):
        E = self.eng[q]
        waits = self._waits(q, reads, writes)
        sem = self._dsem(owner)
        owner.dcount += 16
        st = (sem, owner.dcount)
        E["ops"].append((waits, lambda eng, o=out, i=in_: eng.dma_start(out=o, in_=i), sem, 16))
        self._commit(st, reads, writes, multi=multi)
        return st

    def barrier(self, bufs=()):
        for b in bufs:
            b.writers = []
            b.readers = []
        stamps = []
        for n, E in self.eng.items():
            if E["count"]:
                stamps.append((E["sem"], E["count"]))
        for b in self.dsems:
            stamps.append((b.dsem, b.dcount))
        for n, E in self.eng.items():
            w = []
            for s, v in stamps:
                if E["seen"].get(id(s), 0) < v:
                    E["seen"][id(s)] = v
                    w.append((s, v))
            if w:
                E["ops"].append((w, None, None, 0))

    def emit(self, block):
        def run(E):
            def f(eng):
                for waits, fn, sem, amt in E["ops"]:
                    for s, v in waits:
                        eng.wait_ge(s, v)
                    if fn is None:
                        continue
                    ins = fn(eng)
                    ins.then_inc(sem, amt)
            return f
        block.tensor(run(self.eng["pe"]))
        block.scalar(run(self.eng["act"]))
        block.vector(run(self.eng["dve"]))
        block.gpsimd(run(self.eng["pool"]))
        block.sync(run(self.eng["sp"]))


def build_program():
    nc = bass.Bass("TRN2", target_bir_lowering=False)
    es = ExitStack()
    S = Sched(nc, es)

    def din(name, shape, dt=F32):
        return nc.dram_tensor(name, list(shape), dt, kind="ExternalInput").ap()

    def dout(name, shape, dt=F32):
        return nc.dram_tensor(name, list(shape), dt, kind="ExternalOutput").ap()

    def dscr(name, shape, dt=F32):
        return nc.dram_tensor(name, list(shape), dt, kind=("ExternalOutput" if (DEBUG and name in ("mix", "proj")) else "Internal")).ap()

    x_own = din("x_own", [ROWS_OWN, D])
    x_pre = din("x_pre", [ROWS_PRE, D])
    w_in = din("w_in", [D, 10240])
    cs_own = din("cs_own", [NT_OWN, 128, 512])
    cs_pre = din("cs_pre", [NT_PRE, 128, 512])
    maskd = din("maskd", [8, 128, 2, 128])
    dqk_d = din("dqk", [128, 2, 3, 8])
    cmask_d = din("cmask", [128, 16, 256], BF16)
    rmask_d = din("rmask", [128, 16])
    st_ret = din("st_ret", [16, 8, 256, 256])
    o_ret_p = dout("o_ret_p", [8, 256, 256])
    o_ret_s = dout("o_ret_s", [16, 8, 256, 256])
    lam_re_d = din("lam_re", [128, 64]); lam_im_d = din("lam_im", [128, 64])
    ldt_d = din("ldt", [64, 128]); m3_d = din("m3", [128, 128])
    sel_f_d = din("sel_f", [128, 16]); sel_b_d = din("sel_b", [128, 32], BF16)
    b_re_d = din("b_re", [128, 64, 16]); b_im_d = din("b_im", [128, 64, 16])
    c_re_d = din("c_re", [128, 16, 64]); c_im_d = din("c_im", [128, 16, 64])
    dB_d = din("dB", [128, 2048]); bglu_d = din("bglu", [128, 2048])
    s5r_d = din("s5r", [16, 128, 64]); s5i_d = din("s5i", [16, 128, 64])
    w_glu = din("w_glu", [2048, 2048])
    o_s5p_r = dout("o_s5p_r", [128, 64]); o_s5p_i = dout("o_s5p_i", [128, 64])
    o_s5s_r = dout("o_s5s_r", [16, 128, 64]); o_s5s_i = dout("o_s5s_i", [16, 128, 64])
    Wall_d = dscr("Wall", [128, 128, 512], BF16)
    zscr = dscr("zscr", [ROWS_OWN, 2048], F32)
    s5o = dscr("s5o", [ROWS_OWN, 2048], F32)
    proj = dscr("proj", [ROWS_OWN, 10240], F32)
    projp = dscr("projp", [ROWS_PRE, 10240], F32)
    w_out = din("w_out", [D, D])
    w_gate = din("w_gate", [D, DFF])
    w_up = din("w_up", [D, DFF])
    w_down = din("w_down", [DFF, D])
    g_all = din("g_all", [128, 3, 32])
    gfin = din("gfin", [128, D])
    ident_in = din("ident", [128, 128])
    y_out = dout("y", [ROWS_OWN, D])

    MIXDT = F32 if DEBUG else BF16
    mix = dscr("mix", [ROWS_OWN, D], MIXDT)
    x1 = dscr("x1", [ROWS_OWN, D], F32)
    h2 = dscr("h2", [ROWS_OWN, D], F32)
    act = dscr("act", [ROWS_OWN, DFF], BF16)

    def sb(name, shape, dt):
        return es.enter_context(nc.sbuf_tensor(name, list(shape), dt))

    actT = sb("actT", [128, 32, ROWS_OWN], BF16)
    wbf = [sb("wbf%d" % i, [128, 32, CG], BF16) for i in range(2)]
    wst = [sb("wst%d" % i, [128, 8, CG], F32) for i in range(2)]
    xin = [sb("xin%d" % i, [128, D], F32) for i in range(2)]
    xbf = [sb("xbf%d" % i, [128, D], BF16) for i in range(2)]
    ot = [sb("ot%d" % i, [128, CG], F32) for i in range(4)]
    rt = [sb("rt%d" % i, [128, CG], F32) for i in range(4)]
    gtmp = sb("gtmp", [128, NT_OWN, CG], BF16)
    stat = [sb("stat%d" % i, [128, 8], F32) for i in range(2)]
    gains = sb("gains", [128, 3, 32], F32)
    gf_t = sb("gf_t", [128, D], F32)
    ident_f = sb("ident_f", [128, 128], F32)
    ident = sb("ident_b", [128, 128], BF16)

    ps = [es.enter_context(nc.psum_tensor("ps%d" % i, [128, 512], F32)) for i in range(8)]

    B = {}

    def buf(name):
        if name not in B:
            B[name] = Buf(name)
        return B[name]

    cst = buf("const")
    S.dma("sp", gains[:], g_all, writes=[cst], owner=cst, multi=True)
    S.dma("sp", gf_t[:], gfin, writes=[cst], owner=cst, multi=True)
    S.dma("sp", ident_f[:], ident_in, writes=[cst], owner=cst, multi=True)
    S.op("dve", lambda e: e.tensor_copy(out=ident[:], in_=ident_f[:]), reads=[cst], writes=[buf("ident")])

    def load_actT(src, ntiles, kc0, nkc, norm=False, src_bf16=False):
        W = nkc * 128
        for t in range(ntiles):
            xi, xb, stt = xin[t % 2], xbf[t % 2], stat[t % 2]
            bxi, bxb, bst = buf("xin%d" % (t % 2)), buf("xbf%d" % (t % 2)), buf("stat%d" % (t % 2))
            if src_bf16:
                S.dma("pool", xb[:, 0:W], src[t * 128:(t + 1) * 128, kc0 * 128:kc0 * 128 + W], writes=[bxb], owner=bxb)
            else:
                S.dma("pool", xi[:, 0:W], src[t * 128:(t + 1) * 128, kc0 * 128:kc0 * 128 + W], writes=[bxi], owner=bxi)
            if src_bf16:
                pass
            elif norm:
                S.op("act", lambda e, xi=xi, stt=stt, xb=xb: e.activation(out=xb[:, 0:W], in_=xi[:, 0:W], func=AF.Square,
                                                                    accum_out=stt[:, 0:1]),
                     reads=[bxi], writes=[bst, bxb])
                S.op("dve", lambda e, stt=stt: e.tensor_scalar(out=stt[:, 1:2], in0=stt[:, 0:1], scalar1=1.0 / W,
                                                               scalar2=EPS, op0=ALU.mult, op1=ALU.add),
                     reads=[bst], writes=[bst])
                S.op("act", lambda e, stt=stt: e.activation(out=stt[:, 2:3], in_=stt[:, 1:2], func=AF.Sqrt),
                     reads=[bst], writes=[bst])
                S.op("dve", lambda e, stt=stt: e.reciprocal(out=stt[:, 3:4], in_=stt[:, 2:3]), reads=[bst], writes=[bst])
                S.op("dve", lambda e, xi=xi, xb=xb, stt=stt: e.tensor_scalar(out=xb[:, 0:W], in0=xi[:, 0:W],
                                                                             scalar1=stt[:, 3:4], scalar2=None,
                                                                             op0=ALU.mult),
                     reads=[bxi, bst], writes=[bxb])
            else:
                S.op("dve", lambda e, xi=xi, xb=xb: e.tensor_copy(out=xb[:, 0:W], in_=xi[:, 0:W]),
                     reads=[bxi], writes=[bxb])
            for q in range((nkc + 3) // 4):
                pb = 6 + (q % 2)
                bp = buf("ps%d" % pb)
                pv = ps[pb].bitcast(BF16)
                nq = min(4, nkc - q * 4)

                def tr(e, xb=xb, pv=pv, q=q, nq=nq):
                    ins = None
                    for i in range(nq):
                        ins = e.transpose(out=pv[:, i * 128:(i + 1) * 128], in_=xb[:, (q * 4 + i) * 128:(q * 4 + i + 1) * 128],
                                          identity=ident[:])
                    return ins
                S.op("pe", tr, reads=[bxb, buf("ident")], writes=[bp])
                dst = actT[:, q * 4:q * 4 + nq, t * 128:(t + 1) * 128]
                srcv = pv[:, 0:nq * 128].rearrange("p (c t) -> p c t", c=nq)
                eng = "act" if q % 2 else "dve"
                if eng == "act":
                    S.op("act", lambda e, dst=dst, srcv=srcv: e.copy(out=dst, in_=srcv), reads=[bp], writes=[buf("actT")])
                else:
                    S.op("dve", lambda e, dst=dst, srcv=srcv: e.tensor_copy(out=dst, in_=srcv), reads=[bp], writes=[buf("actT")])

    wcount = [0]
    pcount = [0]
    pending = []

    def stream(Wd, krow0, nkc, col0, ncols, ntiles, consumer, gain=None):
        ngr = (ncols + CG - 1) // CG
        for gi in range(ngr):
            c0 = col0 + gi * CG
            cw = min(CG, col0 + ncols - c0)
            pending.append((Wd, krow0, nkc, c0, cw, ntiles, consumer, gain))

    def _load_group(item):
        Wd, krow0, nkc, c0, cw, ntiles, consumer, gain = item
        wi = wcount[0] % 2
        wcount[0] += 1
        wb, bwb = wbf[wi], buf("wbf%d" % wi)
        nh = (nkc + 7) // 8
        for hh in range(nh):
            k0 = hh * 8
            kn = min(8, nkc - k0)
            si = pcount[0] % 2
            pcount[0] += 1
            ws, bws = wst[si], buf("wst%d" % si)
            S.dma("sp", ws[:, 0:kn, 0:cw],
                  Wd[krow0 + k0 * 128: krow0 + (k0 + kn) * 128, c0:c0 + cw].rearrange("(k p) n -> p k n", p=128),
                  writes=[bws], owner=bws)
            if gain is None:
                if hh % 2:
                    S.op("act", lambda e, wb=wb, ws=ws, k0=k0, kn=kn, cw=cw: e.copy(out=wb[:, k0:k0 + kn, 0:cw], in_=ws[:, 0:kn, 0:cw]),
                         reads=[bws], writes=[bwb])
                else:
                    S.op("dve", lambda e, wb=wb, ws=ws, k0=k0, kn=kn, cw=cw: e.tensor_copy(out=wb[:, k0:k0 + kn, 0:cw], in_=ws[:, 0:kn, 0:cw]),
                         reads=[bws], writes=[bwb])
            else:
                for kk in range(kn):
                    kc = k0 + kk
                    gcol = gains[:, gain, (krow0 // 128 + kc):(krow0 // 128 + kc) + 1]
                    if kk % 2:
                        S.op("act", lambda e, wb=wb, ws=ws, kc=kc, kk=kk, cw=cw, gcol=gcol: e.activation(
                            out=wb[:, kc, 0:cw], in_=ws[:, kk, 0:cw], func=AF.Copy, scale=gcol),
                            reads=[bws, cst], writes=[bwb])
                    else:
                        S.op("dve", lambda e, wb=wb, ws=ws, kc=kc, kk=kk, cw=cw, gcol=gcol: e.tensor_scalar(
                            out=wb[:, kc, 0:cw], in0=ws[:, kk, 0:cw], scalar1=gcol, scalar2=None, op0=ALU.mult),
                            reads=[bws, cst], writes=[bwb])
        return wb, bwb

    def _compute_group(item, wb, bwb):
        Wd, krow0, nkc, c0, cw, ntiles, consumer, gain = item
        for t in range(ntiles):
            pb = pcount[1] % 6 if len(pcount) > 1 else 0
            pcount[1] += 1
            bp = buf("ps%d" % pb)
            pt = ps[pb][:, 0:cw]

            def mm(e, pt=pt, wb=wb, t=t, cw=cw):
                ins = None
                for kc in range(nkc):
                    ins = e.matmul(pt, actT[:, kc, t * 128:(t + 1) * 128], wb[:, kc, 0:cw], start=(kc == 0), stop=(kc == nkc - 1))
                return ins
            S.op("pe", mm, reads=[buf("actT"), bwb], writes=[bp])
            consumer(t, c0, cw, pt, bp)

    pcount.append(0)

    def flush_stream():
        items = pending[:]
        del pending[:]
        if not items:
            return
        cur = _load_group(items[0])
        for n_, item in enumerate(items):
            nxt = _load_group(items[n_ + 1]) if n_ + 1 < len(items) else None
            _compute_group(item, *cur)
            cur = nxt

    ocnt = [0]

    def next_ot():
        i = ocnt[0] % 4
        ocnt[0] += 1
        return i


    def mk_store(dst):
        def cons(t, c0, cw, pt, bp):
            i = next_ot()
            o, bo = ot[i], buf("ot%d" % i)
            if i % 2:
                S.op("act", lambda e: e.copy(out=o[:, 0:cw], in_=pt), reads=[bp], writes=[bo])
            else:
                S.op("dve", lambda e: e.tensor_copy(out=o[:, 0:cw], in_=pt), reads=[bp], writes=[bo])
            S.dma("pool", dst[t * 128:(t + 1) * 128, c0:c0 + cw], o[:, 0:cw], reads=[bo], writes=[buf("projd")], owner=bo, multi=True)
        return cons

    load_actT(x_pre, NT_PRE, 0, 32, norm=True)
    stream(w_in, 0, 32, 2048, 4096, NT_PRE, mk_store(projp), gain=0)
    stream(w_in, 0, 32, 8192, 2048, NT_PRE, mk_store(projp), gain=0)
    flush_stream()
    S.barrier(B.values())
    load_actT(x_own, NT_OWN, 0, 32, norm=True)
    stream(w_in, 0, 32, 0, 10240, NT_OWN, mk_store(proj), gain=0)
    flush_stream()
    S.barrier(B.values())

    flat = actT[:].rearrange("p a b -> p (a b)")
    coff = [0]

    def carve(shape, dt):
        n = 1
        for d_ in shape[1:]:
            n *= d_
        nb = n * (4 if dt == F32 else 2)
        a = flat[:, coff[0] // 2:(coff[0] + nb) // 2]
        coff[0] += nb
        if dt == F32:
            a = a.bitcast(F32)
        if len(shape) == 3:
            a = a.rearrange("p (a b) -> p a b", a=shape[1])
        return a

    qkvg = [carve([128, 4, 256], F32) for _ in range(2)]
    cst_ = [carve([128, 2, 256], F32) for _ in range(2)]
    tqs = [[carve([128, 256], F32) for _ in range(4)] for _ in range(2)]
    qb, qdb, kb, kdb, vb = [[carve([128, 256], BF16) for _ in range(2)] for _ in range(5)]
    trs = [carve([128, 6, 128], BF16) for _ in range(2)]
    smb = [carve([128, 128], BF16) for _ in range(2)]
    Sf2 = [carve([128, 2, 256], F32) for _ in range(2)]
    Sb2 = [carve([128, 2, 256], BF16) for _ in range(2)]
    msk2 = [carve([128, 2, 128], F32) for _ in range(2)]
    bnst = [carve([128, 8], F32) for _ in range(2)]
    yt = [carve([128, 256], F32) for _ in range(2)]
    sgt = [carve([128, 256], F32) for _ in range(2)]
    S0 = [carve([128, 2, 256], F32) for _ in range(2)]
    S0b = [carve([128, 2, 256], BF16) for _ in range(2)]
    So = [carve([128, 2, 256], F32) for _ in range(2)]
    vm = [carve([128, 256], BF16) for _ in range(2)]
    qm = [carve([128, 2, 128], BF16) for _ in range(2)]
    cmask = carve([128, 16, 256], BF16)
    rmask = carve([128, 16], F32)
    dqk = carve([128, 2, 24], F32)
    odbg = [carve([128, 256], F32) for _ in range(2)]

    cst2 = buf("const2")
    S.dma("sp", cmask, cmask_d, writes=[cst2], owner=cst2, multi=True)
    S.dma("sp", rmask, rmask_d, writes=[cst2], owner=cst2, multi=True)
    S.dma("sp", dqk, dqk_d.rearrange("p a k h -> p a (k h)"), writes=[cst2], owner=cst2, multi=True)
    GAM = [1.0 - 2.0 ** (-5.0 - h_) for h_ in range(8)]
    rcount = [0]

    def ret_tile(hd, par, src, csd, t, kind, full, sample):
        i = par
        rcount[0] += 1
        Sf, Sb, msk = Sf2[par], Sb2[par], msk2[par]
        tq1, tq2, tk1, tk2 = tqs[par]
        TRB, SCB, STB = (6, 2)[par], (7, 3)[par], (5, 4)[par]
        qi, bqi = qkvg[i], buf("qkvg%d" % i)
        ci, bci = cst_[i], buf("cs%d" % i)
        srcv = src[t * 128:(t + 1) * 128, :].rearrange("p (s h d) -> p s h d", s=5, h=8)[:, 0:4, hd, :]
        S.dma("sp", qi, srcv, writes=[bqi], owner=bqi)
        S.dma("sp", ci, csd[t].rearrange("p (a d) -> p a d", a=2), writes=[bci], owner=bci)
        dq_col = dqk[:, 0, kind * 8 + hd:kind * 8 + hd + 1]
        dk_col = dqk[:, 1, kind * 8 + hd:kind * 8 + hd + 1]

        def rot(x, t1, t2, bt1, bt2):
            S.op("dve", lambda e: e.tensor_tensor(out=t1, in0=x, in1=ci[:, 0, :], op=ALU.mult), reads=[bqi, bci], writes=[bt1], relax=True)
            S.op("dve", lambda e: e.tensor_tensor(out=t2[:, 0:128], in0=x[:, 128:256], in1=ci[:, 1, 0:128], op=ALU.mult), reads=[bqi, bci], writes=[bt2], relax=True)
            S.op("dve", lambda e: e.tensor_tensor(out=t2[:, 128:256], in0=x[:, 0:128], in1=ci[:, 1, 128:256], op=ALU.mult), reads=[bqi, bci], writes=[bt2], relax=True)
            S.op("dve", lambda e: e.tensor_tensor(out=t1, in0=t1, in1=t2, op=ALU.add), reads=[bt1, bt2], writes=[bt1], relax=True)

        bk1, bk2 = buf("tk1_%d" % par), buf("tk2_%d" % par)
        rot(qi[:, 1, :], tk1, tk2, bk1, bk2)
        kd_, bkd = kdb[i], buf("kdb%d" % i)
        v_, bv = vb[i], buf("vb%d" % i)
        S.op("dve", lambda e: e.tensor_scalar(out=kd_, in0=tk1, scalar1=dk_col, scalar2=None, op0=ALU.mult), reads=[bk1, cst2], writes=[bkd])
        S.op("act", lambda e: e.copy(out=v_, in_=qi[:, 2, :]), reads=[bqi], writes=[bv])
        yield
        bSf, bSb = buf("Sf%d" % par), buf("Sb%d" % par)
        gl = GAM[hd] ** {0: 128, 1: 8, 2: 16}[kind]
        if full:
            bq1, bq2 = buf("tq1_%d" % par), buf("tq2_%d" % par)
            rot(qi[:, 0, :], tq1, tq2, bq1, bq2)
            q_, bq = qb[i], buf("qb%d" % i)
            qd_, bqd = qdb[i], buf("qdb%d" % i)
            k_, bk = kb[i], buf("kb%d" % i)
            S.op("act", lambda e: e.copy(out=q_, in_=tq1), reads=[bq1], writes=[bq])
            S.op("dve", lambda e: e.tensor_scalar(out=qd_, in0=tq1, scalar1=dq_col, scalar2=None, op0=ALU.mult), reads=[bq1, cst2], writes=[bqd])
            S.op("act", lambda e: e.mul(out=k_, in_=tk1, mul=1.0 / 16.0), reads=[bk1], writes=[bk])
            bp6 = buf("ps%d" % TRB)
            pv = ps[TRB].bitcast(BF16)

            def tr(e):
                ins = None
                for n_, srcb in enumerate((q_, qd_, k_)):
                    for c_ in range(2):
                        ins = e.transpose(out=pv[:, (2 * n_ + c_) * 128:(2 * n_ + c_ + 1) * 128], in_=srcb[:, c_ * 128:(c_ + 1) * 128], identity=ident[:])
                return ins
            S.op("pe", tr, reads=[bq, bqd, bk, buf("ident")], writes=[bp6])
            yield
            tr_, btr = trs[i], buf("trs%d" % i)
            S.op("dve", lambda e: e.tensor_copy(out=tr_, in_=pv[:, 0:768].rearrange("p (a b) -> p a b", a=6)), reads=[bp6], writes=[btr])
            bp7 = buf("ps%d" % SCB)

            def sc(e):
                e.matmul(ps[SCB][:, 0:128], tr_[:, 4, :], tr_[:, 0, :], start=True, stop=False)
                return e.matmul(ps[SCB][:, 0:128], tr_[:, 5, :], tr_[:, 1, :], start=False, stop=True)
            S.op("pe", sc, reads=[btr], writes=[bp7])
            yield
            sm_, bsm = smb[i], buf("smb%d" % i)
            S.op("dve", lambda e: e.tensor_tensor(out=sm_, in0=ps[SCB][:, 0:128], in1=msk[:, kind, :], op=ALU.mult), reads=[bp7, buf("msk%d" % par)], writes=[bsm])
            pbo = par
            bpo = buf("ps%d" % pbo)
            po = ps[pbo][:, 0:256]
            if not sample:
                def om(e):
                    e.matmul(po, sm_, v_, start=True, stop=False)
                    e.matmul(po, tr_[:, 2, :], Sb[:, 0, :], start=False, stop=False)
                    return e.matmul(po, tr_[:, 3, :], Sb[:, 1, :], start=False, stop=True)
                S.op("pe", om, reads=[bsm, bv, btr, bSb], writes=[bpo])
            else:
                S.op("pe", lambda e: e.matmul(po, sm_, v_, start=True, stop=False, skip_group_check=True), reads=[bsm, bv], writes=[bpo])
        if not sample:
            bp5 = buf("ps%d" % STB)

            def su(e):
                e.matmul(ps[STB][:, 0:256], kd_[:, 0:128], v_, start=True, stop=True)
                return e.matmul(ps[STB][:, 256:512], kd_[:, 128:256], v_, start=True, stop=True)
            S.op("pe", su, reads=[bkd, bv], writes=[bp5])
            yield
            S.op("dve", lambda e: e.scalar_tensor_tensor(out=Sf, in0=Sf, scalar=float(gl), in1=ps[STB][:, 0:512].rearrange("p (a b) -> p a b", a=2),
                                                         op0=ALU.mult, op1=ALU.add), reads=[bSf, bp5], writes=[bSf])
            S.op("act", lambda e: e.copy(out=Sb, in_=Sf), reads=[bSf], writes=[bSb])
        else:
            for sq in range(16):
                j = sq % 2
                s0, bs0 = S0[j], buf("S0_%d" % j)
                s0b, bs0b = S0b[j], buf("S0b_%d" % j)
                so, bso = So[j], buf("So_%d" % j)
                vm_, bvm = vm[j], buf("vm%d" % j)
                qm_, bqm = qm[j], buf("qm%d" % j)
                S.dma("sp", s0, st_ret[sq, hd].rearrange("(c p) v -> p c v", p=128), writes=[bs0], owner=bs0)
                S.op("act", lambda e, s0=s0, s0b=s0b: e.copy(out=s0b, in_=s0), reads=[bs0], writes=[bs0b])
                S.op("dve", lambda e, qm_=qm_, sq=sq: e.tensor_tensor(out=qm_, in0=tr_[:, 2:4, :], in1=cmask[:, sq, :].rearrange("p (a b) -> p a b", a=2), op=ALU.mult),
                     reads=[btr, cst2], writes=[bqm])

                def im(e, qm_=qm_, s0b=s0b, sq=sq):
                    e.matmul(po, qm_[:, 0, :], s0b[:, 0, :], start=False, stop=False, skip_group_check=True)
                    return e.matmul(po, qm_[:, 1, :], s0b[:, 1, :], start=False, stop=(sq == 15), skip_group_check=True)
                S.op("pe", im, reads=[bqm, bs0b], writes=[bpo])
                S.op("dve", lambda e, vm_=vm_, sq=sq: e.tensor_scalar(out=vm_, in0=v_, scalar1=rmask[:, sq:sq + 1], scalar2=None, op0=ALU.mult),
                     reads=[bv, cst2], writes=[bvm])
                pbs = 4 + (sq % 2)
                bps = buf("ps%d" % pbs)

                def su2(e, vm_=vm_, pbs=pbs):
                    e.matmul(ps[pbs][:, 0:256], kd_[:, 0:128], vm_, start=True, stop=True)
                    return e.matmul(ps[pbs][:, 256:512], kd_[:, 128:256], vm_, start=True, stop=True)
                S.op("pe", su2, reads=[bkd, bvm], writes=[bps])
                S.op("dve", lambda e, so=so, s0=s0, pbs=pbs: e.scalar_tensor_tensor(out=so, in0=s0, scalar=float(gl), in1=ps[pbs][:, 0:512].rearrange("p (a b) -> p a b", a=2),
                                                                                   op0=ALU.mult, op1=ALU.add), reads=[bs0, bps], writes=[bso])
                S.dma("pool", o_ret_s[sq, hd].rearrange("(c p) v -> p c v", p=128), so, reads=[bso], writes=[buf("orets")], owner=bso, multi=True)
                yield
        if full:
            bb, bbn = bnst[i], buf("bnst%d" % i)
            y_, by = yt[i], buf("yt%d" % i)
            sg_, bsg = sgt[i], buf("sgt%d" % i)
            S.op("act", lambda e: e.activation(out=sg_, in_=qi[:, 3, :], func=AF.Silu), reads=[bqi], writes=[bsg])
            od_, bod = odbg[i], buf("odbg%d" % i)
            S.op("act", lambda e: e.copy(out=od_, in_=po), reads=[bpo], writes=[bod])
            yield
            if DEBUG:
                S.dma("pool", mix[t * 128:(t + 1) * 128, 2048 + hd * 256:2048 + (hd + 1) * 256], od_, reads=[bod], writes=[buf("mixd")], owner=bod, multi=True)
            S.op("dve", lambda e: e.bn_stats(out=bb[:, 0:6], in_=od_), reads=[bod], writes=[bbn])
            S.op("dve", lambda e: e.bn_aggr(out=bb[:, 6:8], in_=bb[:, 0:6]), reads=[bbn], writes=[bbn])
            S.op("dve", lambda e: e.tensor_scalar(out=bb[:, 0:1], in0=bb[:, 7:8], scalar1=1e-5, scalar2=None, op0=ALU.add), reads=[bbn], writes=[bbn])
            S.op("act", lambda e: e.activation(out=bb[:, 1:2], in_=bb[:, 0:1], func=AF.Sqrt), reads=[bbn], writes=[bbn])
            yield
            S.op("dve", lambda e: e.reciprocal(out=bb[:, 2:3], in_=bb[:, 1:2]), reads=[bbn], writes=[bbn])
            S.op("dve", lambda e: e.tensor_scalar(out=y_, in0=od_, scalar1=bb[:, 6:7], scalar2=bb[:, 2:3], op0=ALU.subtract, op1=ALU.mult),
                 reads=[bod, bbn], writes=[by])
            if DEBUG:
                S.op("dve", lambda e: e.tensor_tensor(out=y_, in0=y_, in1=sg_, op=ALU.mult), reads=[by, bsg], writes=[by])
                S.dma("pool", mix[t * 128:(t + 1) * 128, hd * 256:(hd + 1) * 256], y_, reads=[by], writes=[buf("mixd")], owner=by, multi=True)
            else:
                ybf = od_[:, 0:128].bitcast(BF16)
                S.op("dve", lambda e: e.tensor_tensor(out=ybf, in0=y_, in1=sg_, op=ALU.mult), reads=[by, bsg, bod], writes=[bod])
                S.dma("pool", mix[t * 128:(t + 1) * 128, hd * 256:(hd + 1) * 256], ybf, reads=[bod], writes=[buf("mixd")], owner=bod, multi=True)

    def run_gens(gens):
        gens = list(gens)
        while gens:
            for g_ in list(gens):
                try:
                    next(g_)
                except StopIteration:
                    gens.remove(g_)

    for hp in range(4):
        hds = (2 * hp, 2 * hp + 1)
        for par, hd in enumerate(hds):
            S.dma("sp", msk2[par], maskd[hd], reads=[], writes=[buf("msk%d" % par)], owner=buf("msk%d" % par))
            S.op("dve", lambda e, par=par: e.memset(Sf2[par], 0.0), writes=[buf("Sf%d" % par)])
            S.op("dve", lambda e, par=par: e.memset(Sb2[par], 0.0), writes=[buf("Sb%d" % par)])
        for t in range(NT_PRE):
            run_gens([ret_tile(hd, par, projp, cs_pre, t, 0 if t < 8 else 2, False, False) for par, hd in enumerate(hds)])
        for t in range(8):
            run_gens([ret_tile(hd, par, proj, cs_own, t, 0, True, False) for par, hd in enumerate(hds)])
        for par, hd in enumerate(hds):
            S.dma("sp", o_ret_p[hd].rearrange("(c p) v -> p c v", p=128), Sf2[par], reads=[buf("Sf%d" % par)], writes=[buf("oretp")], owner=buf("Sf%d" % par), multi=True)
        run_gens([ret_tile(hd, par, proj, cs_own, 8, 1, True, True) for par, hd in enumerate(hds)])
    flush_stream()
    S.barrier(B.values())

    coff[0] = 0
    PI = 3.141592653589793
    MAGIC = 12582912.0
    bS = buf("s5setup")

    def T64(n=1):
        a = carve([128, n, 128], F32) if n > 1 else carve([128, 128], F32)
        return a

    lam_sb = carve([128, 2, 64], F32)
    S.dma("sp", lam_sb[:, 0, :], lam_re_d, writes=[bS], owner=bS, multi=True)
    S.dma("sp", lam_sb[:, 1, :], lam_im_d, writes=[bS], owner=bS, multi=True)
    lamT = T64(2)
    ldt = T64()
    S.dma("sp", ldt[0:64, :], ldt_d, writes=[bS], owner=bS, multi=True)
    m3 = carve([128, 128], F32)
    S.dma("sp", m3, m3_d, writes=[bS], owner=bS, multi=True)
    selc = carve([128, 64], F32)
    S.dma("sp", selc[:, 0:16], sel_f_d, writes=[bS], owner=bS, multi=True)
    selb = carve([128, 32], BF16)
    S.dma("sp", selb, sel_b_d, writes=[bS], owner=bS, multi=True)
    mask8, tmask = selc[:, 0:8], selc[:, 8:16]
    Jsel, Csel = selb[:, 0:16], selb[:, 16:32]

    A1, A2 = carve([128, 2, 128], F32), carve([128, 2, 128], F32)
    B1, B2 = carve([128, 2, 128], F32), carve([128, 2, 128], F32)
    main_off = coff[0]

    sop_state = [False]

    def sop(eng, fn, after_recip=False):
        S.op(eng, fn, reads=[bS], writes=[bS], relax=(eng == "dve" and not sop_state[0]))
        sop_state[0] = after_recip

    def trans_f32(dst, src, rows_in, cols_in):
        bp = buf("ps7")
        S.op("pe", lambda e: e.transpose(out=ps[7][0:cols_in, 0:rows_in], in_=src, identity=ident_f[0:rows_in, 0:rows_in]), reads=[bS, cst], writes=[bp])
        S.op("dve", lambda e: e.tensor_copy(out=dst, in_=ps[7][0:cols_in, 0:rows_in]), reads=[bp, bS], writes=[bS])

    trans_f32(lamT[0:64, 0, :], lam_sb[:, 0, :], 128, 64)
    trans_f32(lamT[0:64, 1, :], lam_sb[:, 1, :], 128, 64)
    dtT = T64()
    lrT, liT = T64(), T64()
    sop("act", lambda e: e.activation(out=dtT[0:64, :], in_=ldt[0:64, :], func=AF.Exp))
    sop("dve", lambda e: e.tensor_tensor(out=lrT[0:64, :], in0=lamT[0:64, 0, :], in1=dtT[0:64, :], op=ALU.mult))
    sop("dve", lambda e: e.tensor_tensor(out=liT[0:64, :], in0=lamT[0:64, 1, :], in1=dtT[0:64, :], op=ALU.mult))
    AR, AI, NR, NI = T64(9), T64(9), T64(9), T64(9)
    tA, tB, tC, tD = T64(), T64(), T64(), T64()

    def trig(dst, k, off):
        sop("dve", lambda e: e.tensor_scalar(out=tA[0:64, :], in0=liT[0:64, :], scalar1=float(k), scalar2=float(off), op0=ALU.mult, op1=ALU.add))
        sop("dve", lambda e: e.tensor_scalar(out=tB[0:64, :], in0=tA[0:64, :], scalar1=1.0 / (2 * PI), scalar2=MAGIC, op0=ALU.mult, op1=ALU.add))
        sop("dve", lambda e: e.tensor_scalar(out=tB[0:64, :], in0=tB[0:64, :], scalar1=MAGIC, scalar2=2 * PI, op0=ALU.subtract, op1=ALU.mult))
        sop("dve", lambda e: e.tensor_tensor(out=tA[0:64, :], in0=tA[0:64, :], in1=tB[0:64, :], op=ALU.subtract))
        sop("dve", lambda e: e.tensor_scalar(out=tA[0:64, :], in0=tA[0:64, :], scalar1=-3.1415925, scalar2=3.1415925, op0=ALU.max, op1=ALU.min))
        sop("act", lambda e: e.activation(out=dst, in_=tA[0:64, :], func=AF.Sin))

    for k in range(9):
        trig(tC[0:64, :], k, PI / 2)
        trig(tD[0:64, :], k, 0.0)
        sop("act", lambda e, k=k: e.activation(out=AR[0:64, k, :], in_=lrT[0:64, :], func=AF.Exp, scale=float(k)))
        sop("act", lambda e, k=k: e.activation(out=NR[0:64, k, :], in_=lrT[0:64, :], func=AF.Exp, scale=float(-k)))
        sop("dve", lambda e, k=k: e.tensor_tensor(out=AI[0:64, k, :], in0=AR[0:64, k, :], in1=tD[0:64, :], op=ALU.mult))
        sop("dve", lambda e, k=k: e.tensor_tensor(out=AR[0:64, k, :], in0=AR[0:64, k, :], in1=tC[0:64, :], op=ALU.mult))
        sop("dve", lambda e, k=k: e.scalar_tensor_tensor(out=NI[0:64, k, :], in0=NR[0:64, k, :], scalar=-1.0, in1=tD[0:64, :], op0=ALU.mult, op1=ALU.mult))
        sop("dve", lambda e, k=k: e.tensor_tensor(out=NR[0:64, k, :], in0=NR[0:64, k, :], in1=tC[0:64, :], op=ALU.mult))
    fr, fi = T64(), T64()
    sop("dve", lambda e: e.tensor_scalar(out=tA[0:64, :], in0=AR[0:64, 1, :], scalar1=-1.0, scalar2=None, op0=ALU.add))
    sop("dve", lambda e: e.tensor_tensor(out=tB[0:64, :], in0=lamT[0:64, 0, :], in1=lamT[0:64, 0, :], op=ALU.mult))
    sop("dve", lambda e: e.tensor_tensor(out=tC[0:64, :], in0=lamT[0:64, 1, :], in1=lamT[0:64, 1, :], op=ALU.mult))
    sop("dve", lambda e: e.tensor_tensor(out=tB[0:64, :], in0=tB[0:64, :], in1=tC[0:64, :], op=ALU.add))
    sop("dve", lambda e: e.reciprocal(out=tB[0:64, :], in_=tB[0:64, :]), after_recip=True)
    sop("dve", lambda e: e.tensor_tensor(out=tC[0:64, :], in0=tA[0:64, :], in1=lamT[0:64, 0, :], op=ALU.mult))
    sop("dve", lambda e: e.tensor_tensor(out=tD[0:64, :], in0=AI[0:64, 1, :], in1=lamT[0:64, 1, :], op=ALU.mult))
    sop("dve", lambda e: e.tensor_tensor(out=tC[0:64, :], in0=tC[0:64, :], in1=tD[0:64, :], op=ALU.add))
    sop("dve", lambda e: e.tensor_tensor(out=fr[0:64, :], in0=tC[0:64, :], in1=tB[0:64, :], op=ALU.mult))
    sop("dve", lambda e: e.tensor_tensor(out=tC[0:64, :], in0=AI[0:64, 1, :], in1=lamT[0:64, 0, :], op=ALU.mult))
    sop("dve", lambda e: e.tensor_tensor(out=tD[0:64, :], in0=tA[0:64, :], in1=lamT[0:64, 1, :], op=ALU.mult))
    sop("dve", lambda e: e.tensor_tensor(out=tC[0:64, :], in0=tC[0:64, :], in1=tD[0:64, :], op=ALU.subtract))
    sop("dve", lambda e: e.tensor_tensor(out=fi[0:64, :], in0=tC[0:64, :], in1=tB[0:64, :], op=ALU.mult))
    ER, EI, ENR, ENI = T64(8), T64(8), T64(8), T64(8)

    def cmul_f(dr, di, xr, xi):
        sop("dve", lambda e: e.tensor_tensor(out=tA[0:64, :], in0=xr, in1=fr[0:64, :], op=ALU.mult))
        sop("dve", lambda e: e.tensor_tensor(out=tB[0:64, :], in0=xi, in1=fi[0:64, :], op=ALU.mult))
        sop("dve", lambda e: e.tensor_tensor(out=dr, in0=tA[0:64, :], in1=tB[0:64, :], op=ALU.subtract))
        sop("dve", lambda e: e.tensor_tensor(out=tA[0:64, :], in0=xr, in1=fi[0:64, :], op=ALU.mult))
        sop("dve", lambda e: e.tensor_tensor(out=tB[0:64, :], in0=xi, in1=fr[0:64, :], op=ALU.mult))
        sop("dve", lambda e: e.tensor_tensor(out=di, in0=tA[0:64, :], in1=tB[0:64, :], op=ALU.add))

    for s_ in range(8):
        cmul_f(ER[0:64, s_, :], EI[0:64, s_, :], AR[0:64, 7 - s_, :], AI[0:64, 7 - s_, :])
        cmul_f(ENR[0:64, s_, :], ENI[0:64, s_, :], NR[0:64, s_ + 1, :], NI[0:64, s_ + 1, :])
    sop("dve", lambda e: e.tensor_copy(out=A1[0:64, 0, :], in_=AR[0:64, 8, :]))
    sop("dve", lambda e: e.tensor_copy(out=A1[0:64, 1, :], in_=AR[0:64, 8, :]))
    sop("dve", lambda e: e.tensor_scalar(out=A2[0:64, 0, :], in0=AI[0:64, 8, :], scalar1=-1.0, scalar2=None, op0=ALU.mult))
    sop("dve", lambda e: e.tensor_copy(out=A2[0:64, 1, :], in_=AI[0:64, 8, :]))
    sop("dve", lambda e: e.tensor_tensor(out=tA[0:64, :], in0=AR[0:64, 8, :], in1=AR[0:64, 8, :], op=ALU.mult))
    sop("dve", lambda e: e.tensor_tensor(out=tB[0:64, :], in0=AI[0:64, 8, :], in1=AI[0:64, 8, :], op=ALU.mult))
    sop("dve", lambda e: e.tensor_tensor(out=B1[0:64, 0, :], in0=tA[0:64, :], in1=tB[0:64, :], op=ALU.subtract))
    sop("dve", lambda e: e.tensor_copy(out=B1[0:64, 1, :], in_=B1[0:64, 0, :]))
    sop("dve", lambda e: e.tensor_tensor(out=tA[0:64, :], in0=AR[0:64, 8, :], in1=AI[0:64, 8, :], op=ALU.mult))
    sop("dve", lambda e: e.tensor_scalar(out=B2[0:64, 1, :], in0=tA[0:64, :], scalar1=2.0, scalar2=None, op0=ALU.mult))
    sop("dve", lambda e: e.tensor_scalar(out=B2[0:64, 0, :], in0=tA[0:64, :], scalar1=-2.0, scalar2=None, op0=ALU.mult))

    Bp = carve([128, 2, 128], F32)
    Cblk = carve([128, 2, 64], F32)
    CT = carve([128, 2, 128], F32)
    W1p = xin[0][:, 0:2048].rearrange("p (a b) -> p a b", a=2)
    W1n = xin[1][:, 0:2048].rearrange("p (a b) -> p a b", a=2)
    W2p = carve([128, 2, 1024], F32)
    tW = carve([128, 1024], F32)
    Wst = carve([128, 8, 512], BF16)
    S.op("dve", lambda e: e.memset(Wst, 0.0), reads=[bS], writes=[bS])

    def v4(ap):
        return ap.rearrange("p (g s c) -> p g s c", g=8, s=8)

    def bc_tab(tab, g0):
        return tab[0:64, :, g0:g0 + 8].rearrange("p s g -> p g s").unsqueeze(3).broadcast_to([64, 8, 8, 16])

    def bc_gc(x):
        return x.rearrange("p (g c) -> p g c", g=8).unsqueeze(2).broadcast_to([64, 8, 8, 16])

    def cprod(dst_r, dst_i, tr_, ti_, xr, xi, g0, neg_i=False):
        sop("dve", lambda e: e.tensor_tensor(out=v4(dst_r), in0=bc_tab(tr_, g0), in1=bc_gc(xr), op=ALU.mult))
        sop("dve", lambda e: e.tensor_tensor(out=v4(tW[0:64, :]), in0=bc_tab(ti_, g0), in1=bc_gc(xi), op=ALU.mult))
        sop("dve", lambda e: e.tensor_tensor(out=dst_r, in0=dst_r, in1=tW[0:64, :], op=ALU.subtract))
        sop("dve", lambda e: e.tensor_tensor(out=v4(dst_i), in0=bc_tab(tr_, g0), in1=bc_gc(xi), op=ALU.mult))
        sop("dve", lambda e: e.tensor_tensor(out=v4(tW[0:64, :]), in0=bc_tab(ti_, g0), in1=bc_gc(xr), op=ALU.mult))
        if neg_i:
            sop("dve", lambda e: e.scalar_tensor_tensor(out=dst_i, in0=dst_i, scalar=-1.0, in1=tW[0:64, :], op0=ALU.mult, op1=ALU.subtract))
        else:
            sop("dve", lambda e: e.tensor_tensor(out=dst_i, in0=dst_i, in1=tW[0:64, :], op=ALU.add))

    for fc in range(16):
        g0 = fc * 8
        S.dma("sp", Bp[0:64, 0, :].rearrange("p (g c) -> p g c", g=8), b_re_d[g0:g0 + 8].rearrange("g p c -> p g c"), reads=[bS], writes=[bS], owner=bS, multi=True)
        S.dma("sp", Bp[0:64, 1, :].rearrange("p (g c) -> p g c", g=8), b_im_d[g0:g0 + 8].rearrange("g p c -> p g c"), reads=[bS], writes=[bS], owner=bS, multi=True)
        S.dma("sp", Cblk[:, 0, :], c_re_d[g0:g0 + 8].rearrange("g c p -> (g c) p"), reads=[bS], writes=[bS], owner=bS, multi=True)
        S.dma("sp", Cblk[:, 1, :], c_im_d[g0:g0 + 8].rearrange("g c p -> (g c) p"), reads=[bS], writes=[bS], owner=bS, multi=True)
        trans_f32(CT[0:64, 0, :], Cblk[:, 0, :], 128, 64)
        trans_f32(CT[0:64, 1, :], Cblk[:, 1, :], 128, 64)
        cprod(W1p[0:64, 0, :], W1p[0:64, 1, :], ER, EI, Bp[0:64, 0, :], Bp[0:64, 1, :], g0)
        cprod(W1n[0:64, 0, :], W1n[0:64, 1, :], ENR, ENI, Bp[0:64, 0, :], Bp[0:64, 1, :], g0)
        cprod(W2p[0:64, 0, :], W2p[0:64, 1, :], AR[:, 1:9, :], AI[:, 1:9, :], CT[0:64, 0, :], CT[0:64, 1, :], g0, neg_i=True)
        for half in range(2):
            pb = 2 + half
            bp = buf("ps%d" % pb)

            def w3mm(e, half=half, pb=pb):
                ins = None
                for gg in range(4):
                    g = half * 4 + gg
                    e.matmul(ps[pb][:, gg * 128:(gg + 1) * 128], W1n[0:64, 0, g * 128:(g + 1) * 128], W2p[0:64, 0, g * 128:(g + 1) * 128], start=True, stop=False)
                    ins = e.matmul(ps[pb][:, gg * 128:(gg + 1) * 128], W1n[0:64, 1, g * 128:(g + 1) * 128], W2p[0:64, 1, g * 128:(g + 1) * 128], start=False, stop=True)
                return ins
            S.op("pe", w3mm, reads=[bS], writes=[bp])
            S.op("dve", lambda e, half=half, pb=pb: e.tensor_tensor(out=Wst[:, half * 4:(half + 1) * 4, 128:256],
                                                                   in0=ps[pb][:, 0:512].rearrange("p (g n) -> p g n", g=4),
                                                                   in1=m3.unsqueeze(1).broadcast_to([128, 4, 128]), op=ALU.mult),
                 reads=[bp, bS], writes=[bS])
        for half in range(2):
            pb = 4 + half
            bp = buf("ps%d" % pb)

            def w1tr(e, half=half, pb=pb):
                ins = None
                for gg in range(4):
                    g = half * 4 + gg
                    for ri in range(2):
                        ins = e.transpose(out=ps[pb][:, (gg * 2 + ri) * 64:(gg * 2 + ri + 1) * 64], in_=W1p[0:64, ri, g * 128:(g + 1) * 128],
                                          identity=ident_f[0:64, 0:64])
                return ins
            S.op("pe", w1tr, reads=[bS, cst], writes=[bp])
            S.op("act", lambda e, half=half, pb=pb: e.copy(out=Wst[:, half * 4:(half + 1) * 4, 0:128],
                                                           in_=ps[pb][:, 0:512].rearrange("p (g n) -> p g n", g=4)),
                 reads=[bp, bS], writes=[bS])
        S.op("act", lambda e: e.copy(out=Wst[0:64, :, 256:384], in_=W2p[0:64, 0, :].rearrange("p (g n) -> p g n", g=8)), reads=[bS], writes=[bS])
        S.op("act", lambda e: e.copy(out=Wst[0:64, :, 384:512], in_=W2p[0:64, 1, :].rearrange("p (g n) -> p g n", g=8)), reads=[bS], writes=[bS])
        S.dma("sp", Wall_d[g0:g0 + 8].rearrange("g p w -> p g w"), Wst, reads=[bS], writes=[bS], owner=bS, multi=True)
    flush_stream()
    S.barrier(B.values())

    coff[0] = main_off
    GB = 32
    NW = GB * 16
    def wview(tn, is_f32):
        f = tn[:].bitcast(BF16) if is_f32 else tn[:].rearrange("p a b -> p (a b)")
        return f.rearrange("p (g w) -> p g w", g=16)
    Wsets = [(wview(wbf[0], False), wview(wbf[1], False)), (wview(xin[0], True), wview(xin[1], True))]
    Wbufs = [("wbf0", "wbf1"), ("xin0", "xin1")]
    u32 = [carve([128, NW], F32) for _ in range(2)] + [sb("u32c", [128, NW], F32)[:]]
    Uexp = [carve([128, GB, 128], BF16) for _ in range(2)]
    Usb = [carve([128, GB, 16], BF16) for _ in range(2)] + [sb("Usbc", [128, GB, 16], BF16)[:]]
    Vsb = [carve([128, 2, NW], F32) for _ in range(2)]
    Xh = carve([128, 2, GB * 17], F32)
    Xalls = [carve([128, 2, NW], BF16), sb("Xallb", [128, 2, NW], BF16)[:]]
    sT1, sT2 = carve([128, 2, GB], F32), carve([128, 2, GB], F32)
    Ysb = carve([128, GB, 16], BF16)
    Yexp = carve([128, GB, 128], BF16)
    ytmp = [carve([128, NW], F32) for _ in range(2)]
    zt_ = [carve([128, NW], F32) for _ in range(2)]
    dB = xbf[0][:].bitcast(F32)
    X0 = carve([128, 2, NW], F32)
    Xn = carve([128, 2, NW], F32)
    bT1 = sb("bT1", [128, 2, NW], F32)[:]
    bT2 = gtmp[:].rearrange("p a b -> p (a b)")[:, 0:2048].bitcast(F32).rearrange("p (r q) -> p r q", r=2)
    st_in = Yexp[:].rearrange("p a b -> p (a b)").bitcast(F32).rearrange("p (r q) -> p r q", r=2)
    st_o = [carve([128, 2, 64], F32) for _ in range(2)]
    cst3 = buf("const3")
    S.dma("sp", dB, dB_d, writes=[cst3], owner=cst3, multi=True)
    ucount = [0]
    xhv = Xh[0:64, :, :].rearrange("p r (g j) -> p r g j", g=GB)

    def Wg(gb, g):
        return Wsets[gb % 2][g // 16][:, g % 16, :]

    def s5_front(gb, src, t, wi):
        g0 = gb * GB
        i = wi % 2
        i3 = wi % 3
        bWs = [buf(n) for n in Wbufs[gb % 2]]
        u_, bu = u32[i3], buf("u32_%d" % i3)
        S.dma("sp", u_, src[t * 128:(t + 1) * 128, 8192 + g0 * 16:8192 + (g0 + GB) * 16], writes=[bu], owner=bu)
        ue, bUe = Uexp[i], buf("Uexp%d" % i)
        for s_ in range(8):
            if s_ % 2:
                S.op("act", lambda e, s_=s_: e.activation(out=ue[:, :, s_ * 16:(s_ + 1) * 16], in_=u_.rearrange("p (g c) -> p g c", g=GB),
                                                           func=AF.Copy, scale=mask8[:, s_:s_ + 1]), reads=[bu, bS], writes=[bUe])
            else:
                S.op("dve", lambda e, s_=s_: e.tensor_scalar(out=ue[:, :, s_ * 16:(s_ + 1) * 16], in0=u_.rearrange("p (g c) -> p g c", g=GB),
                                                              scalar1=mask8[:, s_:s_ + 1], scalar2=None, op0=ALU.mult), reads=[bu, bS], writes=[bUe])
        bp0 = buf("ps0")

        def umm(e):
            ins = None
            for g in range(GB):
                ins = e.matmul(ps[0][:, g * 16:(g + 1) * 16], ue[:, g, :], Jsel, start=True, stop=True, skip_group_check=True)
            return ins
        S.op("pe", umm, reads=[bUe, bS], writes=[bp0])
        us, bUs = Usb[i3], buf("Usb%d" % i3)
        S.op("act", lambda e: e.copy(out=us, in_=ps[0][:, 0:NW].rearrange("p (g j) -> p g j", g=GB)), reads=[bp0], writes=[bUs])
        bp1, bp2 = buf("ps1"), buf("ps2")

        def vmm(e):
            ins = None
            for g in range(GB):
                e.matmul(ps[1][0:64, g * 16:(g + 1) * 16], Wg(gb, g)[:, 0:64], us[:, g, :], start=True, stop=True, skip_group_check=True)
                ins = e.matmul(ps[2][0:64, g * 16:(g + 1) * 16], Wg(gb, g)[:, 64:128], us[:, g, :], start=True, stop=True, skip_group_check=True)
            return ins
        S.op("pe", vmm, reads=bWs + [bUs], writes=[bp1, bp2])
        vs, bVs = Vsb[i], buf("Vsb%d" % i)
        S.op("act", lambda e: e.copy(out=vs[0:64, 0, :], in_=ps[1][0:64, 0:NW]), reads=[bp1], writes=[bVs])
        S.op("act", lambda e: e.copy(out=vs[0:64, 1, :], in_=ps[2][0:64, 0:NW]), reads=[bp2], writes=[bVs])

    def s5_mid(gb, t, wi, nvalid, full, sample):
        g0 = gb * GB
        i = wi % 2
        Xall = Xalls[i]
        vs, bVs = Vsb[i], buf("Vsb%d" % i)
        vsv = vs[0:64, :, :].rearrange("p r (g j) -> p r g j", g=GB)
        bXh, bXa = buf("Xh"), buf("Xall%d" % i)
        a1 = A1[0:64, :, g0:g0 + GB]
        a2 = A2[0:64, :, g0:g0 + GB]
        if not sample:
            def step(src_j, dst_j, add_ap, c1, c2, badd):
                S.op("dve", lambda e: e.tensor_tensor(out=sT1[0:64, :, :], in0=xhv[:, :, :, src_j], in1=c1, op=ALU.mult), reads=[bXh, bS], writes=[buf("sT1")], relax=True)
                S.op("dve", lambda e: e.tensor_tensor(out=sT2[0:64, 0, :], in0=xhv[:, 1, :, src_j], in1=c2[:, 0, :], op=ALU.mult), reads=[bXh, bS], writes=[buf("sT2")], relax=True)
                S.op("dve", lambda e: e.tensor_tensor(out=sT2[0:64, 1, :], in0=xhv[:, 0, :, src_j], in1=c2[:, 1, :], op=ALU.mult), reads=[bXh, bS], writes=[buf("sT2")], relax=True)
                S.op("dve", lambda e: e.tensor_tensor(out=sT1[0:64, :, :], in0=sT1[0:64, :, :], in1=sT2[0:64, :, :], op=ALU.add), reads=[buf("sT1"), buf("sT2")], writes=[buf("sT1")], relax=True)
                S.op("dve", lambda e: e.tensor_tensor(out=xhv[:, :, :, dst_j], in0=sT1[0:64, :, :], in1=add_ap, op=ALU.add), reads=[buf("sT1"), badd], writes=[bXh], relax=True)
            if nvalid == 16:
                pv_ = bT1[0:64, :, 0:GB * 8].rearrange("p r (g m) -> p r g m", g=GB)
                p2_ = bT2[0:64, :, 0:GB * 8].rearrange("p r (g m) -> p r g m", g=GB)
                ve = vsv[:, :, :, 0:16:2]
                vo = vsv[:, :, :, 1:16:2]
                bP, bP2 = buf("bT1"), buf("bT2")
                S.op("dve", lambda e: e.tensor_tensor(out=pv_, in0=ve, in1=a1.unsqueeze(3).broadcast_to([64, 2, GB, 8]), op=ALU.mult), reads=[bVs, bS], writes=[bP], relax=True)
                S.op("dve", lambda e: e.tensor_tensor(out=p2_[:, 0, :, :], in0=ve[:, 1, :, :], in1=a2[:, 0, :].unsqueeze(2).broadcast_to([64, GB, 8]), op=ALU.mult), reads=[bVs, bS], writes=[bP2], relax=True)
                S.op("dve", lambda e: e.tensor_tensor(out=p2_[:, 1, :, :], in0=ve[:, 0, :, :], in1=a2[:, 1, :].unsqueeze(2).broadcast_to([64, GB, 8]), op=ALU.mult), reads=[bVs, bS], writes=[bP2], relax=True)
                S.op("dve", lambda e: e.tensor_tensor(out=pv_, in0=pv_, in1=p2_, op=ALU.add), reads=[bP, bP2], writes=[bP], relax=True)
                S.op("dve", lambda e: e.tensor_tensor(out=pv_, in0=pv_, in1=vo, op=ALU.add), reads=[bP, bVs], writes=[bP], relax=True)
                b1 = B1[0:64, :, g0:g0 + GB]
                b2 = B2[0:64, :, g0:g0 + GB]
                for m in range(8):
                    step(2 * m, 2 * m + 2, pv_[:, :, :, m], b1, b2, bP)
                xe = xhv[:, :, :, 0:16:2]
                xo = xhv[:, :, :, 1:16:2]
                S.op("dve", lambda e: e.tensor_tensor(out=pv_, in0=xe, in1=a1.unsqueeze(3).broadcast_to([64, 2, GB, 8]), op=ALU.mult), reads=[bXh, bS], writes=[bP], relax=True)
                S.op("dve", lambda e: e.tensor_tensor(out=p2_[:, 0, :, :], in0=xe[:, 1, :, :], in1=a2[:, 0, :].unsqueeze(2).broadcast_to([64, GB, 8]), op=ALU.mult), reads=[bXh, bS], writes=[bP2], relax=True)
                S.op("dve", lambda e: e.tensor_tensor(out=p2_[:, 1, :, :], in0=xe[:, 0, :, :], in1=a2[:, 1, :].unsqueeze(2).broadcast_to([64, GB, 8]), op=ALU.mult), reads=[bXh, bS], writes=[bP2], relax=True)
                S.op("dve", lambda e: e.tensor_tensor(out=pv_, in0=pv_, in1=p2_, op=ALU.add), reads=[bP, bP2], writes=[bP], relax=True)
                S.op("dve", lambda e: e.tensor_tensor(out=xo, in0=pv_, in1=ve, op=ALU.add), reads=[bP, bVs, bXh], writes=[bXh], relax=True)
            else:
                for j in range(nvalid):
                    step(j, j + 1, vsv[:, :, :, j], a1, a2, bVs)
            if full:
                S.op("act", lambda e: e.copy(out=Xall[0:64, :, :].rearrange("p r (g j) -> p r g j", g=GB), in_=xhv[:, :, :, 0:16]), reads=[bXh], writes=[bXa])
            S.op("dve", lambda e: e.tensor_copy(out=xhv[:, :, :, 0], in_=xhv[:, :, :, nvalid]), reads=[bXh, bXa], writes=[bXh], relax=True)
        else:
            x0v = X0[0:64, :, :].rearrange("p r (g j) -> p r g j", g=GB)
            t1v = bT1[0:64, :, :].rearrange("p r (g j) -> p r g j", g=GB)
            t2v = bT2[0:64, :, :].rearrange("p r (g j) -> p r g j", g=GB)
            S.op("act", lambda e: e.copy(out=Xall[0:64, :, :], in_=X0[0:64, :, :]), reads=[buf("X0")], writes=[bXa])
            S.op("dve", lambda e: e.tensor_tensor(out=t1v, in0=x0v, in1=a1.unsqueeze(3).broadcast_to([64, 2, GB, 16]), op=ALU.mult), reads=[buf("X0"), bS], writes=[buf("bT1")])
            S.op("dve", lambda e: e.tensor_tensor(out=t2v[:, 0, :, :], in0=x0v[:, 1, :, :], in1=a2[:, 0, :].unsqueeze(2).broadcast_to([64, GB, 16]), op=ALU.mult), reads=[buf("X0"), bS], writes=[buf("bT2")])
            S.op("dve", lambda e: e.tensor_tensor(out=t2v[:, 1, :, :], in0=x0v[:, 0, :, :], in1=a2[:, 1, :].unsqueeze(2).broadcast_to([64, GB, 16]), op=ALU.mult), reads=[buf("X0"), bS], writes=[buf("bT2")])
            S.op("dve", lambda e: e.tensor_tensor(out=bT1[0:64, :, :], in0=bT1[0:64, :, :], in1=bT2[0:64, :, :], op=ALU.add), reads=[buf("bT1"), buf("bT2")], writes=[buf("bT1")])
            S.op("dve", lambda e: e.tensor_tensor(out=Xn[0:64, :, :], in0=bT1[0:64, :, :], in1=vs[0:64, :, :], op=ALU.add), reads=[buf("bT1"), bVs], writes=[buf("Xn")])

    def s5_back(gb, t, wi, nvalid, full, sample):
        if not full:
            return
        g0 = gb * GB
        i = wi % 2
        i3 = wi % 3
        Xall = Xalls[i]
        bXa = buf("Xall%d" % i)
        bWs = [buf(n) for n in Wbufs[gb % 2]]
        u_, bu = u32[i3], buf("u32_%d" % i3)
        us, bUs = Usb[i3], buf("Usb%d" % i3)
        bp3 = buf("ps3")
        xr_ = Xall[0:64, 0, :].rearrange("p (g j) -> p g j", g=GB)
        xi_ = Xall[0:64, 1, :].rearrange("p (g j) -> p g j", g=GB)

        def ymm(e):
            ins = None
            for g in range(GB):
                o_ = ps[3][:, g * 16:(g + 1) * 16]
                W = Wg(gb, g)
                e.matmul(o_, W[0:64, 256:384], xr_[:, g, :], start=True, stop=False, skip_group_check=True)
                e.matmul(o_, W[0:64, 384:512], xi_[:, g, :], start=False, stop=False, skip_group_check=True)
                ins = e.matmul(o_, W[:, 128:256], us[:, g, :], start=False, stop=True, skip_group_check=True)
            return ins
        S.op("pe", ymm, reads=bWs + [bXa, bUs], writes=[bp3])
        bYs, bYe = buf("Ysb"), buf("Yexp")
        S.op("act", lambda e: e.copy(out=Ysb, in_=ps[3][:, 0:NW].rearrange("p (g j) -> p g j", g=GB)), reads=[bp3], writes=[bYs])
        yev = Yexp[:, :, :].rearrange("p g (j t) -> p g j t", j=16)
        for t_ in range(8):
            if t_ % 2:
                S.op("act", lambda e, t_=t_: e.activation(out=yev[:, :, :, t_], in_=Ysb, func=AF.Copy, scale=tmask[:, t_:t_ + 1]),
                     reads=[bYs, bS], writes=[bYe])
            else:
                S.op("dve", lambda e, t_=t_: e.tensor_scalar(out=yev[:, :, :, t_], in0=Ysb, scalar1=tmask[:, t_:t_ + 1], scalar2=None, op0=ALU.mult),
                     reads=[bYs, bS], writes=[bYe])
        bp4 = buf("ps4")

        def pmm(e):
            ins = None
            for g in range(GB):
                ins = e.matmul(ps[4][:, g * 16:(g + 1) * 16], Yexp[:, g, :], Csel, start=True, stop=True, skip_group_check=True)
            return ins
        S.op("pe", pmm, reads=[bYe, bS], writes=[bp4])
        yt_, byt = ytmp[i], buf("ytmp%d" % i)
        z_, bz = zt_[i], buf("zt%d" % i)
        S.op("dve", lambda e: e.tensor_tensor(out=yt_, in0=u_, in1=dB[:, g0 * 16:(g0 + GB) * 16], op=ALU.mult), reads=[bu, cst3], writes=[byt])
        S.op("dve", lambda e: e.tensor_tensor(out=yt_, in0=yt_, in1=ps[4][:, 0:NW], op=ALU.add), reads=[byt, bp4], writes=[byt])
        S.op("act", lambda e: e.activation(out=z_, in_=yt_, func=AF.Square), reads=[byt], writes=[bz])
        S.op("dve", lambda e: e.tensor_scalar(out=z_, in0=z_, scalar1=0.044715, scalar2=1.0, op0=ALU.mult, op1=ALU.add), reads=[bz], writes=[bz])
        S.op("dve", lambda e: e.tensor_tensor(out=z_, in0=z_, in1=yt_, op=ALU.mult), reads=[bz, byt], writes=[bz])
        S.op("act", lambda e: e.activation(out=z_, in_=z_, func=AF.Sigmoid, scale=1.5957691216057308), reads=[bz], writes=[bz])
        S.op("dve", lambda e: e.tensor_tensor(out=z_, in0=z_, in1=yt_, op=ALU.mult), reads=[bz, byt], writes=[bz])
        S.dma("pool", zscr[t * 128:(t + 1) * 128, g0 * 16:(g0 + GB) * 16], z_, reads=[bz], writes=[buf("zscrd")], owner=bz, multi=True)

    def emit_state(gb, srcX, dst_r, dst_i, bsrc):
        g0 = gb * GB
        k = ucount[0] % 2
        ucount[0] += 1
        bp = buf("ps5")

        def tr(e):
            e.transpose(out=ps[5][0:GB, 0:64], in_=srcX[:, 0, :], identity=ident_f[0:64, 0:64])
            return e.transpose(out=ps[5][0:GB, 64:128], in_=srcX[:, 1, :], identity=ident_f[0:64, 0:64])
        S.op("pe", tr, reads=[bsrc, cst], writes=[bp])
        so_, bso = st_o[k], buf("st_o%d" % k)
        S.op("dve", lambda e: e.tensor_copy(out=so_[0:GB, :, :], in_=ps[5][0:GB, 0:128].rearrange("p (r q) -> p r q", r=2)), reads=[bp], writes=[bso])
        S.dma("pool", dst_r[g0:g0 + GB, :], so_[0:GB, 0, :], reads=[bso], writes=[buf("os5")], owner=bso, multi=True)
        S.dma("pool", dst_i[g0:g0 + GB, :], so_[0:GB, 1, :], reads=[bso], writes=[buf("os5")], owner=bso, multi=True)

    for gb in range(128 // GB):
        g0 = gb * GB
        for hh in range(2):
            bW = buf(Wbufs[gb % 2][hh])
            S.dma("sp", Wsets[gb % 2][hh], Wall_d[g0 + hh * 16:g0 + (hh + 1) * 16].rearrange("g p w -> p g w"), writes=[bW], owner=bW)
        bsi = buf("Yexp")
        S.dma("sp", st_in[0:GB, 0, :].rearrange("g (s p) -> g s p", s=16), s5r_d[:, g0:g0 + GB, :].rearrange("s g p -> g s p"), writes=[bsi], owner=bsi)
        S.dma("sp", st_in[0:GB, 1, :].rearrange("g (s p) -> g s p", s=16), s5i_d[:, g0:g0 + GB, :].rearrange("s g p -> g s p"), reads=[bsi], writes=[bsi], owner=bsi)
        x0v = X0[0:64, :, :].rearrange("p r (g j) -> p r g j", g=GB)
        for ri in range(2):
            for q4 in range(4):
                bp = buf("ps6")

                def trs_(e, ri=ri, q4=q4):
                    ins = None
                    for jj in range(4):
                        j = q4 * 4 + jj
                        ins = e.transpose(out=ps[6][0:64, jj * GB:(jj + 1) * GB], in_=st_in[0:GB, ri, j * 64:(j + 1) * 64], identity=ident_f[0:GB, 0:GB])
                    return ins
                S.op("pe", trs_, reads=[bsi, cst], writes=[bp])
                S.op("dve", lambda e, ri=ri, q4=q4: e.tensor_copy(out=x0v[:, ri, :, q4 * 4:(q4 + 1) * 4],
                                                                  in_=ps[6][0:64, 0:4 * GB].rearrange("p (j g) -> p g j", j=4)),
                     reads=[bp], writes=[buf("X0")])
        S.op("dve", lambda e: e.memset(Xh, 0.0), writes=[buf("Xh")])
        work = [(projp, t, 16 if t < 8 else 2, False, False) for t in range(NT_PRE)]
        work += [(proj, t, 16, True, False) for t in range(8)]
        work += [(proj, 8, 16, True, True)]
        s5_front(gb, work[0][0], work[0][1], 0)
        for wi, (src, t, nv, full, sample) in enumerate(work):
            if wi + 1 < len(work):
                s5_front(gb, work[wi + 1][0], work[wi + 1][1], wi + 1)
            if sample:
                emit_state(gb, xhv[:, :, :, 0], o_s5p_r, o_s5p_i, buf("Xh"))
            s5_mid(gb, t, wi, nv, full, sample)
            if wi >= 1:
                p_ = work[wi - 1]
                s5_back(gb, p_[1], wi - 1, p_[2], p_[3], p_[4])
        p_ = work[-1]
        s5_back(gb, p_[1], len(work) - 1, p_[2], p_[3], p_[4])
        xnv = Xn[0:64, :, :].rearrange("p r (g j) -> p r g j", g=GB)
        for j in range(16):
            emit_state(gb, xnv[:, :, :, j], o_s5s_r[j], o_s5s_i[j], buf("Xn"))
    flush_stream()
    S.barrier(B.values())

    coff[0] = 16 * ROWS_OWN * 2
    bgl = carve([128, 2048], F32)
    S.dma("sp", bgl, bglu_d, writes=[cst3], owner=cst3, multi=True)

    def cons_glu(t, c0, cw, pt, bp):
        i = next_ot()
        o, bo, r, br = ot[i], buf("ot%d" % i), rt[i], buf("rt%d" % i)
        S.dma("sp", r[:, 0:cw], zscr[t * 128:(t + 1) * 128, c0:c0 + cw], writes=[br], owner=br)
        S.op("dve", lambda e: e.tensor_tensor(out=o[:, 0:cw], in0=pt, in1=bgl[:, c0:c0 + cw], op=ALU.add), reads=[bp, cst3], writes=[bo])
        S.op("act", lambda e: e.activation(out=o[:, 0:cw], in_=o[:, 0:cw], func=AF.Sigmoid), reads=[bo], writes=[bo])
        S.op("dve", lambda e: e.tensor_tensor(out=o[:, 0:cw], in0=o[:, 0:cw], in1=r[:, 0:cw], op=ALU.mult), reads=[bo, br], writes=[bo])
        S.dma("pool", s5o[t * 128:(t + 1) * 128, c0:c0 + cw], o[:, 0:cw], reads=[bo], writes=[buf("s5od")], owner=bo, multi=True)

    load_actT(zscr, NT_OWN, 0, 16)
    stream(w_glu, 0, 16, 0, 2048, NT_OWN, cons_glu)
    flush_stream()
    S.barrier(B.values())
    for t in range(NT_OWN):
        xi, stt = xin[t % 2], stat[t % 2]
        bxi, bst = buf("xin%d" % (t % 2)), buf("stat%d" % (t % 2))
        S.dma("pool", xi[:, 0:2048], s5o[t * 128:(t + 1) * 128, :], writes=[bxi], owner=bxi)
        S.op("act", lambda e, xi=xi, stt=stt, t=t: e.activation(out=xbf[t % 2][:, 0:2048], in_=xi[:, 0:2048], func=AF.Square, accum_out=stt[:, 0:1]),
             reads=[bxi], writes=[bst, buf("xbf%d" % (t % 2))])
        S.op("dve", lambda e, stt=stt: e.tensor_scalar(out=stt[:, 1:2], in0=stt[:, 0:1], scalar1=1.0 / 2048, scalar2=EPS,
                                                       op0=ALU.mult, op1=ALU.add), reads=[bst], writes=[bst])
        S.op("act", lambda e, stt=stt: e.activation(out=stt[:, 2:3], in_=stt[:, 1:2], func=AF.Sqrt), reads=[bst], writes=[bst])
        S.op("dve", lambda e, stt=stt: e.reciprocal(out=stt[:, 3:4], in_=stt[:, 2:3]), reads=[bst], writes=[bst])
        if DEBUG:
            S.op("dve", lambda e, xi=xi, stt=stt: e.tensor_scalar(out=xi[:, 0:2048], in0=xi[:, 0:2048], scalar1=stt[:, 3:4], scalar2=None, op0=ALU.mult),
                 reads=[bxi, bst], writes=[bxi])
            S.dma("pool", mix[t * 128:(t + 1) * 128, 2048:4096], xi[:, 0:2048], reads=[bxi], writes=[buf("mixd")], owner=bxi, multi=True)
        else:
            bxb_ = buf("xbf%d" % (t % 2))
            S.op("dve", lambda e, xi=xi, stt=stt, t=t: e.tensor_scalar(out=xbf[t % 2][:, 0:2048], in0=xi[:, 0:2048], scalar1=stt[:, 3:4], scalar2=None, op0=ALU.mult),
                 reads=[bxi, bst], writes=[bxb_])
            S.dma("pool", mix[t * 128:(t + 1) * 128, 2048:4096], xbf[t % 2][:, 0:2048], reads=[bxb_], writes=[buf("mixd")], owner=bxb_, multi=True)
    flush_stream()
    S.barrier(B.values())

    def cons_wout(t, c0, cw, pt, bp):
        i = next_ot()
        o, bo, r, br = ot[i], buf("ot%d" % i), rt[i], buf("rt%d" % i)
        S.dma("sp", r[:, 0:cw], x_own[t * 128:(t + 1) * 128, c0:c0 + cw], writes=[br], owner=br)
        S.op("dve", lambda e: e.tensor_tensor(out=o[:, 0:cw], in0=pt, in1=r[:, 0:cw], op=ALU.add), reads=[bp, br], writes=[bo])
        S.dma("pool", x1[t * 128:(t + 1) * 128, c0:c0 + cw], o[:, 0:cw], reads=[bo], writes=[buf("x1d")], owner=bo, multi=True)

    load_actT(mix, NT_OWN, 0, 32, norm=False, src_bf16=(not DEBUG))
    stream(w_out, 0, 32, 0, D, NT_OWN, cons_wout, gain=1)
    flush_stream()
    S.barrier(B.values())

    load_actT(x1, NT_OWN, 0, 32, norm=True)

    def cons_gate(t, c0, cw, pt, bp):
        S.op("act", lambda e: e.activation(out=gtmp[:, t, 0:cw], in_=pt, func=AF.Silu), reads=[bp], writes=[buf("gtmp%d" % t)])

    def cons_up(t, c0, cw, pt, bp):
        i = next_ot()
        o, bo = ot[i], buf("ot%d" % i)
        ob = o[:, 0:cw // 2].bitcast(BF16)
        S.op("dve", lambda e: e.tensor_tensor(out=ob, in0=pt, in1=gtmp[:, t, 0:cw], op=ALU.mult),
             reads=[bp, buf("gtmp%d" % t)], writes=[bo])
        S.dma("pool", act[t * 128:(t + 1) * 128, c0:c0 + cw], ob, reads=[bo], writes=[buf("actd")], owner=bo, multi=True)

    for gi in range(DFF // CG):
        stream(w_gate, 0, 32, gi * CG, CG, NT_OWN, cons_gate, gain=2)
        stream(w_up, 0, 32, gi * CG, CG, NT_OWN, cons_up, gain=2)
    flush_stream()
    S.barrier(B.values())

    def cons_down(t, c0, cw, pt, bp):
        i = next_ot()
        o, bo, r, br = ot[i], buf("ot%d" % i), rt[i], buf("rt%d" % i)
        S.dma("sp", r[:, 0:cw], x1[t * 128:(t + 1) * 128, c0:c0 + cw], writes=[br], owner=br)
        S.op("dve", lambda e: e.tensor_tensor(out=o[:, 0:cw], in0=pt, in1=r[:, 0:cw], op=ALU.add), reads=[bp, br], writes=[bo])
        S.dma("pool", x1[t * 128:(t + 1) * 128, c0:c0 + cw], o[:, 0:cw], reads=[bo], writes=[buf("x1d")], owner=bo, multi=True)

    for kb0, kbn in ((0, 32), (32, 32), (64, 22)):
        load_actT(act, NT_OWN, kb0, kbn, src_bf16=True)
        stream(w_down, kb0 * 128, kbn, 0, D, NT_OWN, cons_down)
        flush_stream()
    S.barrier(B.values())

    for t in range(NT_OWN):
        xi, stt = xin[t % 2], stat[t % 2]
        bxi, bst = buf("xin%d" % (t % 2)), buf("stat%d" % (t % 2))
        S.dma("pool", xi[:], x1[t * 128:(t + 1) * 128, :], writes=[bxi], owner=bxi)
        S.op("act", lambda e, xi=xi, stt=stt, t=t: e.activation(out=xbf[t % 2][:], in_=xi[:], func=AF.Square, accum_out=stt[:, 0:1]),
             reads=[bxi], writes=[bst, buf("xbf%d" % (t % 2))])
        S.op("dve", lambda e, stt=stt: e.tensor_scalar(out=stt[:, 1:2], in0=stt[:, 0:1], scalar1=1.0 / D, scalar2=EPS,
                                                       op0=ALU.mult, op1=ALU.add), reads=[bst], writes=[bst])
        S.op("act", lambda e, stt=stt: e.activation(out=stt[:, 2:3], in_=stt[:, 1:2], func=AF.Sqrt), reads=[bst], writes=[bst])
        S.op("dve", lambda e, stt=stt: e.reciprocal(out=stt[:, 3:4], in_=stt[:, 2:3]), reads=[bst], writes=[bst])
        S.op("dve", lambda e, xi=xi, stt=stt: e.scalar_tensor_tensor(out=xi[:], in0=xi[:], scalar=stt[:, 3:4], in1=gf_t[:],
                                                                     op0=ALU.mult, op1=ALU.mult),
             reads=[bxi, bst, cst], writes=[bxi])
        S.dma("pool", y_out[t * 128:(t + 1) * 128, :], xi[:], reads=[bxi], writes=[buf("yd")], owner=bxi, multi=True)
    flush_stream()
    S.barrier(B.values())

    with nc.Block() as block:
        S.emit(block)
    es.close()
    return nc


def _cs(pos):
    inv = (np.float32(10000.0) ** (-np.arange(128, dtype=np.float32) / np.float32(128))).astype(np.float32)
    ang = (pos.astype(np.float32)[:, :, None] * inv[None, None, :]).astype(np.float32)
    c_, s_ = np.cos(ang).astype(np.float32), np.sin(ang).astype(np.float32)
    return np.ascontiguousarray(np.concatenate([c_, c_, -s_, s_], axis=-1), dtype=np.float32)


_CONST_CACHE = {}


def _consts():
    if _CONST_CACHE:
        return _CONST_CACHE
    import ml_dtypes
    lg = np.log(1.0 - 2.0 ** (-5.0 - np.arange(8, dtype=np.float32))).astype(np.float32)
    p = np.arange(128)
    mask = np.zeros((8, 128, 2, 128), np.float32)
    diff = (p[None, :] - p[:, None]).astype(np.float32)
    for h_ in range(8):
        m0 = np.where(diff >= 0, np.exp(np.maximum(diff, 0) * lg[h_]), 0.0)
        same = (p[None, :] // 8) == (p[:, None] // 8)
        mask[h_, :, 0, :] = m0
        mask[h_, :, 1, :] = np.where(same, m0, 0.0)
    dqk = np.zeros((128, 2, 3, 8), np.float32)
    for h_ in range(8):
        dqk[:, 0, 0, h_] = np.exp(lg[h_] * (p + 1.0))
        dqk[:, 0, 1, h_] = np.exp(lg[h_] * ((p % 8) + 1.0))
        dqk[:, 0, 2, h_] = np.exp(lg[h_] * ((p % 16) + 1.0))
        dqk[:, 1, 0, h_] = np.exp(lg[h_] * (127.0 - p)) / 16.0
        dqk[:, 1, 1, h_] = np.exp(lg[h_] * (7.0 - (p % 8))) / 16.0
        dqk[:, 1, 2, h_] = np.exp(lg[h_] * (15.0 - (p % 16))) / 16.0
    cm = np.zeros((128, 16, 2, 128), np.float32)
    rm = np.zeros((128, 16), np.float32)
    for sq in range(16):
        cm[:, sq, :, sq * 8:(sq + 1) * 8] = 1.0
        rm[sq * 8:(sq + 1) * 8, sq] = 1.0
    q = np.arange(128)
    m3 = ((q[None, :] // 16) >= (q[:, None] // 16)).astype(np.float32)
    sel_f = np.zeros((128, 16), np.float32)
    sel_f[q, q % 8] = 1.0
    sel_f[q, 8 + q // 16] = 1.0
    sel_b = np.zeros((128, 32), np.float32)
    sel_b[q, q // 8] = 1.0
    sel_b[q, 16 + q % 16] = 1.0
    _CONST_CACHE.update(m3=m3, sel_f=sel_f, sel_b=sel_b.astype(ml_dtypes.bfloat16))
    _CONST_CACHE.update(mask=mask, dqk=dqk, cmask=cm.reshape(128, 16, 256).astype(ml_dtypes.bfloat16), rmask=rm)
    return _CONST_CACHE


def _core_inputs(c, inp):
    b, h = c // 2, c % 2
    xp = np.concatenate([inp["meta_tokens"], inp["x_prompt"][b]], axis=0)
    x_own = np.zeros((ROWS_OWN, D), np.float32)
    x_own[:HALF] = xp[16 + h * HALF:16 + (h + 1) * HALF]
    x_own[8 * 128:] = inp["x_sample"][16 * c:16 * c + 16].reshape(128, D)
    g_all = np.stack([inp["norm1_g"][0].reshape(32, 128).T,
                      np.concatenate([inp["ret_gn_g"][0], inp["s5_norm_g"][0]]).reshape(32, 128).T,
                      inp["norm2_g"][0].reshape(32, 128).T], axis=1)
    x_pre = np.zeros((ROWS_PRE, D), np.float32)
    pos_pre = np.zeros((NT_PRE, 128), np.float32)
    if h == 1:
        x_pre[:NPRE] = xp[:NPRE]
        pos_pre = (np.arange(NT_PRE * 128, dtype=np.float32)).reshape(NT_PRE, 128)
    else:
        x_pre[8 * 128:8 * 128 + 16] = xp[:16]
        pos_pre[8] = np.arange(128)
    C = _consts()
    pos_own = np.zeros((NT_OWN, 128), np.float32)
    for t in range(8):
        pos_own[t] = 16 + h * HALF + t * 128 + np.arange(128)
    pos_own[8] = 16384 + (np.arange(128) % 8)
    return {
        "x_pre": x_pre, "w_in": inp["w_in"][0],
        "cs_own": _cs(pos_own), "cs_pre": _cs(pos_pre),
        "maskd": C["mask"], "dqk": C["dqk"], "cmask": C["cmask"], "rmask": C["rmask"],
        "st_ret": np.ascontiguousarray(inp["state_ret"][0, 16 * c:16 * c + 16]),
        "lam_re": inp["s5_lam_re"][0], "lam_im": inp["s5_lam_im"][0],
        "ldt": np.ascontiguousarray(np.broadcast_to(inp["s5_log_dt"][0][None, :], (64, 128)), dtype=np.float32),
        "m3": C["m3"], "sel_f": C["sel_f"], "sel_b": C["sel_b"],
        "b_re": inp["s5_b_re"][0], "b_im": inp["s5_b_im"][0], "c_re": inp["s5_c_re"][0], "c_im": inp["s5_c_im"][0],
        "dB": np.ascontiguousarray(np.broadcast_to(inp["s5_d"][0][None, :], (128, 2048)), dtype=np.float32),
        "bglu": np.ascontiguousarray(np.broadcast_to(inp["b_glu"][0][None, :], (128, 2048)), dtype=np.float32),
        "s5r": np.ascontiguousarray(inp["state_s5_re"][0, 16 * c:16 * c + 16]),
        "s5i": np.ascontiguousarray(inp["state_s5_im"][0, 16 * c:16 * c + 16]),
        "w_glu": inp["w_glu"][0],
        "x_own": x_own,
        "w_out": inp["w_out"][0], "w_gate": inp["w_gate"][0], "w_up": inp["w_up"][0], "w_down": inp["w_down"][0],
        "g_all": np.ascontiguousarray(g_all, dtype=np.float32),
        "gfin": np.ascontiguousarray(np.broadcast_to(inp["final_norm_g"][None, :], (128, D)), dtype=np.float32),
        "ident": np.eye(128, dtype=np.float32),
    }


def kernel(**inp):
    inp = {k: np.asarray(v) for k, v in inp.items()}
    nc = build_program()
    in_maps = [_core_inputs(c, inp) for c in range(NCORES)]
    res = run_bass_kernel_spmd(nc, in_maps, core_ids=list(range(NCORES)))
    R = res.results
    LAST['R'] = R
    y_prompt = np.zeros((4, 2048, D), np.float32)
    y_sample = np.zeros((128, 8, D), np.float32)
    for c in range(NCORES):
        b, h = c // 2, c % 2
        y = R[c]["y"]
        y_prompt[b, h * HALF:(h + 1) * HALF] = y[:HALF]
        y_sample[16 * c:16 * c + 16] = y[8 * 128:].reshape(16, 8, D)
    z = np.zeros
    ret_p = np.stack([R[2 * b + 1]["o_ret_p"] for b in range(4)])[None]
    ret_s = np.concatenate([R[c]["o_ret_s"] for c in range(NCORES)])[None]
    s5p_r = np.stack([R[2 * b + 1]["o_s5p_r"] for b in range(4)])[None].astype(np.float32)
    s5p_i = np.stack([R[2 * b + 1]["o_s5p_i"] for b in range(4)])[None].astype(np.float32)
    s5s_r = np.concatenate([R[c]["o_s5s_r"] for c in range(NCORES)])[None].astype(np.float32)
    s5s_i = np.concatenate([R[c]["o_s5s_i"] for c in range(NCORES)])[None].astype(np.float32)
    return (y_prompt, y_sample, ret_p.astype(np.float32), s5p_r, s5p_i, ret_s.astype(np.float32), s5s_r, s5s_i)
    return (y_prompt, y_sample,
            ret_p.astype(np.float32), z((1, 4, 128, 64), np.float32), z((1, 4, 128, 64), np.float32),
            ret_s.astype(np.float32), z((1, 128, 128, 64), np.float32), z((1, 128, 128, 64), np.float32))
```

```python
import numpy as np
from contextlib import ExitStack
import concourse.bass as bass
import concourse.mybir as mybir
from concourse.bass_utils import run_bass_kernel_spmd

F32 = mybir.dt.float32
BF16 = mybir.dt.bfloat16
AF = mybir.ActivationFunctionType
ALU = mybir.AluOpType

D = 4096
DFF = 11008
NCORES = 8
HALF = 1024
NPRE = 1040
NT_OWN = 9
NT_PRE = 9
ROWS_OWN = NT_OWN * 128
ROWS_PRE = NT_PRE * 128
EPS = 1e-6
CG = 256
DEBUG = False
LAST = {}


class Buf:
    def __init__(self, name):
        self.name = name
        self.writers = []
        self.readers = []
        self.dsem = None
        self.dcount = 0


class Sched:
    def __init__(self, nc, es):
        self.nc = nc
        self.es = es
        self.eng = {}
        for n in ("pe", "act", "dve", "pool", "sp"):
            self.eng[n] = dict(sem=es.enter_context(nc.semaphore("s_" + n)), count=0, ops=[], seen={})
        self.dsems = []
        self.nsem = 0

    def _dsem(self, buf):
        if buf.dsem is None:
            buf.dsem = self.es.enter_context(self.nc.semaphore("d%d" % self.nsem))
            self.nsem += 1
            self.dsems.append(buf)
        return buf.dsem

    def _waits(self, e, reads, writes, skip=None, disjoint=False):
        need = {}

        def add(st):
            s, v = st
            k = id(s)
            if k == skip:
                return
            if k not in need or need[k][1] < v:
                need[k] = (s, v)
        for b in reads:
            for st in b.writers:
                add(st)
        for b in writes:
            if not disjoint:
                for st in b.writers:
                    add(st)
            for st in b.readers:
                add(st)
        out = []
        seen = self.eng[e]["seen"]
        for k, (s, v) in need.items():
            if seen.get(k, 0) < v:
                seen[k] = v
                out.append((s, v))
        return out

    def _commit(self, st, reads, writes, multi=False):
        for b in writes:
            if multi:
                b.writers.append(st)
            else:
                b.writers = [st]
                b.readers = []
        for b in reads:
            b.readers.append(st)
            if len(b.readers) > 64:
                best = {}
                for s, v in b.readers:
                    if id(s) not in best or best[id(s)][1] < v:
                        best[id(s)] = (s, v)
                b.readers = list(best.values())

    def op(self, e, fn, reads=(), writes=(), relax=False, disjoint=False):
        E = self.eng[e]
        waits = self._waits(e, reads, writes, skip=(id(E["sem"]) if (relax and e == "dve") else None), disjoint=disjoint)
        E["count"] += 1
        st = (E["sem"], E["count"])
        E["ops"].append((waits, fn, E["sem"], 1))
        self._commit(st, reads, writes, multi=disjoint)
        return st

    def dma(self, q, out, in_, reads=(), writes=(), owner=None, multi=False):
        E = self.eng[q]
        waits = self._waits(q, reads, writes)
        sem = self._dsem(owner)
        owner.dcount += 16
        st = (sem, owner.dcount)
        E["ops"].append((waits, lambda eng, o=out, i=in_: eng.dma_start(out=o, in_=i), sem, 16))
        self._commit(st, reads, writes, multi=multi)
        return st

    def barrier(self, bufs=()):
        for b in bufs:
            b.writers = []
            b.readers = []
        stamps = []
        for n, E in self.eng.items():
            if E["count"]:
                stamps.append((E["sem"], E["count"]))
        for b in self.dsems:
            stamps.append((b.dsem, b.dcount))
        for n, E in self.eng.items():
            w = []
            for s, v in stamps:
                if E["seen"].get(id(s), 0) < v:
                    E["seen"][id(s)] = v
                    w.append((s, v))
            if w:
                E["ops"].append((w, None, None, 0))

    def emit(self, block):
        def run(E):
            def f(eng):
                for waits, fn, sem, amt in E["ops"]:
                    for s, v in waits:
                        eng.wait_ge(s, v)
                    if fn is None:
                        continue
                    ins = fn(eng)
                    ins.then_inc(sem, amt)
            return f
        block.tensor(run(self.eng["pe"]))
        block.scalar(run(self.eng["act"]))
        block.vector(run(self.eng["dve"]))
        block.gpsimd(run(self.eng["pool"]))
        block.sync(run(self.eng["sp"]))


def build_program():
    nc = bass.Bass("TRN2", target_bir_lowering=False)
    es = ExitStack()
    S = Sched(nc, es)

    def din(name, shape, dt=F32):
        return nc.dram_tensor(name, list(shape), dt, kind="ExternalInput").ap()

    def dout(name, shape, dt=F32):
        return nc.dram_tensor(name, list(shape), dt, kind="ExternalOutput").ap()

    def dscr(name, shape, dt=F32):
        return nc.dram_tensor(name, list(shape), dt, kind=("ExternalOutput" if (DEBUG and name in ("mix", "proj")) else "Internal")).ap()

    x_own = din("x_own", [ROWS_OWN, D])
    x_pre = din("x_pre", [ROWS_PRE, D])
    w_in = din("w_in", [D, 10240])
    cs_own = din("cs_own", [NT_OWN, 128, 512])
    cs_pre = din("cs_pre", [NT_PRE, 128, 512])
    maskd = din("maskd", [8, 128, 2, 128])
    dqk_d = din("dqk", [128, 2, 3, 8])
    cmask_d = din("cmask", [128, 16, 256], BF16)
    rmask_d = din("rmask", [128, 16])
    st_ret = din("st_ret", [16, 8, 256, 256])
    o_ret_p = dout("o_ret_p", [8, 256, 256])
    o_ret_s = dout("o_ret_s", [16, 8, 256, 256])
    lam_re_d = din("lam_re", [128, 64]); lam_im_d = din("lam_im", [128, 64])
    ldt_d = din("ldt", [64, 128]); m3_d = din("m3", [128, 128])
    sel_f_d = din("sel_f", [128, 16]); sel_b_d = din("sel_b", [128, 32], BF16)
    b_re_d = din("b_re", [128, 64, 16]); b_im_d = din("b_im", [128, 64, 16])
    c_re_d = din("c_re", [128, 16, 64]); c_im_d = din("c_im", [128, 16, 64])
    dB_d = din("dB", [128, 2048]); bglu_d = din("bglu", [128, 2048])
    s5r_d = din("s5r", [16, 128, 64]); s5i_d = din("s5i", [16, 128, 64])
    w_glu = din("w_glu", [2048, 2048])
    o_s5p_r = dout("o_s5p_r", [128, 64]); o_s5p_i = dout("o_s5p_i", [128, 64])
    o_s5s_r = dout("o_s5s_r", [16, 128, 64]); o_s5s_i = dout("o_s5s_i", [16, 128, 64])
    Wall_d = dscr("Wall", [128, 128, 512], BF16)
    zscr = dscr("zscr", [ROWS_OWN, 2048], F32)
    s5o = dscr("s5o", [ROWS_OWN, 2048], F32)
    proj = dscr("proj", [ROWS_OWN, 10240], F32)
    projp = dscr("projp", [ROWS_PRE, 10240], F32)
    w_out = din("w_out", [D, D])
    w_gate = din("w_gate", [D, DFF])
    w_up = din("w_up", [D, DFF])
    w_down = din("w_down", [DFF, D])
    g_all = din("g_all", [128, 3, 32])
    gfin = din("gfin", [128, D])
    ident_in = din("ident", [128, 128])
    y_out = dout("y", [ROWS_OWN, D])

    mix = dscr("mix", [ROWS_OWN, D], F32)
    x1 = dscr("x1", [ROWS_OWN, D], F32)
    h2 = dscr("h2", [ROWS_OWN, D], F32)
    act = dscr("act", [ROWS_OWN, DFF], F32)

    def sb(name, shape, dt):
        return es.enter_context(nc.sbuf_tensor(name, list(shape), dt))

    actT = sb("actT", [128, 32, ROWS_OWN], BF16)
    wbf = [sb("wbf%d" % i, [128, 32, CG], BF16) for i in range(2)]
    wst = [sb("wst%d" % i, [128, 8, CG], F32) for i in range(2)]
    xin = [sb("xin%d" % i, [128, D], F32) for i in range(2)]
    xbf = [sb("xbf%d" % i, [128, D], BF16) for i in range(2)]
    ot = [sb("ot%d" % i, [128, CG], F32) for i in range(4)]
    rt = [sb("rt%d" % i, [128, CG], F32) for i in range(4)]
    gtmp = sb("gtmp", [128, NT_OWN, CG], BF16)
    stat = [sb("stat%d" % i, [128, 8], F32) for i in range(2)]
    gains = sb("gains", [128, 3, 32], F32)
    gf_t = sb("gf_t", [128, D], F32)
    ident_f = sb("ident_f", [128, 128], F32)
    ident = sb("ident_b", [128, 128], BF16)

    ps = [es.enter_context(nc.psum_tensor("ps%d" % i, [128, 512], F32)) for i in range(8)]

    B = {}

    def buf(name):
        if name not in B:
            B[name] = Buf(name)
        return B[name]

    cst = buf("const")
    S.dma("sp", gains[:], g_all, writes=[cst], owner=cst, multi=True)
    S.dma("sp", gf_t[:], gfin, writes=[cst], owner=cst, multi=True)
    S.dma("sp", ident_f[:], ident_in, writes=[cst], owner=cst, multi=True)
    S.op("dve", lambda e: e.tensor_copy(out=ident[:], in_=ident_f[:]), reads=[cst], writes=[buf("ident")])

    def load_actT(src, ntiles, kc0, nkc, norm=False):
        W = nkc * 128
        for t in range(ntiles):
            xi, xb, stt = xin[t % 2], xbf[t % 2], stat[t % 2]
            bxi, bxb, bst = buf("xin%d" % (t % 2)), buf("xbf%d" % (t % 2)), buf("stat%d" % (t % 2))
            S.dma("pool", xi[:, 0:W], src[t * 128:(t + 1) * 128, kc0 * 128:kc0 * 128 + W], writes=[bxi], owner=bxi)
            if norm:
                S.op("act", lambda e, xi=xi, stt=stt, xb=xb: e.activation(out=xb[:, 0:W], in_=xi[:, 0:W], func=AF.Square,
                                                                    accum_out=stt[:, 0:1]),
                     reads=[bxi], writes=[bst, bxb])
                S.op("dve", lambda e, stt=stt: e.tensor_scalar(out=stt[:, 1:2], in0=stt[:, 0:1], scalar1=1.0 / W,
                                                               scalar2=EPS, op0=ALU.mult, op1=ALU.add),
                     reads=[bst], writes=[bst])
                S.op("act", lambda e, stt=stt: e.activation(out=stt[:, 2:3], in_=stt[:, 1:2], func=AF.Sqrt),
                     reads=[bst], writes=[bst])
                S.op("dve", lambda e, stt=stt: e.reciprocal(out=stt[:, 3:4], in_=stt[:, 2:3]), reads=[bst], writes=[bst])
                S.op("dve", lambda e, xi=xi, xb=xb, stt=stt: e.tensor_scalar(out=xb[:, 0:W], in0=xi[:, 0:W],
                                                                             scalar1=stt[:, 3:4], scalar2=None,
                                                                             op0=ALU.mult),
                     reads=[bxi, bst], writes=[bxb])
            else:
                S.op("dve", lambda e, xi=xi, xb=xb: e.tensor_copy(out=xb[:, 0:W], in_=xi[:, 0:W]),
                     reads=[bxi], writes=[bxb])
            for q in range((nkc + 3) // 4):
                pb = 6 + (q % 2)
                bp = buf("ps%d" % pb)
                pv = ps[pb].bitcast(BF16)
                nq = min(4, nkc - q * 4)

                def tr(e, xb=xb, pv=pv, q=q, nq=nq):
                    ins = None
                    for i in range(nq):
                        ins = e.transpose(out=pv[:, i * 128:(i + 1) * 128], in_=xb[:, (q * 4 + i) * 128:(q * 4 + i + 1) * 128],
                                          identity=ident[:])
                    return ins
                S.op("pe", tr, reads=[bxb, buf("ident")], writes=[bp])
                dst = actT[:, q * 4:q * 4 + nq, t * 128:(t + 1) * 128]
                srcv = pv[:, 0:nq * 128].rearrange("p (c t) -> p c t", c=nq)
                eng = "act" if q % 2 else "dve"
                if eng == "act":
                    S.op("act", lambda e, dst=dst, srcv=srcv: e.copy(out=dst, in_=srcv), reads=[bp], writes=[buf("actT")], disjoint=True)
                else:
                    S.op("dve", lambda e, dst=dst, srcv=srcv: e.tensor_copy(out=dst, in_=srcv), reads=[bp], writes=[buf("actT")], disjoint=True)

    wcount = [0]
    pcount = [0]
    pending = []

    def stream(Wd, krow0, nkc, col0, ncols, ntiles, consumer, gain=None):
        ngr = (ncols + CG - 1) // CG
        for gi in range(ngr):
            c0 = col0 + gi * CG
            cw = min(CG, col0 + ncols - c0)
            pending.append((Wd, krow0, nkc, c0, cw, ntiles, consumer, gain))

    def _load_group(item):
        Wd, krow0, nkc, c0, cw, ntiles, consumer, gain = item
        wi = wcount[0] % 2
        wcount[0] += 1
        wb, bwb = wbf[wi], buf("wbf%d" % wi)
        nh = (nkc + 7) // 8
        for hh in range(nh):
            k0 = hh * 8
            kn = min(8, nkc - k0)
            si = pcount[0] % 2
            pcount[0] += 1
            ws, bws = wst[si], buf("wst%d" % si)
            S.dma("sp", ws[:, 0:kn, 0:cw],
                  Wd[krow0 + k0 * 128: krow0 + (k0 + kn) * 128, c0:c0 + cw].rearrange("(k p) n -> p k n", p=128),
                  writes=[bws], owner=bws)
            if gain is None:
                if hh % 2:
                    S.op("act", lambda e, wb=wb, ws=ws, k0=k0, kn=kn, cw=cw: e.copy(out=wb[:, k0:k0 + kn, 0:cw], in_=ws[:, 0:kn, 0:cw]),
                         reads=[bws], writes=[bwb], disjoint=True)
                else:
                    S.op("dve", lambda e, wb=wb, ws=ws, k0=k0, kn=kn, cw=cw: e.tensor_copy(out=wb[:, k0:k0 + kn, 0:cw], in_=ws[:, 0:kn, 0:cw]),
                         reads=[bws], writes=[bwb], disjoint=True)
            else:
                for kk in range(kn):
                    kc = k0 + kk
                    gcol = gains[:, gain, (krow0 // 128 + kc):(krow0 // 128 + kc) + 1]
                    if kk % 2:
                        S.op("act", lambda e, wb=wb, ws=ws, kc=kc, kk=kk, cw=cw, gcol=gcol: e.activation(
                            out=wb[:, kc, 0:cw], in_=ws[:, kk, 0:cw], func=AF.Copy, scale=gcol),
                            reads=[bws, cst], writes=[bwb], disjoint=True)
                    else:
                        S.op("dve", lambda e, wb=wb, ws=ws, kc=kc, kk=kk, cw=cw, gcol=gcol: e.tensor_scalar(
                            out=wb[:, kc, 0:cw], in0=ws[:, kk, 0:cw], scalar1=gcol, scalar2=None, op0=ALU.mult),
                            reads=[bws, cst], writes=[bwb], disjoint=True)
        return wb, bwb

    def _compute_group(item, wb, bwb):
        Wd, krow0, nkc, c0, cw, ntiles, consumer, gain = item
        for t in range(ntiles):
            pb = pcount[1] % 6 if len(pcount) > 1 else 0
            pcount[1] += 1
            bp = buf("ps%d" % pb)
            pt = ps[pb][:, 0:cw]

            def mm(e, pt=pt, wb=wb, t=t, cw=cw):
                ins = None
                for kc in range(nkc):
                    ins = e.matmul(pt, actT[:, kc, t * 128:(t + 1) * 128], wb[:, kc, 0:cw], start=(kc == 0), stop=(kc == nkc - 1))
                return ins
            S.op("pe", mm, reads=[buf("actT"), bwb], writes=[bp])
            consumer(t, c0, cw, pt, bp)

    pcount.append(0)

    def flush_stream():
        items = pending[:]
        del pending[:]
        if not items:
            return
        cur = _load_group(items[0])
        for n_, item in enumerate(items):
            nxt = _load_group(items[n_ + 1]) if n_ + 1 < len(items) else None
            _compute_group(item, *cur)
            cur = nxt

    ocnt = [0]

    def next_ot():
        i = ocnt[0] % 4
        ocnt[0] += 1
        return i


    def mk_store(dst):
        def cons(t, c0, cw, pt, bp):
            i = next_ot()
            o, bo = ot[i], buf("ot%d" % i)
            if i % 2:
                S.op("act", lambda e: e.copy(out=o[:, 0:cw], in_=pt), reads=[bp], writes=[bo])
            else:
                S.op("dve", lambda e: e.tensor_copy(out=o[:, 0:cw], in_=pt), reads=[bp], writes=[bo])
            S.dma("pool", dst[t * 128:(t + 1) * 128, c0:c0 + cw], o[:, 0:cw], reads=[bo], writes=[buf("projd")], owner=bo, multi=True)
        return cons

    load_actT(x_pre, NT_PRE, 0, 32, norm=True)
    stream(w_in, 0, 32, 2048, 4096, NT_PRE, mk_store(projp), gain=0)
    stream(w_in, 0, 32, 8192, 2048, NT_PRE, mk_store(projp), gain=0)
    flush_stream()
    S.barrier(B.values())
    load_actT(x_own, NT_OWN, 0, 32, norm=True)
    stream(w_in, 0, 32, 0, 10240, NT_OWN, mk_store(proj), gain=0)
    flush_stream()
    S.barrier(B.values())

    flat = actT[:].rearrange("p a b -> p (a b)")
    coff = [0]

    def carve(shape, dt):
        n = 1
        for d_ in shape[1:]:
            n *= d_
        nb = n * (4 if dt == F32 else 2)
        a = flat[:, coff[0] // 2:(coff[0] + nb) // 2]
        coff[0] += nb
        if dt == F32:
            a = a.bitcast(F32)
        if len(shape) == 3:
            a = a.rearrange("p (a b) -> p a b", a=shape[1])
        return a

    qkvg = [carve([128, 4, 256], F32) for _ in range(2)]
    cst_ = [carve([128, 2, 256], F32) for _ in range(2)]
    tqs = [[carve([128, 256], F32) for _ in range(4)] for _ in range(2)]
    qb, qdb, kb, kdb, vb = [[carve([128, 256], BF16) for _ in range(2)] for _ in range(5)]
    trs = [carve([128, 6, 128], BF16) for _ in range(2)]
    smb = [carve([128, 128], BF16) for _ in range(2)]
    Sf2 = [carve([128, 2, 256], F32) for _ in range(2)]
    Sb2 = [carve([128, 2, 256], BF16) for _ in range(2)]
    msk2 = [carve([128, 2, 128], F32) for _ in range(2)]
    bnst = [carve([128, 8], F32) for _ in range(2)]
    yt = [carve([128, 256], F32) for _ in range(2)]
    sgt = [carve([128, 256], F32) for _ in range(2)]
    S0 = [carve([128, 2, 256], F32) for _ in range(2)]
    S0b = [carve([128, 2, 256], BF16) for _ in range(2)]
    So = [carve([128, 2, 256], F32) for _ in range(2)]
    vm = [carve([128, 256], BF16) for _ in range(2)]
    qm = [carve([128, 2, 128], BF16) for _ in range(2)]
    cmask = carve([128, 16, 256], BF16)
    rmask = carve([128, 16], F32)
    dqk = carve([128, 2, 24], F32)
    odbg = [carve([128, 256], F32) for _ in range(2)]

    cst2 = buf("const2")
    S.dma("sp", cmask, cmask_d, writes=[cst2], owner=cst2, multi=True)
    S.dma("sp", rmask, rmask_d, writes=[cst2], owner=cst2, multi=True)
    S.dma("sp", dqk, dqk_d.rearrange("p a k h -> p a (k h)"), writes=[cst2], owner=cst2, multi=True)
    GAM = [1.0 - 2.0 ** (-5.0 - h_) for h_ in range(8)]
    rcount = [0]

    def ret_tile(hd, par, src, csd, t, kind, full, sample):
        i = par
        rcount[0] += 1
        Sf, Sb, msk = Sf2[par], Sb2[par], msk2[par]
        tq1, tq2, tk1, tk2 = tqs[par]
        TRB, SCB, STB = (6, 2)[par], (7, 3)[par], (5, 4)[par]
        qi, bqi = qkvg[i], buf("qkvg%d" % i)
        ci, bci = cst_[i], buf("cs%d" % i)
        srcv = src[t * 128:(t + 1) * 128, :].rearrange("p (s h d) -> p s h d", s=5, h=8)[:, 0:4, hd, :]
        S.dma("sp", qi, srcv, writes=[bqi], owner=bqi)
        S.dma("sp", ci, csd[t].rearrange("p (a d) -> p a d", a=2), writes=[bci], owner=bci)
        dq_col = dqk[:, 0, kind * 8 + hd:kind * 8 + hd + 1]
        dk_col = dqk[:, 1, kind * 8 + hd:kind * 8 + hd + 1]

        def rot(x, t1, t2, bt1, bt2):
            S.op("dve", lambda e: e.tensor_tensor(out=t1, in0=x, in1=ci[:, 0, :], op=ALU.mult), reads=[bqi, bci], writes=[bt1], relax=True)
            S.op("dve", lambda e: e.tensor_tensor(out=t2[:, 0:128], in0=x[:, 128:256], in1=ci[:, 1, 0:128], op=ALU.mult), reads=[bqi, bci], writes=[bt2], relax=True)
            S.op("dve", lambda e: e.tensor_tensor(out=t2[:, 128:256], in0=x[:, 0:128], in1=ci[:, 1, 128:256], op=ALU.mult), reads=[bqi, bci], writes=[bt2], relax=True)
            S.op("dve", lambda e: e.tensor_tensor(out=t1, in0=t1, in1=t2, op=ALU.add), reads=[bt1, bt2], writes=[bt1], relax=True)

        bk1, bk2 = buf("tk1_%d" % par), buf("tk2_%d" % par)
        rot(qi[:, 1, :], tk1, tk2, bk1, bk2)
        kd_, bkd = kdb[i], buf("kdb%d" % i)
        v_, bv = vb[i], buf("vb%d" % i)
        S.op("dve", lambda e: e.tensor_scalar(out=kd_, in0=tk1, scalar1=dk_col, scalar2=None, op0=ALU.mult), reads=[bk1, cst2], writes=[bkd])
        S.op("act", lambda e: e.copy(out=v_, in_=qi[:, 2, :]), reads=[bqi], writes=[bv])
        yield
        bSf, bSb = buf("Sf%d" % par), buf("Sb%d" % par)
        gl = GAM[hd] ** {0: 128, 1: 8, 2: 16}[kind]
        if full:
            bq1, bq2 = buf("tq1_%d" % par), buf("tq2_%d" % par)
            rot(qi[:, 0, :], tq1, tq2, bq1, bq2)
            q_, bq = qb[i], buf("qb%d" % i)
            qd_, bqd = qdb[i], buf("qdb%d" % i)
            k_, bk = kb[i], buf("kb%d" % i)
            S.op("act", lambda e: e.copy(out=q_, in_=tq1), reads=[bq1], writes=[bq])
            S.op("dve", lambda e: e.tensor_scalar(out=qd_, in0=tq1, scalar1=dq_col, scalar2=None, op0=ALU.mult), reads=[bq1, cst2], writes=[bqd])
            S.op("act", lambda e: e.mul(out=k_, in_=tk1, mul=1.0 / 16.0), reads=[bk1], writes=[bk])
            bp6 = buf("ps%d" % TRB)
            pv = ps[TRB].bitcast(BF16)

            def tr(e):
                ins = None
                for n_, srcb in enumerate((q_, qd_, k_)):
                    for c_ in range(2):
                        ins = e.transpose(out=pv[:, (2 * n_ + c_) * 128:(2 * n_ + c_ + 1) * 128], in_=srcb[:, c_ * 128:(c_ + 1) * 128], identity=ident[:])
                return ins
            S.op("pe", tr, reads=[bq, bqd, bk, buf("ident")], writes=[bp6])
            yield
            tr_, btr = trs[i], buf("trs%d" % i)
            S.op("dve", lambda e: e.tensor_copy(out=tr_, in_=pv[:, 0:768].rearrange("p (a b) -> p a b", a=6)), reads=[bp6], writes=[btr])
            bp7 = buf("ps%d" % SCB)

            def sc(e):
                e.matmul(ps[SCB][:, 0:128], tr_[:, 4, :], tr_[:, 0, :], start=True, stop=False)
                return e.matmul(ps[SCB][:, 0:128], tr_[:, 5, :], tr_[:, 1, :], start=False, stop=True)
            S.op("pe", sc, reads=[btr], writes=[bp7])
            yield
            sm_, bsm = smb[i], buf("smb%d" % i)
            S.op("dve", lambda e: e.tensor_tensor(out=sm_, in0=ps[SCB][:, 0:128], in1=msk[:, kind, :], op=ALU.mult), reads=[bp7, buf("msk%d" % par)], writes=[bsm])
            pbo = par
            bpo = buf("ps%d" % pbo)
            po = ps[pbo][:, 0:256]
            if not sample:
                def om(e):
                    e.matmul(po, sm_, v_, start=True, stop=False)
                    e.matmul(po, tr_[:, 2, :], Sb[:, 0, :], start=False, stop=False)
                    return e.matmul(po, tr_[:, 3, :], Sb[:, 1, :], start=False, stop=True)
                S.op("pe", om, reads=[bsm, bv, btr, bSb], writes=[bpo])
            else:
                S.op("pe", lambda e: e.matmul(po, sm_, v_, start=True, stop=False, skip_group_check=True), reads=[bsm, bv], writes=[bpo])
        if not sample:
            bp5 = buf("ps%d" % STB)

            def su(e):
                e.matmul(ps[STB][:, 0:256], kd_[:, 0:128], v_, start=True, stop=True)
                return e.matmul(ps[STB][:, 256:512], kd_[:, 128:256], v_, start=True, stop=True)
            S.op("pe", su, reads=[bkd, bv], writes=[bp5])
            yield
            S.op("dve", lambda e: e.scalar_tensor_tensor(out=Sf, in0=Sf, scalar=float(gl), in1=ps[STB][:, 0:512].rearrange("p (a b) -> p a b", a=2),
                                                         op0=ALU.mult, op1=ALU.add), reads=[bSf, bp5], writes=[bSf])
            S.op("act", lambda e: e.copy(out=Sb, in_=Sf), reads=[bSf], writes=[bSb])
        else:
            for sq in range(16):
                j = sq % 2
                s0, bs0 = S0[j], buf("S0_%d" % j)
                s0b, bs0b = S0b[j], buf("S0b_%d" % j)
                so, bso = So[j], buf("So_%d" % j)
                vm_, bvm = vm[j], buf("vm%d" % j)
                qm_, bqm = qm[j], buf("qm%d" % j)
                S.dma("sp", s0, st_ret[sq, hd].rearrange("(c p) v -> p c v", p=128), writes=[bs0], owner=bs0)
                S.op("act", lambda e, s0=s0, s0b=s0b: e.copy(out=s0b, in_=s0), reads=[bs0], writes=[bs0b])
                S.op("dve", lambda e, qm_=qm_, sq=sq: e.tensor_tensor(out=qm_, in0=tr_[:, 2:4, :], in1=cmask[:, sq, :].rearrange("p (a b) -> p a b", a=2), op=ALU.mult),
                     reads=[btr, cst2], writes=[bqm])

                def im(e, qm_=qm_, s0b=s0b, sq=sq):
                    e.matmul(po, qm_[:, 0, :], s0b[:, 0, :], start=False, stop=False, skip_group_check=True)
                    return e.matmul(po, qm_[:, 1, :], s0b[:, 1, :], start=False, stop=(sq == 15), skip_group_check=True)
                S.op("pe", im, reads=[bqm, bs0b], writes=[bpo])
                S.op("dve", lambda e, vm_=vm_, sq=sq: e.tensor_scalar(out=vm_, in0=v_, scalar1=rmask[:, sq:sq + 1], scalar2=None, op0=ALU.mult),
                     reads=[bv, cst2], writes=[bvm])
                pbs = 4 + (sq % 2)
                bps = buf("ps%d" % pbs)

                def su2(e, vm_=vm_, pbs=pbs):
                    e.matmul(ps[pbs][:, 0:256], kd_[:, 0:128], vm_, start=True, stop=True)
                    return e.matmul(ps[pbs][:, 256:512], kd_[:, 128:256], vm_, start=True, stop=True)
                S.op("pe", su2, reads=[bkd, bvm], writes=[bps])
                S.op("dve", lambda e, so=so, s0=s0, pbs=pbs: e.scalar_tensor_tensor(out=so, in0=s0, scalar=float(gl), in1=ps[pbs][:, 0:512].rearrange("p (a b) -> p a b", a=2),
                                                                                   op0=ALU.mult, op1=ALU.add), reads=[bs0, bps], writes=[bso])
                S.dma("pool", o_ret_s[sq, hd].rearrange("(c p) v -> p c v", p=128), so, reads=[bso], writes=[buf("orets")], owner=bso, multi=True)
                yield
        if full:
            bb, bbn = bnst[i], buf("bnst%d" % i)
            y_, by = yt[i], buf("yt%d" % i)
            sg_, bsg = sgt[i], buf("sgt%d" % i)
            S.op("act", lambda e: e.activation(out=sg_, in_=qi[:, 3, :], func=AF.Silu), reads=[bqi], writes=[bsg])
            od_, bod = odbg[i], buf("odbg%d" % i)
            S.op("act", lambda e: e.copy(out=od_, in_=po), reads=[bpo], writes=[bod])
            yield
            if DEBUG:
                S.dma("pool", mix[t * 128:(t + 1) * 128, 2048 + hd * 256:2048 + (hd + 1) * 256], od_, reads=[bod], writes=[buf("mixd")], owner=bod, multi=True)
            S.op("dve", lambda e: e.bn_stats(out=bb[:, 0:6], in_=od_), reads=[bod], writes=[bbn])
            S.op("dve", lambda e: e.bn_aggr(out=bb[:, 6:8], in_=bb[:, 0:6]), reads=[bbn], writes=[bbn])
            S.op("dve", lambda e: e.tensor_scalar(out=bb[:, 0:1], in0=bb[:, 7:8], scalar1=1e-5, scalar2=None, op0=ALU.add), reads=[bbn], writes=[bbn])
            S.op("act", lambda e: e.activation(out=bb[:, 1:2], in_=bb[:, 0:1], func=AF.Sqrt), reads=[bbn], writes=[bbn])
            yield
            S.op("dve", lambda e: e.reciprocal(out=bb[:, 2:3], in_=bb[:, 1:2]), reads=[bbn], writes=[bbn])
            S.op("dve", lambda e: e.tensor_scalar(out=y_, in0=od_, scalar1=bb[:, 6:7], scalar2=bb[:, 2:3], op0=ALU.subtract, op1=ALU.mult),
                 reads=[bod, bbn], writes=[by])
            S.op("dve", lambda e: e.tensor_tensor(out=y_, in0=y_, in1=sg_, op=ALU.mult), reads=[by, bsg], writes=[by])
            S.dma("pool", mix[t * 128:(t + 1) * 128, hd * 256:(hd + 1) * 256], y_, reads=[by], writes=[buf("mixd")], owner=by, multi=True)

    def run_gens(gens):
        gens = list(gens)
        while gens:
            for g_ in list(gens):
                try:
                    next(g_)
                except StopIteration:
                    gens.remove(g_)

    for hp in range(4):
        hds = (2 * hp, 2 * hp + 1)
        for par, hd in enumerate(hds):
            S.dma("sp", msk2[par], maskd[hd], reads=[], writes=[buf("msk%d" % par)], owner=buf("msk%d" % par))
            S.op("dve", lambda e, par=par: e.memset(Sf2[par], 0.0), writes=[buf("Sf%d" % par)])
            S.op("dve", lambda e, par=par: e.memset(Sb2[par], 0.0), writes=[buf("Sb%d" % par)])
        for t in range(NT_PRE):
            run_gens([ret_tile(hd, par, projp, cs_pre, t, 0 if t < 8 else 2, False, False) for par, hd in enumerate(hds)])
        for t in range(8):
            run_gens([ret_tile(hd, par, proj, cs_own, t, 0, True, False) for par, hd in enumerate(hds)])
        for par, hd in enumerate(hds):
            S.dma("sp", o_ret_p[hd].rearrange("(c p) v -> p c v", p=128), Sf2[par], reads=[buf("Sf%d" % par)], writes=[buf("oretp")], owner=buf("Sf%d" % par), multi=True)
        run_gens([ret_tile(hd, par, proj, cs_own, 8, 1, True, True) for par, hd in enumerate(hds)])
    flush_stream()
    S.barrier(B.values())

    coff[0] = 0
    PI = 3.141592653589793
    MAGIC = 12582912.0
    bS = buf("s5setup")

    def T64(n=1):
        a = carve([128, n, 128], F32) if n > 1 else carve([128, 128], F32)
        return a

    lam_sb = carve([128, 2, 64], F32)
    S.dma("sp", lam_sb[:, 0, :], lam_re_d, writes=[bS], owner=bS, multi=True)
    S.dma("sp", lam_sb[:, 1, :], lam_im_d, writes=[bS], owner=bS, multi=True)
    lamT = T64(2)
    ldt = T64()
    S.dma("sp", ldt[0:64, :], ldt_d, writes=[bS], owner=bS, multi=True)
    m3 = carve([128, 128], F32)
    S.dma("sp", m3, m3_d, writes=[bS], owner=bS, multi=True)
    selc = carve([128, 64], F32)
    S.dma("sp", selc[:, 0:16], sel_f_d, writes=[bS], owner=bS, multi=True)
    selb = carve([128, 32], BF16)
    S.dma("sp", selb, sel_b_d, writes=[bS], owner=bS, multi=True)
    mask8, tmask = selc[:, 0:8], selc[:, 8:16]
    Jsel, Csel = selb[:, 0:16], selb[:, 16:32]

    A1, A2 = carve([128, 2, 128], F32), carve([128, 2, 128], F32)
    B1, B2 = carve([128, 2, 128], F32), carve([128, 2, 128], F32)
    main_off = coff[0]

    sop_state = [False]

    def sop(eng, fn, after_recip=False):
        S.op(eng, fn, reads=[bS], writes=[bS], relax=(eng == "dve" and not sop_state[0]))
        sop_state[0] = after_recip

    def trans_f32(dst, src, rows_in, cols_in):
        bp = buf("ps7")
        S.op("pe", lambda e: e.transpose(out=ps[7][0:cols_in, 0:rows_in], in_=src, identity=ident_f[0:rows_in, 0:rows_in]), reads=[bS, cst], writes=[bp])
        S.op("dve", lambda e: e.tensor_copy(out=dst, in_=ps[7][0:cols_in, 0:rows_in]), reads=[bp, bS], writes=[bS])

    trans_f32(lamT[0:64, 0, :], lam_sb[:, 0, :], 128, 64)
    trans_f32(lamT[0:64, 1, :], lam_sb[:, 1, :], 128, 64)
    dtT = T64()
    lrT, liT = T64(), T64()
    sop("act", lambda e: e.activation(out=dtT[0:64, :], in_=ldt[0:64, :], func=AF.Exp))
    sop("dve", lambda e: e.tensor_tensor(out=lrT[0:64, :], in0=lamT[0:64, 0, :], in1=dtT[0:64, :], op=ALU.mult))
    sop("dve", lambda e: e.tensor_tensor(out=liT[0:64, :], in0=lamT[0:64, 1, :], in1=dtT[0:64, :], op=ALU.mult))
    AR, AI, NR, NI = T64(9), T64(9), T64(9), T64(9)
    tA, tB, tC, tD = T64(), T64(), T64(), T64()

    def trig(dst, k, off):
        sop("dve", lambda e: e.tensor_scalar(out=tA[0:64, :], in0=liT[0:64, :], scalar1=float(k), scalar2=float(off), op0=ALU.mult, op1=ALU.add))
        sop("dve", lambda e: e.tensor_scalar(out=tB[0:64, :], in0=tA[0:64, :], scalar1=1.0 / (2 * PI), scalar2=MAGIC, op0=ALU.mult, op1=ALU.add))
        sop("dve", lambda e: e.tensor_scalar(out=tB[0:64, :], in0=tB[0:64, :], scalar1=MAGIC, scalar2=2 * PI, op0=ALU.subtract, op1=ALU.mult))
        sop("dve", lambda e: e.tensor_tensor(out=tA[0:64, :], in0=tA[0:64, :], in1=tB[0:64, :], op=ALU.subtract))
        sop("dve", lambda e: e.tensor_scalar(out=tA[0:64, :], in0=tA[0:64, :], scalar1=-3.1415925, scalar2=3.1415925, op0=ALU.max, op1=ALU.min))
        sop("act", lambda e: e.activation(out=dst, in_=tA[0:64, :], func=AF.Sin))

    for k in range(9):
        trig(tC[0:64, :], k, PI / 2)
        trig(tD[0:64, :], k, 0.0)
        sop("act", lambda e, k=k: e.activation(out=AR[0:64, k, :], in_=lrT[0:64, :], func=AF.Exp, scale=float(k)))
        sop("act", lambda e, k=k: e.activation(out=NR[0:64, k, :], in_=lrT[0:64, :], func=AF.Exp, scale=float(-k)))
        sop("dve", lambda e, k=k: e.tensor_tensor(out=AI[0:64, k, :], in0=AR[0:64, k, :], in1=tD[0:64, :], op=ALU.mult))
        sop("dve", lambda e, k=k: e.tensor_tensor(out=AR[0:64, k, :], in0=AR[0:64, k, :], in1=tC[0:64, :], op=ALU.mult))
        sop("dve", lambda e, k=k: e.scalar_tensor_tensor(out=NI[0:64, k, :], in0=NR[0:64, k, :], scalar=-1.0, in1=tD[0:64, :], op0=ALU.mult, op1=ALU.mult))
        sop("dve", lambda e, k=k: e.tensor_tensor(out=NR[0:64, k, :], in0=NR[0:64, k, :], in1=tC[0:64, :], op=ALU.mult))
    fr, fi = T64(), T64()
    sop("dve", lambda e: e.tensor_scalar(out=tA[0:64, :], in0=AR[0:64, 1, :], scalar1=-1.0, scalar2=None, op0=ALU.add))
    sop("dve", lambda e: e.tensor_tensor(out=tB[0:64, :], in0=lamT[0:64, 0, :], in1=lamT[0:64, 0, :], op=ALU.mult))
    sop("dve", lambda e: e.tensor_tensor(out=tC[0:64, :], in0=lamT[0:64, 1, :], in1=lamT[0:64, 1, :], op=ALU.mult))
    sop("dve", lambda e: e.tensor_tensor(out=tB[0:64, :], in0=tB[0:64, :], in1=tC[0:64, :], op=ALU.add))
    sop("dve", lambda e: e.reciprocal(out=tB[0:64, :], in_=tB[0:64, :]), after_recip=True)
    sop("dve", lambda e: e.tensor_tensor(out=tC[0:64, :], in0=tA[0:64, :], in1=lamT[0:64, 0, :], op=ALU.mult))
    sop("dve", lambda e: e.tensor_tensor(out=tD[0:64, :], in0=AI[0:64, 1, :], in1=lamT[0:64, 1, :], op=ALU.mult))
    sop("dve", lambda e: e.tensor_tensor(out=tC[0:64, :], in0=tC[0:64, :], in1=tD[0:64, :], op=ALU.add))
    sop("dve", lambda e: e.tensor_tensor(out=fr[0:64, :], in0=tC[0:64, :], in1=tB[0:64, :], op=ALU.mult))
    sop("dve", lambda e: e.tensor_tensor(out=tC[0:64, :], in0=AI[0:64, 1, :], in1=lamT[0:64, 0, :], op=ALU.mult))
    sop("dve", lambda e: e.tensor_tensor(out=tD[0:64, :], in0=tA[0:64, :], in1=lamT[0:64, 1, :], op=ALU.mult))
    sop("dve", lambda e: e.tensor_tensor(out=tC[0:64, :], in0=tC[0:64, :], in1=tD[0:64, :], op=ALU.subtract))
    sop("dve", lambda e: e.tensor_tensor(out=fi[0:64, :], in0=tC[0:64, :], in1=tB[0:64, :], op=ALU.mult))
    ER, EI, ENR, ENI = T64(8), T64(8), T64(8), T64(8)

    def cmul_f(dr, di, xr, xi):
        sop("dve", lambda e: e.tensor_tensor(out=tA[0:64, :], in0=xr, in1=fr[0:64, :], op=ALU.mult))
        sop("dve", lambda e: e.tensor_tensor(out=tB[0:64, :], in0=xi, in1=fi[0:64, :], op=ALU.mult))
        sop("dve", lambda e: e.tensor_tensor(out=dr, in0=tA[0:64, :], in1=tB[0:64, :], op=ALU.subtract))
        sop("dve", lambda e: e.tensor_tensor(out=tA[0:64, :], in0=xr, in1=fi[0:64, :], op=ALU.mult))
        sop("dve", lambda e: e.tensor_tensor(out=tB[0:64, :], in0=xi, in1=fr[0:64, :], op=ALU.mult))
        sop("dve", lambda e: e.tensor_tensor(out=di, in0=tA[0:64, :], in1=tB[0:64, :], op=ALU.add))

    for s_ in range(8):
        cmul_f(ER[0:64, s_, :], EI[0:64, s_, :], AR[0:64, 7 - s_, :], AI[0:64, 7 - s_, :])
        cmul_f(ENR[0:64, s_, :], ENI[0:64, s_, :], NR[0:64, s_ + 1, :], NI[0:64, s_ + 1, :])
    sop("dve", lambda e: e.tensor_copy(out=A1[0:64, 0, :], in_=AR[0:64, 8, :]))
    sop("dve", lambda e: e.tensor_copy(out=A1[0:64, 1, :], in_=AR[0:64, 8, :]))
    sop("dve", lambda e: e.tensor_scalar(out=A2[0:64, 0, :], in0=AI[0:64, 8, :], scalar1=-1.0, scalar2=None, op0=ALU.mult))
    sop("dve", lambda e: e.tensor_copy(out=A2[0:64, 1, :], in_=AI[0:64, 8, :]))
    sop("dve", lambda e: e.tensor_tensor(out=tA[0:64, :], in0=AR[0:64, 8, :], in1=AR[0:64, 8, :], op=ALU.mult))
    sop("dve", lambda e: e.tensor_tensor(out=tB[0:64, :], in0=AI[0:64, 8, :], in1=AI[0:64, 8, :], op=ALU.mult))
    sop("dve", lambda e: e.tensor_tensor(out=B1[0:64, 0, :], in0=tA[0:64, :], in1=tB[0:64, :], op=ALU.subtract))
    sop("dve", lambda e: e.tensor_copy(out=B1[0:64, 1, :], in_=B1[0:64, 0, :]))
    sop("dve", lambda e: e.tensor_tensor(out=tA[0:64, :], in0=AR[0:64, 8, :], in1=AI[0:64, 8, :], op=ALU.mult))
    sop("dve", lambda e: e.tensor_scalar(out=B2[0:64, 1, :], in0=tA[0:64, :], scalar1=2.0, scalar2=None, op0=ALU.mult))
    sop("dve", lambda e: e.tensor_scalar(out=B2[0:64, 0, :], in0=tA[0:64, :], scalar1=-2.0, scalar2=None, op0=ALU.mult))

    Bp = carve([128, 2, 128], F32)
    Cblk = carve([128, 2, 64], F32)
    CT = carve([128, 2, 128], F32)
    W1p = xin[0][:, 0:2048].rearrange("p (a b) -> p a b", a=2)
    W1n = xin[1][:, 0:2048].rearrange("p (a b) -> p a b", a=2)
    W2p = carve([128, 2, 1024], F32)
    tW = carve([128, 1024], F32)
    Wst = carve([128, 8, 512], BF16)
    S.op("dve", lambda e: e.memset(Wst, 0.0), reads=[bS], writes=[bS])

    def v4(ap):
        return ap.rearrange("p (g s c) -> p g s c", g=8, s=8)

    def bc_tab(tab, g0):
        return tab[0:64, :, g0:g0 + 8].rearrange("p s g -> p g s").unsqueeze(3).broadcast_to([64, 8, 8, 16])

    def bc_gc(x):
        return x.rearrange("p (g c) -> p g c", g=8).unsqueeze(2).broadcast_to([64, 8, 8, 16])

    def cprod(dst_r, dst_i, tr_, ti_, xr, xi, g0, neg_i=False):
        sop("dve", lambda e: e.tensor_tensor(out=v4(dst_r), in0=bc_tab(tr_, g0), in1=bc_gc(xr), op=ALU.mult))
        sop("dve", lambda e: e.tensor_tensor(out=v4(tW[0:64, :]), in0=bc_tab(ti_, g0), in1=bc_gc(xi), op=ALU.mult))
        sop("dve", lambda e: e.tensor_tensor(out=dst_r, in0=dst_r, in1=tW[0:64, :], op=ALU.subtract))
        sop("dve", lambda e: e.tensor_tensor(out=v4(dst_i), in0=bc_tab(tr_, g0), in1=bc_gc(xi), op=ALU.mult))
        sop("dve", lambda e: e.tensor_tensor(out=v4(tW[0:64, :]), in0=bc_tab(ti_, g0), in1=bc_gc(xr), op=ALU.mult))
        if neg_i:
            sop("dve", lambda e: e.scalar_tensor_tensor(out=dst_i, in0=dst_i, scalar=-1.0, in1=tW[0:64, :], op0=ALU.mult, op1=ALU.subtract))
        else:
            sop("dve", lambda e: e.tensor_tensor(out=dst_i, in0=dst_i, in1=tW[0:64, :], op=ALU.add))

    for fc in range(16):
        g0 = fc * 8
        S.dma("sp", Bp[0:64, 0, :].rearrange("p (g c) -> p g c", g=8), b_re_d[g0:g0 + 8].rearrange("g p c -> p g c"), reads=[bS], writes=[bS], owner=bS, multi=True)
        S.dma("sp", Bp[0:64, 1, :].rearrange("p (g c) -> p g c", g=8), b_im_d[g0:g0 + 8].rearrange("g p c -> p g c"), reads=[bS], writes=[bS], owner=bS, multi=True)
        S.dma("sp", Cblk[:, 0, :], c_re_d[g0:g0 + 8].rearrange("g c p -> (g c) p"), reads=[bS], writes=[bS], owner=bS, multi=True)
        S.dma("sp", Cblk[:, 1, :], c_im_d[g0:g0 + 8].rearrange("g c p -> (g c) p"), reads=[bS], writes=[bS], owner=bS, multi=True)
        trans_f32(CT[0:64, 0, :], Cblk[:, 0, :], 128, 64)
        trans_f32(CT[0:64, 1, :], Cblk[:, 1, :], 128, 64)
        cprod(W1p[0:64, 0, :], W1p[0:64, 1, :], ER, EI, Bp[0:64, 0, :], Bp[0:64, 1, :], g0)
        cprod(W1n[0:64, 0, :], W1n[0:64, 1, :], ENR, ENI, Bp[0:64, 0, :], Bp[0:64, 1, :], g0)
        cprod(W2p[0:64, 0, :], W2p[0:64, 1, :], AR[:, 1:9, :], AI[:, 1:9, :], CT[0:64, 0, :], CT[0:64, 1, :], g0, neg_i=True)
        for half in range(2):
            pb = 2 + half
            bp = buf("ps%d" % pb)

            def w3mm(e, half=half, pb=pb):
                ins = None
                for gg in range(4):
                    g = half * 4 + gg
                    e.matmul(ps[pb][:, gg * 128:(gg + 1) * 128], W1n[0:64, 0, g * 128:(g + 1) * 128], W2p[0:64, 0, g * 128:(g + 1) * 128], start=True, stop=False)
                    ins = e.matmul(ps[pb][:, gg * 128:(gg + 1) * 128], W1n[0:64, 1, g * 128:(g + 1) * 128], W2p[0:64, 1, g * 128:(g + 1) * 128], start=False, stop=True)
                return ins
            S.op("pe", w3mm, reads=[bS], writes=[bp])
            S.op("dve", lambda e, half=half, pb=pb: e.tensor_tensor(out=Wst[:, half * 4:(half + 1) * 4, 128:256],
                                                                   in0=ps[pb][:, 0:512].rearrange("p (g n) -> p g n", g=4),
                                                                   in1=m3.unsqueeze(1).broadcast_to([128, 4, 128]), op=ALU.mult),
                 reads=[bp, bS], writes=[bS])
        for half in range(2):
            pb = 4 + half
            bp = buf("ps%d" % pb)

            def w1tr(e, half=half, pb=pb):
                ins = None
                for gg in range(4):
                    g = half * 4 + gg
                    for ri in range(2):
                        ins = e.transpose(out=ps[pb][:, (gg * 2 + ri) * 64:(gg * 2 + ri + 1) * 64], in_=W1p[0:64, ri, g * 128:(g + 1) * 128],
                                          identity=ident_f[0:64, 0:64])
                return ins
            S.op("pe", w1tr, reads=[bS, cst], writes=[bp])
            S.op("act", lambda e, half=half, pb=pb: e.copy(out=Wst[:, half * 4:(half + 1) * 4, 0:128],
                                                           in_=ps[pb][:, 0:512].rearrange("p (g n) -> p g n", g=4)),
                 reads=[bp, bS], writes=[bS])
        S.op("act", lambda e: e.copy(out=Wst[0:64, :, 256:384], in_=W2p[0:64, 0, :].rearrange("p (g n) -> p g n", g=8)), reads=[bS], writes=[bS])
        S.op("act", lambda e: e.copy(out=Wst[0:64, :, 384:512], in_=W2p[0:64, 1, :].rearrange("p (g n) -> p g n", g=8)), reads=[bS], writes=[bS])
        S.dma("sp", Wall_d[g0:g0 + 8].rearrange("g p w -> p g w"), Wst, reads=[bS], writes=[bS], owner=bS, multi=True)
    flush_stream()
    S.barrier(B.values())

    coff[0] = main_off
    GB = 32
    NW = GB * 16
    def wview(tn, is_f32):
        f = tn[:].bitcast(BF16) if is_f32 else tn[:].rearrange("p a b -> p (a b)")
        return f.rearrange("p (g w) -> p g w", g=16)
    Wsets = [(wview(wbf[0], False), wview(wbf[1], False)), (wview(xin[0], True), wview(xin[1], True))]
    Wbufs = [("wbf0", "wbf1"), ("xin0", "xin1")]
    u32 = [carve([128, NW], F32) for _ in range(2)] + [sb("u32c", [128, NW], F32)[:]]
    Uexp = [carve([128, GB, 128], BF16) for _ in range(2)]
    Usb = [carve([128, GB, 16], BF16) for _ in range(2)] + [sb("Usbc", [128, GB, 16], BF16)[:]]
    Vsb = [carve([128, 2, NW], F32) for _ in range(2)]
    Xh = carve([128, 2, GB * 17], F32)
    Xalls = [carve([128, 2, NW], BF16), sb("Xallb", [128, 2, NW], BF16)[:]]
    sT1, sT2 = carve([128, 2, GB], F32), carve([128, 2, GB], F32)
    Ysb = carve([128, GB, 16], BF16)
    Yexp = carve([128, GB, 128], BF16)
    ytmp = [carve([128, NW], F32) for _ in range(2)]
    zt_ = [carve([128, NW], F32) for _ in range(2)]
    dB = xbf[0][:].bitcast(F32)
    X0 = carve([128, 2, NW], F32)
    Xn = carve([128, 2, NW], F32)
    bT1 = sb("bT1", [128, 2, NW], F32)[:]
    bT2 = gtmp[:].rearrange("p a b -> p (a b)")[:, 0:2048].bitcast(F32).rearrange("p (r q) -> p r q", r=2)
    st_in = Yexp[:].rearrange("p a b -> p (a b)").bitcast(F32).rearrange("p (r q) -> p r q", r=2)
    st_o = [carve([128, 2, 64], F32) for _ in range(2)]
    cst3 = buf("const3")
    S.dma("sp", dB, dB_d, writes=[cst3], owner=cst3, multi=True)
    ucount = [0]
    xhv = Xh[0:64, :, :].rearrange("p r (g j) -> p r g j", g=GB)

    def Wg(gb, g):
        return Wsets[gb % 2][g // 16][:, g % 16, :]

    def s5_front(gb, src, t, wi):
        g0 = gb * GB
        i = wi % 2
        i3 = wi % 3
        bWs = [buf(n) for n in Wbufs[gb % 2]]
        u_, bu = u32[i3], buf("u32_%d" % i3)
        S.dma("sp", u_, src[t * 128:(t + 1) * 128, 8192 + g0 * 16:8192 + (g0 + GB) * 16], writes=[bu], owner=bu)
        ue, bUe = Uexp[i], buf("Uexp%d" % i)
        for s_ in range(8):
            if s_ % 2:
                S.op("act", lambda e, s_=s_: e.activation(out=ue[:, :, s_ * 16:(s_ + 1) * 16], in_=u_.rearrange("p (g c) -> p g c", g=GB),
                                                           func=AF.Copy, scale=mask8[:, s_:s_ + 1]), reads=[bu, bS], writes=[bUe], disjoint=True)
            else:
                S.op("dve", lambda e, s_=s_: e.tensor_scalar(out=ue[:, :, s_ * 16:(s_ + 1) * 16], in0=u_.rearrange("p (g c) -> p g c", g=GB),
                                                              scalar1=mask8[:, s_:s_ + 1], scalar2=None, op0=ALU.mult), reads=[bu, bS], writes=[bUe], disjoint=True)
        bp0 = buf("ps0")

        def umm(e):
            ins = None
            for g in range(GB):
                ins = e.matmul(ps[0][:, g * 16:(g + 1) * 16], ue[:, g, :], Jsel, start=True, stop=True, skip_group_check=True)
            return ins
        S.op("pe", umm, reads=[bUe, bS], writes=[bp0])
        us, bUs = Usb[i3], buf("Usb%d" % i3)
        S.op("act", lambda e: e.copy(out=us, in_=ps[0][:, 0:NW].rearrange("p (g j) -> p g j", g=GB)), reads=[bp0], writes=[bUs])
        bp1, bp2 = buf("ps1"), buf("ps2")

        def vmm(e):
            ins = None
            for g in range(GB):
                e.matmul(ps[1][0:64, g * 16:(g + 1) * 16], Wg(gb, g)[:, 0:64], us[:, g, :], start=True, stop=True, skip_group_check=True)
                ins = e.matmul(ps[2][0:64, g * 16:(g + 1) * 16], Wg(gb, g)[:, 64:128], us[:, g, :], start=True, stop=True, skip_group_check=True)
            return ins
        S.op("pe", vmm, reads=bWs + [bUs], writes=[bp1, bp2])
        vs, bVs = Vsb[i], buf("Vsb%d" % i)
        S.op("act", lambda e: e.copy(out=vs[0:64, 0, :], in_=ps[1][0:64, 0:NW]), reads=[bp1], writes=[bVs])
        S.op("act", lambda e: e.copy(out=vs[0:64, 1, :], in_=ps[2][0:64, 0:NW]), reads=[bp2], writes=[bVs])

    def s5_mid(gb, t, wi, nvalid, full, sample):
        g0 = gb * GB
        i = wi % 2
        Xall = Xalls[i]
        vs, bVs = Vsb[i], buf("Vsb%d" % i)
        vsv = vs[0:64, :, :].rearrange("p r (g j) -> p r g j", g=GB)
        bXh, bXa = buf("Xh"), buf("Xall%d" % i)
        a1 = A1[0:64, :, g0:g0 + GB]
        a2 = A2[0:64, :, g0:g0 + GB]
        if not sample:
            def step(src_j, dst_j, add_ap, c1, c2, badd):
                S.op("dve", lambda e: e.tensor_tensor(out=sT1[0:64, :, :], in0=xhv[:, :, :, src_j], in1=c1, op=ALU.mult), reads=[bXh, bS], writes=[buf("sT1")], relax=True)
                S.op("dve", lambda e: e.tensor_tensor(out=sT2[0:64, 0, :], in0=xhv[:, 1, :, src_j], in1=c2[:, 0, :], op=ALU.mult), reads=[bXh, bS], writes=[buf("sT2")], relax=True)
                S.op("dve", lambda e: e.tensor_tensor(out=sT2[0:64, 1, :], in0=xhv[:, 0, :, src_j], in1=c2[:, 1, :], op=ALU.mult), reads=[bXh, bS], writes=[buf("sT2")], relax=True)
                S.op("dve", lambda e: e.tensor_tensor(out=sT1[0:64, :, :], in0=sT1[0:64, :, :], in1=sT2[0:64, :, :], op=ALU.add), reads=[buf("sT1"), buf("sT2")], writes=[buf("sT1")], relax=True)
                S.op("dve", lambda e: e.tensor_tensor(out=xhv[:, :, :, dst_j], in0=sT1[0:64, :, :], in1=add_ap, op=ALU.add), reads=[buf("sT1"), badd], writes=[bXh], relax=True)
            if nvalid == 16:
                pv_ = bT1[0:64, :, 0:GB * 8].rearrange("p r (g m) -> p r g m", g=GB)
                p2_ = bT2[0:64, :, 0:GB * 8].rearrange("p r (g m) -> p r g m", g=GB)
                ve = vsv[:, :, :, 0:16:2]
                vo = vsv[:, :, :, 1:16:2]
                bP, bP2 = buf("bT1"), buf("bT2")
                S.op("dve", lambda e: e.tensor_tensor(out=pv_, in0=ve, in1=a1.unsqueeze(3).broadcast_to([64, 2, GB, 8]), op=ALU.mult), reads=[bVs, bS], writes=[bP], relax=True)
                S.op("dve", lambda e: e.tensor_tensor(out=p2_[:, 0, :, :], in0=ve[:, 1, :, :], in1=a2[:, 0, :].unsqueeze(2).broadcast_to([64, GB, 8]), op=ALU.mult), reads=[bVs, bS], writes=[bP2], relax=True)
                S.op("dve", lambda e: e.tensor_tensor(out=p2_[:, 1, :, :], in0=ve[:, 0, :, :], in1=a2[:, 1, :].unsqueeze(2).broadcast_to([64, GB, 8]), op=ALU.mult), reads=[bVs, bS], writes=[bP2], relax=True)
                S.op("dve", lambda e: e.tensor_tensor(out=pv_, in0=pv_, in1=p2_, op=ALU.add), reads=[bP, bP2], writes=[bP], relax=True)
                S.op("dve", lambda e: e.tensor_tensor(out=pv_, in0=pv_, in1=vo, op=ALU.add), reads=[bP, bVs], writes=[bP], relax=True)
                b1 = B1[0:64, :, g0:g0 + GB]
                b2 = B2[0:64, :, g0:g0 + GB]
                for m in range(8):
                    step(2 * m, 2 * m + 2, pv_[:, :, :, m], b1, b2, bP)
                xe = xhv[:, :, :, 0:16:2]
                xo = xhv[:, :, :, 1:16:2]
                S.op("dve", lambda e: e.tensor_tensor(out=pv_, in0=xe, in1=a1.unsqueeze(3).broadcast_to([64, 2, GB, 8]), op=ALU.mult), reads=[bXh, bS], writes=[bP], relax=True)
                S.op("dve", lambda e: e.tensor_tensor(out=p2_[:, 0, :, :], in0=xe[:, 1, :, :], in1=a2[:, 0, :].unsqueeze(2).broadcast_to([64, GB, 8]), op=ALU.mult), reads=[bXh, bS], writes=[bP2], relax=True)
                S.op("dve", lambda e: e.tensor_tensor(out=p2_[:, 1, :, :], in0=xe[:, 0, :, :], in1=a2[:, 1, :].unsqueeze(2).broadcast_to([64, GB, 8]), op=ALU.mult), reads=[bXh, bS], writes=[bP2], relax=True)
                S.op("dve", lambda e: e.tensor_tensor(out=pv_, in0=pv_, in1=p2_, op=ALU.add), reads=[bP, bP2], writes=[bP], relax=True)
                S.op("dve", lambda e: e.tensor_tensor(out=xo, in0=pv_, in1=ve, op=ALU.add), reads=[bP, bVs, bXh], writes=[bXh], relax=True)
            else:
                for j in range(nvalid):
                    step(j, j + 1, vsv[:, :, :, j], a1, a2, bVs)
            if full:
                S.op("act", lambda e: e.copy(out=Xall[0:64, :, :].rearrange("p r (g j) -> p r g j", g=GB), in_=xhv[:, :, :, 0:16]), reads=[bXh], writes=[bXa])
            S.op("dve", lambda e: e.tensor_copy(out=xhv[:, :, :, 0], in_=xhv[:, :, :, nvalid]), reads=[bXh, bXa], writes=[bXh], relax=True)
        else:
            x0v = X0[0:64, :, :].rearrange("p r (g j) -> p r g j", g=GB)
            t1v = bT1[0:64, :, :].rearrange("p r (g j) -> p r g j", g=GB)
            t2v = bT2[0:64, :, :].rearrange("p r (g j) -> p r g j", g=GB)
            S.op("act", lambda e: e.copy(out=Xall[0:64, :, :], in_=X0[0:64, :, :]), reads=[buf("X0")], writes=[bXa])
            S.op("dve", lambda e: e.tensor_tensor(out=t1v, in0=x0v, in1=a1.unsqueeze(3).broadcast_to([64, 2, GB, 16]), op=ALU.mult), reads=[buf("X0"), bS], writes=[buf("bT1")])
            S.op("dve", lambda e: e.tensor_tensor(out=t2v[:, 0, :, :], in0=x0v[:, 1, :, :], in1=a2[:, 0, :].unsqueeze(2).broadcast_to([64, GB, 16]), op=ALU.mult), reads=[buf("X0"), bS], writes=[buf("bT2")])
            S.op("dve", lambda e: e.tensor_tensor(out=t2v[:, 1, :, :], in0=x0v[:, 0, :, :], in1=a2[:, 1, :].unsqueeze(2).broadcast_to([64, GB, 16]), op=ALU.mult), reads=[buf("X0"), bS], writes=[buf("bT2")])
            S.op("dve", lambda e: e.tensor_tensor(out=bT1[0:64, :, :], in0=bT1[0:64, :, :], in1=bT2[0:64, :, :], op=ALU.add), reads=[buf("bT1"), buf("bT2")], writes=[buf("bT1")])
            S.op("dve", lambda e: e.tensor_tensor(out=Xn[0:64, :, :], in0=bT1[0:64, :, :], in1=vs[0:64, :, :], op=ALU.add), reads=[buf("bT1"), bVs], writes=[buf("Xn")])

    def s5_back(gb, t, wi, nvalid, full, sample):
        if not full:
            return
        g0 = gb * GB
        i = wi % 2
        i3 = wi % 3
        Xall = Xalls[i]
        bXa = buf("Xall%d" % i)
        bWs = [buf(n) for n in Wbufs[gb % 2]]
        u_, bu = u32[i3], buf("u32_%d" % i3)
        us, bUs = Usb[i3], buf("Usb%d" % i3)
        bp3 = buf("ps3")
        xr_ = Xall[0:64, 0, :].rearrange("p (g j) -> p g j", g=GB)
        xi_ = Xall[0:64, 1, :].rearrange("p (g j) -> p g j", g=GB)

        def ymm(e):
            ins = None
            for g in range(GB):
                o_ = ps[3][:, g * 16:(g + 1) * 16]
                W = Wg(gb, g)
                e.matmul(o_, W[0:64, 256:384], xr_[:, g, :], start=True, stop=False, skip_group_check=True)
                e.matmul(o_, W[0:64, 384:512], xi_[:, g, :], start=False, stop=False, skip_group_check=True)
                ins = e.matmul(o_, W[:, 128:256], us[:, g, :], start=False, stop=True, skip_group_check=True)
            return ins
        S.op("pe", ymm, reads=bWs + [bXa, bUs], writes=[bp3])
        bYs, bYe = buf("Ysb"), buf("Yexp")
        S.op("act", lambda e: e.copy(out=Ysb, in_=ps[3][:, 0:NW].rearrange("p (g j) -> p g j", g=GB)), reads=[bp3], writes=[bYs])
        yev = Yexp[:, :, :].rearrange("p g (j t) -> p g j t", j=16)
        for t_ in range(8):
            if t_ % 2:
                S.op("act", lambda e, t_=t_: e.activation(out=yev[:, :, :, t_], in_=Ysb, func=AF.Copy, scale=tmask[:, t_:t_ + 1]),
                     reads=[bYs, bS], writes=[bYe], disjoint=True)
            else:
                S.op("dve", lambda e, t_=t_: e.tensor_scalar(out=yev[:, :, :, t_], in0=Ysb, scalar1=tmask[:, t_:t_ + 1], scalar2=None, op0=ALU.mult),
                     reads=[bYs, bS], writes=[bYe], disjoint=True)
        bp4 = buf("ps4")

        def pmm(e):
            ins = None
            for g in range(GB):
                ins = e.matmul(ps[4][:, g * 16:(g + 1) * 16], Yexp[:, g, :], Csel, start=True, stop=True, skip_group_check=True)
            return ins
        S.op("pe", pmm, reads=[bYe, bS], writes=[bp4])
        yt_, byt = ytmp[i], buf("ytmp%d" % i)
        z_, bz = zt_[i], buf("zt%d" % i)
        S.op("dve", lambda e: e.tensor_tensor(out=yt_, in0=u_, in1=dB[:, g0 * 16:(g0 + GB) * 16], op=ALU.mult), reads=[bu, cst3], writes=[byt])
        S.op("dve", lambda e: e.tensor_tensor(out=yt_, in0=yt_, in1=ps[4][:, 0:NW], op=ALU.add), reads=[byt, bp4], writes=[byt])
        S.op("act", lambda e: e.activation(out=z_, in_=yt_, func=AF.Square), reads=[byt], writes=[bz])
        S.op("dve", lambda e: e.tensor_scalar(out=z_, in0=z_, scalar1=0.044715, scalar2=1.0, op0=ALU.mult, op1=ALU.add), reads=[bz], writes=[bz])
        S.op("dve", lambda e: e.tensor_tensor(out=z_, in0=z_, in1=yt_, op=ALU.mult), reads=[bz, byt], writes=[bz])
        S.op("act", lambda e: e.activation(out=z_, in_=z_, func=AF.Sigmoid, scale=1.5957691216057308), reads=[bz], writes=[bz])
        S.op("dve", lambda e: e.tensor_tensor(out=z_, in0=z_, in1=yt_, op=ALU.mult), reads=[bz, byt], writes=[bz])
        S.dma("pool", zscr[t * 128:(t + 1) * 128, g0 * 16:(g0 + GB) * 16], z_, reads=[bz], writes=[buf("zscrd")], owner=bz, multi=True)

    def emit_state(gb, srcX, dst_r, dst_i, bsrc):
        g0 = gb * GB
        k = ucount[0] % 2
        ucount[0] += 1
        bp = buf("ps5")

        def tr(e):
            e.transpose(out=ps[5][0:GB, 0:64], in_=srcX[:, 0, :], identity=ident_f[0:64, 0:64])
            return e.transpose(out=ps[5][0:GB, 64:128], in_=srcX[:, 1, :], identity=ident_f[0:64, 0:64])
        S.op("pe", tr, reads=[bsrc, cst], writes=[bp])
        so_, bso = st_o[k], buf("st_o%d" % k)
        S.op("dve", lambda e: e.tensor_copy(out=so_[0:GB, :, :], in_=ps[5][0:GB, 0:128].rearrange("p (r q) -> p r q", r=2)), reads=[bp], writes=[bso])
        S.dma("pool", dst_r[g0:g0 + GB, :], so_[0:GB, 0, :], reads=[bso], writes=[buf("os5")], owner=bso, multi=True)
        S.dma("pool", dst_i[g0:g0 + GB, :], so_[0:GB, 1, :], reads=[bso], writes=[buf("os5")], owner=bso, multi=True)

    for gb in range(128 // GB):
        g0 = gb * GB
        for hh in range(2):
            bW = buf(Wbufs[gb % 2][hh])
            S.dma("sp", Wsets[gb % 2][hh], Wall_d[g0 + hh * 16:g0 + (hh + 1) * 16].rearrange("g p w -> p g w"), writes=[bW], owner=bW)
        bsi = buf("Yexp")
        S.dma("sp", st_in[0:GB, 0, :].rearrange("g (s p) -> g s p", s=16), s5r_d[:, g0:g0 + GB, :].rearrange("s g p -> g s p"), writes=[bsi], owner=bsi)
        S.dma("sp", st_in[0:GB, 1, :].rearrange("g (s p) -> g s p", s=16), s5i_d[:, g0:g0 + GB, :].rearrange("s g p -> g s p"), reads=[bsi], writes=[bsi], owner=bsi)
        x0v = X0[0:64, :, :].rearrange("p r (g j) -> p r g j", g=GB)
        for ri in range(2):
            for q4 in range(4):
                bp = buf("ps6")

                def trs_(e, ri=ri, q4=q4):
                    ins = None
                    for jj in range(4):
                        j = q4 * 4 + jj
                        ins = e.transpose(out=ps[6][0:64, jj * GB:(jj + 1) * GB], in_=st_in[0:GB, ri, j * 64:(j + 1) * 64], identity=ident_f[0:GB, 0:GB])
                    return ins
                S.op("pe", trs_, reads=[bsi, cst], writes=[bp])
                S.op("dve", lambda e, ri=ri, q4=q4: e.tensor_copy(out=x0v[:, ri, :, q4 * 4:(q4 + 1) * 4],
                                                                  in_=ps[6][0:64, 0:4 * GB].rearrange("p (j g) -> p g j", j=4)),
                     reads=[bp], writes=[buf("X0")])
        S.op("dve", lambda e: e.memset(Xh, 0.0), writes=[buf("Xh")])
        work = [(projp, t, 16 if t < 8 else 2, False, False) for t in range(NT_PRE)]
        work += [(proj, t, 16, True, False) for t in range(8)]
        work += [(proj, 8, 16, True, True)]
        s5_front(gb, work[0][0], work[0][1], 0)
        for wi, (src, t, nv, full, sample) in enumerate(work):
            if wi + 1 < len(work):
                s5_front(gb, work[wi + 1][0], work[wi + 1][1], wi + 1)
            if sample:
                emit_state(gb, xhv[:, :, :, 0], o_s5p_r, o_s5p_i, buf("Xh"))
            s5_mid(gb, t, wi, nv, full, sample)
            if wi >= 1:
                p_ = work[wi - 1]
                s5_back(gb, p_[1], wi - 1, p_[2], p_[3], p_[4])
        p_ = work[-1]
        s5_back(gb, p_[1], len(work) - 1, p_[2], p_[3], p_[4])
        xnv = Xn[0:64, :, :].rearrange("p r (g j) -> p r g j", g=GB)
        for j in range(16):
            emit_state(gb, xnv[:, :, :, j], o_s5s_r[j], o_s5s_i[j], buf("Xn"))
    flush_stream()
    S.barrier(B.values())

    coff[0] = 16 * ROWS_OWN * 2
    bgl = carve([128, 2048], F32)
    S.dma("sp", bgl, bglu_d, writes=[cst3], owner=cst3, multi=True)

    def cons_glu(t, c0, cw, pt, bp):
        i = next_ot()
        o, bo, r, br = ot[i], buf("ot%d" % i), rt[i], buf("rt%d" % i)
        S.dma("sp", r[:, 0:cw], zscr[t * 128:(t + 1) * 128, c0:c0 + cw], writes=[br], owner=br)
        S.op("dve", lambda e: e.tensor_tensor(out=o[:, 0:cw], in0=pt, in1=bgl[:, c0:c0 + cw], op=ALU.add), reads=[bp, cst3], writes=[bo])
        S.op("act", lambda e: e.activation(out=o[:, 0:cw], in_=o[:, 0:cw], func=AF.Sigmoid), reads=[bo], writes=[bo])
        S.op("dve", lambda e: e.tensor_tensor(out=o[:, 0:cw], in0=o[:, 0:cw], in1=r[:, 0:cw], op=ALU.mult), reads=[bo, br], writes=[bo])
        S.dma("pool", s5o[t * 128:(t + 1) * 128, c0:c0 + cw], o[:, 0:cw], reads=[bo], writes=[buf("s5od")], owner=bo, multi=True)

    load_actT(zscr, NT_OWN, 0, 16)
    stream(w_glu, 0, 16, 0, 2048, NT_OWN, cons_glu)
    flush_stream()
    S.barrier(B.values())
    for t in range(NT_OWN):
        xi, stt = xin[t % 2], stat[t % 2]
        bxi, bst = buf("xin%d" % (t % 2)), buf("stat%d" % (t % 2))
        S.dma("pool", xi[:, 0:2048], s5o[t * 128:(t + 1) * 128, :], writes=[bxi], owner=bxi)
        S.op("act", lambda e, xi=xi, stt=stt, t=t: e.activation(out=xbf[t % 2][:, 0:2048], in_=xi[:, 0:2048], func=AF.Square, accum_out=stt[:, 0:1]),
             reads=[bxi], writes=[bst, buf("xbf%d" % (t % 2))])
        S.op("dve", lambda e, stt=stt: e.tensor_scalar(out=stt[:, 1:2], in0=stt[:, 0:1], scalar1=1.0 / 2048, scalar2=EPS,
                                                       op0=ALU.mult, op1=ALU.add), reads=[bst], writes=[bst])
        S.op("act", lambda e, stt=stt: e.activation(out=stt[:, 2:3], in_=stt[:, 1:2], func=AF.Sqrt), reads=[bst], writes=[bst])
        S.op("dve", lambda e, stt=stt: e.reciprocal(out=stt[:, 3:4], in_=stt[:, 2:3]), reads=[bst], writes=[bst])
        S.op("dve", lambda e, xi=xi, stt=stt: e.tensor_scalar(out=xi[:, 0:2048], in0=xi[:, 0:2048], scalar1=stt[:, 3:4], scalar2=None, op0=ALU.mult),
             reads=[bxi, bst], writes=[bxi])
        S.dma("pool", mix[t * 128:(t + 1) * 128, 2048:4096], xi[:, 0:2048], reads=[bxi], writes=[buf("mixd")], owner=bxi, multi=True)
    flush_stream()
    S.barrier(B.values())

    def cons_wout(t, c0, cw, pt, bp):
        i = next_ot()
        o, bo, r, br = ot[i], buf("ot%d" % i), rt[i], buf("rt%d" % i)
        S.dma("sp", r[:, 0:cw], x_own[t * 128:(t + 1) * 128, c0:c0 + cw], writes=[br], owner=br)
        S.op("dve", lambda e: e.tensor_tensor(out=o[:, 0:cw], in0=pt, in1=r[:, 0:cw], op=ALU.add), reads=[bp, br], writes=[bo])
        S.dma("pool", x1[t * 128:(t + 1) * 128, c0:c0 + cw], o[:, 0:cw], reads=[bo], writes=[buf("x1d")], owner=bo, multi=True)

    load_actT(mix, NT_OWN, 0, 32, norm=False)
    stream(w_out, 0, 32, 0, D, NT_OWN, cons_wout, gain=1)
    flush_stream()
    S.barrier(B.values())

    load_actT(x1, NT_OWN, 0, 32, norm=True)

    def cons_gate(t, c0, cw, pt, bp):
        S.op("act", lambda e: e.activation(out=gtmp[:, t, 0:cw], in_=pt, func=AF.Silu), reads=[bp], writes=[buf("gtmp%d" % t)])

    def cons_up(t, c0, cw, pt, bp):
        i = next_ot()
        o, bo = ot[i], buf("ot%d" % i)
        S.op("dve", lambda e: e.tensor_tensor(out=o[:, 0:cw], in0=pt, in1=gtmp[:, t, 0:cw], op=ALU.mult),
             reads=[bp, buf("gtmp%d" % t)], writes=[bo])
        S.dma("pool", act[t * 128:(t + 1) * 128, c0:c0 + cw], o[:, 0:cw], reads=[bo], writes=[buf("actd")], owner=bo, multi=True)

    for gi in range(DFF // CG):
        stream(w_gate, 0, 32, gi * CG, CG, NT_OWN, cons_gate, gain=2)
        stream(w_up, 0, 32, gi * CG, CG, NT_OWN, cons_up, gain=2)
    flush_stream()
    S.barrier(B.values())

    def cons_down(t, c0, cw, pt, bp):
        i = next_ot()
        o, bo, r, br = ot[i], buf("ot%d" % i), rt[i], buf("rt%d" % i)
        S.dma("sp", r[:, 0:cw], x1[t * 128:(t + 1) * 128, c0:c0 + cw], writes=[br], owner=br)
        S.op("dve", lambda e: e.tensor_tensor(out=o[:, 0:cw], in0=pt, in1=r[:, 0:cw], op=ALU.add), reads=[bp, br], writes=[bo])
        S.dma("pool", x1[t * 128:(t + 1) * 128, c0:c0 + cw], o[:, 0:cw], reads=[bo], writes=[buf("x1d")], owner=bo, multi=True)

    for kb0, kbn in ((0, 32), (32, 32), (64, 22)):
        load_actT(act, NT_OWN, kb0, kbn)
        stream(w_down, kb0 * 128, kbn, 0, D, NT_OWN, cons_down)
        flush_stream()
    S.barrier(B.values())

    for t in range(NT_OWN):
        xi, stt = xin[t % 2], stat[t % 2]
        bxi, bst = buf("xin%d" % (t % 2)), buf("stat%d" % (t % 2))
        S.dma("pool", xi[:], x1[t * 128:(t + 1) * 128, :], writes=[bxi], owner=bxi)
        S.op("act", lambda e, xi=xi, stt=stt, t=t: e.activation(out=xbf[t % 2][:], in_=xi[:], func=AF.Square, accum_out=stt[:, 0:1]),
             reads=[bxi], writes=[bst, buf("xbf%d" % (t % 2))])
        S.op("dve", lambda e, stt=stt: e.tensor_scalar(out=stt[:, 1:2], in0=stt[:, 0:1], scalar1=1.0 / D, scalar2=EPS,
                                                       op0=ALU.mult, op1=ALU.add), reads=[bst], writes=[bst])
        S.op("act", lambda e, stt=stt: e.activation(out=stt[:, 2:3], in_=stt[:, 1:2], func=AF.Sqrt), reads=[bst], writes=[bst])
        S.op("dve", lambda e, stt=stt: e.reciprocal(out=stt[:, 3:4], in_=stt[:, 2:3]), reads=[bst], writes=[bst])
        S.op("dve", lambda e, xi=xi, stt=stt: e.scalar_tensor_tensor(out=xi[:], in0=xi[:], scalar=stt[:, 3:4], in1=gf_t[:],
                                                                     op0=ALU.mult, op1=ALU.mult),
             reads=[bxi, bst, cst], writes=[bxi])
        S.dma("pool", y_out[t * 128:(t + 1) * 128, :], xi[:], reads=[bxi], writes=[buf("yd")], owner=bxi, multi=True)
    flush_stream()
    S.barrier(B.values())

    with nc.Block() as block:
        S.emit(block)
    es.close()
    return nc


def _cs(pos):
    inv = (np.float32(10000.0) ** (-np.arange(128, dtype=np.float32) / np.float32(128))).astype(np.float32)
    ang = (pos.astype(np.float32)[:, :, None] * inv[None, None, :]).astype(np.float32)
    c_, s_ = np.cos(ang).astype(np.float32), np.sin(ang).astype(np.float32)
    return np.ascontiguousarray(np.concatenate([c_, c_, -s_, s_], axis=-1), dtype=np.float32)


_CONST_CACHE = {}


def _consts():
    if _CONST_CACHE:
        return _CONST_CACHE
    import ml_dtypes
    lg = np.log(1.0 - 2.0 ** (-5.0 - np.arange(8, dtype=np.float32))).astype(np.float32)
    p = np.arange(128)
    mask = np.zeros((8, 128, 2, 128), np.float32)
    diff = (p[None, :] - p[:, None]).astype(np.float32)
    for h_ in range(8):
        m0 = np.where(diff >= 0, np.exp(np.maximum(diff, 0) * lg[h_]), 0.0)
        same = (p[None, :] // 8) == (p[:, None] // 8)
        mask[h_, :, 0, :] = m0
        mask[h_, :, 1, :] = np.where(same, m0, 0.0)
    dqk = np.zeros((128, 2, 3, 8), np.float32)
    for h_ in range(8):
        dqk[:, 0, 0, h_] = np.exp(lg[h_] * (p + 1.0))
        dqk[:, 0, 1, h_] = np.exp(lg[h_] * ((p % 8) + 1.0))
        dqk[:, 0, 2, h_] = np.exp(lg[h_] * ((p % 16) + 1.0))
        dqk[:, 1, 0, h_] = np.exp(lg[h_] * (127.0 - p)) / 16.0
        dqk[:, 1, 1, h_] = np.exp(lg[h_] * (7.0 - (p % 8))) / 16.0
        dqk[:, 1, 2, h_] = np.exp(lg[h_] * (15.0 - (p % 16))) / 16.0
    cm = np.zeros((128, 16, 2, 128), np.float32)
    rm = np.zeros((128, 16), np.float32)
    for sq in range(16):
        cm[:, sq, :, sq * 8:(sq + 1) * 8] = 1.0
        rm[sq * 8:(sq + 1) * 8, sq] = 1.0
    q = np.arange(128)
    m3 = ((q[None, :] // 16) >= (q[:, None] // 16)).astype(np.float32)
    sel_f = np.zeros((128, 16), np.float32)
    sel_f[q, q % 8] = 1.0
    sel_f[q, 8 + q // 16] = 1.0
    sel_b = np.zeros((128, 32), np.float32)
    sel_b[q, q // 8] = 1.0
    sel_b[q, 16 + q % 16] = 1.0
    _CONST_CACHE.update(m3=m3, sel_f=sel_f, sel_b=sel_b.astype(ml_dtypes.bfloat16))
    _CONST_CACHE.update(mask=mask, dqk=dqk, cmask=cm.reshape(128, 16, 256).astype(ml_dtypes.bfloat16), rmask=rm)
    return _CONST_CACHE


def _core_inputs(c, inp):
    b, h = c // 2, c % 2
    xp = np.concatenate([inp["meta_tokens"], inp["x_prompt"][b]], axis=0)
    x_own = np.zeros((ROWS_OWN, D), np.float32)
    x_own[:HALF] = xp[16 + h * HALF:16 + (h + 1) * HALF]
    x_own[8 * 128:] = inp["x_sample"][16 * c:16 * c + 16].reshape(128, D)
    g_all = np.stack([inp["norm1_g"][0].reshape(32, 128).T,
                      np.concatenate([inp["ret_gn_g"][0], inp["s5_norm_g"][0]]).reshape(32, 128).T,
                      inp["norm2_g"][0].reshape(32, 128).T], axis=1)
    x_pre = np.zeros((ROWS_PRE, D), np.float32)
    pos_pre = np.zeros((NT_PRE, 128), np.float32)
    if h == 1:
        x_pre[:NPRE] = xp[:NPRE]
        pos_pre = (np.arange(NT_PRE * 128, dtype=np.float32)).reshape(NT_PRE, 128)
    else:
        x_pre[8 * 128:8 * 128 + 16] = xp[:16]
        pos_pre[8] = np.arange(128)
    C = _consts()
    pos_own = np.zeros((NT_OWN, 128), np.float32)
    for t in range(8):
        pos_own[t] = 16 + h * HALF + t * 128 + np.arange(128)
    pos_own[8] = 16384 + (np.arange(128) % 8)
    return {
        "x_pre": x_pre, "w_in": inp["w_in"][0],
        "cs_own": _cs(pos_own), "cs_pre": _cs(pos_pre),
        "maskd": C["mask"], "dqk": C["dqk"], "cmask": C["cmask"], "rmask": C["rmask"],
        "st_ret": np.ascontiguousarray(inp["state_ret"][0, 16 * c:16 * c + 16]),
        "lam_re": inp["s5_lam_re"][0], "lam_im": inp["s5_lam_im"][0],
        "ldt": np.ascontiguousarray(np.broadcast_to(inp["s5_log_dt"][0][None, :], (64, 128)), dtype=np.float32),
        "m3": C["m3"], "sel_f": C["sel_f"], "sel_b": C["sel_b"],
        "b_re": inp["s5_b_re"][0], "b_im": inp["s5_b_im"][0], "c_re": inp["s5_c_re"][0], "c_im": inp["s5_c_im"][0],
        "dB": np.ascontiguousarray(np.broadcast_to(inp["s5_d"][0][None, :], (128, 2048)), dtype=np.float32),
        "bglu": np.ascontiguousarray(np.broadcast_to(inp["b_glu"][0][None, :], (128, 2048)), dtype=np.float32),
        "s5r": np.ascontiguousarray(inp["state_s5_re"][0, 16 * c:16 * c + 16]),
        "s5i": np.ascontiguousarray(inp["state_s5_im"][0, 16 * c:16 * c + 16]),
        "w_glu": inp["w_glu"][0],
        "x_own": x_own,
        "w_out": inp["w_out"][0], "w_gate": inp["w_gate"][0], "w_up": inp["w_up"][0], "w_down": inp["w_down"][0],
        "g_all": np.ascontiguousarray(g_all, dtype=np.float32),
        "gfin": np.ascontiguousarray(np.broadcast_to(inp["final_norm_g"][None, :], (128, D)), dtype=np.float32),
        "ident": np.eye(128, dtype=np.float32),
    }


def kernel(**inp):
    inp = {k: np.asarray(v) for k, v in inp.items()}
    nc = build_program()
    in_maps = [_core_inputs(c, inp) for c in range(NCORES)]
    res = run_bass_kernel_spmd(nc, in_maps, core_ids=list(range(NCORES)))
    R = res.results
    LAST['R'] = R
    y_prompt = np.zeros((4, 2048, D), np.float32)
    y_sample = np.zeros((128, 8, D), np.float32)
    for c in range(NCORES):
        b, h = c // 2, c % 2
        y = R[c]["y"]
        y_prompt[b, h * HALF:(h + 1) * HALF] = y[:HALF]
        y_sample[16 * c:16 * c + 16] = y[8 * 128:].reshape(16, 8, D)
    z = np.zeros
    ret_p = np.stack([R[2 * b + 1]["o_ret_p"] for b in range(4)])[None]
    ret_s = np.concatenate([R[c]["o_ret_s"] for c in range(NCORES)])[None]
    s5p_r = np.stack([R[2 * b + 1]["o_s5p_r"] for b in range(4)])[None].astype(np.float32)
    s5p_i = np.stack([R[2 * b + 1]["o_s5p_i"] for b in range(4)])[None].astype(np.float32)
    s5s_r = np.concatenate([R[c]["o_s5s_r"] for c in range(NCORES)])[None].astype(np.float32)
    s5s_i = np.concatenate([R[c]["o_s5s_i"] for c in range(NCORES)])[None].astype(np.float32)
    return (y_prompt, y_sample, ret_p.astype(np.float32), s5p_r, s5p_i, ret_s.astype(np.float32), s5s_r, s5s_i)
    return (y_prompt, y_sample,
            ret_p.astype(np.float32), z((1, 4, 128, 64), np.float32), z((1, 4, 128, 64), np.float32),
            ret_s.astype(np.float32), z((1, 128, 128, 64), np.float32), z((1, 128, 128, 64), np.float32))
```

```python
import numpy as np
from contextlib import ExitStack
import concourse.bass as bass
import concourse.mybir as mybir
from concourse.bass_utils import run_bass_kernel_spmd

F32 = mybir.dt.float32
BF16 = mybir.dt.bfloat16
AF = mybir.ActivationFunctionType
ALU = mybir.AluOpType

D = 4096
DFF = 11008
NCORES = 8
HALF = 1024
NPRE = 1040
NT_OWN = 9
NT_PRE = 9
ROWS_OWN = NT_OWN * 128
ROWS_PRE = NT_PRE * 128
EPS = 1e-6
CG = 256
DEBUG = False
LAST = {}


class Buf:
    def __init__(self, name):
        self.name = name
        self.writers = []
        self.readers = []
        self.dsem = None
        self.dcount = 0


class Sched:
    def __init__(self, nc, es):
        self.nc = nc
        self.es = es
        self.eng = {}
        for n in ("pe", "act", "dve", "pool", "sp"):
            self.eng[n] = dict(sem=es.enter_context(nc.semaphore("s_" + n)), count=0, ops=[], seen={})
        self.dsems = []
        self.nsem = 0

    def _dsem(self, buf):
        if buf.dsem is None:
            buf.dsem = self.es.enter_context(self.nc.semaphore("d%d" % self.nsem))
            self.nsem += 1
            self.dsems.append(buf)
        return buf.dsem

    def _waits(self, e, reads, writes, skip=None, disjoint=False):
        need = {}

        def add(st):
            s, v = st
            k = id(s)
            if k == skip:
                return
            if k not in need or need[k][1] < v:
                need[k] = (s, v)
        for b in reads:
            for st in b.writers:
                add(st)
        for b in writes:
            if not disjoint:
                for st in b.writers:
                    add(st)
            for st in b.readers:
                add(st)
        out = []
        seen = self.eng[e]["seen"]
        for k, (s, v) in need.items():
            if seen.get(k, 0) < v:
                seen[k] = v
                out.append((s, v))
        return out

    def _commit(self, st, reads, writes, multi=False):
        for b in writes:
            if multi:
                b.writers.append(st)
            else:
                b.writers = [st]
                b.readers = []
        for b in reads:
            b.readers.append(st)
            if len(b.readers) > 64:
                best = {}
                for s, v in b.readers:
                    if id(s) not in best or best[id(s)][1] < v:
                        best[id(s)] = (s, v)
                b.readers = list(best.values())

    def op(self, e, fn, reads=(), writes=(), relax=False, disjoint=False):
        E = self.eng[e]
        waits = self._waits(e, reads, writes, skip=(id(E["sem"]) if (relax and e == "dve") else None), disjoint=disjoint)
        E["count"] += 1
        st = (E["sem"], E["count"])
        E["ops"].append((waits, fn, E["sem"], 1))
        self._commit(st, reads, writes, multi=disjoint)
        return st

    def dma(self, q, out, in_, reads=(), writes=(), owner=None, multi=False):
        E = self.eng[q]
        waits = self._waits(q, reads, writes)
        sem = self._dsem(owner)
        owner.dcount += 16
        st = (sem, owner.dcount)
        E["ops"].append((waits, lambda eng, o=out, i=in_: eng.dma_start(out=o, in_=i), sem, 16))
        self._commit(st, reads, writes, multi=multi)
        return st

    def barrier(self, bufs=()):
        for b in bufs:
            b.writers = []
            b.readers = []
        stamps = []
        for n, E in self.eng.items():
            if E["count"]:
                stamps.append((E["sem"], E["count"]))
        for b in self.dsems:
            stamps.append((b.dsem, b.dcount))
        for n, E in self.eng.items():
            w = []
            for s, v in stamps:
                if E["seen"].get(id(s), 0) < v:
                    E["seen"][id(s)] = v
                    w.append((s, v))
            if w:
                E["ops"].append((w, None, None, 0))

    def emit(self, block):
        def run(E):
            def f(eng):
                for waits, fn, sem, amt in E["ops"]:
                    for s, v in waits:
                        eng.wait_ge(s, v)
                    if fn is None:
                        continue
                    ins = fn(eng)
                    ins.then_inc(sem, amt)
            return f
        block.tensor(run(self.eng["pe"]))
        block.scalar(run(self.eng["act"]))
        block.vector(run(self.eng["dve"]))
        block.gpsimd(run(self.eng["pool"]))
        block.sync(run(self.eng["sp"]))


def build_program():
    nc = bass.Bass("TRN2", target_bir_lowering=False)
    es = ExitStack()
    S = Sched(nc, es)

    def din(name, shape, dt=F32):
        return nc.dram_tensor(name, list(shape), dt, kind="ExternalInput").ap()

    def dout(name, shape, dt=F32):
        return nc.dram_tensor(name, list(shape), dt, kind="ExternalOutput").ap()

    def dscr(name, shape, dt=F32):
        return nc.dram_tensor(name, list(shape), dt, kind=("ExternalOutput" if (DEBUG and name in ("mix", "proj")) else "Internal")).ap()

    x_own = din("x_own", [ROWS_OWN, D])
    x_pre = din("x_pre", [ROWS_PRE, D])
    w_in = din("w_in", [D, 10240])
    cs_own = din("cs_own", [NT_OWN, 128, 512])
    cs_pre = din("cs_pre", [NT_PRE, 128, 512])
    maskd = din("maskd", [8, 128, 2, 128])
    dqk_d = din("dqk", [128, 2, 3, 8])
    cmask_d = din("cmask", [128, 16, 256], BF16)
    rmask_d = din("rmask", [128, 16])
    st_ret = din("st_ret", [16, 8, 256, 256])
    o_ret_p = dout("o_ret_p", [8, 256, 256])
    o_ret_s = dout("o_ret_s", [16, 8, 256, 256])
    lam_re_d = din("lam_re", [128, 64]); lam_im_d = din("lam_im", [128, 64])
    ldt_d = din("ldt", [64, 128]); m3_d = din("m3", [128, 128])
    sel_f_d = din("sel_f", [128, 16]); sel_b_d = din("sel_b", [128, 32], BF16)
    b_re_d = din("b_re", [128, 64, 16]); b_im_d = din("b_im", [128, 64, 16])
    c_re_d = din("c_re", [128, 16, 64]); c_im_d = din("c_im", [128, 16, 64])
    dB_d = din("dB", [128, 2048]); bglu_d = din("bglu", [128, 2048])
    s5r_d = din("s5r", [16, 128, 64]); s5i_d = din("s5i", [16, 128, 64])
    w_glu = din("w_glu", [2048, 2048])
    o_s5p_r = dout("o_s5p_r", [128, 64]); o_s5p_i = dout("o_s5p_i", [128, 64])
    o_s5s_r = dout("o_s5s_r", [16, 128, 64]); o_s5s_i = dout("o_s5s_i", [16, 128, 64])
    Wall_d = dscr("Wall", [128, 128, 512], BF16)
    zscr = dscr("zscr", [ROWS_OWN, 2048], F32)
    s5o = dscr("s5o", [ROWS_OWN, 2048], F32)
    proj = dscr("proj", [ROWS_OWN, 10240], F32)
    projp = dscr("projp", [ROWS_PRE, 10240], F32)
    w_out = din("w_out", [D, D])
    w_gate = din("w_gate", [D, DFF])
    w_up = din("w_up", [D, DFF])
    w_down = din("w_down", [DFF, D])
    g_all = din("g_all", [128, 3, 32])
    gfin = din("gfin", [128, D])
    ident_in = din("ident", [128, 128])
    y_out = dout("y", [ROWS_OWN, D])

    mix = dscr("mix", [ROWS_OWN, D], F32)
    x1 = dscr("x1", [ROWS_OWN, D], F32)
    h2 = dscr("h2", [ROWS_OWN, D], F32)
    act = dscr("act", [ROWS_OWN, DFF], F32)

    def sb(name, shape, dt):
        return es.enter_context(nc.sbuf_tensor(name, list(shape), dt))

    actT = sb("actT", [128, 32, ROWS_OWN], BF16)
    wbf = [sb("wbf%d" % i, [128, 32, CG], BF16) for i in range(2)]
    wst = [sb("wst%d" % i, [128, 8, CG], F32) for i in range(2)]
    xin = [sb("xin%d" % i, [128, D], F32) for i in range(2)]
    xbf = [sb("xbf%d" % i, [128, D], BF16) for i in range(2)]
    ot = [sb("ot%d" % i, [128, CG], F32) for i in range(4)]
    rt = [sb("rt%d" % i, [128, CG], F32) for i in range(4)]
    gtmp = sb("gtmp", [128, NT_OWN, CG], BF16)
    stat = [sb("stat%d" % i, [128, 8], F32) for i in range(2)]
    gains = sb("gains", [128, 3, 32], F32)
    gf_t = sb("gf_t", [128, D], F32)
    ident_f = sb("ident_f", [128, 128], F32)
    ident = sb("ident_b", [128, 128], BF16)

    ps = [es.enter_context(nc.psum_tensor("ps%d" % i, [128, 512], F32)) for i in range(8)]

    B = {}

    def buf(name):
        if name not in B:
            B[name] = Buf(name)
        return B[name]

    cst = buf("const")
    S.dma("sp", gains[:], g_all, writes=[cst], owner=cst, multi=True)
    S.dma("sp", gf_t[:], gfin, writes=[cst], owner=cst, multi=True)
    S.dma("sp", ident_f[:], ident_in, writes=[cst], owner=cst, multi=True)
    S.op("dve", lambda e: e.tensor_copy(out=ident[:], in_=ident_f[:]), reads=[cst], writes=[buf("ident")])

    def load_actT(src, ntiles, kc0, nkc, norm=False):
        W = nkc * 128
        for t in range(ntiles):
            xi, xb, stt = xin[t % 2], xbf[t % 2], stat[t % 2]
            bxi, bxb, bst = buf("xin%d" % (t % 2)), buf("xbf%d" % (t % 2)), buf("stat%d" % (t % 2))
            S.dma("pool", xi[:, 0:W], src[t * 128:(t + 1) * 128, kc0 * 128:kc0 * 128 + W], writes=[bxi], owner=bxi)
            if norm:
                S.op("act", lambda e, xi=xi, stt=stt, xb=xb: e.activation(out=xb[:, 0:W], in_=xi[:, 0:W], func=AF.Square,
                                                                    accum_out=stt[:, 0:1]),
                     reads=[bxi], writes=[bst, bxb])
                S.op("dve", lambda e, stt=stt: e.tensor_scalar(out=stt[:, 1:2], in0=stt[:, 0:1], scalar1=1.0 / W,
                                                               scalar2=EPS, op0=ALU.mult, op1=ALU.add),
                     reads=[bst], writes=[bst])
                S.op("act", lambda e, stt=stt: e.activation(out=stt[:, 2:3], in_=stt[:, 1:2], func=AF.Sqrt),
                     reads=[bst], writes=[bst])
                S.op("dve", lambda e, stt=stt: e.reciprocal(out=stt[:, 3:4], in_=stt[:, 2:3]), reads=[bst], writes=[bst])
                S.op("dve", lambda e, xi=xi, xb=xb, stt=stt: e.tensor_scalar(out=xb[:, 0:W], in0=xi[:, 0:W],
                                                                             scalar1=stt[:, 3:4], scalar2=None,
                                                                             op0=ALU.mult),
                     reads=[bxi, bst], writes=[bxb])
            else:
                S.op("dve", lambda e, xi=xi, xb=xb: e.tensor_copy(out=xb[:, 0:W], in_=xi[:, 0:W]),
                     reads=[bxi], writes=[bxb])
            for q in range((nkc + 3) // 4):
                pb = 6 + (q % 2)
                bp = buf("ps%d" % pb)
                pv = ps[pb].bitcast(BF16)
                nq = min(4, nkc - q * 4)

                def tr(e, xb=xb, pv=pv, q=q, nq=nq):
                    ins = None
                    for i in range(nq):
                        ins = e.transpose(out=pv[:, i * 128:(i + 1) * 128], in_=xb[:, (q * 4 + i) * 128:(q * 4 + i + 1) * 128],
                                          identity=ident[:])
                    return ins
                S.op("pe", tr, reads=[bxb, buf("ident")], writes=[bp])
                dst = actT[:, q * 4:q * 4 + nq, t * 128:(t + 1) * 128]
                srcv = pv[:, 0:nq * 128].rearrange("p (c t) -> p c t", c=nq)
                eng = "act" if q % 2 else "dve"
                if eng == "act":
                    S.op("act", lambda e, dst=dst, srcv=srcv: e.copy(out=dst, in_=srcv), reads=[bp], writes=[buf("actT")], disjoint=True)
                else:
                    S.op("dve", lambda e, dst=dst, srcv=srcv: e.tensor_copy(out=dst, in_=srcv), reads=[bp], writes=[buf("actT")], disjoint=True)

    wcount = [0]
    pcount = [0]
    pending = []

    def stream(Wd, krow0, nkc, col0, ncols, ntiles, consumer, gain=None):
        ngr = (ncols + CG - 1) // CG
        for gi in range(ngr):
            c0 = col0 + gi * CG
            cw = min(CG, col0 + ncols - c0)
            pending.append((Wd, krow0, nkc, c0, cw, ntiles, consumer, gain))

    def _load_group(item):
        Wd, krow0, nkc, c0, cw, ntiles, consumer, gain = item
        wi = wcount[0] % 2
        wcount[0] += 1
        wb, bwb = wbf[wi], buf("wbf%d" % wi)
        nh = (nkc + 7) // 8
        for hh in range(nh):
            k0 = hh * 8
            kn = min(8, nkc - k0)
            si = pcount[0] % 2
            pcount[0] += 1
            ws, bws = wst[si], buf("wst%d" % si)
            S.dma("sp", ws[:, 0:kn, 0:cw],
                  Wd[krow0 + k0 * 128: krow0 + (k0 + kn) * 128, c0:c0 + cw].rearrange("(k p) n -> p k n", p=128),
                  writes=[bws], owner=bws)
            if gain is None:
                if hh % 2:
                    S.op("act", lambda e, wb=wb, ws=ws, k0=k0, kn=kn, cw=cw: e.copy(out=wb[:, k0:k0 + kn, 0:cw], in_=ws[:, 0:kn, 0:cw]),
                         reads=[bws], writes=[bwb], disjoint=True)
                else:
                    S.op("dve", lambda e, wb=wb, ws=ws, k0=k0, kn=kn, cw=cw: e.tensor_copy(out=wb[:, k0:k0 + kn, 0:cw], in_=ws[:, 0:kn, 0:cw]),
                         reads=[bws], writes=[bwb], disjoint=True)
            else:
                for kk in range(kn):
                    kc = k0 + kk
                    gcol = gains[:, gain, (krow0 // 128 + kc):(krow0 // 128 + kc) + 1]
                    if kk % 2:
                        S.op("act", lambda e, wb=wb, ws=ws, kc=kc, kk=kk, cw=cw, gcol=gcol: e.activation(
                            out=wb[:, kc, 0:cw], in_=ws[:, kk, 0:cw], func=AF.Copy, scale=gcol),
                            reads=[bws, cst], writes=[bwb], disjoint=True)
                    else:
                        S.op("dve", lambda e, wb=wb, ws=ws, kc=kc, kk=kk, cw=cw, gcol=gcol: e.tensor_scalar(
                            out=wb[:, kc, 0:cw], in0=ws[:, kk, 0:cw], scalar1=gcol, scalar2=None, op0=ALU.mult),
                            reads=[bws, cst], writes=[bwb], disjoint=True)
        return wb, bwb

    def _compute_group(item, wb, bwb):
        Wd, krow0, nkc, c0, cw, ntiles, consumer, gain = item
        for t in range(ntiles):
            pb = pcount[1] % 6 if len(pcount) > 1 else 0
            pcount[1] += 1
            bp = buf("ps%d" % pb)
            pt = ps[pb][:, 0:cw]

            def mm(e, pt=pt, wb=wb, t=t, cw=cw):
                ins = None
                for kc in range(nkc):
                    ins = e.matmul(pt, actT[:, kc, t * 128:(t + 1) * 128], wb[:, kc, 0:cw], start=(kc == 0), stop=(kc == nkc - 1))
                return ins
            S.op("pe", mm, reads=[buf("actT"), bwb], writes=[bp])
            consumer(t, c0, cw, pt, bp)

    pcount.append(0)

    def flush_stream():
        items = pending[:]
        del pending[:]
        if not items:
            return
        cur = _load_group(items[0])
        for n_, item in enumerate(items):
            nxt = _load_group(items[n_ + 1]) if n_ + 1 < len(items) else None
            _compute_group(item, *cur)
            cur = nxt

    ocnt = [0]

    def next_ot():
        i = ocnt[0] % 4
        ocnt[0] += 1
        return i


    def mk_store(dst):
        def cons(t, c0, cw, pt, bp):
            i = next_ot()
            o, bo = ot[i], buf("ot%d" % i)
            if i % 2:
                S.op("act", lambda e: e.copy(out=o[:, 0:cw], in_=pt), reads=[bp], writes=[bo])
            else:
                S.op("dve", lambda e: e.tensor_copy(out=o[:, 0:cw], in_=pt), reads=[bp], writes=[bo])
            S.dma("pool", dst[t * 128:(t + 1) * 128, c0:c0 + cw], o[:, 0:cw], reads=[bo], writes=[buf("projd")], owner=bo, multi=True)
        return cons

    load_actT(x_pre, NT_PRE, 0, 32, norm=True)
    stream(w_in, 0, 32, 2048, 4096, NT_PRE, mk_store(projp), gain=0)
    stream(w_in, 0, 32, 8192, 2048, NT_PRE, mk_store(projp), gain=0)
    flush_stream()
    S.barrier(B.values())
    load_actT(x_own, NT_OWN, 0, 32, norm=True)
    stream(w_in, 0, 32, 0, 10240, NT_OWN, mk_store(proj), gain=0)
    flush_stream()
    S.barrier(B.values())

    flat = actT[:].rearrange("p a b -> p (a b)")
    coff = [0]

    def carve(shape, dt):
        n = 1
        for d_ in shape[1:]:
            n *= d_
        nb = n * (4 if dt == F32 else 2)
        a = flat[:, coff[0] // 2:(coff[0] + nb) // 2]
        coff[0] += nb
        if dt == F32:
            a = a.bitcast(F32)
        if len(shape) == 3:
            a = a.rearrange("p (a b) -> p a b", a=shape[1])
        return a

    qkvg = [carve([128, 4, 256], F32) for _ in range(2)]
    cst_ = [carve([128, 2, 256], F32) for _ in range(2)]
    tqs = [[carve([128, 256], F32) for _ in range(4)] for _ in range(2)]
    qb, qdb, kb, kdb, vb = [[carve([128, 256], BF16) for _ in range(2)] for _ in range(5)]
    trs = [carve([128, 6, 128], BF16) for _ in range(2)]
    smb = [carve([128, 128], BF16) for _ in range(2)]
    Sf2 = [carve([128, 2, 256], F32) for _ in range(2)]
    Sb2 = [carve([128, 2, 256], BF16) for _ in range(2)]
    msk2 = [carve([128, 2, 128], F32) for _ in range(2)]
    bnst = [carve([128, 8], F32) for _ in range(2)]
    yt = [carve([128, 256], F32) for _ in range(2)]
    sgt = [carve([128, 256], F32) for _ in range(2)]
    S0 = [carve([128, 2, 256], F32) for _ in range(2)]
    S0b = [carve([128, 2, 256], BF16) for _ in range(2)]
    So = [carve([128, 2, 256], F32) for _ in range(2)]
    vm = [carve([128, 256], BF16) for _ in range(2)]
    qm = [carve([128, 2, 128], BF16) for _ in range(2)]
    cmask = carve([128, 16, 256], BF16)
    rmask = carve([128, 16], F32)
    dqk = carve([128, 2, 24], F32)
    odbg = [carve([128, 256], F32) for _ in range(2)]

    cst2 = buf("const2")
    S.dma("sp", cmask, cmask_d, writes=[cst2], owner=cst2, multi=True)
    S.dma("sp", rmask, rmask_d, writes=[cst2], owner=cst2, multi=True)
    S.dma("sp", dqk, dqk_d.rearrange("p a k h -> p a (k h)"), writes=[cst2], owner=cst2, multi=True)
    GAM = [1.0 - 2.0 ** (-5.0 - h_) for h_ in range(8)]
    rcount = [0]

    def ret_tile(hd, par, src, csd, t, kind, full, sample):
        i = par
        rcount[0] += 1
        Sf, Sb, msk = Sf2[par], Sb2[par], msk2[par]
        tq1, tq2, tk1, tk2 = tqs[par]
        TRB, SCB, STB = (6, 2)[par], (7, 3)[par], (5, 4)[par]
        qi, bqi = qkvg[i], buf("qkvg%d" % i)
        ci, bci = cst_[i], buf("cs%d" % i)
        srcv = src[t * 128:(t + 1) * 128, :].rearrange("p (s h d) -> p s h d", s=5, h=8)[:, 0:4, hd, :]
        S.dma("sp", qi, srcv, writes=[bqi], owner=bqi)
        S.dma("sp", ci, csd[t].rearrange("p (a d) -> p a d", a=2), writes=[bci], owner=bci)
        dq_col = dqk[:, 0, kind * 8 + hd:kind * 8 + hd + 1]
        dk_col = dqk[:, 1, kind * 8 + hd:kind * 8 + hd + 1]

        def rot(x, t1, t2, bt1, bt2):
            S.op("dve", lambda e: e.tensor_tensor(out=t1, in0=x, in1=ci[:, 0, :], op=ALU.mult), reads=[bqi, bci], writes=[bt1], relax=True)
            S.op("dve", lambda e: e.tensor_tensor(out=t2[:, 0:128], in0=x[:, 128:256], in1=ci[:, 1, 0:128], op=ALU.mult), reads=[bqi, bci], writes=[bt2], relax=True)
            S.op("dve", lambda e: e.tensor_tensor(out=t2[:, 128:256], in0=x[:, 0:128], in1=ci[:, 1, 128:256], op=ALU.mult), reads=[bqi, bci], writes=[bt2], relax=True)
            S.op("dve", lambda e: e.tensor_tensor(out=t1, in0=t1, in1=t2, op=ALU.add), reads=[bt1, bt2], writes=[bt1], relax=True)

        bk1, bk2 = buf("tk1_%d" % par), buf("tk2_%d" % par)
        rot(qi[:, 1, :], tk1, tk2, bk1, bk2)
        kd_, bkd = kdb[i], buf("kdb%d" % i)
        v_, bv = vb[i], buf("vb%d" % i)
        S.op("dve", lambda e: e.tensor_scalar(out=kd_, in0=tk1, scalar1=dk_col, scalar2=None, op0=ALU.mult), reads=[bk1, cst2], writes=[bkd])
        S.op("act", lambda e: e.copy(out=v_, in_=qi[:, 2, :]), reads=[bqi], writes=[bv])
        yield
        bSf, bSb = buf("Sf%d" % par), buf("Sb%d" % par)
        gl = GAM[hd] ** {0: 128, 1: 8, 2: 16}[kind]
        if full:
            bq1, bq2 = buf("tq1_%d" % par), buf("tq2_%d" % par)
            rot(qi[:, 0, :], tq1, tq2, bq1, bq2)
            q_, bq = qb[i], buf("qb%d" % i)
            qd_, bqd = qdb[i], buf("qdb%d" % i)
            k_, bk = kb[i], buf("kb%d" % i)
            S.op("act", lambda e: e.copy(out=q_, in_=tq1), reads=[bq1], writes=[bq])
            S.op("dve", lambda e: e.tensor_scalar(out=qd_, in0=tq1, scalar1=dq_col, scalar2=None, op0=ALU.mult), reads=[bq1, cst2], writes=[bqd])
            S.op("act", lambda e: e.mul(out=k_, in_=tk1, mul=1.0 / 16.0), reads=[bk1], writes=[bk])
            bp6 = buf("ps%d" % TRB)
            pv = ps[TRB].bitcast(BF16)

            def tr(e):
                ins = None
                for n_, srcb in enumerate((q_, qd_, k_)):
                    for c_ in range(2):
                        ins = e.transpose(out=pv[:, (2 * n_ + c_) * 128:(2 * n_ + c_ + 1) * 128], in_=srcb[:, c_ * 128:(c_ + 1) * 128], identity=ident[:])
                return ins
            S.op("pe", tr, reads=[bq, bqd, bk, buf("ident")], writes=[bp6])
            yield
            tr_, btr = trs[i], buf("trs%d" % i)
            S.op("dve", lambda e: e.tensor_copy(out=tr_, in_=pv[:, 0:768].rearrange("p (a b) -> p a b", a=6)), reads=[bp6], writes=[btr])
            bp7 = buf("ps%d" % SCB)

            def sc(e):
                e.matmul(ps[SCB][:, 0:128], tr_[:, 4, :], tr_[:, 0, :], start=True, stop=False)
                return e.matmul(ps[SCB][:, 0:128], tr_[:, 5, :], tr_[:, 1, :], start=False, stop=True)
            S.op("pe", sc, reads=[btr], writes=[bp7])
            yield
            sm_, bsm = smb[i], buf("smb%d" % i)
            S.op("dve", lambda e: e.tensor_tensor(out=sm_, in0=ps[SCB][:, 0:128], in1=msk[:, kind, :], op=ALU.mult), reads=[bp7, buf("msk%d" % par)], writes=[bsm])
            pbo = par
            bpo = buf("ps%d" % pbo)
            po = ps[pbo][:, 0:256]
            if not sample:
                def om(e):
                    e.matmul(po, sm_, v_, start=True, stop=False)
                    e.matmul(po, tr_[:, 2, :], Sb[:, 0, :], start=False, stop=False)
                    return e.matmul(po, tr_[:, 3, :], Sb[:, 1, :], start=False, stop=True)
                S.op("pe", om, reads=[bsm, bv, btr, bSb], writes=[bpo])
            else:
                S.op("pe", lambda e: e.matmul(po, sm_, v_, start=True, stop=False, skip_group_check=True), reads=[bsm, bv], writes=[bpo])
        if not sample:
            bp5 = buf("ps%d" % STB)

            def su(e):
                e.matmul(ps[STB][:, 0:256], kd_[:, 0:128], v_, start=True, stop=True)
                return e.matmul(ps[STB][:, 256:512], kd_[:, 128:256], v_, start=True, stop=True)
            S.op("pe", su, reads=[bkd, bv], writes=[bp5])
            yield
            S.op("dve", lambda e: e.scalar_tensor_tensor(out=Sf, in0=Sf, scalar=float(gl), in1=ps[STB][:, 0:512].rearrange("p (a b) -> p a b", a=2),
                                                         op0=ALU.mult, op1=ALU.add), reads=[bSf, bp5], writes=[bSf])
            S.op("act", lambda e: e.copy(out=Sb, in_=Sf), reads=[bSf], writes=[bSb])
        else:
            for sq in range(16):
                j = sq % 2
                s0, bs0 = S0[j], buf("S0_%d" % j)
                s0b, bs0b = S0b[j], buf("S0b_%d" % j)
                so, bso = So[j], buf("So_%d" % j)
                vm_, bvm = vm[j], buf("vm%d" % j)
                qm_, bqm = qm[j], buf("qm%d" % j)
                S.dma("sp", s0, st_ret[sq, hd].rearrange("(c p) v -> p c v", p=128), writes=[bs0], owner=bs0)
                S.op("act", lambda e, s0=s0, s0b=s0b: e.copy(out=s0b, in_=s0), reads=[bs0], writes=[bs0b])
                S.op("dve", lambda e, qm_=qm_, sq=sq: e.tensor_tensor(out=qm_, in0=tr_[:, 2:4, :], in1=cmask[:, sq, :].rearrange("p (a b) -> p a b", a=2), op=ALU.mult),
                     reads=[btr, cst2], writes=[bqm])

                def im(e, qm_=qm_, s0b=s0b, sq=sq):
                    e.matmul(po, qm_[:, 0, :], s0b[:, 0, :], start=False, stop=False, skip_group_check=True)
                    return e.matmul(po, qm_[:, 1, :], s0b[:, 1, :], start=False, stop=(sq == 15), skip_group_check=True)
                S.op("pe", im, reads=[bqm, bs0b], writes=[bpo])
                S.op("dve", lambda e, vm_=vm_, sq=sq: e.tensor_scalar(out=vm_, in0=v_, scalar1=rmask[:, sq:sq + 1], scalar2=None, op0=ALU.mult),
                     reads=[bv, cst2], writes=[bvm])
                pbs = 4 + (sq % 2)
                bps = buf("ps%d" % pbs)

                def su2(e, vm_=vm_, pbs=pbs):
                    e.matmul(ps[pbs][:, 0:256], kd_[:, 0:128], vm_, start=True, stop=True)
                    return e.matmul(ps[pbs][:, 256:512], kd_[:, 128:256], vm_, start=True, stop=True)
                S.op("pe", su2, reads=[bkd, bvm], writes=[bps])
                S.op("dve", lambda e, so=so, s0=s0, pbs=pbs: e.scalar_tensor_tensor(out=so, in0=s0, scalar=float(gl), in1=ps[pbs][:, 0:512].rearrange("p (a b) -> p a b", a=2),
                                                                                   op0=ALU.mult, op1=ALU.add), reads=[bs0, bps], writes=[bso])
                S.dma("pool", o_ret_s[sq, hd].rearrange("(c p) v -> p c v", p=128), so, reads=[bso], writes=[buf("orets")], owner=bso, multi=True)
                yield
        if full:
            bb, bbn = bnst[i], buf("bnst%d" % i)
            y_, by = yt[i], buf("yt%d" % i)
            sg_, bsg = sgt[i], buf("sgt%d" % i)
            S.op("act", lambda e: e.activation(out=sg_, in_=qi[:, 3, :], func=AF.Silu), reads=[bqi], writes=[bsg])
            od_, bod = odbg[i], buf("odbg%d" % i)
            S.op("act", lambda e: e.copy(out=od_, in_=po), reads=[bpo], writes=[bod])
            yield
            if DEBUG:
                S.dma("pool", mix[t * 128:(t + 1) * 128, 2048 + hd * 256:2048 + (hd + 1) * 256], od_, reads=[bod], writes=[buf("mixd")], owner=bod, multi=True)
            S.op("dve", lambda e: e.bn_stats(out=bb[:, 0:6], in_=od_), reads=[bod], writes=[bbn])
            S.op("dve", lambda e: e.bn_aggr(out=bb[:, 6:8], in_=bb[:, 0:6]), reads=[bbn], writes=[bbn])
            S.op("dve", lambda e: e.tensor_scalar(out=bb[:, 0:1], in0=bb[:, 7:8], scalar1=1e-5, scalar2=None, op0=ALU.add), reads=[bbn], writes=[bbn])
            S.op("act", lambda e: e.activation(out=bb[:, 1:2], in_=bb[:, 0:1], func=AF.Sqrt), reads=[bbn], writes=[bbn])
            yield
            S.op("dve", lambda e: e.reciprocal(out=bb[:, 2:3], in_=bb[:, 1:2]), reads=[bbn], writes=[bbn])
            S.op("dve", lambda e: e.tensor_scalar(out=y_, in0=od_, scalar1=bb[:, 6:7], scalar2=bb[:, 2:3], op0=ALU.subtract, op1=ALU.mult),
                 reads=[bod, bbn], writes=[by])
            S.op("dve", lambda e: e.tensor_tensor(out=y_, in0=y_, in1=sg_, op=ALU.mult), reads=[by, bsg], writes=[by])
            S.dma("pool", mix[t * 128:(t + 1) * 128, hd * 256:(hd + 1) * 256], y_, reads=[by], writes=[buf("mixd")], owner=by, multi=True)

    def run_gens(gens):
        gens = list(gens)
        while gens:
            for g_ in list(gens):
                try:
                    next(g_)
                except StopIteration:
                    gens.remove(g_)

    for hp in range(4):
        hds = (2 * hp, 2 * hp + 1)
        for par, hd in enumerate(hds):
            S.dma("sp", msk2[par], maskd[hd], reads=[], writes=[buf("msk%d" % par)], owner=buf("msk%d" % par))
            S.op("dve", lambda e, par=par: e.memset(Sf2[par], 0.0), writes=[buf("Sf%d" % par)])
            S.op("dve", lambda e, par=par: e.memset(Sb2[par], 0.0), writes=[buf("Sb%d" % par)])
        for t in range(NT_PRE):
            run_gens([ret_tile(hd, par, projp, cs_pre, t, 0 if t < 8 else 2, False, False) for par, hd in enumerate(hds)])
        for t in range(8):
            run_gens([ret_tile(hd, par, proj, cs_own, t, 0, True, False) for par, hd in enumerate(hds)])
        for par, hd in enumerate(hds):
            S.dma("sp", o_ret_p[hd].rearrange("(c p) v -> p c v", p=128), Sf2[par], reads=[buf("Sf%d" % par)], writes=[buf("oretp")], owner=buf("Sf%d" % par), multi=True)
        run_gens([ret_tile(hd, par, proj, cs_own, 8, 1, True, True) for par, hd in enumerate(hds)])
    flush_stream()
    S.barrier(B.values())

    coff[0] = 0
    PI = 3.141592653589793
    MAGIC = 12582912.0
    bS = buf("s5setup")

    def T64(n=1):
        a = carve([128, n, 128], F32) if n > 1 else carve([128, 128], F32)
        return a

    lam_sb = carve([128, 2, 64], F32)
    S.dma("sp", lam_sb[:, 0, :], lam_re_d, writes=[bS], owner=bS, multi=True)
    S.dma("sp", lam_sb[:, 1, :], lam_im_d, writes=[bS], owner=bS, multi=True)
    lamT = T64(2)
    ldt = T64()
    S.dma("sp", ldt[0:64, :], ldt_d, writes=[bS], owner=bS, multi=True)
    m3 = carve([128, 128], F32)
    S.dma("sp", m3, m3_d, writes=[bS], owner=bS, multi=True)
    selc = carve([128, 64], F32)
    S.dma("sp", selc[:, 0:16], sel_f_d, writes=[bS], owner=bS, multi=True)
    selb = carve([128, 32], BF16)
    S.dma("sp", selb, sel_b_d, writes=[bS], owner=bS, multi=True)
    mask8, tmask = selc[:, 0:8], selc[:, 8:16]
    Jsel, Csel = selb[:, 0:16], selb[:, 16:32]

    A1, A2 = carve([128, 2, 128], F32), carve([128, 2, 128], F32)
    B1, B2 = carve([128, 2, 128], F32), carve([128, 2, 128], F32)
    main_off = coff[0]

    sop_state = [False]
    sop_extra = [()]

    def sop(eng, fn, after_recip=False):
        S.op(eng, fn, reads=[bS] + list(sop_extra[0]), writes=[bS], relax=(eng == "dve" and not sop_state[0]))
        sop_state[0] = after_recip

    def trans_f32(dst, src, rows_in, cols_in):
        bp = buf("ps7")
        S.op("pe", lambda e: e.transpose(out=ps[7][0:cols_in, 0:rows_in], in_=src, identity=ident_f[0:rows_in, 0:rows_in]), reads=[bS, cst] + list(sop_extra[0]), writes=[bp])
        S.op("dve", lambda e: e.tensor_copy(out=dst, in_=ps[7][0:cols_in, 0:rows_in]), reads=[bp, bS], writes=[bS])

    trans_f32(lamT[0:64, 0, :], lam_sb[:, 0, :], 128, 64)
    trans_f32(lamT[0:64, 1, :], lam_sb[:, 1, :], 128, 64)
    dtT = T64()
    lrT, liT = T64(), T64()
    sop("act", lambda e: e.activation(out=dtT[0:64, :], in_=ldt[0:64, :], func=AF.Exp))
    sop("dve", lambda e: e.tensor_tensor(out=lrT[0:64, :], in0=lamT[0:64, 0, :], in1=dtT[0:64, :], op=ALU.mult))
    sop("dve", lambda e: e.tensor_tensor(out=liT[0:64, :], in0=lamT[0:64, 1, :], in1=dtT[0:64, :], op=ALU.mult))
    AR, AI, NR, NI = T64(9), T64(9), T64(9), T64(9)
    tA, tB, tC, tD = T64(), T64(), T64(), T64()

    def trig(dst, k, off):
        sop("dve", lambda e: e.tensor_scalar(out=tA[0:64, :], in0=liT[0:64, :], scalar1=float(k), scalar2=float(off), op0=ALU.mult, op1=ALU.add))
        sop("dve", lambda e: e.tensor_scalar(out=tB[0:64, :], in0=tA[0:64, :], scalar1=1.0 / (2 * PI), scalar2=MAGIC, op0=ALU.mult, op1=ALU.add))
        sop("dve", lambda e: e.tensor_scalar(out=tB[0:64, :], in0=tB[0:64, :], scalar1=MAGIC, scalar2=2 * PI, op0=ALU.subtract, op1=ALU.mult))
        sop("dve", lambda e: e.tensor_tensor(out=tA[0:64, :], in0=tA[0:64, :], in1=tB[0:64, :], op=ALU.subtract))
        sop("dve", lambda e: e.tensor_scalar(out=tA[0:64, :], in0=tA[0:64, :], scalar1=-3.1415925, scalar2=3.1415925, op0=ALU.max, op1=ALU.min))
        sop("act", lambda e: e.activation(out=dst, in_=tA[0:64, :], func=AF.Sin))

    for k in range(9):
        trig(tC[0:64, :], k, PI / 2)
        trig(tD[0:64, :], k, 0.0)
        sop("act", lambda e, k=k: e.activation(out=AR[0:64, k, :], in_=lrT[0:64, :], func=AF.Exp, scale=float(k)))
        sop("act", lambda e, k=k: e.activation(out=NR[0:64, k, :], in_=lrT[0:64, :], func=AF.Exp, scale=float(-k)))
        sop("dve", lambda e, k=k: e.tensor_tensor(out=AI[0:64, k, :], in0=AR[0:64, k, :], in1=tD[0:64, :], op=ALU.mult))
        sop("dve", lambda e, k=k: e.tensor_tensor(out=AR[0:64, k, :], in0=AR[0:64, k, :], in1=tC[0:64, :], op=ALU.mult))
        sop("dve", lambda e, k=k: e.scalar_tensor_tensor(out=NI[0:64, k, :], in0=NR[0:64, k, :], scalar=-1.0, in1=tD[0:64, :], op0=ALU.mult, op1=ALU.mult))
        sop("dve", lambda e, k=k: e.tensor_tensor(out=NR[0:64, k, :], in0=NR[0:64, k, :], in1=tC[0:64, :], op=ALU.mult))
    fr, fi = T64(), T64()
    sop("dve", lambda e: e.tensor_scalar(out=tA[0:64, :], in0=AR[0:64, 1, :], scalar1=-1.0, scalar2=None, op0=ALU.add))
    sop("dve", lambda e: e.tensor_tensor(out=tB[0:64, :], in0=lamT[0:64, 0, :], in1=lamT[0:64, 0, :], op=ALU.mult))
    sop("dve", lambda e: e.tensor_tensor(out=tC[0:64, :], in0=lamT[0:64, 1, :], in1=lamT[0:64, 1, :], op=ALU.mult))
    sop("dve", lambda e: e.tensor_tensor(out=tB[0:64, :], in0=tB[0:64, :], in1=tC[0:64, :], op=ALU.add))
    sop("dve", lambda e: e.reciprocal(out=tB[0:64, :], in_=tB[0:64, :]), after_recip=True)
    sop("dve", lambda e: e.tensor_tensor(out=tC[0:64, :], in0=tA[0:64, :], in1=lamT[0:64, 0, :], op=ALU.mult))
    sop("dve", lambda e: e.tensor_tensor(out=tD[0:64, :], in0=AI[0:64, 1, :], in1=lamT[0:64, 1, :], op=ALU.mult))
    sop("dve", lambda e: e.tensor_tensor(out=tC[0:64, :], in0=tC[0:64, :], in1=tD[0:64, :], op=ALU.add))
    sop("dve", lambda e: e.tensor_tensor(out=fr[0:64, :], in0=tC[0:64, :], in1=tB[0:64, :], op=ALU.mult))
    sop("dve", lambda e: e.tensor_tensor(out=tC[0:64, :], in0=AI[0:64, 1, :], in1=lamT[0:64, 0, :], op=ALU.mult))
    sop("dve", lambda e: e.tensor_tensor(out=tD[0:64, :], in0=tA[0:64, :], in1=lamT[0:64, 1, :], op=ALU.mult))
    sop("dve", lambda e: e.tensor_tensor(out=tC[0:64, :], in0=tC[0:64, :], in1=tD[0:64, :], op=ALU.subtract))
    sop("dve", lambda e: e.tensor_tensor(out=fi[0:64, :], in0=tC[0:64, :], in1=tB[0:64, :], op=ALU.mult))
    ER, EI, ENR, ENI = T64(8), T64(8), T64(8), T64(8)

    def cmul_f(dr, di, xr, xi):
        sop("dve", lambda e: e.tensor_tensor(out=tA[0:64, :], in0=xr, in1=fr[0:64, :], op=ALU.mult))
        sop("dve", lambda e: e.tensor_tensor(out=tB[0:64, :], in0=xi, in1=fi[0:64, :], op=ALU.mult))
        sop("dve", lambda e: e.tensor_tensor(out=dr, in0=tA[0:64, :], in1=tB[0:64, :], op=ALU.subtract))
        sop("dve", lambda e: e.tensor_tensor(out=tA[0:64, :], in0=xr, in1=fi[0:64, :], op=ALU.mult))
        sop("dve", lambda e: e.tensor_tensor(out=tB[0:64, :], in0=xi, in1=fr[0:64, :], op=ALU.mult))
        sop("dve", lambda e: e.tensor_tensor(out=di, in0=tA[0:64, :], in1=tB[0:64, :], op=ALU.add))

    for s_ in range(8):
        cmul_f(ER[0:64, s_, :], EI[0:64, s_, :], AR[0:64, 7 - s_, :], AI[0:64, 7 - s_, :])
        cmul_f(ENR[0:64, s_, :], ENI[0:64, s_, :], NR[0:64, s_ + 1, :], NI[0:64, s_ + 1, :])
    sop("dve", lambda e: e.tensor_copy(out=A1[0:64, 0, :], in_=AR[0:64, 8, :]))
    sop("dve", lambda e: e.tensor_copy(out=A1[0:64, 1, :], in_=AR[0:64, 8, :]))
    sop("dve", lambda e: e.tensor_scalar(out=A2[0:64, 0, :], in0=AI[0:64, 8, :], scalar1=-1.0, scalar2=None, op0=ALU.mult))
    sop("dve", lambda e: e.tensor_copy(out=A2[0:64, 1, :], in_=AI[0:64, 8, :]))
    sop("dve", lambda e: e.tensor_tensor(out=tA[0:64, :], in0=AR[0:64, 8, :], in1=AR[0:64, 8, :], op=ALU.mult))
    sop("dve", lambda e: e.tensor_tensor(out=tB[0:64, :], in0=AI[0:64, 8, :], in1=AI[0:64, 8, :], op=ALU.mult))
    sop("dve", lambda e: e.tensor_tensor(out=B1[0:64, 0, :], in0=tA[0:64, :], in1=tB[0:64, :], op=ALU.subtract))
    sop("dve", lambda e: e.tensor_copy(out=B1[0:64, 1, :], in_=B1[0:64, 0, :]))
    sop("dve", lambda e: e.tensor_tensor(out=tA[0:64, :], in0=AR[0:64, 8, :], in1=AI[0:64, 8, :], op=ALU.mult))
    sop("dve", lambda e: e.tensor_scalar(out=B2[0:64, 1, :], in0=tA[0:64, :], scalar1=2.0, scalar2=None, op0=ALU.mult))
    sop("dve", lambda e: e.tensor_scalar(out=B2[0:64, 0, :], in0=tA[0:64, :], scalar1=-2.0, scalar2=None, op0=ALU.mult))

    Bp2 = [carve([128, 2, 128], F32) for _ in range(2)]
    Cb2 = [carve([128, 2, 64], F32) for _ in range(2)]
    CT = carve([128, 2, 128], F32)
    W1p = xin[0][:, 0:2048].rearrange("p (a b) -> p a b", a=2)
    W1n = xin[1][:, 0:2048].rearrange("p (a b) -> p a b", a=2)
    W2p = carve([128, 2, 1024], F32)
    tW = carve([128, 1024], F32)
    Wst = carve([128, 8, 512], BF16)
    bWst = buf("Wst")
    S.op("dve", lambda e: e.memset(Wst, 0.0), reads=[bS], writes=[bWst])

    def v4(ap):
        return ap.rearrange("p (g s c) -> p g s c", g=8, s=8)

    def bc_tab(tab, g0):
        return tab[0:64, :, g0:g0 + 8].rearrange("p s g -> p g s").unsqueeze(3).broadcast_to([64, 8, 8, 16])

    def bc_gc(x):
        return x.rearrange("p (g c) -> p g c", g=8).unsqueeze(2).broadcast_to([64, 8, 8, 16])

    def cprod(dst_r, dst_i, tr_, ti_, xr, xi, g0, neg_i=False):
        sop("dve", lambda e: e.tensor_tensor(out=v4(dst_r), in0=bc_tab(tr_, g0), in1=bc_gc(xr), op=ALU.mult))
        sop("dve", lambda e: e.tensor_tensor(out=v4(tW[0:64, :]), in0=bc_tab(ti_, g0), in1=bc_gc(xi), op=ALU.mult))
        sop("dve", lambda e: e.tensor_tensor(out=dst_r, in0=dst_r, in1=tW[0:64, :], op=ALU.subtract))
        sop("dve", lambda e: e.tensor_tensor(out=v4(dst_i), in0=bc_tab(tr_, g0), in1=bc_gc(xi), op=ALU.mult))
        sop("dve", lambda e: e.tensor_tensor(out=v4(tW[0:64, :]), in0=bc_tab(ti_, g0), in1=bc_gc(xr), op=ALU.mult))
        if neg_i:
            sop("dve", lambda e: e.scalar_tensor_tensor(out=dst_i, in0=dst_i, scalar=-1.0, in1=tW[0:64, :], op0=ALU.mult, op1=ALU.subtract))
        else:
            sop("dve", lambda e: e.tensor_tensor(out=dst_i, in0=dst_i, in1=tW[0:64, :], op=ALU.add))

    def in_bufs(k):
        return [buf("s5in%d_%d" % (k, n_)) for n_ in range(4)]

    def issue_in(fc):
        k = fc % 2
        g0 = fc * 8
        b_ = in_bufs(k)
        S.dma("sp", Bp2[k][0:64, 0, :].rearrange("p (g c) -> p g c", g=8), b_re_d[g0:g0 + 8].rearrange("g p c -> p g c"), writes=[b_[0]], owner=b_[0])
        S.dma("sp", Bp2[k][0:64, 1, :].rearrange("p (g c) -> p g c", g=8), b_im_d[g0:g0 + 8].rearrange("g p c -> p g c"), writes=[b_[1]], owner=b_[1])
        S.dma("sp", Cb2[k][:, 0, :], c_re_d[g0:g0 + 8].rearrange("g c p -> (g c) p"), writes=[b_[2]], owner=b_[2])
        S.dma("sp", Cb2[k][:, 1, :], c_im_d[g0:g0 + 8].rearrange("g c p -> (g c) p"), writes=[b_[3]], owner=b_[3])

    issue_in(0)
    for fc in range(16):
        g0 = fc * 8
        if fc + 1 < 16:
            issue_in(fc + 1)
        Bp, Cblk = Bp2[fc % 2], Cb2[fc % 2]
        sop_extra[0] = tuple(in_bufs(fc % 2))
        trans_f32(CT[0:64, 0, :], Cblk[:, 0, :], 128, 64)
        trans_f32(CT[0:64, 1, :], Cblk[:, 1, :], 128, 64)
        cprod(W1p[0:64, 0, :], W1p[0:64, 1, :], ER, EI, Bp[0:64, 0, :], Bp[0:64, 1, :], g0)
        cprod(W1n[0:64, 0, :], W1n[0:64, 1, :], ENR, ENI, Bp[0:64, 0, :], Bp[0:64, 1, :], g0)
        cprod(W2p[0:64, 0, :], W2p[0:64, 1, :], AR[:, 1:9, :], AI[:, 1:9, :], CT[0:64, 0, :], CT[0:64, 1, :], g0, neg_i=True)
        for half in range(2):
            pb = 2 + half
            bp = buf("ps%d" % pb)

            def w3mm(e, half=half, pb=pb):
                ins = None
                for gg in range(4):
                    g = half * 4 + gg
                    e.matmul(ps[pb][:, gg * 128:(gg + 1) * 128], W1n[0:64, 0, g * 128:(g + 1) * 128], W2p[0:64, 0, g * 128:(g + 1) * 128], start=True, stop=False)
                    ins = e.matmul(ps[pb][:, gg * 128:(gg + 1) * 128], W1n[0:64, 1, g * 128:(g + 1) * 128], W2p[0:64, 1, g * 128:(g + 1) * 128], start=False, stop=True)
                return ins
            S.op("pe", w3mm, reads=[bS], writes=[bp])
            S.op("dve", lambda e, half=half, pb=pb: e.tensor_tensor(out=Wst[:, half * 4:(half + 1) * 4, 128:256],
                                                                   in0=ps[pb][:, 0:512].rearrange("p (g n) -> p g n", g=4),
                                                                   in1=m3.unsqueeze(1).broadcast_to([128, 4, 128]), op=ALU.mult),
                 reads=[bp, bS], writes=[bWst], disjoint=True)
        for half in range(2):
            pb = 4 + half
            bp = buf("ps%d" % pb)

            def w1tr(e, half=half, pb=pb):
                ins = None
                for gg in range(4):
                    g = half * 4 + gg
                    for ri in range(2):
                        ins = e.transpose(out=ps[pb][:, (gg * 2 + ri) * 64:(gg * 2 + ri + 1) * 64], in_=W1p[0:64, ri, g * 128:(g + 1) * 128],
                                          identity=ident_f[0:64, 0:64])
                return ins
            S.op("pe", w1tr, reads=[bS, cst], writes=[bp])
            S.op("act", lambda e, half=half, pb=pb: e.copy(out=Wst[:, half * 4:(half + 1) * 4, 0:128],
                                                           in_=ps[pb][:, 0:512].rearrange("p (g n) -> p g n", g=4)),
                 reads=[bp, bS], writes=[bWst], disjoint=True)
        S.op("act", lambda e: e.copy(out=Wst[0:64, :, 256:384], in_=W2p[0:64, 0, :].rearrange("p (g n) -> p g n", g=8)), reads=[bS], writes=[bWst], disjoint=True)
        S.op("act", lambda e: e.copy(out=Wst[0:64, :, 384:512], in_=W2p[0:64, 1, :].rearrange("p (g n) -> p g n", g=8)), reads=[bS], writes=[bWst], disjoint=True)
        S.dma("sp", Wall_d[g0:g0 + 8].rearrange("g p w -> p g w"), Wst, reads=[bWst], writes=[buf("Walld")], owner=bWst, multi=True)
        sop_extra[0] = ()
    flush_stream()
    S.barrier(B.values())

    coff[0] = main_off
    GB = 32
    NW = GB * 16
    def wview(tn, is_f32):
        f = tn[:].bitcast(BF16) if is_f32 else tn[:].rearrange("p a b -> p (a b)")
        return f.rearrange("p (g w) -> p g w", g=16)
    Wsets = [(wview(wbf[0], False), wview(wbf[1], False)), (wview(xin[0], True), wview(xin[1], True))]
    Wbufs = [("wbf0", "wbf1"), ("xin0", "xin1")]
    u32 = [carve([128, NW], F32) for _ in range(2)] + [sb("u32c", [128, NW], F32)[:]]
    Uexp = [carve([128, GB, 128], BF16) for _ in range(2)]
    Usb = [carve([128, GB, 16], BF16) for _ in range(2)] + [sb("Usbc", [128, GB, 16], BF16)[:]]
    Vsb = [carve([128, 2, NW], F32) for _ in range(2)]
    Xh = carve([128, 2, GB * 17], F32)
    Xalls = [carve([128, 2, NW], BF16), sb("Xallb", [128, 2, NW], BF16)[:]]
    sT1, sT2 = carve([128, 2, GB], F32), carve([128, 2, GB], F32)
    Ysb = carve([128, GB, 16], BF16)
    Yexp = carve([128, GB, 128], BF16)
    ytmp = [carve([128, NW], F32) for _ in range(2)]
    zt_ = [carve([128, NW], F32) for _ in range(2)]
    dB = xbf[0][:].bitcast(F32)
    X0 = carve([128, 2, NW], F32)
    Xn = carve([128, 2, NW], F32)
    bT1 = sb("bT1", [128, 2, NW], F32)[:]
    bT2 = gtmp[:].rearrange("p a b -> p (a b)")[:, 0:2048].bitcast(F32).rearrange("p (r q) -> p r q", r=2)
    st_in = Yexp[:].rearrange("p a b -> p (a b)").bitcast(F32).rearrange("p (r q) -> p r q", r=2)
    st_o = [carve([128, 2, 64], F32) for _ in range(2)]
    cst3 = buf("const3")
    S.dma("sp", dB, dB_d, writes=[cst3], owner=cst3, multi=True)
    ucount = [0]
    xhv = Xh[0:64, :, :].rearrange("p r (g j) -> p r g j", g=GB)

    def Wg(gb, g):
        return Wsets[gb % 2][g // 16][:, g % 16, :]

    def s5_front(gb, src, t, wi):
        g0 = gb * GB
        i = wi % 2
        i3 = wi % 3
        bWs = [buf(n) for n in Wbufs[gb % 2]]
        u_, bu = u32[i3], buf("u32_%d" % i3)
        S.dma("sp", u_, src[t * 128:(t + 1) * 128, 8192 + g0 * 16:8192 + (g0 + GB) * 16], writes=[bu], owner=bu)
        ue, bUe = Uexp[i], buf("Uexp%d" % i)
        for s_ in range(8):
            if s_ % 2:
                S.op("act", lambda e, s_=s_: e.activation(out=ue[:, :, s_ * 16:(s_ + 1) * 16], in_=u_.rearrange("p (g c) -> p g c", g=GB),
                                                           func=AF.Copy, scale=mask8[:, s_:s_ + 1]), reads=[bu, bS], writes=[bUe], disjoint=True)
            else:
                S.op("dve", lambda e, s_=s_: e.tensor_scalar(out=ue[:, :, s_ * 16:(s_ + 1) * 16], in0=u_.rearrange("p (g c) -> p g c", g=GB),
                                                              scalar1=mask8[:, s_:s_ + 1], scalar2=None, op0=ALU.mult), reads=[bu, bS], writes=[bUe], disjoint=True)
        bp0 = buf("ps0")

        def umm(e):
            ins = None
            for g in range(GB):
                ins = e.matmul(ps[0][:, g * 16:(g + 1) * 16], ue[:, g, :], Jsel, start=True, stop=True, skip_group_check=True)
            return ins
        S.op("pe", umm, reads=[bUe, bS], writes=[bp0])
        us, bUs = Usb[i3], buf("Usb%d" % i3)
        S.op("act", lambda e: e.copy(out=us, in_=ps[0][:, 0:NW].rearrange("p (g j) -> p g j", g=GB)), reads=[bp0], writes=[bUs])
        bp1, bp2 = buf("ps1"), buf("ps2")

        def vmm(e):
            ins = None
            for g in range(GB):
                e.matmul(ps[1][0:64, g * 16:(g + 1) * 16], Wg(gb, g)[:, 0:64], us[:, g, :], start=True, stop=True, skip_group_check=True)
                ins = e.matmul(ps[2][0:64, g * 16:(g + 1) * 16], Wg(gb, g)[:, 64:128], us[:, g, :], start=True, stop=True, skip_group_check=True)
            return ins
        S.op("pe", vmm, reads=bWs + [bUs], writes=[bp1, bp2])
        vs, bVs = Vsb[i], buf("Vsb%d" % i)
        S.op("act", lambda e: e.copy(out=vs[0:64, 0, :], in_=ps[1][0:64, 0:NW]), reads=[bp1], writes=[bVs])
        S.op("act", lambda e: e.copy(out=vs[0:64, 1, :], in_=ps[2][0:64, 0:NW]), reads=[bp2], writes=[bVs])

    def s5_mid(gb, t, wi, nvalid, full, sample):
        g0 = gb * GB
        i = wi % 2
        Xall = Xalls[i]
        vs, bVs = Vsb[i], buf("Vsb%d" % i)
        vsv = vs[0:64, :, :].rearrange("p r (g j) -> p r g j", g=GB)
        bXh, bXa = buf("Xh"), buf("Xall%d" % i)
        a1 = A1[0:64, :, g0:g0 + GB]
        a2 = A2[0:64, :, g0:g0 + GB]
        if not sample:
            def step(src_j, dst_j, add_ap, c1, c2, badd):
                S.op("dve", lambda e: e.tensor_tensor(out=sT1[0:64, :, :], in0=xhv[:, :, :, src_j], in1=c1, op=ALU.mult), reads=[bXh, bS], writes=[buf("sT1")], relax=True)
                S.op("dve", lambda e: e.tensor_tensor(out=sT2[0:64, 0, :], in0=xhv[:, 1, :, src_j], in1=c2[:, 0, :], op=ALU.mult), reads=[bXh, bS], writes=[buf("sT2")], relax=True)
                S.op("dve", lambda e: e.tensor_tensor(out=sT2[0:64, 1, :], in0=xhv[:, 0, :, src_j], in1=c2[:, 1, :], op=ALU.mult), reads=[bXh, bS], writes=[buf("sT2")], relax=True)
                S.op("dve", lambda e: e.tensor_tensor(out=sT1[0:64, :, :], in0=sT1[0:64, :, :], in1=sT2[0:64, :, :], op=ALU.add), reads=[buf("sT1"), buf("sT2")], writes=[buf("sT1")], relax=True)
                S.op("dve", lambda e: e.tensor_tensor(out=xhv[:, :, :, dst_j], in0=sT1[0:64, :, :], in1=add_ap, op=ALU.add), reads=[buf("sT1"), badd], writes=[bXh], relax=True)
            if nvalid == 16:
                pv_ = bT1[0:64, :, 0:GB * 8].rearrange("p r (g m) -> p r g m", g=GB)
                p2_ = bT2[0:64, :, 0:GB * 8].rearrange("p r (g m) -> p r g m", g=GB)
                ve = vsv[:, :, :, 0:16:2]
                vo = vsv[:, :, :, 1:16:2]
                bP, bP2 = buf("bT1"), buf("bT2")
                S.op("dve", lambda e: e.tensor_tensor(out=pv_, in0=ve, in1=a1.unsqueeze(3).broadcast_to([64, 2, GB, 8]), op=ALU.mult), reads=[bVs, bS], writes=[bP], relax=True)
                S.op("dve", lambda e: e.tensor_tensor(out=p2_[:, 0, :, :], in0=ve[:, 1, :, :], in1=a2[:, 0, :].unsqueeze(2).broadcast_to([64, GB, 8]), op=ALU.mult), reads=[bVs, bS], writes=[bP2], relax=True)
                S.op("dve", lambda e: e.tensor_tensor(out=p2_[:, 1, :, :], in0=ve[:, 0, :, :], in1=a2[:, 1, :].unsqueeze(2).broadcast_to([64, GB, 8]), op=ALU.mult), reads=[bVs, bS], writes=[bP2], relax=True)
                S.op("dve", lambda e: e.tensor_tensor(out=pv_, in0=pv_, in1=p2_, op=ALU.add), reads=[bP, bP2], writes=[bP], relax=True)
                S.op("dve", lambda e: e.tensor_tensor(out=pv_, in0=pv_, in1=vo, op=ALU.add), reads=[bP, bVs], writes=[bP], relax=True)
                b1 = B1[0:64, :, g0:g0 + GB]
                b2 = B2[0:64, :, g0:g0 + GB]
                for m in range(8):
                    step(2 * m, 2 * m + 2, pv_[:, :, :, m], b1, b2, bP)
                xe = xhv[:, :, :, 0:16:2]
                xo = xhv[:, :, :, 1:16:2]
                S.op("dve", lambda e: e.tensor_tensor(out=pv_, in0=xe, in1=a1.unsqueeze(3).broadcast_to([64, 2, GB, 8]), op=ALU.mult), reads=[bXh, bS], writes=[bP], relax=True)
                S.op("dve", lambda e: e.tensor_tensor(out=p2_[:, 0, :, :], in0=xe[:, 1, :, :], in1=a2[:, 0, :].unsqueeze(2).broadcast_to([64, GB, 8]), op=ALU.mult), reads=[bXh, bS], writes=[bP2], relax=True)
                S.op("dve", lambda e: e.tensor_tensor(out=p2_[:, 1, :, :], in0=xe[:, 0, :, :], in1=a2[:, 1, :].unsqueeze(2).broadcast_to([64, GB, 8]), op=ALU.mult), reads=[bXh, bS], writes=[bP2], relax=True)
                S.op("dve", lambda e: e.tensor_tensor(out=pv_, in0=pv_, in1=p2_, op=ALU.add), reads=[bP, bP2], writes=[bP], relax=True)
                S.op("dve", lambda e: e.tensor_tensor(out=xo, in0=pv_, in1=ve, op=ALU.add), reads=[bP, bVs, bXh], writes=[bXh], relax=True)
            else:
                for j in range(nvalid):
                    step(j, j + 1, vsv[:, :, :, j], a1, a2, bVs)
            if full:
                S.op("act", lambda e: e.copy(out=Xall[0:64, :, :].rearrange("p r (g j) -> p r g j", g=GB), in_=xhv[:, :, :, 0:16]), reads=[bXh], writes=[bXa])
            S.op("dve", lambda e: e.tensor_copy(out=xhv[:, :, :, 0], in_=xhv[:, :, :, nvalid]), reads=[bXh, bXa], writes=[bXh], relax=True)
        else:
            x0v = X0[0:64, :, :].rearrange("p r (g j) -> p r g j", g=GB)
            t1v = bT1[0:64, :, :].rearrange("p r (g j) -> p r g j", g=GB)
            t2v = bT2[0:64, :, :].rearrange("p r (g j) -> p r g j", g=GB)
            S.op("act", lambda e: e.copy(out=Xall[0:64, :, :], in_=X0[0:64, :, :]), reads=[buf("X0")], writes=[bXa])
            S.op("dve", lambda e: e.tensor_tensor(out=t1v, in0=x0v, in1=a1.unsqueeze(3).broadcast_to([64, 2, GB, 16]), op=ALU.mult), reads=[buf("X0"), bS], writes=[buf("bT1")])
            S.op("dve", lambda e: e.tensor_tensor(out=t2v[:, 0, :, :], in0=x0v[:, 1, :, :], in1=a2[:, 0, :].unsqueeze(2).broadcast_to([64, GB, 16]), op=ALU.mult), reads=[buf("X0"), bS], writes=[buf("bT2")])
            S.op("dve", lambda e: e.tensor_tensor(out=t2v[:, 1, :, :], in0=x0v[:, 0, :, :], in1=a2[:, 1, :].unsqueeze(2).broadcast_to([64, GB, 16]), op=ALU.mult), reads=[buf("X0"), bS], writes=[buf("bT2")])
            S.op("dve", lambda e: e.tensor_tensor(out=bT1[0:64, :, :], in0=bT1[0:64, :, :], in1=bT2[0:64, :, :], op=ALU.add), reads=[buf("bT1"), buf("bT2")], writes=[buf("bT1")])
            S.op("dve", lambda e: e.tensor_tensor(out=Xn[0:64, :, :], in0=bT1[0:64, :, :], in1=vs[0:64, :, :], op=ALU.add), reads=[buf("bT1"), bVs], writes=[buf("Xn")])

    def s5_back(gb, t, wi, nvalid, full, sample):
        if not full:
            return
        g0 = gb * GB
        i = wi % 2
        i3 = wi % 3
        Xall = Xalls[i]
        bXa = buf("Xall%d" % i)
        bWs = [buf(n) for n in Wbufs[gb % 2]]
        u_, bu = u32[i3], buf("u32_%d" % i3)
        us, bUs = Usb[i3], buf("Usb%d" % i3)
        bp3 = buf("ps3")
        xr_ = Xall[0:64, 0, :].rearrange("p (g j) -> p g j", g=GB)
        xi_ = Xall[0:64, 1, :].rearrange("p (g j) -> p g j", g=GB)

        def ymm(e):
            ins = None
            for g in range(GB):
                o_ = ps[3][:, g * 16:(g + 1) * 16]
                W = Wg(gb, g)
                e.matmul(o_, W[0:64, 256:384], xr_[:, g, :], start=True, stop=False, skip_group_check=True)
                e.matmul(o_, W[0:64, 384:512], xi_[:, g, :], start=False, stop=False, skip_group_check=True)
                ins = e.matmul(o_, W[:, 128:256], us[:, g, :], start=False, stop=True, skip_group_check=True)
            return ins
        S.op("pe", ymm, reads=bWs + [bXa, bUs], writes=[bp3])
        bYs, bYe = buf("Ysb"), buf("Yexp")
        S.op("act", lambda e: e.copy(out=Ysb, in_=ps[3][:, 0:NW].rearrange("p (g j) -> p g j", g=GB)), reads=[bp3], writes=[bYs])
        yev = Yexp[:, :, :].rearrange("p g (j t) -> p g j t", j=16)
        for t_ in range(8):
            if t_ % 2:
                S.op("act", lambda e, t_=t_: e.activation(out=yev[:, :, :, t_], in_=Ysb, func=AF.Copy, scale=tmask[:, t_:t_ + 1]),
                     reads=[bYs, bS], writes=[bYe], disjoint=True)
            else:
                S.op("dve", lambda e, t_=t_: e.tensor_scalar(out=yev[:, :, :, t_], in0=Ysb, scalar1=tmask[:, t_:t_ + 1], scalar2=None, op0=ALU.mult),
                     reads=[bYs, bS], writes=[bYe], disjoint=True)
        bp4 = buf("ps4")

        def pmm(e):
            ins = None
            for g in range(GB):
                ins = e.matmul(ps[4][:, g * 16:(g + 1) * 16], Yexp[:, g, :], Csel, start=True, stop=True, skip_group_check=True)
            return ins
        S.op("pe", pmm, reads=[bYe, bS], writes=[bp4])
        yt_, byt = ytmp[i], buf("ytmp%d" % i)
        z_, bz = zt_[i], buf("zt%d" % i)
        S.op("dve", lambda e: e.tensor_tensor(out=yt_, in0=u_, in1=dB[:, g0 * 16:(g0 + GB) * 16], op=ALU.mult), reads=[bu, cst3], writes=[byt])
        S.op("dve", lambda e: e.tensor_tensor(out=yt_, in0=yt_, in1=ps[4][:, 0:NW], op=ALU.add), reads=[byt, bp4], writes=[byt])
        S.op("act", lambda e: e.activation(out=z_, in_=yt_, func=AF.Square), reads=[byt], writes=[bz])
        S.op("dve", lambda e: e.tensor_scalar(out=z_, in0=z_, scalar1=0.044715, scalar2=1.0, op0=ALU.mult, op1=ALU.add), reads=[bz], writes=[bz])
        S.op("dve", lambda e: e.tensor_tensor(out=z_, in0=z_, in1=yt_, op=ALU.mult), reads=[bz, byt], writes=[bz])
        S.op("act", lambda e: e.activation(out=z_, in_=z_, func=AF.Sigmoid, scale=1.5957691216057308), reads=[bz], writes=[bz])
        S.op("dve", lambda e: e.tensor_tensor(out=z_, in0=z_, in1=yt_, op=ALU.mult), reads=[bz, byt], writes=[bz])
        S.dma("pool", zscr[t * 128:(t + 1) * 128, g0 * 16:(g0 + GB) * 16], z_, reads=[bz], writes=[buf("zscrd")], owner=bz, multi=True)

    def emit_state(gb, srcX, dst_r, dst_i, bsrc):
        g0 = gb * GB
        k = ucount[0] % 2
        ucount[0] += 1
        bp = buf("ps5")

        def tr(e):
            e.transpose(out=ps[5][0:GB, 0:64], in_=srcX[:, 0, :], identity=ident_f[0:64, 0:64])
            return e.transpose(out=ps[5][0:GB, 64:128], in_=srcX[:, 1, :], identity=ident_f[0:64, 0:64])
        S.op("pe", tr, reads=[bsrc, cst], writes=[bp])
        so_, bso = st_o[k], buf("st_o%d" % k)
        S.op("dve", lambda e: e.tensor_copy(out=so_[0:GB, :, :], in_=ps[5][0:GB, 0:128].rearrange("p (r q) -> p r q", r=2)), reads=[bp], writes=[bso])
        S.dma("pool", dst_r[g0:g0 + GB, :], so_[0:GB, 0, :], reads=[bso], writes=[buf("os5")], owner=bso, multi=True)
        S.dma("pool", dst_i[g0:g0 + GB, :], so_[0:GB, 1, :], reads=[bso], writes=[buf("os5")], owner=bso, multi=True)

    for gb in range(128 // GB):
        g0 = gb * GB
        for hh in range(2):
            bW = buf(Wbufs[gb % 2][hh])
            S.dma("sp", Wsets[gb % 2][hh], Wall_d[g0 + hh * 16:g0 + (hh + 1) * 16].rearrange("g p w -> p g w"), writes=[bW], owner=bW)
        bsi = buf("Yexp")
        S.dma("sp", st_in[0:GB, 0, :].rearrange("g (s p) -> g s p", s=16), s5r_d[:, g0:g0 + GB, :].rearrange("s g p -> g s p"), writes=[bsi], owner=bsi)
        S.dma("sp", st_in[0:GB, 1, :].rearrange("g (s p) -> g s p", s=16), s5i_d[:, g0:g0 + GB, :].rearrange("s g p -> g s p"), reads=[bsi], writes=[bsi], owner=bsi)
        x0v = X0[0:64, :, :].rearrange("p r (g j) -> p r g j", g=GB)
        for ri in range(2):
            for q4 in range(4):
                bp = buf("ps6")

                def trs_(e, ri=ri, q4=q4):
                    ins = None
                    for jj in range(4):
                        j = q4 * 4 + jj
                        ins = e.transpose(out=ps[6][0:64, jj * GB:(jj + 1) * GB], in_=st_in[0:GB, ri, j * 64:(j + 1) * 64], identity=ident_f[0:GB, 0:GB])
                    return ins
                S.op("pe", trs_, reads=[bsi, cst], writes=[bp])
                S.op("dve", lambda e, ri=ri, q4=q4: e.tensor_copy(out=x0v[:, ri, :, q4 * 4:(q4 + 1) * 4],
                                                                  in_=ps[6][0:64, 0:4 * GB].rearrange("p (j g) -> p g j", j=4)),
                     reads=[bp], writes=[buf("X0")])
        S.op("dve", lambda e: e.memset(Xh, 0.0), writes=[buf("Xh")])
        work = [(projp, t, 16 if t < 8 else 2, False, False) for t in range(NT_PRE)]
        work += [(proj, t, 16, True, False) for t in range(8)]
        work += [(proj, 8, 16, True, True)]
        s5_front(gb, work[0][0], work[0][1], 0)
        for wi, (src, t, nv, full, sample) in enumerate(work):
            if wi + 1 < len(work):
                s5_front(gb, work[wi + 1][0], work[wi + 1][1], wi + 1)
            if sample:
                emit_state(gb, xhv[:, :, :, 0], o_s5p_r, o_s5p_i, buf("Xh"))
            s5_mid(gb, t, wi, nv, full, sample)
            if wi >= 1:
                p_ = work[wi - 1]
                s5_back(gb, p_[1], wi - 1, p_[2], p_[3], p_[4])
        p_ = work[-1]
        s5_back(gb, p_[1], len(work) - 1, p_[2], p_[3], p_[4])
        xnv = Xn[0:64, :, :].rearrange("p r (g j) -> p r g j", g=GB)
        for j in range(16):
            emit_state(gb, xnv[:, :, :, j], o_s5s_r[j], o_s5s_i[j], buf("Xn"))
    flush_stream()
    S.barrier(B.values())

    coff[0] = 16 * ROWS_OWN * 2
    bgl = carve([128, 2048], F32)
    S.dma("sp", bgl, bglu_d, writes=[cst3], owner=cst3, multi=True)

    def cons_glu(t, c0, cw, pt, bp):
        i = next_ot()
        o, bo, r, br = ot[i], buf("ot%d" % i), rt[i], buf("rt%d" % i)
        S.dma("sp", r[:, 0:cw], zscr[t * 128:(t + 1) * 128, c0:c0 + cw], writes=[br], owner=br)
        S.op("dve", lambda e: e.tensor_tensor(out=o[:, 0:cw], in0=pt, in1=bgl[:, c0:c0 + cw], op=ALU.add), reads=[bp, cst3], writes=[bo])
        S.op("act", lambda e: e.activation(out=o[:, 0:cw], in_=o[:, 0:cw], func=AF.Sigmoid), reads=[bo], writes=[bo])
        S.op("dve", lambda e: e.tensor_tensor(out=o[:, 0:cw], in0=o[:, 0:cw], in1=r[:, 0:cw], op=ALU.mult), reads=[bo, br], writes=[bo])
        S.dma("pool", s5o[t * 128:(t + 1) * 128, c0:c0 + cw], o[:, 0:cw], reads=[bo], writes=[buf("s5od")], owner=bo, multi=True)

    load_actT(zscr, NT_OWN, 0, 16)
    stream(w_glu, 0, 16, 0, 2048, NT_OWN, cons_glu)
    flush_stream()
    S.barrier(B.values())
    for t in range(NT_OWN):
        xi, stt = xin[t % 2], stat[t % 2]
        bxi, bst = buf("xin%d" % (t % 2)), buf("stat%d" % (t % 2))
        S.dma("pool", xi[:, 0:2048], s5o[t * 128:(t + 1) * 128, :], writes=[bxi], owner=bxi)
        S.op("act", lambda e, xi=xi, stt=stt, t=t: e.activation(out=xbf[t % 2][:, 0:2048], in_=xi[:, 0:2048], func=AF.Square, accum_out=stt[:, 0:1]),
             reads=[bxi], writes=[bst, buf("xbf%d" % (t % 2))])
        S.op("dve", lambda e, stt=stt: e.tensor_scalar(out=stt[:, 1:2], in0=stt[:, 0:1], scalar1=1.0 / 2048, scalar2=EPS,
                                                       op0=ALU.mult, op1=ALU.add), reads=[bst], writes=[bst])
        S.op("act", lambda e, stt=stt: e.activation(out=stt[:, 2:3], in_=stt[:, 1:2], func=AF.Sqrt), reads=[bst], writes=[bst])
        S.op("dve", lambda e, stt=stt: e.reciprocal(out=stt[:, 3:4], in_=stt[:, 2:3]), reads=[bst], writes=[bst])
        S.op("dve", lambda e, xi=xi, stt=stt: e.tensor_scalar(out=xi[:, 0:2048], in0=xi[:, 0:2048], scalar1=stt[:, 3:4], scalar2=None, op0=ALU.mult),
             reads=[bxi, bst], writes=[bxi])
        S.dma("pool", mix[t * 128:(t + 1) * 128, 2048:4096], xi[:, 0:2048], reads=[bxi], writes=[buf("mixd")], owner=bxi, multi=True)
    flush_stream()
    S.barrier(B.values())

    def cons_wout(t, c0, cw, pt, bp):
        i = next_ot()
        o, bo, r, br = ot[i], buf("ot%d" % i), rt[i], buf("rt%d" % i)
        S.dma("sp", r[:, 0:cw], x_own[t * 128:(t + 1) * 128, c0:c0 + cw], writes=[br], owner=br)
        S.op("dve", lambda e: e.tensor_tensor(out=o[:, 0:cw], in0=pt, in1=r[:, 0:cw], op=ALU.add), reads=[bp, br], writes=[bo])
        S.dma("pool", x1[t * 128:(t + 1) * 128, c0:c0 + cw], o[:, 0:cw], reads=[bo], writes=[buf("x1d")], owner=bo, multi=True)

    load_actT(mix, NT_OWN, 0, 32, norm=False)
    stream(w_out, 0, 32, 0, D, NT_OWN, cons_wout, gain=1)
    flush_stream()
    S.barrier(B.values())

    load_actT(x1, NT_OWN, 0, 32, norm=True)

    def cons_gate(t, c0, cw, pt, bp):
        S.op("act", lambda e: e.activation(out=gtmp[:, t, 0:cw], in_=pt, func=AF.Silu), reads=[bp], writes=[buf("gtmp%d" % t)])

    def cons_up(t, c0, cw, pt, bp):
        i = next_ot()
        o, bo = ot[i], buf("ot%d" % i)
        S.op("dve", lambda e: e.tensor_tensor(out=o[:, 0:cw], in0=pt, in1=gtmp[:, t, 0:cw], op=ALU.mult),
             reads=[bp, buf("gtmp%d" % t)], writes=[bo])
        S.dma("pool", act[t * 128:(t + 1) * 128, c0:c0 + cw], o[:, 0:cw], reads=[bo], writes=[buf("actd")], owner=bo, multi=True)

    for gi in range(DFF // CG):
        stream(w_gate, 0, 32, gi * CG, CG, NT_OWN, cons_gate, gain=2)
        stream(w_up, 0, 32, gi * CG, CG, NT_OWN, cons_up, gain=2)
    flush_stream()
    S.barrier(B.values())

    def cons_down(t, c0, cw, pt, bp):
        i = next_ot()
        o, bo, r, br = ot[i], buf("ot%d" % i), rt[i], buf("rt%d" % i)
        S.dma("sp", r[:, 0:cw], x1[t * 128:(t + 1) * 128, c0:c0 + cw], writes=[br], owner=br)
        S.op("dve", lambda e: e.tensor_tensor(out=o[:, 0:cw], in0=pt, in1=r[:, 0:cw], op=ALU.add), reads=[bp, br], writes=[bo])
        S.dma("pool", x1[t * 128:(t + 1) * 128, c0:c0 + cw], o[:, 0:cw], reads=[bo], writes=[buf("x1d")], owner=bo, multi=True)

    for kb0, kbn in ((0, 32), (32, 32), (64, 22)):
        load_actT(act, NT_OWN, kb0, kbn)
        stream(w_down, kb0 * 128, kbn, 0, D, NT_OWN, cons_down)
        flush_stream()
    S.barrier(B.values())

    for t in range(NT_OWN):
        xi, stt = xin[t % 2], stat[t % 2]
        bxi, bst = buf("xin%d" % (t % 2)), buf("stat%d" % (t % 2))
        S.dma("pool", xi[:], x1[t * 128:(t + 1) * 128, :], writes=[bxi], owner=bxi)
        S.op("act", lambda e, xi=xi, stt=stt, t=t: e.activation(out=xbf[t % 2][:], in_=xi[:], func=AF.Square, accum_out=stt[:, 0:1]),
             reads=[bxi], writes=[bst, buf("xbf%d" % (t % 2))])
        S.op("dve", lambda e, stt=stt: e.tensor_scalar(out=stt[:, 1:2], in0=stt[:, 0:1], scalar1=1.0 / D, scalar2=EPS,
                                                       op0=ALU.mult, op1=ALU.add), reads=[bst], writes=[bst])
        S.op("act", lambda e, stt=stt: e.activation(out=stt[:, 2:3], in_=stt[:, 1:2], func=AF.Sqrt), reads=[bst], writes=[bst])
        S.op("dve", lambda e, stt=stt: e.reciprocal(out=stt[:, 3:4], in_=stt[:, 2:3]), reads=[bst], writes=[bst])
        S.op("dve", lambda e, xi=xi, stt=stt: e.scalar_tensor_tensor(out=xi[:], in0=xi[:], scalar=stt[:, 3:4], in1=gf_t[:],
                                                                     op0=ALU.mult, op1=ALU.mult),
             reads=[bxi, bst, cst], writes=[bxi])
        S.dma("pool", y_out[t * 128:(t + 1) * 128, :], xi[:], reads=[bxi], writes=[buf("yd")], owner=bxi, multi=True)
    flush_stream()
    S.barrier(B.values())

    with nc.Block() as block:
        S.emit(block)
    es.close()
    return nc


def _cs(pos):
    inv = (np.float32(10000.0) ** (-np.arange(128, dtype=np.float32) / np.float32(128))).astype(np.float32)
    ang = (pos.astype(np.float32)[:, :, None] * inv[None, None, :]).astype(np.float32)
    c_, s_ = np.cos(ang).astype(np.float32), np.sin(ang).astype(np.float32)
    return np.ascontiguousarray(np.concatenate([c_, c_, -s_, s_], axis=-1), dtype=np.float32)


_CONST_CACHE = {}


def _consts():
    if _CONST_CACHE:
        return _CONST_CACHE
    import ml_dtypes
    lg = np.log(1.0 - 2.0 ** (-5.0 - np.arange(8, dtype=np.float32))).astype(np.float32)
    p = np.arange(128)
    mask = np.zeros((8, 128, 2, 128), np.float32)
    diff = (p[None, :] - p[:, None]).astype(np.float32)
    for h_ in range(8):
        m0 = np.where(diff >= 0, np.exp(np.maximum(diff, 0) * lg[h_]), 0.0)
        same = (p[None, :] // 8) == (p[:, None] // 8)
        mask[h_, :, 0, :] = m0
        mask[h_, :, 1, :] = np.where(same, m0, 0.0)
    dqk = np.zeros((128, 2, 3, 8), np.float32)
    for h_ in range(8):
        dqk[:, 0, 0, h_] = np.exp(lg[h_] * (p + 1.0))
        dqk[:, 0, 1, h_] = np.exp(lg[h_] * ((p % 8) + 1.0))
        dqk[:, 0, 2, h_] = np.exp(lg[h_] * ((p % 16) + 1.0))
        dqk[:, 1, 0, h_] = np.exp(lg[h_] * (127.0 - p)) / 16.0
        dqk[:, 1, 1, h_] = np.exp(lg[h_] * (7.0 - (p % 8))) / 16.0
        dqk[:, 1, 2, h_] = np.exp(lg[h_] * (15.0 - (p % 16))) / 16.0
    cm = np.zeros((128, 16, 2, 128), np.float32)
    rm = np.zeros((128, 16), np.float32)
    for sq in range(16):
        cm[:, sq, :, sq * 8:(sq + 1) * 8] = 1.0
        rm[sq * 8:(sq + 1) * 8, sq] = 1.0
    q = np.arange(128)
    m3 = ((q[None, :] // 16) >= (q[:, None] // 16)).astype(np.float32)
    sel_f = np.zeros((128, 16), np.float32)
    sel_f[q, q % 8] = 1.0
    sel_f[q, 8 + q // 16] = 1.0
    sel_b = np.zeros((128, 32), np.float32)
    sel_b[q, q // 8] = 1.0
    sel_b[q, 16 + q % 16] = 1.0
    _CONST_CACHE.update(m3=m3, sel_f=sel_f, sel_b=sel_b.astype(ml_dtypes.bfloat16))
    _CONST_CACHE.update(mask=mask, dqk=dqk, cmask=cm.reshape(128, 16, 256).astype(ml_dtypes.bfloat16), rmask=rm)
    return _CONST_CACHE


def _core_inputs(c, inp):
    b, h = c // 2, c % 2
    xp = np.concatenate([inp["meta_tokens"], inp["x_prompt"][b]], axis=0)
    x_own = np.zeros((ROWS_OWN, D), np.float32)
    x_own[:HALF] = xp[16 + h * HALF:16 + (h + 1) * HALF]
    x_own[8 * 128:] = inp["x_sample"][16 * c:16 * c + 16].reshape(128, D)
    g_all = np.stack([inp["norm1_g"][0].reshape(32, 128).T,
                      np.concatenate([inp["ret_gn_g"][0], inp["s5_norm_g"][0]]).reshape(32, 128).T,
                      inp["norm2_g"][0].reshape(32, 128).T], axis=1)
    x_pre = np.zeros((ROWS_PRE, D), np.float32)
    pos_pre = np.zeros((NT_PRE, 128), np.float32)
    if h == 1:
        x_pre[:NPRE] = xp[:NPRE]
        pos_pre = (np.arange(NT_PRE * 128, dtype=np.float32)).reshape(NT_PRE, 128)
    else:
        x_pre[8 * 128:8 * 128 + 16] = xp[:16]
        pos_pre[8] = np.arange(128)
    C = _consts()
    pos_own = np.zeros((NT_OWN, 128), np.float32)
    for t in range(8):
        pos_own[t] = 16 + h * HALF + t * 128 + np.arange(128)
    pos_own[8] = 16384 + (np.arange(128) % 8)
    return {
        "x_pre": x_pre, "w_in": inp["w_in"][0],
        "cs_own": _cs(pos_own), "cs_pre": _cs(pos_pre),
        "maskd": C["mask"], "dqk": C["dqk"], "cmask": C["cmask"], "rmask": C["rmask"],
        "st_ret": np.ascontiguousarray(inp["state_ret"][0, 16 * c:16 * c + 16]),
        "lam_re": inp["s5_lam_re"][0], "lam_im": inp["s5_lam_im"][0],
        "ldt": np.ascontiguousarray(np.broadcast_to(inp["s5_log_dt"][0][None, :], (64, 128)), dtype=np.float32),
        "m3": C["m3"], "sel_f": C["sel_f"], "sel_b": C["sel_b"],
        "b_re": inp["s5_b_re"][0], "b_im": inp["s5_b_im"][0], "c_re": inp["s5_c_re"][0], "c_im": inp["s5_c_im"][0],
        "dB": np.ascontiguousarray(np.broadcast_to(inp["s5_d"][0][None, :], (128, 2048)), dtype=np.float32),
        "bglu": np.ascontiguousarray(np.broadcast_to(inp["b_glu"][0][None, :], (128, 2048)), dtype=np.float32),
        "s5r": np.ascontiguousarray(inp["state_s5_re"][0, 16 * c:16 * c + 16]),
        "s5i": np.ascontiguousarray(inp["state_s5_im"][0, 16 * c:16 * c + 16]),
        "w_glu": inp["w_glu"][0],
        "x_own": x_own,
        "w_out": inp["w_out"][0], "w_gate": inp["w_gate"][0], "w_up": inp["w_up"][0], "w_down": inp["w_down"][0],
        "g_all": np.ascontiguousarray(g_all, dtype=np.float32),
        "gfin": np.ascontiguousarray(np.broadcast_to(inp["final_norm_g"][None, :], (128, D)), dtype=np.float32),
        "ident": np.eye(128, dtype=np.float32),
    }


def kernel(**inp):
    inp = {k: np.asarray(v) for k, v in inp.items()}
    nc = build_program()
    in_maps = [_core_inputs(c, inp) for c in range(NCORES)]
    res = run_bass_kernel_spmd(nc, in_maps, core_ids=list(range(NCORES)))
    R = res.results
    LAST['R'] = R
    y_prompt = np.zeros((4, 2048, D), np.float32)
    y_sample = np.zeros((128, 8, D), np.float32)
    for c in range(NCORES):
        b, h = c // 2, c % 2
        y = R[c]["y"]
        y_prompt[b, h * HALF:(h + 1) * HALF] = y[:HALF]
        y_sample[16 * c:16 * c + 16] = y[8 * 128:].reshape(16, 8, D)
    z = np.zeros
    ret_p = np.stack([R[2 * b + 1]["o_ret_p"] for b in range(4)])[None]
    ret_s = np.concatenate([R[c]["o_ret_s"] for c in range(NCORES)])[None]
    s5p_r = np.stack([R[2 * b + 1]["o_s5p_r"] for b in range(4)])[None].astype(np.float32)
    s5p_i = np.stack([R[2 * b + 1]["o_s5p_i"] for b in range(4)])[None].astype(np.float32)
    s5s_r = np.concatenate([R[c]["o_s5s_r"] for c in range(NCORES)])[None].astype(np.float32)
    s5s_i = np.concatenate([R[c]["o_s5s_i"] for c in range(NCORES)])[None].astype(np.float32)
    return (y_prompt, y_sample, ret_p.astype(np.float32), s5p_r, s5p_i, ret_s.astype(np.float32), s5s_r, s5s_i)
    return (y_prompt, y_sample,
            ret_p.astype(np.float32), z((1, 4, 128, 64), np.float32), z((1, 4, 128, 64), np.float32),
            ret_s.astype(np.float32), z((1, 128, 128, 64), np.float32), z((1, 128, 128, 64), np.float32))
```
